# Optimizing a Trainium2 kernel written in Bass

```python
import jax, jax.numpy as jnp
from jax import lax
import numpy as np

D_MODEL = 1024
BATCH = 4
SEQ = 4096
DEPTH = 2

N_MIXERS = 2
N_A_LAYERS = (DEPTH + 1) // 2
N_B_LAYERS = DEPTH // 2
CHUNK = 128
A_WIDTH = D_MODEL
A_GROUPS = 16
A_GROUP_DIM = A_WIDTH // A_GROUPS
B_HEADS = 16
B_HEAD_DIM = D_MODEL // B_HEADS
B_PATTERNS = ((128, 1), (512, 4), (2048, 16))
N_PAT = len(B_PATTERNS)
D_FF = 4 * D_MODEL
ALPHA = (2 * DEPTH) ** 0.25
BETA = (8 * DEPTH) ** -0.25
LN_EPS = 1e-5
ADA_SCALE = 0.1
NEG = -1e30

kernel_name = "hybrid_gmlp_dilated_attn_deepnorm_adaln"


def layer_norm(x, g, b):
    xf = x.astype(jnp.float32)
    mu = jnp.mean(xf, axis=-1, keepdims=True)
    var = jnp.mean(jnp.square(xf - mu), axis=-1, keepdims=True)
    y = (xf - mu) * lax.rsqrt(var + LN_EPS) * g.astype(jnp.float32) + b.astype(jnp.float32)
    return y.astype(x.dtype)


def ada_mod(c, w, b):
    m = jax.nn.silu(c) @ w + b
    shift, scale, gate = jnp.split(m, 3, axis=-1)
    return shift[:, None, :], scale[:, None, :], 1.0 + gate[:, None, :]


def alibi_slopes(n_heads):
    h = jnp.arange(1, n_heads + 1, dtype=jnp.float32)
    return jnp.exp2(-8.0 * h / n_heads)


def mixer_a(h, w_in, b_in, vn_g, vn_b, w_s, b_s, w_out):
    B, S, _ = h.shape
    uv = jax.nn.gelu(h @ w_in + b_in)
    u, v = jnp.split(uv, 2, axis=-1)
    v = layer_norm(v, vn_g, vn_b)
    v = v.reshape(B, S // CHUNK, CHUNK, A_GROUPS, A_GROUP_DIM)
    causal = jnp.tril(jnp.ones((CHUNK, CHUNK), dtype=bool))
    w_causal = jnp.where(causal, w_s, 0.0)
    z = jnp.einsum('gts,bnsgc->bntgc', w_causal, v) + b_s.T[:, :, None]
    z = z.reshape(B, S, A_WIDTH)
    return (u * z) @ w_out


def dilated_branch(q, k, v, window, dilation, slopes):
    B, S, H, E = q.shape
    span = window // dilation
    seg = span * dilation
    S_pad = -(-S // seg) * seg
    nb = S_pad // seg
    pad = ((0, 0), (0, S_pad - S), (0, 0), (0, 0))

    def to_blocks(t):
        return jnp.pad(t, pad).reshape(B, nb, span, dilation, H, E)

    def with_prev(t):
        prev = jnp.concatenate([jnp.zeros_like(t[:, :1]), t[:, :-1]], axis=1)
        return jnp.concatenate([prev, t], axis=2)

    qb = to_blocks(q)
    kb = with_prev(to_blocks(k))
    vb = with_prev(to_blocks(v))
    s = jnp.einsum('bnqrhe,bnkrhe->bnrhqk', qb, kb,
                   preferred_element_type=jnp.float32) * (E ** -0.5)
    qi = jnp.arange(span)[:, None]
    ki = jnp.arange(2 * span)[None, :]
    diff = span + qi - ki
    blk = jnp.arange(nb)[:, None, None]
    valid = (diff >= 0) & (diff <= span) & (blk * span + ki - span >= 0)
    bias = -slopes[:, None, None] * (dilation * diff).astype(jnp.float32)
    s = s + bias
    s = jnp.where(valid[None, :, None, None], s, NEG)
    m = jnp.max(s, axis=-1, keepdims=True)
    p = jnp.exp(s - m)
    l = jnp.sum(p, axis=-1, keepdims=True)
    o = jnp.einsum('bnrhqk,bnkrhe->bnqrhe', p / l, vb.astype(jnp.float32))
    lse = (m + jnp.log(l))[..., 0].transpose(0, 1, 4, 2, 3)
    o = o.reshape(B, S_pad, H, E)[:, :S]
    lse = lse.reshape(B, S_pad, H)[:, :S]
    return o, lse


def mixer_b(h, w_qkv, w_out, slopes):
    B, S, _ = h.shape
    qkv = (h @ w_qkv).reshape(B, S, N_PAT, 3, B_HEADS, B_HEAD_DIM)
    outs, lses = [], []
    for g, (window, dilation) in enumerate(B_PATTERNS):
        o, lse = dilated_branch(qkv[:, :, g, 0], qkv[:, :, g, 1], qkv[:, :, g, 2],
                                window, dilation, slopes)
        outs.append(o)
        lses.append(lse)
    wts = jax.nn.softmax(jnp.stack(lses, axis=0), axis=0)[..., None]
    o = jnp.sum(wts * jnp.stack(outs, axis=0), axis=0)
    return o.reshape(B, S, B_HEADS * B_HEAD_DIM).astype(h.dtype) @ w_out


def squared_relu_mlp(h, w_up, w_down):
    return jnp.square(jax.nn.relu(h @ w_up)) @ w_down


def setup_inputs(seed: int = 0) -> dict:
    key = jax.random.key(seed)
    ks = jax.random.split(key, 18)
    f32 = jnp.float32
    nrm = lambda k, shape: jax.random.normal(k, shape, dtype=f32)
    return {
        "x": nrm(ks[0], (BATCH, SEQ, D_MODEL)),
        "c": nrm(ks[1], (BATCH, D_MODEL)),
        "ada_w": nrm(ks[2], (DEPTH, 2, D_MODEL, 3 * D_MODEL)) * (D_MODEL ** -0.5) * ADA_SCALE,
        "ada_b": nrm(ks[3], (DEPTH, 2, 3 * D_MODEL)) * 0.01,
        "ln_g": 1.0 + 0.02 * nrm(ks[4], (DEPTH, 2, D_MODEL)),
        "ln_b": 0.02 * nrm(ks[5], (DEPTH, 2, D_MODEL)),
        "a_w_in": nrm(ks[6], (N_A_LAYERS, D_MODEL, 2 * A_WIDTH)) * (D_MODEL ** -0.5),
        "a_b_in": 0.02 * nrm(ks[7], (N_A_LAYERS, 2 * A_WIDTH)),
        "a_vn_g": 1.0 + 0.02 * nrm(ks[8], (N_A_LAYERS, A_WIDTH)),
        "a_vn_b": 0.02 * nrm(ks[9], (N_A_LAYERS, A_WIDTH)),
        "a_w_s": nrm(ks[10], (N_A_LAYERS, A_GROUPS, CHUNK, CHUNK)) * (CHUNK ** -0.5),
        "a_b_s": 1.0 + 0.02 * nrm(ks[11], (N_A_LAYERS, A_GROUPS, CHUNK)),
        "a_w_out": nrm(ks[12], (N_A_LAYERS, A_WIDTH, D_MODEL)) * (A_WIDTH ** -0.5) * BETA,
        "b_w_qkv": nrm(ks[13], (N_B_LAYERS, D_MODEL, N_PAT * 3 * B_HEADS * B_HEAD_DIM)) * (D_MODEL ** -0.5),
        "b_w_out": nrm(ks[14], (N_B_LAYERS, B_HEADS * B_HEAD_DIM, D_MODEL)) * ((B_HEADS * B_HEAD_DIM) ** -0.5) * BETA,
        "mlp_w_up": nrm(ks[15], (DEPTH, D_MODEL, D_FF)) * (D_MODEL ** -0.5),
        "mlp_w_down": nrm(ks[16], (DEPTH, D_FF, D_MODEL)) * (D_FF ** -0.5) * BETA,
    }


def reference(x, c, ada_w, ada_b, ln_g, ln_b, a_w_in, a_b_in, a_vn_g, a_vn_b, a_w_s, a_b_s,
              a_w_out, b_w_qkv, b_w_out, mlp_w_up, mlp_w_down):
    slopes = alibi_slopes(B_HEADS)
    for i in range(DEPTH):
        j = i // N_MIXERS
        shift, scale, gate = ada_mod(c, ada_w[i, 0], ada_b[i, 0])
        h = x * (1.0 + scale) + shift
        if i % N_MIXERS == 0:
            y = mixer_a(h, a_w_in[j], a_b_in[j], a_vn_g[j], a_vn_b[j], a_w_s[j], a_b_s[j], a_w_out[j])
        else:
            y = mixer_b(h, b_w_qkv[j], b_w_out[j], slopes)
        x = layer_norm(ALPHA * x + gate * y, ln_g[i, 0], ln_b[i, 0])
        shift, scale, gate = ada_mod(c, ada_w[i, 1], ada_b[i, 1])
        h = x * (1.0 + scale) + shift
        y = squared_relu_mlp(h, mlp_w_up[i], mlp_w_down[i])
        x = layer_norm(ALPHA * x + gate * y, ln_g[i, 1], ln_b[i, 1])
    return x
```

```python
import contextlib
import numpy as np
import concourse.bass as bass
import concourse.mybir as mybir
from concourse.bass_utils import run_bass_kernel_spmd

F32 = mybir.dt.float32
BF16 = mybir.dt.bfloat16
AF = mybir.ActivationFunctionType
ALU = mybir.AluOpType
AX = mybir.AxisListType

D = 1024
T = 2048
NT = 4
TT = 512
NCORES = 8
ALPHA = 4.0 ** 0.25
INV_ALPHA = 1.0 / ALPHA
LN_EPS = 1e-5
EPS_P = LN_EPS / (ALPHA * ALPHA)
GELU = AF.Gelu_apprx_tanh


class Eng:
    def __init__(self, name, sem):
        self.name = name
        self.sem = sem
        self.count = 0
        self.ops = []
        self.waited = {}


class Prog:
    def __init__(self, nc, es, nsem=100):
        self.nc = nc
        self.free_sems = [es.enter_context(nc.semaphore(f"s{i}")) for i in range(nsem)]
        self.E = {}
        for n in ["pe", "act", "dve", "pool", "sp"]:
            self.E[n] = Eng(n, self.free_sems.pop())
        self.last_w = {}
        self.readers = {}
        self.sem_id = {}

    def new_sem(self):
        return self.free_sems.pop()

    def slot(self):
        sl = {"sem": self.new_sem(), "count": 0}
        self.slots = getattr(self, "slots", [])
        self.slots.append(sl)
        return sl

    def barrier(self, extra=()):
        toks = [(e.sem, e.count) for e in self.E.values() if e.count > 0]
        toks += [(sl["sem"], sl["count"]) for sl in getattr(self, "slots", []) if sl["count"] > 0]
        toks += list(extra)
        for eng in self.E:
            self.wait_all(eng, toks)

    def _sid(self, s):
        return id(s)

    @staticmethod
    def _norm(reads, writes):
        r2, w2 = [], list(writes)
        for k in reads:
            if isinstance(k, tuple) and k and k[0] == "ps":
                if k not in w2:
                    w2.append(k)
            else:
                r2.append(k)
        return r2, w2

    def _deps(self, eng, reads, writes, extra):
        toks = []
        for r in reads:
            t = self.last_w.get(r)
            if t is not None:
                toks.append(t)
        for w in writes:
            t = self.last_w.get(w)
            if t is not None:
                toks.append(t)
            for t in self.readers.get(w, {}).values():
                toks.append(t)
        for t in extra:
            if t is not None:
                toks.append(t)
        e = self.E[eng]
        waits = []
        for (s, v) in toks:
            if eng == "pe" and s is e.sem:
                continue
            k = self._sid(s)
            if e.waited.get(k, 0) < v:
                e.waited[k] = v
                waits.append((s, v))
        return waits

    def _record(self, tok, reads, writes):
        for r in reads:
            d = self.readers.setdefault(r, {})
            k = self._sid(tok[0])
            if k not in d or d[k][1] < tok[1]:
                d[k] = tok
        for w in writes:
            self.last_w[w] = tok
            self.readers[w] = {}

    def op(self, eng, fn, reads=(), writes=(), extra=()):
        reads, writes = self._norm(reads, writes)
        e = self.E[eng]
        waits = self._deps(eng, reads, writes, extra)
        e.count += 1
        tok = (e.sem, e.count)
        e.ops.append((waits, fn, (e.sem, 1)))
        self._record(tok, reads, writes)
        return tok

    def mm_group(self, fns, reads=(), writes=(), extra=()):
        reads, writes = self._norm(reads, writes)
        e = self.E["pe"]
        waits = self._deps("pe", reads, writes, extra)
        for i, fn in enumerate(fns):
            last = i == len(fns) - 1
            e.ops.append((waits if i == 0 else [], fn, (e.sem, 1) if last else None))
        e.count += 1
        tok = (e.sem, e.count)
        self._record(tok, reads, writes)
        return tok

    def dma(self, eng, pairs, slot, reads=(), writes=(), extra=()):
        e = self.E[eng]
        waits = self._deps(eng, reads, writes, extra)
        for i, (o, i_) in enumerate(pairs):
            e.ops.append((waits if i == 0 else [],
                          (lambda h, o=o, i_=i_: h.dma_start(out=o, in_=i_)),
                          (slot["sem"], 16)))
            slot["count"] += 16
        tok = (slot["sem"], slot["count"])
        self._record(tok, reads, writes)
        return tok

    def wait_all(self, eng, toks):
        e = self.E[eng]
        waits = self._deps(eng, (), (), toks)
        e.ops.append((waits, None, None))

    def replay(self, eng, h):
        for waits, fn, inc in self.E[eng].ops:
            for s, v in waits:
                h.wait_ge(s, v)
            if fn is None:
                continue
            inst = fn(h)
            if inc is not None:
                inst.then_inc(inc[0], inc[1])

    def finish(self):
        nc = self.nc
        with nc.Block() as block:
            @block.tensor
            def _(h):
                self.replay("pe", h)

            @block.scalar
            def _(h):
                self.replay("act", h)

            @block.vector
            def _(h):
                self.replay("dve", h)

            @block.gpsimd
            def _(h):
                self.replay("pool", h)

            @block.sync
            def _(h):
                self.replay("sp", h)


class Arena:
    def __init__(self, nc, base=16512, cap=229376):
        self.nc = nc
        self.cap = cap
        self.off = base
        self.n = 0

    def alloc(self, shape, dtype, at=None, name=None):
        nbytes = int(np.prod(shape[1:])) * (4 if dtype == F32 else 2)
        nbytes = (nbytes + 63) // 64 * 64
        if at is None:
            at = self.off
            self.off += nbytes
            assert self.off <= self.cap, f"SBUF overflow {self.off}"
        else:
            assert at + nbytes <= self.cap, "SBUF overflow (at)"
        self.n += 1
        t = self.nc.alloc_sbuf_tensor_at(name or f"sb{self.n}", list(shape), dtype, offset=at, align_bytes=64)
        return t, at, nbytes


class Psum:
    def __init__(self, nc, P):
        self.banks = [nc.alloc_psum_tensor(f"psb{i}", [128, 512], F32) for i in range(8)]
        self.i = 0

    def next(self):
        b = self.i
        self.i = (self.i + 1) % 7
        return b, self.banks[b]


def col(v):
    v = np.asarray(v, dtype=np.float32)
    return np.ascontiguousarray(v.reshape(-1, 128).T)


class Ctx:
    pass


def emit_ada(P, cx, wada_ap, bias_cols, out_mod, tag):
    wv = wada_ap.rearrange("(kc p) n -> p kc n", p=128)
    ps = cx.ps_ada
    NPIECE = 24
    PW = 3072 // NPIECE
    for pc in range(NPIECE):
        bi = cx.ada_i % 2
        cx.ada_i += 1
        buf = cx.ada_buf[bi]
        P.dma("pool", [(buf[:], wv[:, :, pc * PW:(pc + 1) * PW])], cx.ada_slot[bi],
              writes=[("adabuf", bi)])
        for fl in range(PW // 128):
            fch = pc * (PW // 128) + fl
            fns = []
            for kc in range(8):
                fns.append(lambda h, kc=kc, fl=fl, fch=fch, buf=buf: h.matmul(
                    ps[:, fch:fch + 1], buf[:, kc, fl * 128:(fl + 1) * 128], cx.sc[:, kc:kc + 1],
                    start=(kc == 0), stop=(kc == 7)))
            P.mm_group(fns, reads=[("adabuf", bi), "sc"], writes=[("ps_ada", fch)])
    rd = [("ps_ada", f) for f in range(24)]
    P.op("dve", lambda h: h.tensor_tensor(out_mod[:, :], ps[:, 0:24], bias_cols, ALU.add),
         reads=rd + ["cols"], writes=[("mod", tag)])
    P.op("dve", lambda h: h.tensor_scalar(out_mod[:, 8:16], out_mod[:, 8:16], 1.0, None, ALU.add),
         reads=[("mod", tag)], writes=[("mod", tag)])
    P.op("dve", lambda h: h.tensor_scalar(out_mod[:, 16:24], out_mod[:, 16:24], 1.0, INV_ALPHA, ALU.add, ALU.mult),
         reads=[("mod", tag)], writes=[("mod", tag)])


def emit_modulate(P, cx, xT, hT, mod, tag, tt, eng="dve"):
    sl = slice(tt * TT, (tt + 1) * TT)
    for c in range(8):
        if eng == "act":
            P.op("act", lambda h, c=c: h.activation(hT[:, c, sl], xT[:, c, sl], AF.Identity,
                                                    bias=mod[:, c:c + 1], scale=mod[:, 8 + c:9 + c]),
                 reads=[("x", c, tt), ("mod", tag)], writes=[("h", c, tt)])
        else:
            P.op(eng, lambda h, c=c: h.tensor_scalar(hT[:, c, sl], xT[:, c, sl], mod[:, 8 + c:9 + c],
                                                     mod[:, c:c + 1], ALU.mult, ALU.add),
                 reads=[("x", c, tt), ("mod", tag)], writes=[("h", c, tt)])


def emit_ln(P, cx, xT, tt, g_cols, b_cols):
    sl = slice(tt * TT, (tt + 1) * TT)
    s1, s2, sq = cx.ln_s1, cx.ln_s2, cx.ln_sq
    P.op("pool", lambda h: h.tensor_tensor(s1[:], xT[:, 0, sl], xT[:, 1, sl], ALU.add),
         reads=[("x", 0, tt), ("x", 1, tt)], writes=["ln_s1"])
    for c in range(2, 8):
        P.op("pool", lambda h, c=c: h.tensor_tensor(s1[:], s1[:], xT[:, c, sl], ALU.add),
             reads=[("x", c, tt), "ln_s1"], writes=["ln_s1"])
    for c in range(8):
        if c == 0:
            P.op("act", lambda h: h.activation(s2[:], xT[:, 0, sl], AF.Square),
                 reads=[("x", 0, tt)], writes=["ln_s2"])
        else:
            k = c % 2
            P.op("act", lambda h, c=c, k=k: h.activation(sq[k][:], xT[:, c, sl], AF.Square),
                 reads=[("x", c, tt)], writes=[("ln_sq", k)])
            P.op("dve", lambda h, k=k: h.tensor_tensor(s2[:], s2[:], sq[k][:], ALU.add),
                 reads=[("ln_sq", k), "ln_s2"], writes=["ln_s2"])
    b1, ps1 = cx.psum.next()
    P.mm_group([lambda h: h.matmul(ps1[:], cx.onesm[:], s1[:], start=True, stop=True)],
               reads=["ln_s1", "onesm"], writes=[("ps", b1)])
    b2, ps2 = cx.psum.next()
    P.mm_group([lambda h: h.matmul(ps2[:], cx.onesm[:], s2[:], start=True, stop=True)],
               reads=["ln_s2", "onesm"], writes=[("ps", b2)])
    mean, msq, rstd = cx.ln_mean, cx.ln_tmp[1], cx.ln_rstd
    P.op("act", lambda h: h.activation(mean[:], ps1[:], AF.Identity), reads=[("ps", b1)], writes=["ln_mean"])
    P.op("act", lambda h: h.activation(msq[:], ps1[:], AF.Square), reads=[("ps", b1)], writes=[("ln_tmp", 1)])
    P.op("dve", lambda h: h.tensor_tensor(rstd[:], ps2[:], msq[:], ALU.subtract),
         reads=[("ps", b2), ("ln_tmp", 1)], writes=["ln_rstd"])
    P.op("dve", lambda h: h.tensor_scalar(rstd[:], rstd[:], EPS_P, None, ALU.add),
         reads=["ln_rstd"], writes=["ln_rstd"])
    P.op("dve", lambda h: h.reciprocal(rstd[:], rstd[:]), reads=["ln_rstd"], writes=["ln_rstd"])
    P.op("act", lambda h: h.activation(rstd[:], rstd[:], AF.Sqrt), reads=["ln_rstd"], writes=["ln_rstd"])
    for c in range(8):
        k = c % 2
        tmp = cx.ln_tmp[k]
        P.op("dve", lambda h, c=c, tmp=tmp: h.tensor_tensor(tmp[:], xT[:, c, sl], mean[:], ALU.subtract),
             reads=[("x", c, tt), "ln_mean"], writes=[("ln_tmp", k)])
        P.op("pool", lambda h, tmp=tmp: h.tensor_tensor(tmp[:], tmp[:], rstd[:], ALU.mult),
             reads=["ln_rstd", ("ln_tmp", k)], writes=[("ln_tmp", k)])
        P.op("act", lambda h, c=c, tmp=tmp: h.activation(xT[:, c, sl], tmp[:], AF.Identity,
                                                         bias=b_cols[:, c:c + 1], scale=g_cols[:, c:c + 1]),
             reads=[("ln_tmp", k), "cols"], writes=[("x", c, tt)])


def emit_mlp(P, cx, xT, hT, mod, tag, w_up_ap, w_down_ap, g_cols, b_cols, after_ln=None):
    wu_v = w_up_ap.rearrange("(kc p) n -> p kc n", p=128)
    wd_v = w_down_ap.rearrange("(fc p) n -> p fc n", p=128)
    for q in range(4):
        bi = cx.mlp_i % 2
        cx.mlp_i += 1
        wu, wd = cx.wu_buf[bi], cx.wd_buf[bi]
        P.dma("pool", [(wu[:, 0:4, :], wu_v[:, 0:4, q * 1024:(q + 1) * 1024]),
                       (wu[:, 4:8, :], wu_v[:, 4:8, q * 1024:(q + 1) * 1024])],
              cx.wu_slot[bi], writes=[("wu", bi)])
        P.dma("pool", [(wd[:, 0:4, :], wd_v[:, q * 8:q * 8 + 4, :]),
                       (wd[:, 4:8, :], wd_v[:, q * 8 + 4:q * 8 + 8, :])],
              cx.wd_slot[bi], writes=[("wd", bi)])
        def mlp_tile(q, tt, bi, wu, wd):
            sl = slice(tt * TT, (tt + 1) * TT)
            ai = cx.a_i % 2
            cx.a_i += 1
            aT = cx.a_buf[ai]
            for f in range(8):
                b, ps = cx.psum.next()
                fns = [(lambda h, kc=kc, f=f, ps=ps, wu=wu: h.matmul(
                    ps[:], wu[:, kc, f * 128:(f + 1) * 128], hT[:, kc, sl], start=(kc == 0), stop=(kc == 7)))
                    for kc in range(8)]
                P.mm_group(fns, reads=[("wu", bi)] + [("h", kc, tt) for kc in range(8)], writes=[("ps", b)])
                k = cx.r_i % 2
                cx.r_i += 1
                rt = cx.relu_tmp[k]
                P.op("act", lambda h, ps=ps, rt=rt: h.activation(rt[:], ps[:], AF.Relu),
                     reads=[("ps", b)], writes=[("ln_sq", k)])
                P.op("pool", lambda h, rt=rt, aT=aT, f=f: h.tensor_tensor(aT[:, f, :], rt[:], rt[:], ALU.mult),
                     reads=[("ln_sq", k)], writes=[("a", ai, f)])
            for oc in range(8):
                b, ps = cx.psum.next()
                fns = [(lambda h, f=f, oc=oc, ps=ps, wd=wd, aT=aT: h.matmul(
                    ps[:], wd[:, f, oc * 128:(oc + 1) * 128], aT[:, f, :], start=(f == 0), stop=(f == 7)))
                    for f in range(8)]
                P.mm_group(fns, reads=[("wd", bi)] + [("a", ai, f) for f in range(8)], writes=[("ps", b)])
                P.op("dve", lambda h, ps=ps, oc=oc: h.scalar_tensor_tensor(
                    xT[:, oc, sl], ps[:], mod[:, 16 + oc:17 + oc], xT[:, oc, sl], ALU.mult, ALU.add),
                    reads=[("ps", b), ("mod", tag), ("x", oc, tt)], writes=[("x", oc, tt)])
            if q == 3:
                emit_ln(P, cx, xT, tt, g_cols, b_cols)
                if after_ln is not None:
                    after_ln(tt)

        for tt in range(NT):
            mlp_tile(q, tt, bi, wu, wd)


def emit_load_xT(P, cx, x_ap, xT, key="x"):
    xv = x_ap.rearrange("(tb p) f -> p tb f", p=128)
    for tt in range(NT):
        si = cx.stg_i % 2
        cx.stg_i += 1
        stg = cx.stg[si]
        P.dma("sp", [(stg[:, 0:2, :], xv[:, tt * 4:tt * 4 + 2, :]),
                     (stg[:, 2:4, :], xv[:, tt * 4 + 2:tt * 4 + 4, :])], cx.stg_slot[si],
              writes=[("stg", si)])
        for c in range(8):
            b, ps = cx.psum.next()
            fns = [(lambda h, tb=tb, c=c, ps=ps, stg=stg: h.matmul(
                ps[:, tb * 128:(tb + 1) * 128], stg[:, tb, c * 128:(c + 1) * 128], cx.ident[:],
                start=True, stop=True)) for tb in range(4)]
            P.mm_group(fns, reads=[("stg", si), "ident"], writes=[("ps", b)])
            eng = "act" if c % 2 == 0 else "dve"
            sl = slice(tt * TT, (tt + 1) * TT)
            if eng == "act":
                P.op("act", lambda h, ps=ps, c=c, sl=sl: h.activation(xT[:, c, sl], ps[:], AF.Identity),
                     reads=[("ps", b)], writes=[(key, c, tt)])
            else:
                P.op("dve", lambda h, ps=ps, c=c, sl=sl: h.tensor_copy(xT[:, c, sl], ps[:]),
                     reads=[("ps", b)], writes=[(key, c, tt)])


def emit_store_x(P, cx, xT, out_ap, tt):
    ov = out_ap.rearrange("(tb p) f -> p tb f", p=128)
    for tb in range(4):
        si = cx.ost_i % 2
        cx.ost_i += 1
        stg = cx.ost[si]
        tsl = slice(tt * TT + tb * 128, tt * TT + (tb + 1) * 128)
        for half in range(2):
            b, ps = cx.psum.next()
            fns = [(lambda h, cc=cc, half=half, ps=ps, tsl=tsl: h.matmul(
                ps[:, cc * 128:(cc + 1) * 128], xT[:, half * 4 + cc, tsl], cx.ident[:],
                start=True, stop=True)) for cc in range(4)]
            P.mm_group(fns, reads=[("x", half * 4 + cc, tt) for cc in range(4)] + ["ident"],
                       writes=[("ps", b)])
            if half == 0:
                P.op("act", lambda h, ps=ps, stg=stg: h.activation(stg[:, 0:512], ps[:], AF.Identity),
                     reads=[("ps", b)], writes=[("ost", si, 0)])
            else:
                P.op("dve", lambda h, ps=ps, stg=stg: h.tensor_copy(stg[:, 512:1024], ps[:]),
                     reads=[("ps", b)], writes=[("ost", si, 1)])
        tok = P.dma("sp", [(ov[:, tt * 4 + tb, :], stg[:])], cx.out_slot[si],
                    reads=[("ost", si, 0), ("ost", si, 1)])
        cx.out_toks.append(tok)


def common_setup(nc, P, A, cx, ncols):
    cx.psum = Psum(nc, P)
    cx.cols, _, _ = A.alloc([128, ncols], F32)
    cx.ident_f, _, _ = A.alloc([128, 128], F32)
    cx.ident = cx.ident_f
    cx.onesm, _, _ = A.alloc([128, 128], F32)
    cx.sc, _, _ = A.alloc([128, 8], BF16)
    cx.mod = [A.alloc([128, 24], F32)[0] for _ in range(2)]
    cx.ln_s1, _, _ = A.alloc([128, TT], F32)
    cx.ln_s2, _, _ = A.alloc([128, TT], F32)
    cx.ln_sq = [A.alloc([128, TT], F32)[0] for _ in range(2)]
    cx.ln_mean, _, _ = A.alloc([128, TT], F32)
    cx.ln_rstd, _, _ = A.alloc([128, TT], F32)
    cx.ln_tmp = [A.alloc([128, TT], F32)[0] for _ in range(2)]
    cx.relu_tmp = cx.ln_sq
    cx.ada_buf = [A.alloc([128, 8, 128], BF16)[0] for _ in range(2)]
    cx.ada_slot = [P.slot() for _ in range(2)]
    cx.ada_i = 0
    cx.mlp_i = 0
    cx.a_i = 0
    cx.r_i = 0
    cx.stg_i = 0
    cx.ost_i = 0
    cx.out_toks = []
    cx.out_slot = [P.slot() for _ in range(2)]
    cx.const_slot = P.slot()
    cx.stg_slot = [P.slot() for _ in range(2)]
    cx.wu_slot = [P.slot() for _ in range(2)]
    cx.wd_slot = [P.slot() for _ in range(2)]
    cx.ps_ada = cx.psum.banks[7]


def fence_tokens(P, keys):
    toks = []
    for k in keys:
        t = P.last_w.get(k)
        if t is not None:
            toks.append(t)
        toks.extend(P.readers.get(k, {}).values())
    return toks


A_COLS = dict(c=0, adab0=8, adab1=32, lng0=56, lnb0=64, lng1=72, lnb1=80, binu=88)
A_NCOLS = 96


def body_A(nc, P, A, cx, io, xT, stage=None, fused=False):
    x_in, cols_in, rows_in, bb_in, consts_in = io["x"], io["cols"], io["rows"], io["bbias"], io["constsA"]
    ada_w0, ada_w1, w_in, w_s, w_out = io["ada_w00"], io["ada_w01"], io["a_w_in"], io["a_w_s"], io["a_w_out"]
    w_up, w_down, y_out = io["mlp_w_up0"], io["mlp_w_down0"], io["y"]
    mark = A.off
    win, _, _ = A.alloc([128, 8, 2048], BF16)
    wout, _, _ = A.alloc([128, 8, 1024], BF16)
    rows, _, _ = A.alloc([128, 3, 1024], F32)
    bbias, _, _ = A.alloc([128, 8, 128], F32)
    wcT, _, _ = A.alloc([128, 16, 128], BF16)
    vst, _, _ = A.alloc([128, 2, 6], F32)
    vmv, _, _ = A.alloc([128, 2], F32)
    vrs, _, _ = A.alloc([128, 1], F32)
    mark1b = A.off
    hT1 = [A.alloc([128, 8, TT], BF16)[0] for _ in range(2)]
    uT1, _, _ = A.alloc([128, 8, TT], BF16)
    uT = [uT1, uT1]
    vtok1, _, _ = A.alloc([128, 4, 1024], BF16)
    vtok = [vtok1, vtok1]
    vg = [A.alloc([128, 1024], F32)[0] for _ in range(2)]
    gt = [A.alloc([128, TT], F32)[0] for _ in range(2)]
    end1 = A.off
    A.off = mark1b
    cx.stg = [A.alloc([128, 4, 1024], F32)[0] for _ in range(2)]
    tril, _, _ = A.alloc([128, 128], F32)
    identb, _, _ = A.alloc([128, 128], BF16)
    ws_f, _, _ = A.alloc([128, 16, 128], F32)
    ws_b, _, _ = A.alloc([128, 16, 128], BF16)
    end1 = max(end1, A.off)
    A.off = mark
    h2T, _, _ = A.alloc([128, 8, T], BF16)
    cx.wu_buf = [A.alloc([128, 8, 1024], BF16)[0] for _ in range(2)]
    cx.wd_buf = [A.alloc([128, 8, 1024], BF16)[0] for _ in range(2)]
    cx.a_buf = [A.alloc([128, 8, TT], BF16)[0] for _ in range(2)]
    cx.ost = [A.alloc([128, 1024], F32)[0] for _ in range(2)]
    end2 = A.off
    A.off = max(end1, end2)
    print("SBUF plan A: persistent", mark, "phase1 end", end1, "phase2 end", end2)

    P.dma("sp", [(cx.cols[:], cols_in)], P.slot(), writes=["cols"])
    P.dma("sp", [(cx.ident_f[:], consts_in[:, 0, :]), (cx.onesm[:], consts_in[:, 1, :]),
                 (tril[:], consts_in[:, 2, :])], P.slot(), writes=["ident", "onesm", "tril"])
    P.dma("sp", [(rows[:], rows_in)], P.slot(), writes=["rows"])
    P.dma("sp", [(bbias[:], bb_in)], P.slot(), writes=["bbias"])
    P.dma("sp", [(ws_f[:], w_s.rearrange("g t s -> t g s"))], P.slot(), writes=["ws_f"])
    P.op("act", lambda h: h.activation(cx.sc[:], cx.cols[:, 0:8], AF.Silu), reads=["cols"], writes=["sc"])
    emit_ada(P, cx, ada_w0, cx.cols[:, 8:32], cx.mod[0], 0)
    wv = w_in.rearrange("(kc p) n -> p kc n", p=128)
    for j in range(4):
        P.dma("pool", [(win[:, :, j * 512:(j + 1) * 512], wv[:, :, j * 512:(j + 1) * 512])], P.slot(),
              writes=[("win", j)])
    wout_slot = P.slot()
    P.dma("pool", [(wout[:], w_out.rearrange("(kc p) n -> p kc n", p=128))], wout_slot, writes=["wout"])
    P.op("dve", lambda h: h.tensor_copy(identb[:], cx.ident_f[:]), reads=["ident"], writes=["identb"])
    P.op("dve", lambda h: h.tensor_tensor(ws_b[:], ws_f[:], tril[:, None, :].to_broadcast([128, 16, 128]), ALU.mult),
         reads=["ws_f", "tril"], writes=["ws_b"])
    for g4 in range(4):
        b, ps = cx.psum.next()
        fns = [(lambda h, gi=gi, g4=g4, ps=ps: h.matmul(
            ps[:, gi * 128:(gi + 1) * 128], ws_b[:, g4 * 4 + gi, :], identb[:], start=True, stop=True))
            for gi in range(4)]
        P.mm_group(fns, reads=["ws_b", "identb"], writes=[("ps", b)])
        P.op("dve", lambda h, g4=g4, ps=ps: h.tensor_copy(
            wcT[:, g4 * 4:(g4 + 1) * 4, :], ps[:].rearrange("p (g t) -> p g t", g=4)),
            reads=[("ps", b)], writes=[("wcT", g4)])
    emit_load_xT(P, cx, x_in, xT)
    if stage == "load":
        for tt in range(NT):
            emit_store_x(P, cx, xT, y_out, tt)
        return
    f1 = fence_tokens(P, [("stg", 0), ("stg", 1), "tril", "identb", "ws_f", "ws_b"])
    for e_ in ("dve", "act", "pool"):
        P.wait_all(e_, f1)

    def mix_tile(tt):
        sl = slice(tt * TT, (tt + 1) * TT)
        pi = tt % 2
        hT = hT1[pi]
        for c in range(8):
            P.op("dve", lambda h, c=c, hT=hT: h.tensor_scalar(
                hT[:, c, :], xT[:, c, sl], cx.mod[0][:, 8 + c:9 + c], cx.mod[0][:, c:c + 1], ALU.mult, ALU.add),
                reads=[("x", c, tt), ("mod", 0)], writes=[("h1", pi, c)])
        for fc in range(8):
            b, ps = cx.psum.next()
            fns = [(lambda h, kc=kc, fc=fc, ps=ps, hT=hT: h.matmul(
                ps[:], win[:, kc, fc * 128:(fc + 1) * 128], hT[:, kc, :], start=(kc == 0), stop=(kc == 7)))
                for kc in range(8)]
            P.mm_group(fns, reads=[("win", fc // 4)] + [("h1", pi, kc) for kc in range(8)], writes=[("ps", b)])
            P.op("act", lambda h, fc=fc, ps=ps: h.activation(
                uT[pi][:, fc, :], ps[:], GELU, bias=cx.cols[:, A_COLS["binu"] + fc:A_COLS["binu"] + fc + 1]),
                reads=[("ps", b), "cols"], writes=[("u", 0, fc)])
        for tb in range(4):
            vi = (tt * 4 + tb) % 2
            vgb = vg[vi]
            for half in range(2):
                b, ps = cx.psum.next()
                fns = [(lambda h, kc=kc, half=half, ps=ps, hT=hT, tb=tb: h.matmul(
                    ps[:], hT[:, kc, tb * 128:(tb + 1) * 128],
                    win[:, kc, 1024 + half * 512:1024 + (half + 1) * 512], start=(kc == 0), stop=(kc == 7)))
                    for kc in range(8)]
                P.mm_group(fns, reads=[("win", 2 + half)] + [("h1", pi, kc) for kc in range(8)],
                           writes=[("ps", b)])
                P.op("dve", lambda h, ps=ps, half=half, vgb=vgb: h.tensor_tensor(
                    vgb[:, half * 512:(half + 1) * 512], ps[:], rows[:, 0, half * 512:(half + 1) * 512], ALU.add),
                    reads=[("ps", b), "rows"], writes=[("vg", vi, half)])
                P.op("act", lambda h, half=half, vgb=vgb: h.activation(
                    vgb[:, half * 512:(half + 1) * 512], vgb[:, half * 512:(half + 1) * 512], GELU),
                    reads=[("vg", vi, half)], writes=[("vg", vi, half)])
                P.op("dve", lambda h, half=half, vgb=vgb: h.bn_stats(
                    vst[:, half, :], vgb[:, half * 512:(half + 1) * 512]),
                    reads=[("vg", vi, half)], writes=[("vst", half)])
            P.op("dve", lambda h: h.bn_aggr(vmv[:], vst[:].rearrange("p a b -> p (a b)")),
                 reads=[("vst", 0), ("vst", 1)], writes=["vmv"])
            P.op("dve", lambda h: h.tensor_scalar(vrs[:], vmv[:, 1:2], LN_EPS, None, ALU.add),
                 reads=["vmv"], writes=["vrs"])
            P.op("dve", lambda h: h.reciprocal(vrs[:], vrs[:]), reads=["vrs"], writes=["vrs"])
            P.op("act", lambda h: h.activation(vrs[:], vrs[:], AF.Sqrt), reads=["vrs"], writes=["vrs"])
            P.op("dve", lambda h, vgb=vgb: h.tensor_scalar(
                vgb[:], vgb[:], vmv[:, 0:1], vrs[:, 0:1], ALU.subtract, ALU.mult),
                reads=[("vg", vi, 0), ("vg", vi, 1), "vmv", "vrs"], writes=[("vg", vi, 0), ("vg", vi, 1)])
            P.op("pool", lambda h, vgb=vgb: h.tensor_tensor(vgb[:], vgb[:], rows[:, 1, :], ALU.mult),
                 reads=[("vg", vi, 0), ("vg", vi, 1), "rows"], writes=[("vg", vi, 0), ("vg", vi, 1)])
            P.op("pool", lambda h, vgb=vgb, tb=tb: h.tensor_tensor(vtok[pi][:, tb, :], vgb[:], rows[:, 2, :], ALU.add),
                 reads=[("vg", vi, 0), ("vg", vi, 1), "rows"], writes=[("vtok", 0, tb)])
        for fc in range(8):
            b, ps = cx.psum.next()
            fns = []
            for tb in range(4):
                for gi in range(2):
                    g = 2 * fc + gi
                    fns.append(lambda h, tb=tb, gi=gi, g=g, ps=ps: h.matmul(
                        ps[gi * 64:(gi + 1) * 64, tb * 128:(tb + 1) * 128],
                        vtok[pi][:, tb, g * 64:(g + 1) * 64], wcT[:, g, :], start=True, stop=True))
            P.mm_group(fns, reads=[("vtok", 0, tb) for tb in range(4)] + [("wcT", (2 * fc) // 4)],
                       writes=[("ps", b)])
            k = fc % 2
            P.op("dve", lambda h, ps=ps, fc=fc, k=k: h.tensor_tensor(
                gt[k][:].rearrange("p (a t) -> p a t", a=4), ps[:].rearrange("p (a t) -> p a t", a=4),
                bbias[:, fc, None, :].to_broadcast([128, 4, 128]), ALU.add),
                reads=[("ps", b), "bbias"], writes=[("gt", k)])
            P.op("pool", lambda h, fc=fc, k=k: h.tensor_tensor(uT[pi][:, fc, :], gt[k][:], uT[pi][:, fc, :], ALU.mult),
                 reads=[("gt", k), ("u", 0, fc)], writes=[("u", 0, fc)])
        for oc in range(8):
            b, ps = cx.psum.next()
            fns = [(lambda h, fc=fc, oc=oc, ps=ps: h.matmul(
                ps[:], wout[:, fc, oc * 128:(oc + 1) * 128], uT[pi][:, fc, :], start=(fc == 0), stop=(fc == 7)))
                for fc in range(8)]
            P.mm_group(fns, reads=["wout"] + [("u", 0, fc) for fc in range(8)], writes=[("ps", b)])
            P.op("dve", lambda h, ps=ps, oc=oc: h.scalar_tensor_tensor(
                xT[:, oc, sl], ps[:], cx.mod[0][:, 16 + oc:17 + oc], xT[:, oc, sl], ALU.mult, ALU.add),
                reads=[("ps", b), ("mod", 0), ("x", oc, tt)], writes=[("x", oc, tt)])
        emit_ln(P, cx, xT, tt, cx.cols[:, A_COLS["lng0"]:A_COLS["lng0"] + 8],
                cx.cols[:, A_COLS["lnb0"]:A_COLS["lnb0"] + 8])

    for tt in range(NT):
        mix_tile(tt)

    if stage == "mixA":
        dbg = nc.dram_tensor("dbg", [128, 16384], F32, kind="ExternalOutput").ap()
        dsl = P.slot()
        allk = list(P.last_w.keys())
        cx.out_toks.append(P.dma("sp", [(dbg[:, 0:24], cx.mod[0][:])], dsl, reads=allk))
        cx.out_toks.append(P.dma("pool", [(dbg[:, 1024:3072], wcT[:].rearrange("p g t -> p (g t)"))], dsl, reads=allk))
        cx.out_toks.append(P.dma("pool", [(dbg[:, 4096:8192], hT1[1][:].rearrange("p g t -> p (g t)"))], dsl, reads=allk))
        cx.out_toks.append(P.dma("pool", [(dbg[:, 8192:12288], uT1[:].rearrange("p g t -> p (g t)"))], dsl, reads=allk))
        cx.out_toks.append(P.dma("pool", [(dbg[:, 12288:16384], vtok1[:].rearrange("p g t -> p (g t)"))], dsl, reads=allk))
        for e_ in ("pe", "act", "dve", "pool"):
            P.wait_all(e_, list(cx.out_toks))
        for tt in range(NT):
            emit_store_x(P, cx, xT, y_out, tt)
        return
    emit_ada(P, cx, ada_w1, cx.cols[:, 32:56], cx.mod[1], 1)
    p1_keys = [("win", j) for j in range(4)] + ["wout", "rows", "bbias", "vst", "vmv", "vrs"] + \
              [("wcT", g4) for g4 in range(4)] + [("h1", p, c) for p in range(2) for c in range(8)] + \
              [("u", 0, c) for c in range(8)] + [("vtok", 0, tb) for tb in range(4)] + \
              [("vg", i, hf) for i in range(2) for hf in range(2)] + [("gt", 0), ("gt", 1)]
    f2 = fence_tokens(P, p1_keys)
    for e_ in ("dve", "act", "pool"):
        P.wait_all(e_, f2)
    for tt in range(NT):
        emit_modulate(P, cx, xT, h2T, cx.mod[1], 1, tt)

    def after_ln(tt):
        emit_store_x(P, cx, xT, y_out, tt)

    emit_mlp(P, cx, xT, h2T, cx.mod[1], 1, w_up, w_down,
             cx.cols[:, A_COLS["lng1"]:A_COLS["lng1"] + 8], cx.cols[:, A_COLS["lnb1"]:A_COLS["lnb1"] + 8],
             after_ln=(None if fused else after_ln))


def build_A(stage=None):
    nc = bass.Bass("TRN2", target_bir_lowering=False)
    dt = lambda n, s: nc.dram_tensor(n, s, F32, kind="ExternalInput").ap()
    x_in = dt("x", [T, D])
    cols_in = dt("cols", [128, A_NCOLS])
    rows_in = dt("rows", [128, 3, 1024])
    bb_in = dt("bbias", [128, 8, 128])
    consts_in = dt("consts", [128, 3, 128])
    ada_w0 = dt("ada_w0", [D, 3 * D])
    ada_w1 = dt("ada_w1", [D, 3 * D])
    w_in = dt("a_w_in", [D, 2 * D])
    w_s = dt("a_w_s", [16, 128, 128])
    w_out = dt("a_w_out", [D, D])
    w_up = dt("mlp_w_up", [D, 4 * D])
    w_down = dt("mlp_w_down", [4 * D, D])
    y_out = nc.dram_tensor("y", [T, D], F32, kind="ExternalOutput").ap()

    with contextlib.ExitStack() as es:
        P = Prog(nc, es)
        A = Arena(nc)
        cx = Ctx()
        common_setup(nc, P, A, cx, A_NCOLS)
        xT, _, _ = A.alloc([128, 8, T], F32)
        io = dict(x=x_in, cols=cols_in, rows=rows_in, bbias=bb_in, constsA=consts_in, ada_w00=ada_w0, ada_w01=ada_w1,
                  a_w_in=w_in, a_w_s=w_s, a_w_out=w_out, mlp_w_up0=w_up, mlp_w_down0=w_down, y=y_out)
        body_A(nc, P, A, cx, io, xT, stage=stage, fused=False)
        P.wait_all("sp", cx.out_toks)
        P.finish()
    return nc


def host_consts():
    ident = np.eye(128, dtype=np.float32)
    ones = np.full((128, 128), 1.0 / 1024.0, dtype=np.float32)
    tril = np.tril(np.ones((128, 128), dtype=np.float32))
    return np.ascontiguousarray(np.stack([ident, ones, tril], axis=1))


def prep_A(inp):
    x = np.asarray(inp["x"], dtype=np.float32)
    c = np.asarray(inp["c"], dtype=np.float32)
    maps = []
    consts = host_consts()
    rows = np.stack([inp["a_b_in"][0][1024:], inp["a_vn_g"][0], inp["a_vn_b"][0]], axis=0).astype(np.float32)
    rows = np.ascontiguousarray(np.broadcast_to(rows[None], (128, 3, 1024)))
    bs = np.asarray(inp["a_b_s"][0], dtype=np.float32)
    bb = np.zeros((128, 8, 128), dtype=np.float32)
    for fc in range(8):
        bb[0:64, fc, :] = bs[2 * fc][None, :]
        bb[64:128, fc, :] = bs[2 * fc + 1][None, :]
    for core in range(NCORES):
        b, half = core // 2, core % 2
        cols = np.zeros((128, A_NCOLS), dtype=np.float32)
        cols[:, 0:8] = col(c[b])
        cols[:, 8:32] = col(inp["ada_b"][0, 0])
        cols[:, 32:56] = col(inp["ada_b"][0, 1])
        cols[:, 56:64] = col(inp["ln_g"][0, 0])
        cols[:, 64:72] = col(inp["ln_b"][0, 0])
        cols[:, 72:80] = col(inp["ln_g"][0, 1])
        cols[:, 80:88] = col(inp["ln_b"][0, 1])
        cols[:, 88:96] = col(inp["a_b_in"][0][:1024])
        maps.append({
            "x": np.ascontiguousarray(x[b, half * T:(half + 1) * T]),
            "cols": cols, "rows": rows, "bbias": bb, "consts": consts,
            "ada_w0": np.ascontiguousarray(inp["ada_w"][0, 0]), "ada_w1": np.ascontiguousarray(inp["ada_w"][0, 1]),
            "a_w_in": np.ascontiguousarray(inp["a_w_in"][0]), "a_w_s": np.ascontiguousarray(inp["a_w_s"][0]),
            "a_w_out": np.ascontiguousarray(inp["a_w_out"][0]),
            "mlp_w_up": np.ascontiguousarray(inp["mlp_w_up"][0]), "mlp_w_down": np.ascontiguousarray(inp["mlp_w_down"][0]),
        })
    return maps


def run_A(inp, trace=False, stage=None, ncores=NCORES):
    nc = build_A(stage)
    maps = prep_A(inp)[:ncores]
    res = run_bass_kernel_spmd(nc, maps, core_ids=list(range(ncores)), trace=trace)
    x1 = np.zeros((4, 4096, D), dtype=np.float32)
    for core in range(ncores):
        b, half = core // 2, core % 2
        x1[b, half * T:(half + 1) * T] = res.results[core]["y"]
    if stage == "mixA":
        np.save("dbg0.npy", res.results[0]["dbg"])
    return x1, res


B_NCOLS = 96
PATTERNS = ((128, 1), (512, 4), (2048, 16))


def host_consts_B():
    ident = np.eye(128, dtype=np.float32)
    ones = np.full((128, 128), 1.0 / 1024.0, dtype=np.float32)
    k = np.arange(128)[:, None]
    q = np.arange(128)[None, :]
    d_prev = np.where(k >= q, 128.0 + q - k, 0.0).astype(np.float32)
    d_cur = np.where(k <= q, (q - k) * 1.0, 0.0).astype(np.float32)
    m_prev = (k >= q).astype(np.float32)
    m_cur = (k <= q).astype(np.float32)
    m = np.arange(128)[None, :]
    permA = ((k == m + 64) & (m < 64)).astype(np.float32)
    permB = ((k == m - 64) & (m >= 64)).astype(np.float32)
    return np.ascontiguousarray(np.stack([ident, ones, d_prev, d_cur, permA, permB, m_prev, m_cur], axis=1))


def body_B(nc, P, A, cx, io, xT, xT_at, stage=None, fused=False, cb=0):
    x_in, xp_in, cols_in, consts_in = io.get("x"), io.get("xp"), io["cols"], io["constsB"]
    ada_w0, ada_w1, w_qkv, w_out = io["ada_w10"], io["ada_w11"], io["b_w_qkv"], io["b_w_out"]
    w_up, w_down, y_out, xsp = io["mlp_w_up1"], io["mlp_w_down1"], io["y"], io["xsp"]
    mark = xT_at
    r1_end = xT_at + 65536
    save_off = A.off
    A.off = mark
    kt0, _, _ = A.alloc([128, 32 * 128], BF16)
    KT = [kt0, kt0]
    VT, _, _ = A.alloc([128, 32 * 128], BF16)
    VA, _, _ = A.alloc([128, 32, 2, 128], BF16)
    ACC, _, _ = A.alloc([128, 2, T], F32)
    QT = [A.alloc([128, T], BF16)[0] for _ in range(2)]
    Praw = [A.alloc([128, 512], F32)[0] for _ in range(2)]
    Pt = [A.alloc([128, 512], BF16)[0] for _ in range(2)]
    assert A.off <= r1_end, (A.off, r1_end)
    A.off = max(r1_end, save_off)
    mark2 = A.off
    hT, _, _ = A.alloc([128, 8, 2 * T], BF16)
    oT, oT_at, _ = A.alloc([128, 8, T], BF16)
    wqkv = [A.alloc([128, 3, 8, 128], BF16)[0] for _ in range(2)]
    Emat = [A.alloc([128, 2, 2, 128], F32)[0] for _ in range(2)]
    rden, _, _ = A.alloc([128, 512], F32)
    identb, _, _ = A.alloc([128, 128], BF16)
    diffm, _, _ = A.alloc([128, 2, 128], F32)
    perm, _, _ = A.alloc([128, 2, 128], F32)
    mask01, _, _ = A.alloc([128, 2, 128], F32)
    endB1 = A.off
    A.off = oT_at
    cx.stg = [A.alloc([128, 4, 1024], F32)[0] for _ in range(2)]
    assert A.off <= oT_at + 32768
    A.off = mark2
    wout, _, _ = A.alloc([128, 8, 1024], BF16)
    A.off = mark2
    h2T, _, _ = A.alloc([128, 8, T], BF16)
    cx.wu_buf = [A.alloc([128, 8, 1024], BF16)[0] for _ in range(2)]
    cx.wd_buf = [A.alloc([128, 8, 1024], BF16)[0] for _ in range(2)]
    cx.a_buf = [A.alloc([128, 8, TT], BF16)[0] for _ in range(2)]
    cx.ost = [A.alloc([128, 1024], F32)[0] for _ in range(2)]
    endB2 = A.off
    print("SBUF plan B: mark", mark, "r1_end", r1_end, "attn end", endB1, "mlp end", endB2)
    assert max(endB1, endB2) <= A.cap

    flag = cx.cols[:, cb + 88:cb + 89]
    if not fused:
        P.dma("sp", [(cx.cols[:], cols_in)], P.slot(), writes=["cols"])
        P.dma("sp", [(cx.ident_f[:], consts_in[:, 0, :]), (cx.onesm[:], consts_in[:, 1, :])], P.slot(),
              writes=["ident", "onesm"])
        P.op("act", lambda h: h.activation(cx.sc[:], cx.cols[:, 0:8], AF.Silu), reads=["cols"], writes=["sc"])
    P.dma("sp", [(diffm[:], consts_in[:, 2:4, :]), (perm[:], consts_in[:, 4:6, :]),
                 (mask01[:], consts_in[:, 6:8, :])], P.slot(), writes=["diffm", "perm"])
    P.op("dve", lambda h: h.tensor_copy(identb[:], cx.ident_f[:]), reads=["ident"], writes=["identb"])
    emit_ada(P, cx, ada_w0, cx.cols[:, cb + 8:cb + 32], cx.mod[0], 0)
    mod0 = cx.mod[0]

    xpv = xp_in.rearrange("(tb p) f -> p tb f", p=128) if xp_in is not None else None

    def load_prev_tile(tt):
        si = cx.stg_i % 2
        cx.stg_i += 1
        stg = cx.stg[si]
        P.dma("sp", [(stg[:, 0:2, :], xpv[:, tt * 4:tt * 4 + 2, :]),
                     (stg[:, 2:4, :], xpv[:, tt * 4 + 2:tt * 4 + 4, :])], cx.stg_slot[si],
              writes=[("stg", si)])
        for c in range(8):
            b, ps = cx.psum.next()
            fns = [(lambda h, tb=tb, c=c, ps=ps, stg=stg: h.matmul(
                ps[:, tb * 128:(tb + 1) * 128], stg[:, tb, c * 128:(c + 1) * 128], cx.ident[:],
                start=True, stop=True)) for tb in range(4)]
            P.mm_group(fns, reads=[("stg", si), "ident"], writes=[("ps", b)])
            P.op("dve", lambda h, ps=ps, c=c, tt=tt: h.tensor_scalar(
                hT[:, c, tt * TT:(tt + 1) * TT], ps[:], mod0[:, 8 + c:9 + c], mod0[:, c:c + 1], ALU.mult, ALU.add),
                reads=[("ps", b), ("mod", 0)], writes=[("hall", c, tt)])

    if not fused:
        for tt in range(NT):
            load_prev_tile(tt)
        emit_load_xT(P, cx, x_in, xT)

    def mod_own(tt):
        for c in range(8):
            P.op("act" if c % 2 else "dve",
                 (lambda h, c=c, tt=tt: h.activation(hT[:, c, T + tt * TT:T + (tt + 1) * TT], xT[:, c, tt * TT:(tt + 1) * TT],
                                                     AF.Identity, bias=mod0[:, c:c + 1], scale=mod0[:, 8 + c:9 + c]))
                 if c % 2 else
                 (lambda h, c=c, tt=tt: h.tensor_scalar(hT[:, c, T + tt * TT:T + (tt + 1) * TT], xT[:, c, tt * TT:(tt + 1) * TT],
                                                        mod0[:, 8 + c:9 + c], mod0[:, c:c + 1], ALU.mult, ALU.add)),
                 reads=[("x", c, tt), ("mod", 0)], writes=[("hall", c, NT + tt)])

    for tt in range(NT):
        mod_own(tt)
    if fused:
        snd, rcv = io["snd"], io["rcv"]
        s_slot = P.slot()
        t_snd = P.dma("sp", [(snd[c], hT[:, c, T:2 * T]) for c in range(8)], s_slot,
                      reads=[("hall", c, NT + tt) for c in range(8) for tt in range(NT)])
        cc_sem = P.new_sem()
        ep = P.E["pool"]
        w_ = P._deps("pool", (), (), [t_snd])
        for c in range(8):
            ep.ops.append((w_ if c == 0 else [],
                           (lambda h, c=c: h.collective_compute(
                               "AllGather", ALU.bypass, replica_groups=[[0, 1], [2, 3], [4, 5], [6, 7]],
                               ins=[snd[c].opt()], outs=[rcv[c].opt()])),
                           (cc_sem, 1)))
        t_cc = (cc_sem, 8)
        r_slot = P.slot()
        P.dma("sp", [(hT[:, c, 0:T], rcv[c][0:128, :]) for c in range(8)], r_slot,
              writes=[("hall", c, tt) for c in range(8) for tt in range(NT)], extra=[t_cc])
    sp_slot = P.slot()
    xkeys = [("x", c, tt) for c in range(8) for tt in range(NT)]
    t_spill = P.dma("sp", [(xsp[:, c * T:(c + 1) * T], xT[:, c, :]) for c in range(8)], sp_slot, reads=xkeys)
    fsp = fence_tokens(P, xkeys) + [t_spill]
    for e_ in ("pe", "act", "dve", "pool"):
        P.wait_all(e_, fsp)
    P.op("pool", lambda h: h.memset(VA[:], 1.0), writes=["va_init"])
    P.op("pool", lambda h: h.tensor_scalar(VA[:, 0:16], VA[:, 0:16], flag, None, ALU.mult),
         reads=["va_init", "cols"], writes=["va_init"])
    va_tok = P.last_w["va_init"]

    wq_v = w_qkv.rearrange("(kc p) n -> p kc n", p=128)
    wq_slot = [P.slot() for _ in range(2)]
    st = {"i": 0, "e": 0, "p": 0}
    hall_keys_all = [("hall", c, t8) for c in range(8) for t8 in range(2 * NT)]

    def perm_view(buf2d, col0, d, tt, ps):
        if d == 1:
            return buf2d[:, col0 + tt * 512:col0 + (tt + 1) * 512], ps[:]
        if d == 4:
            dst = buf2d[:, col0 + tt * 512:col0 + (tt + 1) * 512].rearrange("p (r i) -> p i r", r=4)
            return dst, ps[:].rearrange("p (i r) -> p i r", r=4)
        dst = buf2d[:, col0:col0 + 2048].rearrange("p (r i) -> p i r", r=16)[:, tt * 32:(tt + 1) * 32, :]
        return dst, ps[:].rearrange("p (i r) -> p i r", r=16)

    import os as _os2
    _skip = set(_os2.environ.get("DBG_SKIP", "").split(","))

    def attn_stage(hp, g):
        d = PATTERNS[g][1]
        si = st["i"] % 2
        st["i"] += 1
        wb = wqkv[si]
        kt = KT[si]
        qt = QT[si]
        em = Emat[si]
        pairs = []
        for t3 in range(3):
            c0 = ((g * 3 + t3) * 16 + 2 * hp) * 64
            pairs.append((wb[:, t3, :, :], wq_v[:, :, c0:c0 + 128]))
        P.dma("pool", pairs, wq_slot[si], writes=[("wqkv", si)])
        for hh in range(0 if "E" in _skip else 2):
            slope = 2.0 ** (-8.0 * (2 * hp + hh + 1) / 16.0)
            P.op("act", lambda h, hh=hh, slope=slope: h.activation(
                em[:, hh, :, :], diffm[:], AF.Exp, scale=-slope * d),
                reads=["diffm"], writes=[("emat", si, hh)])
            P.op("pool", lambda h, hh=hh: h.tensor_tensor(em[:, hh, :, :], em[:, hh, :, :], mask01[:], ALU.mult),
                 reads=["diffm", ("emat", si, hh)], writes=[("emat", si, hh)])
        for tt in range(0 if "Q" in _skip else NT):
            b, ps = cx.psum.next()
            fns = [(lambda h, kc=kc, ps=ps, tt=tt: h.matmul(
                ps[:], wb[:, 0, kc, :], hT[:, kc, T + tt * TT:T + (tt + 1) * TT], start=(kc == 0), stop=(kc == 7)))
                for kc in range(8)]
            P.mm_group(fns, reads=[("wqkv", si)] + [("hall", kc, NT + tt) for kc in range(8)], writes=[("ps", b)])
            dst, src = perm_view(qt, 0, d, tt, ps)
            P.op("act", lambda h, dst=dst, src=src: h.activation(dst, src, AF.Identity),
                 reads=[("ps", b)], writes=[("qt", si)])
        for tt in range(0 if "K" in _skip else NT):
            b, ps = cx.psum.next()
            fns = [(lambda h, kc=kc, ps=ps, tt=tt: h.matmul(
                ps[:], wb[:, 1, kc, :], hT[:, kc, T + tt * TT:T + (tt + 1) * TT], start=(kc == 0), stop=(kc == 7)))
                for kc in range(8)]
            P.mm_group(fns, reads=[("wqkv", si)] + [("hall", kc, NT + tt) for kc in range(8)], writes=[("ps", b)])
            dst, src = perm_view(kt, 2048, d, tt, ps)
            P.op("dve", lambda h, dst=dst, src=src: h.tensor_copy(dst, src),
                 reads=[("ps", b)], writes=[("kt", 0)])
        if "K" in _skip:
            pass
        elif d == 1:
            b, ps = cx.psum.next()
            fns = [(lambda h, kc=kc, ps=ps: h.matmul(
                ps[:, 0:128], wb[:, 1, kc, :], hT[:, kc, T - 128:T], start=(kc == 0), stop=(kc == 7)))
                for kc in range(8)]
            P.mm_group(fns, reads=[("wqkv", si)] + [("hall", kc, NT - 1) for kc in range(8)], writes=[("ps", b)])
            P.op("dve", lambda h, ps=ps: h.tensor_copy(kt[:, 15 * 128:16 * 128], ps[:, 0:128]),
                 reads=[("ps", b)], writes=[("kt", 0)])
        elif d == 4:
            b, ps = cx.psum.next()
            fns = [(lambda h, kc=kc, ps=ps: h.matmul(
                ps[:], wb[:, 1, kc, :], hT[:, kc, T - 512:T], start=(kc == 0), stop=(kc == 7)))
                for kc in range(8)]
            P.mm_group(fns, reads=[("wqkv", si)] + [("hall", kc, NT - 1) for kc in range(8)], writes=[("ps", b)])
            dst = kt[:, 12 * 128:16 * 128].rearrange("p (r i) -> p i r", r=4)
            src = ps[:].rearrange("p (i r) -> p i r", r=4)
            P.op("dve", lambda h, dst=dst, src=src: h.tensor_copy(dst, src),
                 reads=[("ps", b)], writes=[("kt", 0)])
        else:
            for tt in range(NT):
                b, ps = cx.psum.next()
                fns = [(lambda h, kc=kc, ps=ps, tt=tt: h.matmul(
                    ps[:], wb[:, 1, kc, :], hT[:, kc, tt * TT:(tt + 1) * TT], start=(kc == 0), stop=(kc == 7)))
                    for kc in range(8)]
                P.mm_group(fns, reads=[("wqkv", si)] + [("hall", kc, tt) for kc in range(8)], writes=[("ps", b)])
                dst, src = perm_view(kt, 0, d, tt, ps)
                P.op("dve", lambda h, dst=dst, src=src: h.tensor_copy(dst, src),
                     reads=[("ps", b)], writes=[("kt", 0)])
        def vproj(tok0, ntok, col0, mode, tt, hkeys):
            b, ps = cx.psum.next()
            fns = [(lambda h, kc=kc, ps=ps: h.matmul(
                ps[:, 0:ntok], wb[:, 2, kc, :], hT[:, kc, tok0:tok0 + ntok], start=(kc == 0), stop=(kc == 7)))
                for kc in range(8)]
            P.mm_group(fns, reads=[("wqkv", si)] + hkeys, writes=[("ps", b)])
            if mode == "plain":
                dst, src = VT[:, col0:col0 + ntok], ps[:, 0:ntok]
            elif mode == "seg4":
                dst = VT[:, col0:col0 + 512].rearrange("p (r i) -> p i r", r=4)
                src = ps[:].rearrange("p (i r) -> p i r", r=4)
            else:
                dst, src = perm_view(VT, col0, d, tt, ps)
            P.op("dve", lambda h, dst=dst, src=src: h.tensor_copy(dst, src),
                 reads=[("ps", b)], writes=["vt"])

        for tt in range(0 if "V" in _skip else NT):
            hk = [("hall", kc, NT + tt) for kc in range(8)]
            if d == 1:
                vproj(T + tt * TT, TT, 2048 + tt * 512, "plain", tt, hk)
            elif d == 4:
                vproj(T + tt * TT, TT, 2048 + tt * 512, "seg4", tt, hk)
            else:
                vproj(T + tt * TT, TT, 2048, "perm", tt, hk)
        if "V" in _skip:
            pass
        elif d == 1:
            vproj(T - 128, 128, 15 * 128, "plain", 0, [("hall", kc, NT - 1) for kc in range(8)])
        elif d == 4:
            vproj(T - 512, 512, 12 * 128, "seg4", 0, [("hall", kc, NT - 1) for kc in range(8)])
        else:
            for tt in range(NT):
                vproj(tt * TT, TT, 0, "perm", tt, [("hall", kc, tt) for kc in range(8)])
        groups = []
        prev_slots = list(range(16 - d, 16))
        own_slots = list(range(16, 32))
        for i0 in range(0, len(prev_slots), 4):
            groups.append(prev_slots[i0:i0 + 4])
        for i0 in range(0, 16, 4):
            groups.append(own_slots[i0:i0 + 4])
        if "T" in _skip or "V" in _skip:
            groups = []
        for grp in groups:
            b, ps = cx.psum.next()
            fns = [(lambda h, gi=gi, s=s, ps=ps: h.matmul(
                ps[:, gi * 128:(gi + 1) * 128], VT[:, s * 128:(s + 1) * 128], identb[:], start=True, stop=True))
                for gi, s in enumerate(grp)]
            P.mm_group(fns, reads=["vt", "identb"], writes=[("ps", b)], extra=[va_tok])
            ng = len(grp)
            s0 = grp[0]
            srcv = ps[:, 0:ng * 128].rearrange("p (s c) -> p s c", c=128)
            if "X" in _skip:
                continue
            if "XP" in _skip and s0 < 16:
                continue
            if "XO" in _skip and s0 >= 16:
                continue
            if "ALTVA" in _skip:
                s0_ = min(grp[0], 30)
                P.op("dve", lambda h, ps=ps, s0_=s0_: h.tensor_copy(
                    VA[:, s0_:s0_ + 2, :, :].rearrange("p a b c -> p (a b c)"), ps[:]),
                    reads=[("ps", b)], writes=[("va", 0)])
                continue
            if "ALT" in _skip:
                P.op("dve", lambda h, ps=ps: h.tensor_copy(Praw[0][:], ps[:]),
                     reads=[("ps", b)], writes=[("praw", 0)])
                continue
            if "F2D" in _skip:
                for gi, sl_ in enumerate(grp):
                    if "NO20" in _skip and sl_ == 20:
                        continue
                    P.op("act", lambda h, gi=gi, sl_=sl_, ps=ps: h.activation(
                        VA[:, sl_, 0, :], ps[:, gi * 128:(gi + 1) * 128], AF.Identity),
                        reads=[("ps", b)], writes=[("va", 0)])
                    P.op("dve", lambda h, gi=gi, sl_=sl_, ps=ps: h.tensor_copy(
                        VA[:, sl_, 1, :], ps[:, gi * 128:(gi + 1) * 128]),
                        reads=[("ps", b)], writes=[("va", 1)])
                continue
            if "FULL" in _skip:
                P.op("act", lambda h, srcv=srcv, s0=s0, ng=ng: h.activation(
                    VA[:, s0:s0 + ng, 0, :], srcv, AF.Identity),
                    reads=[("ps", b)], writes=[("va", 0)])
                P.op("dve", lambda h, srcv=srcv, s0=s0, ng=ng: h.tensor_copy(
                    VA[:, s0:s0 + ng, 1, :], srcv),
                    reads=[("ps", b)], writes=[("va", 1)])
                continue
            if s0 < 16:
                P.op("act", lambda h, srcv=srcv, s0=s0, ng=ng: h.activation(
                    VA[:, s0:s0 + ng, 0, 0:64], srcv[:, :, 0:64], AF.Identity, scale=flag),
                    reads=[("ps", b), "cols"], writes=[("va", 0)])
                P.op("dve", lambda h, srcv=srcv, s0=s0, ng=ng: h.tensor_scalar(
                    VA[:, s0:s0 + ng, 1, 64:128], srcv[:, :, 64:128], flag, None, ALU.mult),
                    reads=[("ps", b), "cols"], writes=[("va", 1)])
            else:
                P.op("act", lambda h, srcv=srcv, s0=s0, ng=ng: h.activation(
                    VA[:, s0:s0 + ng, 0, 0:64], srcv[:, :, 0:64], AF.Identity),
                    reads=[("ps", b)], writes=[("va", 0)])
                P.op("dve", lambda h, srcv=srcv, s0=s0, ng=ng: h.tensor_copy(
                    VA[:, s0:s0 + ng, 1, 64:128], srcv[:, :, 64:128]),
                    reads=[("ps", b)], writes=[("va", 1)])
        for j in range(0 if stage == 'proj' else 16):
            n, r = j // d, j % d
            pi = st["p"] % 2
            st["p"] += 1
            praw, pt = Praw[pi], Pt[pi]
            for hh in range(2):
                b, ps = cx.psum.next()
                fns = []
                for kb in range(2):
                    slot = 16 + j - d if kb == 0 else 16 + j
                    fns.append(lambda h, hh=hh, kb=kb, slot=slot, ps=ps, j=j: h.matmul(
                        ps[:, kb * 128:(kb + 1) * 128],
                        kt[hh * 64:(hh + 1) * 64, slot * 128:(slot + 1) * 128],
                        qt[hh * 64:(hh + 1) * 64, j * 128:(j + 1) * 128], start=True, stop=True))
                P.mm_group(fns, reads=[("kt", 0), ("qt", si)], writes=[("ps", b)])
                P.op("act", lambda h, ps=ps, praw=praw, hh=hh: h.activation(
                    praw[:, hh * 256:(hh + 1) * 256], ps[:, 0:256], AF.Exp, scale=0.125),
                    reads=[("ps", b)], writes=[("praw", pi, hh)])
            P.op("pool", lambda h, praw=praw, pt=pt: h.tensor_tensor(
                pt[:], praw[:], em[:].rearrange("p a b q -> p (a b q)"), ALU.mult),
                reads=[("praw", pi, 0), ("praw", pi, 1), ("emat", si, 0), ("emat", si, 1)], writes=[("pt", pi)])
            b2, ps2 = cx.psum.next()
            fns = []
            for hh in range(2):
                for kb in range(2):
                    slot = 16 + j - d if kb == 0 else 16 + j
                    fns.append(lambda h, hh=hh, kb=kb, slot=slot, ps2=ps2, pt=pt: h.matmul(
                        ps2[:, hh * 128:(hh + 1) * 128], VA[:, slot, hh, :],
                        pt[:, (hh * 2 + kb) * 128:(hh * 2 + kb + 1) * 128], start=(kb == 0), stop=(kb == 1)))
            P.mm_group(fns, reads=[("pt", pi), ("va", 0), ("va", 1)], writes=[("ps", b2)])
            start = n * 128 * d + r
            accv = ACC[:, :, start:start + 127 * d + 1:d]
            src2 = ps2[:, 0:256].rearrange("p (a q) -> p a q", a=2)
            if g == 0:
                P.op("dve", lambda h, accv=accv, src2=src2: h.tensor_copy(accv, src2),
                     reads=[("ps", b2)], writes=["acc"])
            else:
                P.op("dve", lambda h, accv=accv, src2=src2: h.tensor_tensor(accv, src2, accv, ALU.add),
                     reads=[("ps", b2)], writes=["acc"])

    def finalize(hp):
        for tt in range(NT):
            sl = slice(tt * TT, (tt + 1) * TT)
            b, ps = cx.psum.next()
            fns = [lambda h, ps=ps, sl=sl: h.matmul(ps[:], perm[:, 0, :], ACC[:, 0, sl], start=True, stop=False),
                   lambda h, ps=ps, sl=sl: h.matmul(ps[:], perm[:, 1, :], ACC[:, 1, sl], start=False, stop=True)]
            P.mm_group(fns, reads=["acc", "perm"], writes=[("ps", b)])
            P.op("dve", lambda h, ps=ps: h.reciprocal(rden[:], ps[:]), reads=[("ps", b)], writes=["rden"])
            P.op("pool", lambda h, sl=sl: h.tensor_tensor(oT[0:64, hp, sl], ACC[0:64, 0, sl], rden[0:64, :], ALU.mult),
                 reads=["acc", "rden"], writes=[("o", hp, tt, 0)])
            P.op("pool", lambda h, sl=sl: h.tensor_tensor(oT[64:128, hp, sl], ACC[64:128, 1, sl], rden[64:128, :], ALU.mult),
                 reads=["acc", "rden"], writes=[("o", hp, tt, 1)])

    P.wait_all("pool", fence_tokens(P, [("stg", 0), ("stg", 1)]))
    nhp = {"pre": 0, "one": 1, "proj": 1}.get(stage, 8)
    import os as _os
    _gl = [int(x) for x in _os.environ.get("DBG_G", "0,1,2").split(",")]
    for hp in range(nhp):
        for g in _gl:
            attn_stage(hp, g)
        if stage != "proj":
            finalize(hp)
    if stage in ("pre", "one", "proj"):
        fx = fence_tokens(P, ["vt", ("kt", 0), ("kt", 1), ("qt", 0), ("qt", 1), ("va", 0), ("va", 1), "acc", "va_init",
                              ("praw", 0, 0), ("praw", 0, 1), ("praw", 1, 0), ("praw", 1, 1), ("pt", 0), ("pt", 1)])
        P.wait_all("sp", fx)
        t_re = P.dma("sp", [(xT[:, c, :], xsp[:, c * T:(c + 1) * T]) for c in range(8)], P.slot(), writes=xkeys)
        for e_ in ("pe", "act", "dve", "pool"):
            P.wait_all(e_, fence_tokens(P, hall_keys_all + [("o", hp, tt, hf) for hp in range(nhp) for tt in range(NT) for hf in range(2)] + [("wqkv", 0), ("wqkv", 1), ("emat", 0, 0), ("emat", 0, 1), ("emat", 1, 0), ("emat", 1, 1)]))
        for tt in range(NT):
            emit_store_x(P, cx, xT, y_out, tt)
        return

    att_keys = ["vt", ("kt", 0), ("kt", 1), ("qt", 0), ("qt", 1), ("va", 0), ("va", 1), "acc", "va_init",
                ("praw", 0, 0), ("praw", 0, 1), ("praw", 1, 0), ("praw", 1, 1), ("pt", 0), ("pt", 1)]
    fx = fence_tokens(P, att_keys)
    P.wait_all("sp", fx)
    x_slot = P.slot()
    P.dma("sp", [(xT[:, c, :], xsp[:, c * T:(c + 1) * T]) for c in range(8)], x_slot, writes=xkeys)
    fo = fence_tokens(P, hall_keys_all + [("wqkv", 0), ("wqkv", 1)])
    P.wait_all("pool", fo)
    P.dma("pool", [(wout[:], w_out.rearrange("(kc p) n -> p kc n", p=128))], P.slot(), writes=["wout"])

    def outproj_tile(tt):
        sl = slice(tt * TT, (tt + 1) * TT)
        for oc in range(8):
            b, ps = cx.psum.next()
            fns = [(lambda h, hp=hp, oc=oc, ps=ps: h.matmul(
                ps[:], wout[:, hp, oc * 128:(oc + 1) * 128], oT[:, hp, sl], start=(hp == 0), stop=(hp == 7)))
                for hp in range(8)]
            P.mm_group(fns, reads=["wout"] + [("o", hp, tt, hf) for hp in range(8) for hf in range(2)],
                       writes=[("ps", b)])
            P.op("dve", lambda h, ps=ps, oc=oc: h.scalar_tensor_tensor(
                xT[:, oc, sl], ps[:], mod0[:, 16 + oc:17 + oc], xT[:, oc, sl], ALU.mult, ALU.add),
                reads=[("ps", b), ("mod", 0), ("x", oc, tt)], writes=[("x", oc, tt)])
        emit_ln(P, cx, xT, tt, cx.cols[:, cb + 56:cb + 64], cx.cols[:, cb + 64:cb + 72])

    for tt in range(NT):
        outproj_tile(tt)

    if stage == "attn":
        for e_ in ("pe", "act", "dve", "pool"):
            P.wait_all(e_, fence_tokens(P, ["wout"] + [("o", hp, tt, hf) for hp in range(8) for tt in range(NT) for hf in range(2)]))
        for tt in range(NT):
            emit_store_x(P, cx, xT, y_out, tt)
        return

    emit_ada(P, cx, ada_w1, cx.cols[:, cb + 32:cb + 56], cx.mod[1], 1)
    f2 = fence_tokens(P, ["wout"] + [("o", hp, tt, hf) for hp in range(8) for tt in range(NT) for hf in range(2)]
                      + [("emat", i, hh) for i in range(2) for hh in range(2)] + ["rden", "diffm", "perm", "identb", "vt"])
    for e_ in ("dve", "act", "pool"):
        P.wait_all(e_, f2)
    for tt in range(NT):
        emit_modulate(P, cx, xT, h2T, cx.mod[1], 1, tt)

    def after_ln(tt):
        emit_store_x(P, cx, xT, y_out, tt)

    emit_mlp(P, cx, xT, h2T, cx.mod[1], 1, w_up, w_down, cx.cols[:, cb + 72:cb + 80], cx.cols[:, cb + 80:cb + 88], after_ln=after_ln)


def build_B(stage=None):
    nc = bass.Bass("TRN2", target_bir_lowering=False)
    dt = lambda n, s: nc.dram_tensor(n, s, F32, kind="ExternalInput").ap()
    x_in = dt("x", [T, D])
    xp_in = dt("xp", [T, D])
    cols_in = dt("cols", [128, B_NCOLS])
    consts_in = dt("consts", [128, 8, 128])
    ada_w0 = dt("ada_w0", [D, 3 * D])
    ada_w1 = dt("ada_w1", [D, 3 * D])
    w_qkv = dt("b_w_qkv", [D, 9 * D])
    w_out = dt("b_w_out", [D, D])
    w_up = dt("mlp_w_up", [D, 4 * D])
    w_down = dt("mlp_w_down", [4 * D, D])
    y_out = nc.dram_tensor("y", [T, D], F32, kind="ExternalOutput").ap()
    xsp = nc.dram_tensor("xsp", [128, 8 * T], F32).ap()

    with contextlib.ExitStack() as es:
        P = Prog(nc, es)
        A = Arena(nc)
        cx = Ctx()
        common_setup(nc, P, A, cx, B_NCOLS)
        xT, xT_at, _ = A.alloc([128, 8, T], F32)
        io = dict(x=x_in, xp=xp_in, cols=cols_in, constsB=consts_in, ada_w10=ada_w0, ada_w11=ada_w1, b_w_qkv=w_qkv,
                  b_w_out=w_out, mlp_w_up1=w_up, mlp_w_down1=w_down, y=y_out, xsp=xsp)
        body_B(nc, P, A, cx, io, xT, xT_at, stage=stage, fused=False, cb=0)
        P.wait_all("sp", cx.out_toks)
        P.finish()
    return nc


def prep_B(inp, x1):
    c = np.asarray(inp["c"], dtype=np.float32)
    consts = host_consts_B()
    maps = []
    zeros = np.zeros((T, D), dtype=np.float32)
    for core in range(NCORES):
        b, half = core // 2, core % 2
        cols = np.zeros((128, B_NCOLS), dtype=np.float32)
        cols[:, 0:8] = col(c[b])
        cols[:, 8:32] = col(inp["ada_b"][1, 0])
        cols[:, 32:56] = col(inp["ada_b"][1, 1])
        cols[:, 56:64] = col(inp["ln_g"][1, 0])
        cols[:, 64:72] = col(inp["ln_b"][1, 0])
        cols[:, 72:80] = col(inp["ln_g"][1, 1])
        cols[:, 80:88] = col(inp["ln_b"][1, 1])
        cols[:, 88] = float(half)
        maps.append({
            "x": np.ascontiguousarray(x1[b, half * T:(half + 1) * T]) if x1 is not None else None,
            "xp": (np.ascontiguousarray(x1[b, 0:T]) if half == 1 else zeros) if x1 is not None else None,
            "cols": cols, "consts": consts,
            "ada_w0": np.ascontiguousarray(inp["ada_w"][1, 0]), "ada_w1": np.ascontiguousarray(inp["ada_w"][1, 1]),
            "b_w_qkv": np.ascontiguousarray(inp["b_w_qkv"][0]), "b_w_out": np.ascontiguousarray(inp["b_w_out"][0]),
            "mlp_w_up": np.ascontiguousarray(inp["mlp_w_up"][1]), "mlp_w_down": np.ascontiguousarray(inp["mlp_w_down"][1]),
        })
    return maps


def run_B(inp, x1, trace=False, stage=None, ncores=NCORES):
    nc = build_B(stage)
    maps = prep_B(inp, x1)[:ncores]
    res = run_bass_kernel_spmd(nc, maps, core_ids=list(range(ncores)), trace=trace)
    out = np.zeros((4, 4096, D), dtype=np.float32)
    for core in range(ncores):
        b, half = core // 2, core % 2
        out[b, half * T:(half + 1) * T] = res.results[core]["y"]
    return out, res


F_NCOLS = A_NCOLS + B_NCOLS


def build_F():
    nc = bass.Bass("TRN2", target_bir_lowering=False)
    dt = lambda n, s: nc.dram_tensor(n, s, F32, kind="ExternalInput").ap()
    io = dict(
        x=dt("x", [T, D]), cols=dt("cols", [128, F_NCOLS]), rows=dt("rows", [128, 3, 1024]),
        bbias=dt("bbias", [128, 8, 128]), constsA=dt("constsA", [128, 3, 128]), constsB=dt("constsB", [128, 8, 128]),
        ada_w00=dt("ada_w00", [D, 3 * D]), ada_w01=dt("ada_w01", [D, 3 * D]),
        ada_w10=dt("ada_w10", [D, 3 * D]), ada_w11=dt("ada_w11", [D, 3 * D]),
        a_w_in=dt("a_w_in", [D, 2 * D]), a_w_s=dt("a_w_s", [16, 128, 128]), a_w_out=dt("a_w_out", [D, D]),
        mlp_w_up0=dt("mlp_w_up0", [D, 4 * D]), mlp_w_down0=dt("mlp_w_down0", [4 * D, D]),
        mlp_w_up1=dt("mlp_w_up1", [D, 4 * D]), mlp_w_down1=dt("mlp_w_down1", [4 * D, D]),
        b_w_qkv=dt("b_w_qkv", [D, 9 * D]), b_w_out=dt("b_w_out", [D, D]))
    io["y"] = nc.dram_tensor("y", [T, D], F32, kind="ExternalOutput").ap()
    io["xsp"] = nc.dram_tensor("xsp", [128, 8 * T], F32).ap()
    io["snd"] = [nc.dram_tensor(f"snd{c}", [128, T], BF16).ap() for c in range(8)]
    io["rcv"] = [nc.dram_tensor(f"rcv{c}", [256, T], BF16).ap() for c in range(8)]
    with contextlib.ExitStack() as es:
        P = Prog(nc, es)
        A = Arena(nc)
        cx = Ctx()
        common_setup(nc, P, A, cx, F_NCOLS)
        xT, xT_at, _ = A.alloc([128, 8, T], F32)
        mark = A.off
        body_A(nc, P, A, cx, io, xT, stage=None, fused=True)
        P.barrier()
        A.off = mark
        body_B(nc, P, A, cx, io, xT, xT_at, stage=None, fused=True, cb=A_NCOLS)
        P.wait_all("sp", cx.out_toks)
        P.finish()
    return nc


def prep_F(inp):
    mA = prep_A(inp)
    mB = prep_B(inp, None)
    maps = []
    for core in range(NCORES):
        a, b = mA[core], mB[core]
        maps.append({
            "x": a["x"], "cols": np.ascontiguousarray(np.concatenate([a["cols"], b["cols"]], axis=1)),
            "rows": a["rows"], "bbias": a["bbias"], "constsA": a["consts"], "constsB": b["consts"],
            "ada_w00": a["ada_w0"], "ada_w01": a["ada_w1"], "ada_w10": b["ada_w0"], "ada_w11": b["ada_w1"],
            "a_w_in": a["a_w_in"], "a_w_s": a["a_w_s"], "a_w_out": a["a_w_out"],
            "mlp_w_up0": a["mlp_w_up"], "mlp_w_down0": a["mlp_w_down"],
            "mlp_w_up1": b["mlp_w_up"], "mlp_w_down1": b["mlp_w_down"],
            "b_w_qkv": b["b_w_qkv"], "b_w_out": b["b_w_out"],
        })
    return maps


def run_F(inp, trace=False, ncores=NCORES):
    nc = build_F()
    maps = prep_F(inp)[:ncores]
    res = run_bass_kernel_spmd(nc, maps, core_ids=list(range(ncores)), trace=trace)
    out = np.zeros((4, 4096, D), dtype=np.float32)
    for core in range(ncores):
        b, half = core // 2, core % 2
        out[b, half * T:(half + 1) * T] = res.results[core]["y"]
    return out, res


def kernel(**inputs):
    inp = {k: np.asarray(v) for k, v in inputs.items()}
    out, _ = run_F(inp)
    return out
```

```python
import contextlib
import numpy as np
import concourse.bass as bass
import concourse.mybir as mybir
from concourse.bass_utils import run_bass_kernel_spmd

F32 = mybir.dt.float32
BF16 = mybir.dt.bfloat16
AF = mybir.ActivationFunctionType
ALU = mybir.AluOpType
AX = mybir.AxisListType

D = 1024
T = 2048
NT = 4
TT = 512
NCORES = 8
ALPHA = 4.0 ** 0.25
INV_ALPHA = 1.0 / ALPHA
LN_EPS = 1e-5
EPS_P = LN_EPS / (ALPHA * ALPHA)
GELU = AF.Gelu_apprx_tanh


class Eng:
    def __init__(self, name, sem):
        self.name = name
        self.sem = sem
        self.count = 0
        self.ops = []
        self.waited = {}


class Prog:
    def __init__(self, nc, es, nsem=100):
        self.nc = nc
        self.free_sems = [es.enter_context(nc.semaphore(f"s{i}")) for i in range(nsem)]
        self.E = {}
        for n in ["pe", "act", "dve", "pool", "sp"]:
            self.E[n] = Eng(n, self.free_sems.pop())
        self.last_w = {}
        self.readers = {}
        self.sem_id = {}

    def new_sem(self):
        return self.free_sems.pop()

    def slot(self):
        sl = {"sem": self.new_sem(), "count": 0}
        self.slots = getattr(self, "slots", [])
        self.slots.append(sl)
        return sl

    def barrier(self, extra=()):
        toks = [(e.sem, e.count) for e in self.E.values() if e.count > 0]
        toks += [(sl["sem"], sl["count"]) for sl in getattr(self, "slots", []) if sl["count"] > 0]
        toks += list(extra)
        for eng in self.E:
            self.wait_all(eng, toks)

    def _sid(self, s):
        return id(s)

    @staticmethod
    def _norm(reads, writes):
        r2, w2 = [], list(writes)
        for k in reads:
            if isinstance(k, tuple) and k and k[0] == "ps":
                if k not in w2:
                    w2.append(k)
            else:
                r2.append(k)
        return r2, w2

    def _deps(self, eng, reads, writes, extra):
        toks = []
        for r in reads:
            t = self.last_w.get(r)
            if t is not None:
                toks.append(t)
        for w in writes:
            t = self.last_w.get(w)
            if t is not None:
                toks.append(t)
            for t in self.readers.get(w, {}).values():
                toks.append(t)
        for t in extra:
            if t is not None:
                toks.append(t)
        e = self.E[eng]
        waits = []
        for (s, v) in toks:
            if eng == "pe" and s is e.sem:
                continue
            k = self._sid(s)
            if e.waited.get(k, 0) < v:
                e.waited[k] = v
                waits.append((s, v))
        return waits

    def _record(self, tok, reads, writes):
        for r in reads:
            d = self.readers.setdefault(r, {})
            k = self._sid(tok[0])
            if k not in d or d[k][1] < tok[1]:
                d[k] = tok
        for w in writes:
            self.last_w[w] = tok
            self.readers[w] = {}

    def op(self, eng, fn, reads=(), writes=(), extra=()):
        reads, writes = self._norm(reads, writes)
        e = self.E[eng]
        waits = self._deps(eng, reads, writes, extra)
        e.count += 1
        tok = (e.sem, e.count)
        e.ops.append((waits, fn, (e.sem, 1)))
        self._record(tok, reads, writes)
        return tok

    def mm_group(self, fns, reads=(), writes=(), extra=()):
        reads, writes = self._norm(reads, writes)
        e = self.E["pe"]
        waits = self._deps("pe", reads, writes, extra)
        for i, fn in enumerate(fns):
            last = i == len(fns) - 1
            e.ops.append((waits if i == 0 else [], fn, (e.sem, 1) if last else None))
        e.count += 1
        tok = (e.sem, e.count)
        self._record(tok, reads, writes)
        return tok

    def dma(self, eng, pairs, slot, reads=(), writes=(), extra=()):
        e = self.E[eng]
        waits = self._deps(eng, reads, writes, extra)
        for i, (o, i_) in enumerate(pairs):
            e.ops.append((waits if i == 0 else [],
                          (lambda h, o=o, i_=i_: h.dma_start(out=o, in_=i_)),
                          (slot["sem"], 16)))
            slot["count"] += 16
        tok = (slot["sem"], slot["count"])
        self._record(tok, reads, writes)
        return tok

    def wait_all(self, eng, toks):
        e = self.E[eng]
        waits = self._deps(eng, (), (), toks)
        e.ops.append((waits, None, None))

    def replay(self, eng, h):
        for waits, fn, inc in self.E[eng].ops:
            for s, v in waits:
                h.wait_ge(s, v)
            if fn is None:
                continue
            inst = fn(h)
            if inc is not None:
                inst.then_inc(inc[0], inc[1])

    def finish(self):
        nc = self.nc
        with nc.Block() as block:
            @block.tensor
            def _(h):
                self.replay("pe", h)

            @block.scalar
            def _(h):
                self.replay("act", h)

            @block.vector
            def _(h):
                self.replay("dve", h)

            @block.gpsimd
            def _(h):
                self.replay("pool", h)

            @block.sync
            def _(h):
                self.replay("sp", h)


class Arena:
    def __init__(self, nc, base=16512, cap=229376):
        self.nc = nc
        self.cap = cap
        self.off = base
        self.n = 0

    def alloc(self, shape, dtype, at=None, name=None):
        nbytes = int(np.prod(shape[1:])) * (4 if dtype == F32 else 2)
        nbytes = (nbytes + 63) // 64 * 64
        if at is None:
            at = self.off
            self.off += nbytes
            assert self.off <= self.cap, f"SBUF overflow {self.off}"
        else:
            assert at + nbytes <= self.cap, "SBUF overflow (at)"
        self.n += 1
        t = self.nc.alloc_sbuf_tensor_at(name or f"sb{self.n}", list(shape), dtype, offset=at, align_bytes=64)
        return t, at, nbytes


class Psum:
    def __init__(self, nc, P):
        self.banks = [nc.alloc_psum_tensor(f"psb{i}", [128, 512], F32) for i in range(8)]
        self.i = 0

    def next(self):
        b = self.i
        self.i = (self.i + 1) % 7
        return b, self.banks[b]


def col(v):
    v = np.asarray(v, dtype=np.float32)
    return np.ascontiguousarray(v.reshape(-1, 128).T)


class Ctx:
    pass


def emit_ada(P, cx, wada_ap, bias_cols, out_mod, tag):
    wv = wada_ap.rearrange("(kc p) n -> p kc n", p=128)
    ps = cx.ps_ada
    NPIECE = 24
    PW = 3072 // NPIECE
    for pc in range(NPIECE):
        bi = cx.ada_i % 2
        cx.ada_i += 1
        buf = cx.ada_buf[bi]
        P.dma("pool", [(buf[:], wv[:, :, pc * PW:(pc + 1) * PW])], cx.ada_slot[bi],
              writes=[("adabuf", bi)])
        for fl in range(PW // 128):
            fch = pc * (PW // 128) + fl
            fns = []
            for kc in range(8):
                fns.append(lambda h, kc=kc, fl=fl, fch=fch, buf=buf: h.matmul(
                    ps[:, fch:fch + 1], buf[:, kc, fl * 128:(fl + 1) * 128], cx.sc[:, kc:kc + 1],
                    start=(kc == 0), stop=(kc == 7)))
            P.mm_group(fns, reads=[("adabuf", bi), "sc"], writes=[("ps_ada", fch)])
    rd = [("ps_ada", f) for f in range(24)]
    P.op("dve", lambda h: h.tensor_tensor(out_mod[:, :], ps[:, 0:24], bias_cols, ALU.add),
         reads=rd + ["cols"], writes=[("mod", tag)])
    P.op("dve", lambda h: h.tensor_scalar(out_mod[:, 8:16], out_mod[:, 8:16], 1.0, None, ALU.add),
         reads=[("mod", tag)], writes=[("mod", tag)])
    P.op("dve", lambda h: h.tensor_scalar(out_mod[:, 16:24], out_mod[:, 16:24], 1.0, INV_ALPHA, ALU.add, ALU.mult),
         reads=[("mod", tag)], writes=[("mod", tag)])


def emit_modulate(P, cx, xT, hT, mod, tag, tt, eng="dve"):
    sl = slice(tt * TT, (tt + 1) * TT)
    for c in range(8):
        if eng == "act":
            P.op("act", lambda h, c=c: h.activation(hT[:, c, sl], xT[:, c, sl], AF.Identity,
                                                    bias=mod[:, c:c + 1], scale=mod[:, 8 + c:9 + c]),
                 reads=[("x", c, tt), ("mod", tag)], writes=[("h", c, tt)])
        else:
            P.op(eng, lambda h, c=c: h.tensor_scalar(hT[:, c, sl], xT[:, c, sl], mod[:, 8 + c:9 + c],
                                                     mod[:, c:c + 1], ALU.mult, ALU.add),
                 reads=[("x", c, tt), ("mod", tag)], writes=[("h", c, tt)])


def emit_ln(P, cx, xT, tt, g_cols, b_cols):
    sl = slice(tt * TT, (tt + 1) * TT)
    s1, s2, sq = cx.ln_s1, cx.ln_s2, cx.ln_sq
    P.op("pool", lambda h: h.tensor_tensor(s1[:], xT[:, 0, sl], xT[:, 1, sl], ALU.add),
         reads=[("x", 0, tt), ("x", 1, tt)], writes=["ln_s1"])
    for c in range(2, 8):
        P.op("pool", lambda h, c=c: h.tensor_tensor(s1[:], s1[:], xT[:, c, sl], ALU.add),
             reads=[("x", c, tt), "ln_s1"], writes=["ln_s1"])
    for c in range(8):
        if c == 0:
            P.op("act", lambda h: h.activation(s2[:], xT[:, 0, sl], AF.Square),
                 reads=[("x", 0, tt)], writes=["ln_s2"])
        else:
            k = c % 2
            P.op("act", lambda h, c=c, k=k: h.activation(sq[k][:], xT[:, c, sl], AF.Square),
                 reads=[("x", c, tt)], writes=[("ln_sq", k)])
            P.op("dve", lambda h, k=k: h.tensor_tensor(s2[:], s2[:], sq[k][:], ALU.add),
                 reads=[("ln_sq", k), "ln_s2"], writes=["ln_s2"])
    b1, ps1 = cx.psum.next()
    P.mm_group([lambda h: h.matmul(ps1[:], cx.onesm[:], s1[:], start=True, stop=True)],
               reads=["ln_s1", "onesm"], writes=[("ps", b1)])
    b2, ps2 = cx.psum.next()
    P.mm_group([lambda h: h.matmul(ps2[:], cx.onesm[:], s2[:], start=True, stop=True)],
               reads=["ln_s2", "onesm"], writes=[("ps", b2)])
    mean, msq, rstd = cx.ln_mean, cx.ln_tmp[1], cx.ln_rstd
    P.op("act", lambda h: h.activation(mean[:], ps1[:], AF.Identity), reads=[("ps", b1)], writes=["ln_mean"])
    P.op("act", lambda h: h.activation(msq[:], ps1[:], AF.Square), reads=[("ps", b1)], writes=[("ln_tmp", 1)])
    P.op("dve", lambda h: h.tensor_tensor(rstd[:], ps2[:], msq[:], ALU.subtract),
         reads=[("ps", b2), ("ln_tmp", 1)], writes=["ln_rstd"])
    P.op("dve", lambda h: h.tensor_scalar(rstd[:], rstd[:], EPS_P, None, ALU.add),
         reads=["ln_rstd"], writes=["ln_rstd"])
    P.op("dve", lambda h: h.reciprocal(rstd[:], rstd[:]), reads=["ln_rstd"], writes=["ln_rstd"])
    P.op("act", lambda h: h.activation(rstd[:], rstd[:], AF.Sqrt), reads=["ln_rstd"], writes=["ln_rstd"])
    for c in range(8):
        k = c % 2
        tmp = cx.ln_tmp[k]
        P.op("dve", lambda h, c=c, tmp=tmp: h.tensor_tensor(tmp[:], xT[:, c, sl], mean[:], ALU.subtract),
             reads=[("x", c, tt), "ln_mean"], writes=[("ln_tmp", k)])
        P.op("pool", lambda h, tmp=tmp: h.tensor_tensor(tmp[:], tmp[:], rstd[:], ALU.mult),
             reads=["ln_rstd", ("ln_tmp", k)], writes=[("ln_tmp", k)])
        P.op("act", lambda h, c=c, tmp=tmp: h.activation(xT[:, c, sl], tmp[:], AF.Identity,
                                                         bias=b_cols[:, c:c + 1], scale=g_cols[:, c:c + 1]),
             reads=[("ln_tmp", k), "cols"], writes=[("x", c, tt)])


def emit_mlp(P, cx, xT, hT, mod, tag, w_up_ap, w_down_ap, g_cols, b_cols, after_ln=None):
    wu_v = w_up_ap.rearrange("(kc p) n -> p kc n", p=128)
    wd_v = w_down_ap.rearrange("(fc p) n -> p fc n", p=128)
    for q in range(4):
        bi = cx.mlp_i % 2
        cx.mlp_i += 1
        wu, wd = cx.wu_buf[bi], cx.wd_buf[bi]
        P.dma("pool", [(wu[:, 0:4, :], wu_v[:, 0:4, q * 1024:(q + 1) * 1024]),
                       (wu[:, 4:8, :], wu_v[:, 4:8, q * 1024:(q + 1) * 1024])],
              cx.wu_slot[bi], writes=[("wu", bi)])
        P.dma("pool", [(wd[:, 0:4, :], wd_v[:, q * 8:q * 8 + 4, :]),
                       (wd[:, 4:8, :], wd_v[:, q * 8 + 4:q * 8 + 8, :])],
              cx.wd_slot[bi], writes=[("wd", bi)])
        def mlp_tile(q, tt, bi, wu, wd):
            sl = slice(tt * TT, (tt + 1) * TT)
            ai = cx.a_i % 2
            cx.a_i += 1
            aT = cx.a_buf[ai]
            for f in range(8):
                b, ps = cx.psum.next()
                fns = [(lambda h, kc=kc, f=f, ps=ps, wu=wu: h.matmul(
                    ps[:], wu[:, kc, f * 128:(f + 1) * 128], hT[:, kc, sl], start=(kc == 0), stop=(kc == 7)))
                    for kc in range(8)]
                P.mm_group(fns, reads=[("wu", bi)] + [("h", kc, tt) for kc in range(8)], writes=[("ps", b)])
                k = cx.r_i % 2
                cx.r_i += 1
                rt = cx.relu_tmp[k]
                P.op("act", lambda h, ps=ps, rt=rt: h.activation(rt[:], ps[:], AF.Relu),
                     reads=[("ps", b)], writes=[("ln_sq", k)])
                P.op("pool", lambda h, rt=rt, aT=aT, f=f: h.tensor_tensor(aT[:, f, :], rt[:], rt[:], ALU.mult),
                     reads=[("ln_sq", k)], writes=[("a", ai, f)])
            for oc in range(8):
                b, ps = cx.psum.next()
                fns = [(lambda h, f=f, oc=oc, ps=ps, wd=wd, aT=aT: h.matmul(
                    ps[:], wd[:, f, oc * 128:(oc + 1) * 128], aT[:, f, :], start=(f == 0), stop=(f == 7)))
                    for f in range(8)]
                P.mm_group(fns, reads=[("wd", bi)] + [("a", ai, f) for f in range(8)], writes=[("ps", b)])
                P.op("dve", lambda h, ps=ps, oc=oc: h.scalar_tensor_tensor(
                    xT[:, oc, sl], ps[:], mod[:, 16 + oc:17 + oc], xT[:, oc, sl], ALU.mult, ALU.add),
                    reads=[("ps", b), ("mod", tag), ("x", oc, tt)], writes=[("x", oc, tt)])
            if q == 3:
                emit_ln(P, cx, xT, tt, g_cols, b_cols)
                if after_ln is not None:
                    after_ln(tt)

        for tt in range(NT):
            mlp_tile(q, tt, bi, wu, wd)


def emit_load_xT(P, cx, x_ap, xT, key="x"):
    xv = x_ap.rearrange("(tb p) f -> p tb f", p=128)
    for tt in range(NT):
        si = cx.stg_i % 2
        cx.stg_i += 1
        stg = cx.stg[si]
        P.dma("sp", [(stg[:, 0:2, :], xv[:, tt * 4:tt * 4 + 2, :]),
                     (stg[:, 2:4, :], xv[:, tt * 4 + 2:tt * 4 + 4, :])], cx.stg_slot[si],
              writes=[("stg", si)])
        for c in range(8):
            b, ps = cx.psum.next()
            fns = [(lambda h, tb=tb, c=c, ps=ps, stg=stg: h.matmul(
                ps[:, tb * 128:(tb + 1) * 128], stg[:, tb, c * 128:(c + 1) * 128], cx.ident[:],
                start=True, stop=True)) for tb in range(4)]
            P.mm_group(fns, reads=[("stg", si), "ident"], writes=[("ps", b)])
            eng = "act" if c % 2 == 0 else "dve"
            sl = slice(tt * TT, (tt + 1) * TT)
            if eng == "act":
                P.op("act", lambda h, ps=ps, c=c, sl=sl: h.activation(xT[:, c, sl], ps[:], AF.Identity),
                     reads=[("ps", b)], writes=[(key, c, tt)])
            else:
                P.op("dve", lambda h, ps=ps, c=c, sl=sl: h.tensor_copy(xT[:, c, sl], ps[:]),
                     reads=[("ps", b)], writes=[(key, c, tt)])


def emit_store_x(P, cx, xT, out_ap, tt):
    ov = out_ap.rearrange("(tb p) f -> p tb f", p=128)
    for tb in range(4):
        si = cx.ost_i % 2
        cx.ost_i += 1
        stg = cx.ost[si]
        tsl = slice(tt * TT + tb * 128, tt * TT + (tb + 1) * 128)
        for half in range(2):
            b, ps = cx.psum.next()
            fns = [(lambda h, cc=cc, half=half, ps=ps, tsl=tsl: h.matmul(
                ps[:, cc * 128:(cc + 1) * 128], xT[:, half * 4 + cc, tsl], cx.ident[:],
                start=True, stop=True)) for cc in range(4)]
            P.mm_group(fns, reads=[("x", half * 4 + cc, tt) for cc in range(4)] + ["ident"],
                       writes=[("ps", b)])
            if half == 0:
                P.op("act", lambda h, ps=ps, stg=stg: h.activation(stg[:, 0:512], ps[:], AF.Identity),
                     reads=[("ps", b)], writes=[("ost", si, 0)])
            else:
                P.op("dve", lambda h, ps=ps, stg=stg: h.tensor_copy(stg[:, 512:1024], ps[:]),
                     reads=[("ps", b)], writes=[("ost", si, 1)])
        tok = P.dma("sp", [(ov[:, tt * 4 + tb, :], stg[:])], cx.out_slot[si],
                    reads=[("ost", si, 0), ("ost", si, 1)])
        cx.out_toks.append(tok)


def common_setup(nc, P, A, cx, ncols):
    cx.psum = Psum(nc, P)
    cx.cols, _, _ = A.alloc([128, ncols], F32)
    cx.ident_f, _, _ = A.alloc([128, 128], F32)
    cx.ident = cx.ident_f
    cx.onesm, _, _ = A.alloc([128, 128], F32)
    cx.sc, _, _ = A.alloc([128, 8], BF16)
    cx.mod = [A.alloc([128, 24], F32)[0] for _ in range(2)]
    cx.ln_s1, _, _ = A.alloc([128, TT], F32)
    cx.ln_s2, _, _ = A.alloc([128, TT], F32)
    cx.ln_sq = [A.alloc([128, TT], F32)[0] for _ in range(2)]
    cx.ln_mean, _, _ = A.alloc([128, TT], F32)
    cx.ln_rstd, _, _ = A.alloc([128, TT], F32)
    cx.ln_tmp = [A.alloc([128, TT], F32)[0] for _ in range(2)]
    cx.relu_tmp = cx.ln_sq
    cx.ada_buf = [A.alloc([128, 8, 128], BF16)[0] for _ in range(2)]
    cx.ada_slot = [P.slot() for _ in range(2)]
    cx.ada_i = 0
    cx.mlp_i = 0
    cx.a_i = 0
    cx.r_i = 0
    cx.stg_i = 0
    cx.ost_i = 0
    cx.out_toks = []
    cx.out_slot = [P.slot() for _ in range(2)]
    cx.const_slot = P.slot()
    cx.stg_slot = [P.slot() for _ in range(2)]
    cx.wu_slot = [P.slot() for _ in range(2)]
    cx.wd_slot = [P.slot() for _ in range(2)]
    cx.ps_ada = cx.psum.banks[7]


def fence_tokens(P, keys):
    toks = []
    for k in keys:
        t = P.last_w.get(k)
        if t is not None:
            toks.append(t)
        toks.extend(P.readers.get(k, {}).values())
    return toks


A_COLS = dict(c=0, adab0=8, adab1=32, lng0=56, lnb0=64, lng1=72, lnb1=80, binu=88)
A_NCOLS = 96


def body_A(nc, P, A, cx, io, xT, stage=None, fused=False):
    x_in, cols_in, rows_in, bb_in, consts_in = io["x"], io["cols"], io["rows"], io["bbias"], io["constsA"]
    ada_w0, ada_w1, w_in, w_s, w_out = io["ada_w00"], io["ada_w01"], io["a_w_in"], io["a_w_s"], io["a_w_out"]
    w_up, w_down, y_out = io["mlp_w_up0"], io["mlp_w_down0"], io["y"]
    mark = A.off
    win, _, _ = A.alloc([128, 8, 2048], BF16)
    wout, _, _ = A.alloc([128, 8, 1024], BF16)
    rows, _, _ = A.alloc([128, 3, 1024], F32)
    bbias, _, _ = A.alloc([128, 8, 128], F32)
    wcT, _, _ = A.alloc([128, 16, 128], BF16)
    vst, _, _ = A.alloc([128, 2, 6], F32)
    vmv, _, _ = A.alloc([128, 2], F32)
    vrs, _, _ = A.alloc([128, 1], F32)
    mark1b = A.off
    hT1 = [A.alloc([128, 8, TT], BF16)[0] for _ in range(2)]
    uT1, _, _ = A.alloc([128, 8, TT], BF16)
    uT = [uT1, uT1]
    vtok1, _, _ = A.alloc([128, 4, 1024], BF16)
    vtok = [vtok1, vtok1]
    vg = [A.alloc([128, 1024], F32)[0] for _ in range(2)]
    gt = [A.alloc([128, TT], F32)[0] for _ in range(2)]
    end1 = A.off
    A.off = mark1b
    cx.stg = [A.alloc([128, 4, 1024], F32)[0] for _ in range(2)]
    tril, _, _ = A.alloc([128, 128], F32)
    identb, _, _ = A.alloc([128, 128], BF16)
    ws_f, _, _ = A.alloc([128, 16, 128], F32)
    ws_b, _, _ = A.alloc([128, 16, 128], BF16)
    end1 = max(end1, A.off)
    A.off = mark
    h2T, _, _ = A.alloc([128, 8, T], BF16)
    cx.wu_buf = [A.alloc([128, 8, 1024], BF16)[0] for _ in range(2)]
    cx.wd_buf = [A.alloc([128, 8, 1024], BF16)[0] for _ in range(2)]
    cx.a_buf = [A.alloc([128, 8, TT], BF16)[0] for _ in range(2)]
    cx.ost = [A.alloc([128, 1024], F32)[0] for _ in range(2)]
    end2 = A.off
    A.off = max(end1, end2)
    print("SBUF plan A: persistent", mark, "phase1 end", end1, "phase2 end", end2)

    P.dma("sp", [(cx.cols[:], cols_in)], P.slot(), writes=["cols"])
    P.dma("sp", [(cx.ident_f[:], consts_in[:, 0, :]), (cx.onesm[:], consts_in[:, 1, :]),
                 (tril[:], consts_in[:, 2, :])], P.slot(), writes=["ident", "onesm", "tril"])
    P.dma("sp", [(rows[:], rows_in)], P.slot(), writes=["rows"])
    P.dma("sp", [(bbias[:], bb_in)], P.slot(), writes=["bbias"])
    P.dma("sp", [(ws_f[:], w_s.rearrange("g t s -> t g s"))], P.slot(), writes=["ws_f"])
    P.op("act", lambda h: h.activation(cx.sc[:], cx.cols[:, 0:8], AF.Silu), reads=["cols"], writes=["sc"])
    emit_ada(P, cx, ada_w0, cx.cols[:, 8:32], cx.mod[0], 0)
    wv = w_in.rearrange("(kc p) n -> p kc n", p=128)
    for j in range(4):
        P.dma("pool", [(win[:, :, j * 512:(j + 1) * 512], wv[:, :, j * 512:(j + 1) * 512])], P.slot(),
              writes=[("win", j)])
    wout_slot = P.slot()
    P.dma("pool", [(wout[:], w_out.rearrange("(kc p) n -> p kc n", p=128))], wout_slot, writes=["wout"])
    P.op("dve", lambda h: h.tensor_copy(identb[:], cx.ident_f[:]), reads=["ident"], writes=["identb"])
    P.op("dve", lambda h: h.tensor_tensor(ws_b[:], ws_f[:], tril[:, None, :].to_broadcast([128, 16, 128]), ALU.mult),
         reads=["ws_f", "tril"], writes=["ws_b"])
    for g4 in range(4):
        b, ps = cx.psum.next()
        fns = [(lambda h, gi=gi, g4=g4, ps=ps: h.matmul(
            ps[:, gi * 128:(gi + 1) * 128], ws_b[:, g4 * 4 + gi, :], identb[:], start=True, stop=True))
            for gi in range(4)]
        P.mm_group(fns, reads=["ws_b", "identb"], writes=[("ps", b)])
        P.op("dve", lambda h, g4=g4, ps=ps: h.tensor_copy(
            wcT[:, g4 * 4:(g4 + 1) * 4, :], ps[:].rearrange("p (g t) -> p g t", g=4)),
            reads=[("ps", b)], writes=[("wcT", g4)])
    emit_load_xT(P, cx, x_in, xT)
    if stage == "load":
        for tt in range(NT):
            emit_store_x(P, cx, xT, y_out, tt)
        return
    f1 = fence_tokens(P, [("stg", 0), ("stg", 1), "tril", "identb", "ws_f", "ws_b"])
    for e_ in ("dve", "act", "pool"):
        P.wait_all(e_, f1)

    def mix_tile(tt):
        sl = slice(tt * TT, (tt + 1) * TT)
        pi = tt % 2
        hT = hT1[pi]
        for c in range(8):
            P.op("dve", lambda h, c=c, hT=hT: h.tensor_scalar(
                hT[:, c, :], xT[:, c, sl], cx.mod[0][:, 8 + c:9 + c], cx.mod[0][:, c:c + 1], ALU.mult, ALU.add),
                reads=[("x", c, tt), ("mod", 0)], writes=[("h1", pi, c)])
        for fc in range(8):
            b, ps = cx.psum.next()
            fns = [(lambda h, kc=kc, fc=fc, ps=ps, hT=hT: h.matmul(
                ps[:], win[:, kc, fc * 128:(fc + 1) * 128], hT[:, kc, :], start=(kc == 0), stop=(kc == 7)))
                for kc in range(8)]
            P.mm_group(fns, reads=[("win", fc // 4)] + [("h1", pi, kc) for kc in range(8)], writes=[("ps", b)])
            P.op("act", lambda h, fc=fc, ps=ps: h.activation(
                uT[pi][:, fc, :], ps[:], GELU, bias=cx.cols[:, A_COLS["binu"] + fc:A_COLS["binu"] + fc + 1]),
                reads=[("ps", b), "cols"], writes=[("u", 0, fc)])
        for tb in range(4):
            vi = (tt * 4 + tb) % 2
            vgb = vg[vi]
            for half in range(2):
                b, ps = cx.psum.next()
                fns = [(lambda h, kc=kc, half=half, ps=ps, hT=hT, tb=tb: h.matmul(
                    ps[:], hT[:, kc, tb * 128:(tb + 1) * 128],
                    win[:, kc, 1024 + half * 512:1024 + (half + 1) * 512], start=(kc == 0), stop=(kc == 7)))
                    for kc in range(8)]
                P.mm_group(fns, reads=[("win", 2 + half)] + [("h1", pi, kc) for kc in range(8)],
                           writes=[("ps", b)])
                P.op("dve", lambda h, ps=ps, half=half, vgb=vgb: h.tensor_tensor(
                    vgb[:, half * 512:(half + 1) * 512], ps[:], rows[:, 0, half * 512:(half + 1) * 512], ALU.add),
                    reads=[("ps", b), "rows"], writes=[("vg", vi, half)])
                P.op("act", lambda h, half=half, vgb=vgb: h.activation(
                    vgb[:, half * 512:(half + 1) * 512], vgb[:, half * 512:(half + 1) * 512], GELU),
                    reads=[("vg", vi, half)], writes=[("vg", vi, half)])
                P.op("dve", lambda h, half=half, vgb=vgb: h.bn_stats(
                    vst[:, half, :], vgb[:, half * 512:(half + 1) * 512]),
                    reads=[("vg", vi, half)], writes=[("vst", half)])
            P.op("dve", lambda h: h.bn_aggr(vmv[:], vst[:].rearrange("p a b -> p (a b)")),
                 reads=[("vst", 0), ("vst", 1)], writes=["vmv"])
            P.op("dve", lambda h: h.tensor_scalar(vrs[:], vmv[:, 1:2], LN_EPS, None, ALU.add),
                 reads=["vmv"], writes=["vrs"])
            P.op("dve", lambda h: h.reciprocal(vrs[:], vrs[:]), reads=["vrs"], writes=["vrs"])
            P.op("act", lambda h: h.activation(vrs[:], vrs[:], AF.Sqrt), reads=["vrs"], writes=["vrs"])
            P.op("dve", lambda h, vgb=vgb: h.tensor_scalar(
                vgb[:], vgb[:], vmv[:, 0:1], vrs[:, 0:1], ALU.subtract, ALU.mult),
                reads=[("vg", vi, 0), ("vg", vi, 1), "vmv", "vrs"], writes=[("vg", vi, 0), ("vg", vi, 1)])
            P.op("pool", lambda h, vgb=vgb: h.tensor_tensor(vgb[:], vgb[:], rows[:, 1, :], ALU.mult),
                 reads=[("vg", vi, 0), ("vg", vi, 1), "rows"], writes=[("vg", vi, 0), ("vg", vi, 1)])
            P.op("pool", lambda h, vgb=vgb, tb=tb: h.tensor_tensor(vtok[pi][:, tb, :], vgb[:], rows[:, 2, :], ALU.add),
                 reads=[("vg", vi, 0), ("vg", vi, 1), "rows"], writes=[("vtok", 0, tb)])
        for fc in range(8):
            b, ps = cx.psum.next()
            fns = []
            for tb in range(4):
                for gi in range(2):
                    g = 2 * fc + gi
                    fns.append(lambda h, tb=tb, gi=gi, g=g, ps=ps: h.matmul(
                        ps[gi * 64:(gi + 1) * 64, tb * 128:(tb + 1) * 128],
                        vtok[pi][:, tb, g * 64:(g + 1) * 64], wcT[:, g, :], start=True, stop=True))
            P.mm_group(fns, reads=[("vtok", 0, tb) for tb in range(4)] + [("wcT", (2 * fc) // 4)],
                       writes=[("ps", b)])
            k = fc % 2
            P.op("dve", lambda h, ps=ps, fc=fc, k=k: h.tensor_tensor(
                gt[k][:].rearrange("p (a t) -> p a t", a=4), ps[:].rearrange("p (a t) -> p a t", a=4),
                bbias[:, fc, None, :].to_broadcast([128, 4, 128]), ALU.add),
                reads=[("ps", b), "bbias"], writes=[("gt", k)])
            P.op("pool", lambda h, fc=fc, k=k: h.tensor_tensor(uT[pi][:, fc, :], gt[k][:], uT[pi][:, fc, :], ALU.mult),
                 reads=[("gt", k), ("u", 0, fc)], writes=[("u", 0, fc)])
        for oc in range(8):
            b, ps = cx.psum.next()
            fns = [(lambda h, fc=fc, oc=oc, ps=ps: h.matmul(
                ps[:], wout[:, fc, oc * 128:(oc + 1) * 128], uT[pi][:, fc, :], start=(fc == 0), stop=(fc == 7)))
                for fc in range(8)]
            P.mm_group(fns, reads=["wout"] + [("u", 0, fc) for fc in range(8)], writes=[("ps", b)])
            P.op("dve", lambda h, ps=ps, oc=oc: h.scalar_tensor_tensor(
                xT[:, oc, sl], ps[:], cx.mod[0][:, 16 + oc:17 + oc], xT[:, oc, sl], ALU.mult, ALU.add),
                reads=[("ps", b), ("mod", 0), ("x", oc, tt)], writes=[("x", oc, tt)])
        emit_ln(P, cx, xT, tt, cx.cols[:, A_COLS["lng0"]:A_COLS["lng0"] + 8],
                cx.cols[:, A_COLS["lnb0"]:A_COLS["lnb0"] + 8])

    for tt in range(NT):
        mix_tile(tt)

    if stage == "mixA":
        dbg = nc.dram_tensor("dbg", [128, 16384], F32, kind="ExternalOutput").ap()
        dsl = P.slot()
        allk = list(P.last_w.keys())
        cx.out_toks.append(P.dma("sp", [(dbg[:, 0:24], cx.mod[0][:])], dsl, reads=allk))
        cx.out_toks.append(P.dma("pool", [(dbg[:, 1024:3072], wcT[:].rearrange("p g t -> p (g t)"))], dsl, reads=allk))
        cx.out_toks.append(P.dma("pool", [(dbg[:, 4096:8192], hT1[1][:].rearrange("p g t -> p (g t)"))], dsl, reads=allk))
        cx.out_toks.append(P.dma("pool", [(dbg[:, 8192:12288], uT1[:].rearrange("p g t -> p (g t)"))], dsl, reads=allk))
        cx.out_toks.append(P.dma("pool", [(dbg[:, 12288:16384], vtok1[:].rearrange("p g t -> p (g t)"))], dsl, reads=allk))
        for e_ in ("pe", "act", "dve", "pool"):
            P.wait_all(e_, list(cx.out_toks))
        for tt in range(NT):
            emit_store_x(P, cx, xT, y_out, tt)
        return
    emit_ada(P, cx, ada_w1, cx.cols[:, 32:56], cx.mod[1], 1)
    p1_keys = [("win", j) for j in range(4)] + ["wout", "rows", "bbias", "vst", "vmv", "vrs"] + \
              [("wcT", g4) for g4 in range(4)] + [("h1", p, c) for p in range(2) for c in range(8)] + \
              [("u", 0, c) for c in range(8)] + [("vtok", 0, tb) for tb in range(4)] + \
              [("vg", i, hf) for i in range(2) for hf in range(2)] + [("gt", 0), ("gt", 1)]
    f2 = fence_tokens(P, p1_keys)
    for e_ in ("dve", "act", "pool"):
        P.wait_all(e_, f2)
    for tt in range(NT):
        emit_modulate(P, cx, xT, h2T, cx.mod[1], 1, tt)

    def after_ln(tt):
        emit_store_x(P, cx, xT, y_out, tt)

    emit_mlp(P, cx, xT, h2T, cx.mod[1], 1, w_up, w_down,
             cx.cols[:, A_COLS["lng1"]:A_COLS["lng1"] + 8], cx.cols[:, A_COLS["lnb1"]:A_COLS["lnb1"] + 8],
             after_ln=(None if fused else after_ln))


def build_A(stage=None):
    nc = bass.Bass("TRN2", target_bir_lowering=False)
    dt = lambda n, s: nc.dram_tensor(n, s, F32, kind="ExternalInput").ap()
    x_in = dt("x", [T, D])
    cols_in = dt("cols", [128, A_NCOLS])
    rows_in = dt("rows", [128, 3, 1024])
    bb_in = dt("bbias", [128, 8, 128])
    consts_in = dt("consts", [128, 3, 128])
    ada_w0 = dt("ada_w0", [D, 3 * D])
    ada_w1 = dt("ada_w1", [D, 3 * D])
    w_in = dt("a_w_in", [D, 2 * D])
    w_s = dt("a_w_s", [16, 128, 128])
    w_out = dt("a_w_out", [D, D])
    w_up = dt("mlp_w_up", [D, 4 * D])
    w_down = dt("mlp_w_down", [4 * D, D])
    y_out = nc.dram_tensor("y", [T, D], F32, kind="ExternalOutput").ap()

    with contextlib.ExitStack() as es:
        P = Prog(nc, es)
        A = Arena(nc)
        cx = Ctx()
        common_setup(nc, P, A, cx, A_NCOLS)
        xT, _, _ = A.alloc([128, 8, T], F32)
        io = dict(x=x_in, cols=cols_in, rows=rows_in, bbias=bb_in, constsA=consts_in, ada_w00=ada_w0, ada_w01=ada_w1,
                  a_w_in=w_in, a_w_s=w_s, a_w_out=w_out, mlp_w_up0=w_up, mlp_w_down0=w_down, y=y_out)
        body_A(nc, P, A, cx, io, xT, stage=stage, fused=False)
        P.wait_all("sp", cx.out_toks)
        P.finish()
    return nc


def host_consts():
    ident = np.eye(128, dtype=np.float32)
    ones = np.full((128, 128), 1.0 / 1024.0, dtype=np.float32)
    tril = np.tril(np.ones((128, 128), dtype=np.float32))
    return np.ascontiguousarray(np.stack([ident, ones, tril], axis=1))


def prep_A(inp):
    x = np.asarray(inp["x"], dtype=np.float32)
    c = np.asarray(inp["c"], dtype=np.float32)
    maps = []
    consts = host_consts()
    rows = np.stack([inp["a_b_in"][0][1024:], inp["a_vn_g"][0], inp["a_vn_b"][0]], axis=0).astype(np.float32)
    rows = np.ascontiguousarray(np.broadcast_to(rows[None], (128, 3, 1024)))
    bs = np.asarray(inp["a_b_s"][0], dtype=np.float32)
    bb = np.zeros((128, 8, 128), dtype=np.float32)
    for fc in range(8):
        bb[0:64, fc, :] = bs[2 * fc][None, :]
        bb[64:128, fc, :] = bs[2 * fc + 1][None, :]
    for core in range(NCORES):
        b, half = core // 2, core % 2
        cols = np.zeros((128, A_NCOLS), dtype=np.float32)
        cols[:, 0:8] = col(c[b])
        cols[:, 8:32] = col(inp["ada_b"][0, 0])
        cols[:, 32:56] = col(inp["ada_b"][0, 1])
        cols[:, 56:64] = col(inp["ln_g"][0, 0])
        cols[:, 64:72] = col(inp["ln_b"][0, 0])
        cols[:, 72:80] = col(inp["ln_g"][0, 1])
        cols[:, 80:88] = col(inp["ln_b"][0, 1])
        cols[:, 88:96] = col(inp["a_b_in"][0][:1024])
        maps.append({
            "x": np.ascontiguousarray(x[b, half * T:(half + 1) * T]),
            "cols": cols, "rows": rows, "bbias": bb, "consts": consts,
            "ada_w0": np.ascontiguousarray(inp["ada_w"][0, 0]), "ada_w1": np.ascontiguousarray(inp["ada_w"][0, 1]),
            "a_w_in": np.ascontiguousarray(inp["a_w_in"][0]), "a_w_s": np.ascontiguousarray(inp["a_w_s"][0]),
            "a_w_out": np.ascontiguousarray(inp["a_w_out"][0]),
            "mlp_w_up": np.ascontiguousarray(inp["mlp_w_up"][0]), "mlp_w_down": np.ascontiguousarray(inp["mlp_w_down"][0]),
        })
    return maps


def run_A(inp, trace=False, stage=None, ncores=NCORES):
    nc = build_A(stage)
    maps = prep_A(inp)[:ncores]
    res = run_bass_kernel_spmd(nc, maps, core_ids=list(range(ncores)), trace=trace)
    x1 = np.zeros((4, 4096, D), dtype=np.float32)
    for core in range(ncores):
        b, half = core // 2, core % 2
        x1[b, half * T:(half + 1) * T] = res.results[core]["y"]
    if stage == "mixA":
        np.save("dbg0.npy", res.results[0]["dbg"])
    return x1, res


B_NCOLS = 96
PATTERNS = ((128, 1), (512, 4), (2048, 16))


def host_consts_B():
    ident = np.eye(128, dtype=np.float32)
    ones = np.full((128, 128), 1.0 / 1024.0, dtype=np.float32)
    k = np.arange(128)[:, None]
    q = np.arange(128)[None, :]
    d_prev = np.where(k >= q, 128.0 + q - k, 0.0).astype(np.float32)
    d_cur = np.where(k <= q, (q - k) * 1.0, 0.0).astype(np.float32)
    m_prev = (k >= q).astype(np.float32)
    m_cur = (k <= q).astype(np.float32)
    m = np.arange(128)[None, :]
    permA = ((k == m + 64) & (m < 64)).astype(np.float32)
    permB = ((k == m - 64) & (m >= 64)).astype(np.float32)
    return np.ascontiguousarray(np.stack([ident, ones, d_prev, d_cur, permA, permB, m_prev, m_cur], axis=1))


def body_B(nc, P, A, cx, io, xT, xT_at, stage=None, fused=False, cb=0):
    x_in, xp_in, cols_in, consts_in = io.get("x"), io.get("xp"), io["cols"], io["constsB"]
    ada_w0, ada_w1, w_qkv, w_out = io["ada_w10"], io["ada_w11"], io["b_w_qkv"], io["b_w_out"]
    w_up, w_down, y_out, xsp = io["mlp_w_up1"], io["mlp_w_down1"], io["y"], io["xsp"]
    mark = xT_at
    r1_end = xT_at + 65536
    save_off = A.off
    A.off = mark
    kt0, _, _ = A.alloc([128, 32 * 128], BF16)
    KT = [kt0, kt0]
    VT, _, _ = A.alloc([128, 32 * 128], BF16)
    VA, _, _ = A.alloc([128, 32, 2, 128], BF16)
    ACC, _, _ = A.alloc([128, 2, T], F32)
    QT = [A.alloc([128, T], BF16)[0] for _ in range(2)]
    Praw = [A.alloc([128, 512], F32)[0] for _ in range(2)]
    Pt = [A.alloc([128, 512], BF16)[0] for _ in range(2)]
    assert A.off <= r1_end, (A.off, r1_end)
    A.off = max(r1_end, save_off)
    mark2 = A.off
    hT, _, _ = A.alloc([128, 8, 2 * T], BF16)
    oT, oT_at, _ = A.alloc([128, 8, T], BF16)
    wqkv = [A.alloc([128, 3, 8, 128], BF16)[0] for _ in range(2)]
    Emat = [A.alloc([128, 2, 2, 128], F32)[0] for _ in range(2)]
    rden, _, _ = A.alloc([128, 512], F32)
    identb, _, _ = A.alloc([128, 128], BF16)
    diffm, _, _ = A.alloc([128, 2, 128], F32)
    perm, _, _ = A.alloc([128, 2, 128], F32)
    mask01, _, _ = A.alloc([128, 2, 128], F32)
    endB1 = A.off
    A.off = oT_at
    cx.stg = [A.alloc([128, 4, 1024], F32)[0] for _ in range(2)]
    assert A.off <= oT_at + 32768
    A.off = mark2
    wout, _, _ = A.alloc([128, 8, 1024], BF16)
    A.off = mark2
    h2T, _, _ = A.alloc([128, 8, T], BF16)
    cx.wu_buf = [A.alloc([128, 8, 1024], BF16)[0] for _ in range(2)]
    cx.wd_buf = [A.alloc([128, 8, 1024], BF16)[0] for _ in range(2)]
    cx.a_buf = [A.alloc([128, 8, TT], BF16)[0] for _ in range(2)]
    cx.ost = [A.alloc([128, 1024], F32)[0] for _ in range(2)]
    endB2 = A.off
    print("SBUF plan B: mark", mark, "r1_end", r1_end, "attn end", endB1, "mlp end", endB2)
    assert max(endB1, endB2) <= A.cap

    flag = cx.cols[:, cb + 88:cb + 89]
    if not fused:
        P.dma("sp", [(cx.cols[:], cols_in)], P.slot(), writes=["cols"])
        P.dma("sp", [(cx.ident_f[:], consts_in[:, 0, :]), (cx.onesm[:], consts_in[:, 1, :])], P.slot(),
              writes=["ident", "onesm"])
        P.op("act", lambda h: h.activation(cx.sc[:], cx.cols[:, 0:8], AF.Silu), reads=["cols"], writes=["sc"])
    P.dma("sp", [(diffm[:], consts_in[:, 2:4, :]), (perm[:], consts_in[:, 4:6, :]),
                 (mask01[:], consts_in[:, 6:8, :])], P.slot(), writes=["diffm", "perm"])
    P.op("dve", lambda h: h.tensor_copy(identb[:], cx.ident_f[:]), reads=["ident"], writes=["identb"])
    emit_ada(P, cx, ada_w0, cx.cols[:, cb + 8:cb + 32], cx.mod[0], 0)
    mod0 = cx.mod[0]

    xpv = xp_in.rearrange("(tb p) f -> p tb f", p=128) if xp_in is not None else None

    def load_prev_tile(tt):
        si = cx.stg_i % 2
        cx.stg_i += 1
        stg = cx.stg[si]
        P.dma("sp", [(stg[:, 0:2, :], xpv[:, tt * 4:tt * 4 + 2, :]),
                     (stg[:, 2:4, :], xpv[:, tt * 4 + 2:tt * 4 + 4, :])], cx.stg_slot[si],
              writes=[("stg", si)])
        for c in range(8):
            b, ps = cx.psum.next()
            fns = [(lambda h, tb=tb, c=c, ps=ps, stg=stg: h.matmul(
                ps[:, tb * 128:(tb + 1) * 128], stg[:, tb, c * 128:(c + 1) * 128], cx.ident[:],
                start=True, stop=True)) for tb in range(4)]
            P.mm_group(fns, reads=[("stg", si), "ident"], writes=[("ps", b)])
            P.op("dve", lambda h, ps=ps, c=c, tt=tt: h.tensor_scalar(
                hT[:, c, tt * TT:(tt + 1) * TT], ps[:], mod0[:, 8 + c:9 + c], mod0[:, c:c + 1], ALU.mult, ALU.add),
                reads=[("ps", b), ("mod", 0)], writes=[("hall", c, tt)])

    if not fused:
        for tt in range(NT):
            load_prev_tile(tt)
        emit_load_xT(P, cx, x_in, xT)

    def mod_own(tt):
        for c in range(8):
            P.op("act" if c % 2 else "dve",
                 (lambda h, c=c, tt=tt: h.activation(hT[:, c, T + tt * TT:T + (tt + 1) * TT], xT[:, c, tt * TT:(tt + 1) * TT],
                                                     AF.Identity, bias=mod0[:, c:c + 1], scale=mod0[:, 8 + c:9 + c]))
                 if c % 2 else
                 (lambda h, c=c, tt=tt: h.tensor_scalar(hT[:, c, T + tt * TT:T + (tt + 1) * TT], xT[:, c, tt * TT:(tt + 1) * TT],
                                                        mod0[:, 8 + c:9 + c], mod0[:, c:c + 1], ALU.mult, ALU.add)),
                 reads=[("x", c, tt), ("mod", 0)], writes=[("hall", c, NT + tt)])

    for tt in range(NT):
        mod_own(tt)
    if fused:
        snd, rcv = io["snd"], io["rcv"]
        s_slot = P.slot()
        t_snd = P.dma("sp", [(snd[c], hT[:, c, T:2 * T]) for c in range(8)], s_slot,
                      reads=[("hall", c, NT + tt) for c in range(8) for tt in range(NT)])
        cc_sem = P.new_sem()
        ep = P.E["pool"]
        w_ = P._deps("pool", (), (), [t_snd])
        for c in range(8):
            ep.ops.append((w_ if c == 0 else [],
                           (lambda h, c=c: h.collective_compute(
                               "AllGather", ALU.bypass, replica_groups=[[0, 1], [2, 3], [4, 5], [6, 7]],
                               ins=[snd[c].opt()], outs=[rcv[c].opt()])),
                           (cc_sem, 1)))
        t_cc = (cc_sem, 8)
        r_slot = P.slot()
        P.dma("sp", [(hT[:, c, 0:T], rcv[c][0:128, :]) for c in range(8)], r_slot,
              writes=[("hall", c, tt) for c in range(8) for tt in range(NT)], extra=[t_cc])
    sp_slot = P.slot()
    xkeys = [("x", c, tt) for c in range(8) for tt in range(NT)]
    t_spill = P.dma("sp", [(xsp[:, c * T:(c + 1) * T], xT[:, c, :]) for c in range(8)], sp_slot, reads=xkeys)
    fsp = fence_tokens(P, xkeys) + [t_spill]
    for e_ in ("pe", "act", "dve", "pool"):
        P.wait_all(e_, fsp)
    P.op("pool", lambda h: h.memset(VA[:], 1.0), writes=["va_init"])
    P.op("pool", lambda h: h.tensor_scalar(VA[:, 0:16], VA[:, 0:16], flag, None, ALU.mult),
         reads=["va_init", "cols"], writes=["va_init"])
    va_tok = P.last_w["va_init"]

    wq_v = w_qkv.rearrange("(kc p) n -> p kc n", p=128)
    wq_slot = [P.slot() for _ in range(2)]
    st = {"i": 0, "e": 0, "p": 0}
    hall_keys_all = [("hall", c, t8) for c in range(8) for t8 in range(2 * NT)]

    def perm_view(buf2d, col0, d, tt, ps):
        if d == 1:
            return buf2d[:, col0 + tt * 512:col0 + (tt + 1) * 512], ps[:]
        if d == 4:
            dst = buf2d[:, col0 + tt * 512:col0 + (tt + 1) * 512].rearrange("p (r i) -> p i r", r=4)
            return dst, ps[:].rearrange("p (i r) -> p i r", r=4)
        dst = buf2d[:, col0:col0 + 2048].rearrange("p (r i) -> p i r", r=16)[:, tt * 32:(tt + 1) * 32, :]
        return dst, ps[:].rearrange("p (i r) -> p i r", r=16)

    import os as _os2
    _skip = set(_os2.environ.get("DBG_SKIP", "").split(","))

    def attn_stage(hp, g):
        d = PATTERNS[g][1]
        si = st["i"] % 2
        st["i"] += 1
        wb = wqkv[si]
        kt = KT[si]
        qt = QT[si]
        em = Emat[si]
        pairs = []
        for t3 in range(3):
            c0 = ((g * 3 + t3) * 16 + 2 * hp) * 64
            pairs.append((wb[:, t3, :, :], wq_v[:, :, c0:c0 + 128]))
        P.dma("pool", pairs, wq_slot[si], writes=[("wqkv", si)])
        for hh in range(0 if "E" in _skip else 2):
            slope = 2.0 ** (-8.0 * (2 * hp + hh + 1) / 16.0)
            P.op("act", lambda h, hh=hh, slope=slope: h.activation(
                em[:, hh, :, :], diffm[:], AF.Exp, scale=-slope * d),
                reads=["diffm"], writes=[("emat", si, hh)])
            P.op("pool", lambda h, hh=hh: h.tensor_tensor(em[:, hh, :, :], em[:, hh, :, :], mask01[:], ALU.mult),
                 reads=["diffm", ("emat", si, hh)], writes=[("emat", si, hh)])
        for tt in range(0 if "Q" in _skip else NT):
            b, ps = cx.psum.next()
            fns = [(lambda h, kc=kc, ps=ps, tt=tt: h.matmul(
                ps[:], wb[:, 0, kc, :], hT[:, kc, T + tt * TT:T + (tt + 1) * TT], start=(kc == 0), stop=(kc == 7)))
                for kc in range(8)]
            P.mm_group(fns, reads=[("wqkv", si)] + [("hall", kc, NT + tt) for kc in range(8)], writes=[("ps", b)])
            dst, src = perm_view(qt, 0, d, tt, ps)
            P.op("act", lambda h, dst=dst, src=src: h.activation(dst, src, AF.Identity),
                 reads=[("ps", b)], writes=[("qt", si)])
        for tt in range(0 if "K" in _skip else NT):
            b, ps = cx.psum.next()
            fns = [(lambda h, kc=kc, ps=ps, tt=tt: h.matmul(
                ps[:], wb[:, 1, kc, :], hT[:, kc, T + tt * TT:T + (tt + 1) * TT], start=(kc == 0), stop=(kc == 7)))
                for kc in range(8)]
            P.mm_group(fns, reads=[("wqkv", si)] + [("hall", kc, NT + tt) for kc in range(8)], writes=[("ps", b)])
            dst, src = perm_view(kt, 2048, d, tt, ps)
            P.op("dve", lambda h, dst=dst, src=src: h.tensor_copy(dst, src),
                 reads=[("ps", b)], writes=[("kt", 0)])
        if "K" in _skip:
            pass
        elif d == 1:
            b, ps = cx.psum.next()
            fns = [(lambda h, kc=kc, ps=ps: h.matmul(
                ps[:, 0:128], wb[:, 1, kc, :], hT[:, kc, T - 128:T], start=(kc == 0), stop=(kc == 7)))
                for kc in range(8)]
            P.mm_group(fns, reads=[("wqkv", si)] + [("hall", kc, NT - 1) for kc in range(8)], writes=[("ps", b)])
            P.op("dve", lambda h, ps=ps: h.tensor_copy(kt[:, 15 * 128:16 * 128], ps[:, 0:128]),
                 reads=[("ps", b)], writes=[("kt", 0)])
        elif d == 4:
            b, ps = cx.psum.next()
            fns = [(lambda h, kc=kc, ps=ps: h.matmul(
                ps[:], wb[:, 1, kc, :], hT[:, kc, T - 512:T], start=(kc == 0), stop=(kc == 7)))
                for kc in range(8)]
            P.mm_group(fns, reads=[("wqkv", si)] + [("hall", kc, NT - 1) for kc in range(8)], writes=[("ps", b)])
            dst = kt[:, 12 * 128:16 * 128].rearrange("p (r i) -> p i r", r=4)
            src = ps[:].rearrange("p (i r) -> p i r", r=4)
            P.op("dve", lambda h, dst=dst, src=src: h.tensor_copy(dst, src),
                 reads=[("ps", b)], writes=[("kt", 0)])
        else:
            for tt in range(NT):
                b, ps = cx.psum.next()
                fns = [(lambda h, kc=kc, ps=ps, tt=tt: h.matmul(
                    ps[:], wb[:, 1, kc, :], hT[:, kc, tt * TT:(tt + 1) * TT], start=(kc == 0), stop=(kc == 7)))
                    for kc in range(8)]
                P.mm_group(fns, reads=[("wqkv", si)] + [("hall", kc, tt) for kc in range(8)], writes=[("ps", b)])
                dst, src = perm_view(kt, 0, d, tt, ps)
                P.op("dve", lambda h, dst=dst, src=src: h.tensor_copy(dst, src),
                     reads=[("ps", b)], writes=[("kt", 0)])
        def vproj(tok0, ntok, col0, mode, tt, hkeys):
            b, ps = cx.psum.next()
            fns = [(lambda h, kc=kc, ps=ps: h.matmul(
                ps[:, 0:ntok], wb[:, 2, kc, :], hT[:, kc, tok0:tok0 + ntok], start=(kc == 0), stop=(kc == 7)))
                for kc in range(8)]
            P.mm_group(fns, reads=[("wqkv", si)] + hkeys, writes=[("ps", b)])
            if mode == "plain":
                dst, src = VT[:, col0:col0 + ntok], ps[:, 0:ntok]
            elif mode == "seg4":
                dst = VT[:, col0:col0 + 512].rearrange("p (r i) -> p i r", r=4)
                src = ps[:].rearrange("p (i r) -> p i r", r=4)
            else:
                dst, src = perm_view(VT, col0, d, tt, ps)
            P.op("dve", lambda h, dst=dst, src=src: h.tensor_copy(dst, src),
                 reads=[("ps", b)], writes=["vt"])

        for tt in range(0 if "V" in _skip else NT):
            hk = [("hall", kc, NT + tt) for kc in range(8)]
            if d == 1:
                vproj(T + tt * TT, TT, 2048 + tt * 512, "plain", tt, hk)
            elif d == 4:
                vproj(T + tt * TT, TT, 2048 + tt * 512, "seg4", tt, hk)
            else:
                vproj(T + tt * TT, TT, 2048, "perm", tt, hk)
        if "V" in _skip:
            pass
        elif d == 1:
            vproj(T - 128, 128, 15 * 128, "plain", 0, [("hall", kc, NT - 1) for kc in range(8)])
        elif d == 4:
            vproj(T - 512, 512, 12 * 128, "seg4", 0, [("hall", kc, NT - 1) for kc in range(8)])
        else:
            for tt in range(NT):
                vproj(tt * TT, TT, 0, "perm", tt, [("hall", kc, tt) for kc in range(8)])
        groups = []
        prev_slots = list(range(16 - d, 16))
        own_slots = list(range(16, 32))
        for i0 in range(0, len(prev_slots), 4):
            groups.append(prev_slots[i0:i0 + 4])
        for i0 in range(0, 16, 4):
            groups.append(own_slots[i0:i0 + 4])
        if "T" in _skip or "V" in _skip:
            groups = []
        for grp in groups:
            b, ps = cx.psum.next()
            fns = [(lambda h, gi=gi, s=s, ps=ps: h.matmul(
                ps[:, gi * 128:(gi + 1) * 128], VT[:, s * 128:(s + 1) * 128], identb[:], start=True, stop=True))
                for gi, s in enumerate(grp)]
            P.mm_group(fns, reads=["vt", "identb"], writes=[("ps", b)], extra=[va_tok])
            ng = len(grp)
            s0 = grp[0]
            srcv = ps[:, 0:ng * 128].rearrange("p (s c) -> p s c", c=128)
            if "X" in _skip:
                continue
            if "XP" in _skip and s0 < 16:
                continue
            if "XO" in _skip and s0 >= 16:
                continue
            if "ALTVA" in _skip:
                s0_ = min(grp[0], 30)
                P.op("dve", lambda h, ps=ps, s0_=s0_: h.tensor_copy(
                    VA[:, s0_:s0_ + 2, :, :].rearrange("p a b c -> p (a b c)"), ps[:]),
                    reads=[("ps", b)], writes=[("va", 0)])
                continue
            if "ALT" in _skip:
                P.op("dve", lambda h, ps=ps: h.tensor_copy(Praw[0][:], ps[:]),
                     reads=[("ps", b)], writes=[("praw", 0)])
                continue
            if "F2D" in _skip:
                for gi, sl_ in enumerate(grp):
                    if "NO20" in _skip and sl_ == 20:
                        continue
                    P.op("act", lambda h, gi=gi, sl_=sl_, ps=ps: h.activation(
                        VA[:, sl_, 0, :], ps[:, gi * 128:(gi + 1) * 128], AF.Identity),
                        reads=[("ps", b)], writes=[("va", 0)])
                    P.op("dve", lambda h, gi=gi, sl_=sl_, ps=ps: h.tensor_copy(
                        VA[:, sl_, 1, :], ps[:, gi * 128:(gi + 1) * 128]),
                        reads=[("ps", b)], writes=[("va", 1)])
                continue
            if "FULL" in _skip:
                P.op("act", lambda h, srcv=srcv, s0=s0, ng=ng: h.activation(
                    VA[:, s0:s0 + ng, 0, :], srcv, AF.Identity),
                    reads=[("ps", b)], writes=[("va", 0)])
                P.op("dve", lambda h, srcv=srcv, s0=s0, ng=ng: h.tensor_copy(
                    VA[:, s0:s0 + ng, 1, :], srcv),
                    reads=[("ps", b)], writes=[("va", 1)])
                continue
            if s0 < 16:
                P.op("act", lambda h, srcv=srcv, s0=s0, ng=ng: h.activation(
                    VA[:, s0:s0 + ng, 0, 0:64], srcv[:, :, 0:64], AF.Identity, scale=flag),
                    reads=[("ps", b), "cols"], writes=[("va", 0)])
                P.op("dve", lambda h, srcv=srcv, s0=s0, ng=ng: h.tensor_scalar(
                    VA[:, s0:s0 + ng, 1, 64:128], srcv[:, :, 64:128], flag, None, ALU.mult),
                    reads=[("ps", b), "cols"], writes=[("va", 1)])
            else:
                P.op("act", lambda h, srcv=srcv, s0=s0, ng=ng: h.activation(
                    VA[:, s0:s0 + ng, 0, 0:64], srcv[:, :, 0:64], AF.Identity),
                    reads=[("ps", b)], writes=[("va", 0)])
                P.op("dve", lambda h, srcv=srcv, s0=s0, ng=ng: h.tensor_copy(
                    VA[:, s0:s0 + ng, 1, 64:128], srcv[:, :, 64:128]),
                    reads=[("ps", b)], writes=[("va", 1)])
        nun = 0 if stage == 'proj' else 16
        ust = {}

        def unit_front(j):
            pi = st["p"] % 2
            st["p"] += 1
            praw, pt = Praw[pi], Pt[pi]
            for hh in range(2):
                b, ps = cx.psum.next()
                fns = []
                for kb in range(2):
                    slot = 16 + j - d if kb == 0 else 16 + j
                    fns.append(lambda h, hh=hh, kb=kb, slot=slot, ps=ps, j=j: h.matmul(
                        ps[:, kb * 128:(kb + 1) * 128],
                        kt[hh * 64:(hh + 1) * 64, slot * 128:(slot + 1) * 128],
                        qt[hh * 64:(hh + 1) * 64, j * 128:(j + 1) * 128], start=True, stop=True))
                P.mm_group(fns, reads=[("kt", 0), ("qt", si)], writes=[("ps", b)])
                P.op("act", lambda h, ps=ps, praw=praw, hh=hh: h.activation(
                    praw[:, hh * 256:(hh + 1) * 256], ps[:, 0:256], AF.Exp, scale=0.125),
                    reads=[("ps", b)], writes=[("praw", pi, hh)])
            P.op("pool", lambda h, praw=praw, pt=pt: h.tensor_tensor(
                pt[:], praw[:], em[:].rearrange("p a b q -> p (a b q)"), ALU.mult),
                reads=[("praw", pi, 0), ("praw", pi, 1), ("emat", si, 0), ("emat", si, 1)], writes=[("pt", pi)])
            ust[j] = (pi, pt)

        def unit_back(j):
            pi, pt = ust[j]
            n, r = j // d, j % d
            b2, ps2 = cx.psum.next()
            fns = []
            for hh in range(2):
                for kb in range(2):
                    slot = 16 + j - d if kb == 0 else 16 + j
                    fns.append(lambda h, hh=hh, kb=kb, slot=slot, ps2=ps2, pt=pt: h.matmul(
                        ps2[:, hh * 128:(hh + 1) * 128], VA[:, slot, hh, :],
                        pt[:, (hh * 2 + kb) * 128:(hh * 2 + kb + 1) * 128], start=(kb == 0), stop=(kb == 1)))
            P.mm_group(fns, reads=[("pt", pi), ("va", 0), ("va", 1)], writes=[("ps", b2)])
            start = n * 128 * d + r
            accv = ACC[:, :, start:start + 127 * d + 1:d]
            src2 = ps2[:, 0:256].rearrange("p (a q) -> p a q", a=2)
            wr = ["acc"] if j == nun - 1 else []
            if g == 0:
                P.op("dve", lambda h, accv=accv, src2=src2: h.tensor_copy(accv, src2),
                     reads=[("ps", b2)], writes=wr, extra=acc_dep)
            else:
                P.op("dve", lambda h, accv=accv, src2=src2: h.tensor_tensor(accv, src2, accv, ALU.add),
                     reads=[("ps", b2)], writes=wr, extra=acc_dep)

        acc_dep = fence_tokens(P, ["acc"])
        if nun:
            unit_front(0)
        for j in range(nun):
            if j + 1 < nun:
                unit_front(j + 1)
            unit_back(j)

    def finalize(hp):
        for tt in range(NT):
            sl = slice(tt * TT, (tt + 1) * TT)
            b, ps = cx.psum.next()
            fns = [lambda h, ps=ps, sl=sl: h.matmul(ps[:], perm[:, 0, :], ACC[:, 0, sl], start=True, stop=False),
                   lambda h, ps=ps, sl=sl: h.matmul(ps[:], perm[:, 1, :], ACC[:, 1, sl], start=False, stop=True)]
            P.mm_group(fns, reads=["acc", "perm"], writes=[("ps", b)])
            P.op("dve", lambda h, ps=ps: h.reciprocal(rden[:], ps[:]), reads=[("ps", b)], writes=["rden"])
            P.op("pool", lambda h, sl=sl: h.tensor_tensor(oT[0:64, hp, sl], ACC[0:64, 0, sl], rden[0:64, :], ALU.mult),
                 reads=["acc", "rden"], writes=[("o", hp, tt, 0)])
            P.op("pool", lambda h, sl=sl: h.tensor_tensor(oT[64:128, hp, sl], ACC[64:128, 1, sl], rden[64:128, :], ALU.mult),
                 reads=["acc", "rden"], writes=[("o", hp, tt, 1)])

    P.wait_all("pool", fence_tokens(P, [("stg", 0), ("stg", 1)]))
    nhp = {"pre": 0, "one": 1, "proj": 1}.get(stage, 8)
    import os as _os
    _gl = [int(x) for x in _os.environ.get("DBG_G", "0,1,2").split(",")]
    for hp in range(nhp):
        for g in _gl:
            attn_stage(hp, g)
        if stage != "proj":
            finalize(hp)
    if stage in ("pre", "one", "proj"):
        fx = fence_tokens(P, ["vt", ("kt", 0), ("kt", 1), ("qt", 0), ("qt", 1), ("va", 0), ("va", 1), "acc", "va_init",
                              ("praw", 0, 0), ("praw", 0, 1), ("praw", 1, 0), ("praw", 1, 1), ("pt", 0), ("pt", 1)])
        P.wait_all("sp", fx)
        t_re = P.dma("sp", [(xT[:, c, :], xsp[:, c * T:(c + 1) * T]) for c in range(8)], P.slot(), writes=xkeys)
        for e_ in ("pe", "act", "dve", "pool"):
            P.wait_all(e_, fence_tokens(P, hall_keys_all + [("o", hp, tt, hf) for hp in range(nhp) for tt in range(NT) for hf in range(2)] + [("wqkv", 0), ("wqkv", 1), ("emat", 0, 0), ("emat", 0, 1), ("emat", 1, 0), ("emat", 1, 1)]))
        for tt in range(NT):
            emit_store_x(P, cx, xT, y_out, tt)
        return

    att_keys = ["vt", ("kt", 0), ("kt", 1), ("qt", 0), ("qt", 1), ("va", 0), ("va", 1), "acc", "va_init",
                ("praw", 0, 0), ("praw", 0, 1), ("praw", 1, 0), ("praw", 1, 1), ("pt", 0), ("pt", 1)]
    fx = fence_tokens(P, att_keys)
    P.wait_all("sp", fx)
    x_slot = P.slot()
    P.dma("sp", [(xT[:, c, :], xsp[:, c * T:(c + 1) * T]) for c in range(8)], x_slot, writes=xkeys)
    fo = fence_tokens(P, hall_keys_all + [("wqkv", 0), ("wqkv", 1)])
    P.wait_all("pool", fo)
    P.dma("pool", [(wout[:], w_out.rearrange("(kc p) n -> p kc n", p=128))], P.slot(), writes=["wout"])

    def outproj_tile(tt):
        sl = slice(tt * TT, (tt + 1) * TT)
        for oc in range(8):
            b, ps = cx.psum.next()
            fns = [(lambda h, hp=hp, oc=oc, ps=ps: h.matmul(
                ps[:], wout[:, hp, oc * 128:(oc + 1) * 128], oT[:, hp, sl], start=(hp == 0), stop=(hp == 7)))
                for hp in range(8)]
            P.mm_group(fns, reads=["wout"] + [("o", hp, tt, hf) for hp in range(8) for hf in range(2)],
                       writes=[("ps", b)])
            P.op("dve", lambda h, ps=ps, oc=oc: h.scalar_tensor_tensor(
                xT[:, oc, sl], ps[:], mod0[:, 16 + oc:17 + oc], xT[:, oc, sl], ALU.mult, ALU.add),
                reads=[("ps", b), ("mod", 0), ("x", oc, tt)], writes=[("x", oc, tt)])
        emit_ln(P, cx, xT, tt, cx.cols[:, cb + 56:cb + 64], cx.cols[:, cb + 64:cb + 72])

    for tt in range(NT):
        outproj_tile(tt)

    if stage == "attn":
        for e_ in ("pe", "act", "dve", "pool"):
            P.wait_all(e_, fence_tokens(P, ["wout"] + [("o", hp, tt, hf) for hp in range(8) for tt in range(NT) for hf in range(2)]))
        for tt in range(NT):
            emit_store_x(P, cx, xT, y_out, tt)
        return

    emit_ada(P, cx, ada_w1, cx.cols[:, cb + 32:cb + 56], cx.mod[1], 1)
    f2 = fence_tokens(P, ["wout"] + [("o", hp, tt, hf) for hp in range(8) for tt in range(NT) for hf in range(2)]
                      + [("emat", i, hh) for i in range(2) for hh in range(2)] + ["rden", "diffm", "perm", "identb", "vt"])
    for e_ in ("dve", "act", "pool"):
        P.wait_all(e_, f2)
    for tt in range(NT):
        emit_modulate(P, cx, xT, h2T, cx.mod[1], 1, tt)

    def after_ln(tt):
        emit_store_x(P, cx, xT, y_out, tt)

    emit_mlp(P, cx, xT, h2T, cx.mod[1], 1, w_up, w_down, cx.cols[:, cb + 72:cb + 80], cx.cols[:, cb + 80:cb + 88], after_ln=after_ln)


def build_B(stage=None):
    nc = bass.Bass("TRN2", target_bir_lowering=False)
    dt = lambda n, s: nc.dram_tensor(n, s, F32, kind="ExternalInput").ap()
    x_in = dt("x", [T, D])
    xp_in = dt("xp", [T, D])
    cols_in = dt("cols", [128, B_NCOLS])
    consts_in = dt("consts", [128, 8, 128])
    ada_w0 = dt("ada_w0", [D, 3 * D])
    ada_w1 = dt("ada_w1", [D, 3 * D])
    w_qkv = dt("b_w_qkv", [D, 9 * D])
    w_out = dt("b_w_out", [D, D])
    w_up = dt("mlp_w_up", [D, 4 * D])
    w_down = dt("mlp_w_down", [4 * D, D])
    y_out = nc.dram_tensor("y", [T, D], F32, kind="ExternalOutput").ap()
    xsp = nc.dram_tensor("xsp", [128, 8 * T], F32).ap()

    with contextlib.ExitStack() as es:
        P = Prog(nc, es)
        A = Arena(nc)
        cx = Ctx()
        common_setup(nc, P, A, cx, B_NCOLS)
        xT, xT_at, _ = A.alloc([128, 8, T], F32)
        io = dict(x=x_in, xp=xp_in, cols=cols_in, constsB=consts_in, ada_w10=ada_w0, ada_w11=ada_w1, b_w_qkv=w_qkv,
                  b_w_out=w_out, mlp_w_up1=w_up, mlp_w_down1=w_down, y=y_out, xsp=xsp)
        body_B(nc, P, A, cx, io, xT, xT_at, stage=stage, fused=False, cb=0)
        P.wait_all("sp", cx.out_toks)
        P.finish()
    return nc


def prep_B(inp, x1):
    c = np.asarray(inp["c"], dtype=np.float32)
    consts = host_consts_B()
    maps = []
    zeros = np.zeros((T, D), dtype=np.float32)
    for core in range(NCORES):
        b, half = core // 2, core % 2
        cols = np.zeros((128, B_NCOLS), dtype=np.float32)
        cols[:, 0:8] = col(c[b])
        cols[:, 8:32] = col(inp["ada_b"][1, 0])
        cols[:, 32:56] = col(inp["ada_b"][1, 1])
        cols[:, 56:64] = col(inp["ln_g"][1, 0])
        cols[:, 64:72] = col(inp["ln_b"][1, 0])
        cols[:, 72:80] = col(inp["ln_g"][1, 1])
        cols[:, 80:88] = col(inp["ln_b"][1, 1])
        cols[:, 88] = float(half)
        maps.append({
            "x": np.ascontiguousarray(x1[b, half * T:(half + 1) * T]) if x1 is not None else None,
            "xp": (np.ascontiguousarray(x1[b, 0:T]) if half == 1 else zeros) if x1 is not None else None,
            "cols": cols, "consts": consts,
            "ada_w0": np.ascontiguousarray(inp["ada_w"][1, 0]), "ada_w1": np.ascontiguousarray(inp["ada_w"][1, 1]),
            "b_w_qkv": np.ascontiguousarray(inp["b_w_qkv"][0]), "b_w_out": np.ascontiguousarray(inp["b_w_out"][0]),
            "mlp_w_up": np.ascontiguousarray(inp["mlp_w_up"][1]), "mlp_w_down": np.ascontiguousarray(inp["mlp_w_down"][1]),
        })
    return maps


def run_B(inp, x1, trace=False, stage=None, ncores=NCORES):
    nc = build_B(stage)
    maps = prep_B(inp, x1)[:ncores]
    res = run_bass_kernel_spmd(nc, maps, core_ids=list(range(ncores)), trace=trace)
    out = np.zeros((4, 4096, D), dtype=np.float32)
    for core in range(ncores):
        b, half = core // 2, core % 2
        out[b, half * T:(half + 1) * T] = res.results[core]["y"]
    return out, res


F_NCOLS = A_NCOLS + B_NCOLS


def build_F():
    nc = bass.Bass("TRN2", target_bir_lowering=False)
    dt = lambda n, s: nc.dram_tensor(n, s, F32, kind="ExternalInput").ap()
    io = dict(
        x=dt("x", [T, D]), cols=dt("cols", [128, F_NCOLS]), rows=dt("rows", [128, 3, 1024]),
        bbias=dt("bbias", [128, 8, 128]), constsA=dt("constsA", [128, 3, 128]), constsB=dt("constsB", [128, 8, 128]),
        ada_w00=dt("ada_w00", [D, 3 * D]), ada_w01=dt("ada_w01", [D, 3 * D]),
        ada_w10=dt("ada_w10", [D, 3 * D]), ada_w11=dt("ada_w11", [D, 3 * D]),
        a_w_in=dt("a_w_in", [D, 2 * D]), a_w_s=dt("a_w_s", [16, 128, 128]), a_w_out=dt("a_w_out", [D, D]),
        mlp_w_up0=dt("mlp_w_up0", [D, 4 * D]), mlp_w_down0=dt("mlp_w_down0", [4 * D, D]),
        mlp_w_up1=dt("mlp_w_up1", [D, 4 * D]), mlp_w_down1=dt("mlp_w_down1", [4 * D, D]),
        b_w_qkv=dt("b_w_qkv", [D, 9 * D]), b_w_out=dt("b_w_out", [D, D]))
    io["y"] = nc.dram_tensor("y", [T, D], F32, kind="ExternalOutput").ap()
    io["xsp"] = nc.dram_tensor("xsp", [128, 8 * T], F32).ap()
    io["snd"] = [nc.dram_tensor(f"snd{c}", [128, T], BF16).ap() for c in range(8)]
    io["rcv"] = [nc.dram_tensor(f"rcv{c}", [256, T], BF16).ap() for c in range(8)]
    with contextlib.ExitStack() as es:
        P = Prog(nc, es)
        A = Arena(nc)
        cx = Ctx()
        common_setup(nc, P, A, cx, F_NCOLS)
        xT, xT_at, _ = A.alloc([128, 8, T], F32)
        mark = A.off
        body_A(nc, P, A, cx, io, xT, stage=None, fused=True)
        P.barrier()
        A.off = mark
        body_B(nc, P, A, cx, io, xT, xT_at, stage=None, fused=True, cb=A_NCOLS)
        P.wait_all("sp", cx.out_toks)
        P.finish()
    return nc


def prep_F(inp):
    mA = prep_A(inp)
    mB = prep_B(inp, None)
    maps = []
    for core in range(NCORES):
        a, b = mA[core], mB[core]
        maps.append({
            "x": a["x"], "cols": np.ascontiguousarray(np.concatenate([a["cols"], b["cols"]], axis=1)),
            "rows": a["rows"], "bbias": a["bbias"], "constsA": a["consts"], "constsB": b["consts"],
            "ada_w00": a["ada_w0"], "ada_w01": a["ada_w1"], "ada_w10": b["ada_w0"], "ada_w11": b["ada_w1"],
            "a_w_in": a["a_w_in"], "a_w_s": a["a_w_s"], "a_w_out": a["a_w_out"],
            "mlp_w_up0": a["mlp_w_up"], "mlp_w_down0": a["mlp_w_down"],
            "mlp_w_up1": b["mlp_w_up"], "mlp_w_down1": b["mlp_w_down"],
            "b_w_qkv": b["b_w_qkv"], "b_w_out": b["b_w_out"],
        })
    return maps


def run_F(inp, trace=False, ncores=NCORES):
    nc = build_F()
    maps = prep_F(inp)[:ncores]
    res = run_bass_kernel_spmd(nc, maps, core_ids=list(range(ncores)), trace=trace)
    out = np.zeros((4, 4096, D), dtype=np.float32)
    for core in range(ncores):
        b, half = core // 2, core % 2
        out[b, half * T:(half + 1) * T] = res.results[core]["y"]
    return out, res


def kernel(**inputs):
    inp = {k: np.asarray(v) for k, v in inputs.items()}
    out, _ = run_F(inp)
    return out
```

```python
import contextlib
import numpy as np
import concourse.bass as bass
import concourse.mybir as mybir
from concourse.bass_utils import run_bass_kernel_spmd

F32 = mybir.dt.float32
BF16 = mybir.dt.bfloat16
AF = mybir.ActivationFunctionType
ALU = mybir.AluOpType
AX = mybir.AxisListType

D = 1024
T = 2048
NT = 4
TT = 512
NCORES = 8
ALPHA = 4.0 ** 0.25
INV_ALPHA = 1.0 / ALPHA
LN_EPS = 1e-5
EPS_P = LN_EPS / (ALPHA * ALPHA)
GELU = AF.Gelu_apprx_tanh


class Eng:
    def __init__(self, name, sem):
        self.name = name
        self.sem = sem
        self.count = 0
        self.ops = []
        self.waited = {}


class Prog:
    def __init__(self, nc, es, nsem=100):
        self.nc = nc
        self.free_sems = [es.enter_context(nc.semaphore(f"s{i}")) for i in range(nsem)]
        self.E = {}
        for n in ["pe", "act", "dve", "pool", "sp"]:
            self.E[n] = Eng(n, self.free_sems.pop())
        self.last_w = {}
        self.readers = {}
        self.sem_id = {}

    def new_sem(self):
        return self.free_sems.pop()

    def slot(self):
        sl = {"sem": self.new_sem(), "count": 0}
        self.slots = getattr(self, "slots", [])
        self.slots.append(sl)
        return sl

    def barrier(self, extra=()):
        toks = [(e.sem, e.count) for e in self.E.values() if e.count > 0]
        toks += [(sl["sem"], sl["count"]) for sl in getattr(self, "slots", []) if sl["count"] > 0]
        toks += list(extra)
        for eng in self.E:
            self.wait_all(eng, toks)

    def _sid(self, s):
        return id(s)

    @staticmethod
    def _norm(reads, writes):
        r2, w2 = [], list(writes)
        for k in reads:
            if isinstance(k, tuple) and k and k[0] == "ps":
                if k not in w2:
                    w2.append(k)
            else:
                r2.append(k)
        return r2, w2

    def _deps(self, eng, reads, writes, extra):
        toks = []
        for r in reads:
            t = self.last_w.get(r)
            if t is not None:
                toks.append(t)
        for w in writes:
            t = self.last_w.get(w)
            if t is not None:
                toks.append(t)
            for t in self.readers.get(w, {}).values():
                toks.append(t)
        for t in extra:
            if t is not None:
                toks.append(t)
        e = self.E[eng]
        waits = []
        for (s, v) in toks:
            if eng == "pe" and s is e.sem:
                continue
            k = self._sid(s)
            if e.waited.get(k, 0) < v:
                e.waited[k] = v
                waits.append((s, v))
        return waits

    def _record(self, tok, reads, writes):
        for r in reads:
            d = self.readers.setdefault(r, {})
            k = self._sid(tok[0])
            if k not in d or d[k][1] < tok[1]:
                d[k] = tok
        for w in writes:
            self.last_w[w] = tok
            self.readers[w] = {}

    def op(self, eng, fn, reads=(), writes=(), extra=()):
        reads, writes = self._norm(reads, writes)
        e = self.E[eng]
        waits = self._deps(eng, reads, writes, extra)
        e.count += 1
        tok = (e.sem, e.count)
        e.ops.append((waits, fn, (e.sem, 1)))
        self._record(tok, reads, writes)
        return tok

    def mm_group(self, fns, reads=(), writes=(), extra=()):
        reads, writes = self._norm(reads, writes)
        e = self.E["pe"]
        waits = self._deps("pe", reads, writes, extra)
        for i, fn in enumerate(fns):
            last = i == len(fns) - 1
            e.ops.append((waits if i == 0 else [], fn, (e.sem, 1) if last else None))
        e.count += 1
        tok = (e.sem, e.count)
        self._record(tok, reads, writes)
        return tok

    def dma(self, eng, pairs, slot, reads=(), writes=(), extra=()):
        e = self.E[eng]
        waits = self._deps(eng, reads, writes, extra)
        for i, (o, i_) in enumerate(pairs):
            e.ops.append((waits if i == 0 else [],
                          (lambda h, o=o, i_=i_: h.dma_start(out=o, in_=i_)),
                          (slot["sem"], 16)))
            slot["count"] += 16
        tok = (slot["sem"], slot["count"])
        self._record(tok, reads, writes)
        return tok

    def wait_all(self, eng, toks):
        e = self.E[eng]
        waits = self._deps(eng, (), (), toks)
        e.ops.append((waits, None, None))

    def replay(self, eng, h):
        for waits, fn, inc in self.E[eng].ops:
            for s, v in waits:
                h.wait_ge(s, v)
            if fn is None:
                continue
            inst = fn(h)
            if inc is not None:
                inst.then_inc(inc[0], inc[1])

    def finish(self):
        nc = self.nc
        with nc.Block() as block:
            @block.tensor
            def _(h):
                self.replay("pe", h)

            @block.scalar
            def _(h):
                self.replay("act", h)

            @block.vector
            def _(h):
                self.replay("dve", h)

            @block.gpsimd
            def _(h):
                self.replay("pool", h)

            @block.sync
            def _(h):
                self.replay("sp", h)


class Arena:
    def __init__(self, nc, base=16512, cap=229376):
        self.nc = nc
        self.cap = cap
        self.off = base
        self.n = 0

    def alloc(self, shape, dtype, at=None, name=None):
        nbytes = int(np.prod(shape[1:])) * (4 if dtype == F32 else 2)
        nbytes = (nbytes + 63) // 64 * 64
        if at is None:
            at = self.off
            self.off += nbytes
            assert self.off <= self.cap, f"SBUF overflow {self.off}"
        else:
            assert at + nbytes <= self.cap, "SBUF overflow (at)"
        self.n += 1
        t = self.nc.alloc_sbuf_tensor_at(name or f"sb{self.n}", list(shape), dtype, offset=at, align_bytes=64)
        return t, at, nbytes


class Psum:
    def __init__(self, nc, P):
        self.banks = [nc.alloc_psum_tensor(f"psb{i}", [128, 512], F32) for i in range(8)]
        self.i = 0

    def next(self):
        b = self.i
        self.i = (self.i + 1) % 7
        return b, self.banks[b]


def col(v):
    v = np.asarray(v, dtype=np.float32)
    return np.ascontiguousarray(v.reshape(-1, 128).T)


class Ctx:
    pass


def emit_ada(P, cx, wada_ap, bias_cols, out_mod, tag):
    wv = wada_ap.rearrange("(kc p) n -> p kc n", p=128)
    ps = cx.ps_ada
    NPIECE = 24
    PW = 3072 // NPIECE
    for pc in range(NPIECE):
        bi = cx.ada_i % 2
        cx.ada_i += 1
        buf = cx.ada_buf[bi]
        P.dma("pool", [(buf[:], wv[:, :, pc * PW:(pc + 1) * PW])], cx.ada_slot[bi],
              writes=[("adabuf", bi)])
        for fl in range(PW // 128):
            fch = pc * (PW // 128) + fl
            fns = []
            for kc in range(8):
                fns.append(lambda h, kc=kc, fl=fl, fch=fch, buf=buf: h.matmul(
                    ps[:, fch:fch + 1], buf[:, kc, fl * 128:(fl + 1) * 128], cx.sc[:, kc:kc + 1],
                    start=(kc == 0), stop=(kc == 7)))
            P.mm_group(fns, reads=[("adabuf", bi), "sc"], writes=[("ps_ada", fch)])
    rd = [("ps_ada", f) for f in range(24)]
    P.op("dve", lambda h: h.tensor_tensor(out_mod[:, :], ps[:, 0:24], bias_cols, ALU.add),
         reads=rd + ["cols"], writes=[("mod", tag)])
    P.op("dve", lambda h: h.tensor_scalar(out_mod[:, 8:16], out_mod[:, 8:16], 1.0, None, ALU.add),
         reads=[("mod", tag)], writes=[("mod", tag)])
    P.op("dve", lambda h: h.tensor_scalar(out_mod[:, 16:24], out_mod[:, 16:24], 1.0, INV_ALPHA, ALU.add, ALU.mult),
         reads=[("mod", tag)], writes=[("mod", tag)])


def emit_modulate(P, cx, xT, hT, mod, tag, tt, eng="dve"):
    sl = slice(tt * TT, (tt + 1) * TT)
    for c in range(8):
        if eng == "act":
            P.op("act", lambda h, c=c: h.activation(hT[:, c, sl], xT[:, c, sl], AF.Identity,
                                                    bias=mod[:, c:c + 1], scale=mod[:, 8 + c:9 + c]),
                 reads=[("x", c, tt), ("mod", tag)], writes=[("h", c, tt)])
        else:
            P.op(eng, lambda h, c=c: h.tensor_scalar(hT[:, c, sl], xT[:, c, sl], mod[:, 8 + c:9 + c],
                                                     mod[:, c:c + 1], ALU.mult, ALU.add),
                 reads=[("x", c, tt), ("mod", tag)], writes=[("h", c, tt)])


def emit_ln(P, cx, xT, tt, g_cols, b_cols):
    sl = slice(tt * TT, (tt + 1) * TT)
    s1, s2, sq = cx.ln_s1, cx.ln_s2, cx.ln_sq
    P.op("pool", lambda h: h.tensor_tensor(s1[:], xT[:, 0, sl], xT[:, 1, sl], ALU.add),
         reads=[("x", 0, tt), ("x", 1, tt)], writes=["ln_s1"])
    for c in range(2, 8):
        P.op("pool", lambda h, c=c: h.tensor_tensor(s1[:], s1[:], xT[:, c, sl], ALU.add),
             reads=[("x", c, tt), "ln_s1"], writes=["ln_s1"])
    for c in range(8):
        if c == 0:
            P.op("act", lambda h: h.activation(s2[:], xT[:, 0, sl], AF.Square),
                 reads=[("x", 0, tt)], writes=["ln_s2"])
        else:
            k = c % 2
            P.op("act", lambda h, c=c, k=k: h.activation(sq[k][:], xT[:, c, sl], AF.Square),
                 reads=[("x", c, tt)], writes=[("ln_sq", k)])
            P.op("dve", lambda h, k=k: h.tensor_tensor(s2[:], s2[:], sq[k][:], ALU.add),
                 reads=[("ln_sq", k), "ln_s2"], writes=["ln_s2"])
    b1, ps1 = cx.psum.next()
    P.mm_group([lambda h: h.matmul(ps1[:], cx.onesm[:], s1[:], start=True, stop=True)],
               reads=["ln_s1", "onesm"], writes=[("ps", b1)])
    b2, ps2 = cx.psum.next()
    P.mm_group([lambda h: h.matmul(ps2[:], cx.onesm[:], s2[:], start=True, stop=True)],
               reads=["ln_s2", "onesm"], writes=[("ps", b2)])
    mean, msq, rstd = cx.ln_mean, cx.ln_tmp[1], cx.ln_rstd
    P.op("act", lambda h: h.activation(mean[:], ps1[:], AF.Identity), reads=[("ps", b1)], writes=["ln_mean"])
    P.op("act", lambda h: h.activation(msq[:], ps1[:], AF.Square), reads=[("ps", b1)], writes=[("ln_tmp", 1)])
    P.op("dve", lambda h: h.tensor_tensor(rstd[:], ps2[:], msq[:], ALU.subtract),
         reads=[("ps", b2), ("ln_tmp", 1)], writes=["ln_rstd"])
    P.op("dve", lambda h: h.tensor_scalar(rstd[:], rstd[:], EPS_P, None, ALU.add),
         reads=["ln_rstd"], writes=["ln_rstd"])
    P.op("dve", lambda h: h.reciprocal(rstd[:], rstd[:]), reads=["ln_rstd"], writes=["ln_rstd"])
    P.op("act", lambda h: h.activation(rstd[:], rstd[:], AF.Sqrt), reads=["ln_rstd"], writes=["ln_rstd"])
    for c in range(8):
        k = c % 2
        tmp = cx.ln_tmp[k]
        P.op("dve", lambda h, c=c, tmp=tmp: h.tensor_tensor(tmp[:], xT[:, c, sl], mean[:], ALU.subtract),
             reads=[("x", c, tt), "ln_mean"], writes=[("ln_tmp", k)])
        P.op("pool", lambda h, tmp=tmp: h.tensor_tensor(tmp[:], tmp[:], rstd[:], ALU.mult),
             reads=["ln_rstd", ("ln_tmp", k)], writes=[("ln_tmp", k)])
        P.op("act", lambda h, c=c, tmp=tmp: h.activation(xT[:, c, sl], tmp[:], AF.Identity,
                                                         bias=b_cols[:, c:c + 1], scale=g_cols[:, c:c + 1]),
             reads=[("ln_tmp", k), "cols"], writes=[("x", c, tt)])


def emit_mlp(P, cx, xT, hT, mod, tag, w_up_ap, w_down_ap, g_cols, b_cols, after_ln=None):
    wu_v = w_up_ap.rearrange("(kc p) n -> p kc n", p=128)
    wd_v = w_down_ap.rearrange("(fc p) n -> p fc n", p=128)
    for q in range(4):
        bi = cx.mlp_i % 2
        cx.mlp_i += 1
        wu, wd = cx.wu_buf[bi], cx.wd_buf[bi]
        P.dma("pool", [(wu[:, 0:4, :], wu_v[:, 0:4, q * 1024:(q + 1) * 1024]),
                       (wu[:, 4:8, :], wu_v[:, 4:8, q * 1024:(q + 1) * 1024])],
              cx.wu_slot[bi], writes=[("wu", bi)])
        P.dma("pool", [(wd[:, 0:4, :], wd_v[:, q * 8:q * 8 + 4, :]),
                       (wd[:, 4:8, :], wd_v[:, q * 8 + 4:q * 8 + 8, :])],
              cx.wd_slot[bi], writes=[("wd", bi)])
        def mlp_tile(q, tt, bi, wu, wd):
            sl = slice(tt * TT, (tt + 1) * TT)
            ai = cx.a_i % 2
            cx.a_i += 1
            aT = cx.a_buf[ai]
            for f in range(8):
                b, ps = cx.psum.next()
                fns = [(lambda h, kc=kc, f=f, ps=ps, wu=wu: h.matmul(
                    ps[:], wu[:, kc, f * 128:(f + 1) * 128], hT[:, kc, sl], start=(kc == 0), stop=(kc == 7)))
                    for kc in range(8)]
                P.mm_group(fns, reads=[("wu", bi)] + [("h", kc, tt) for kc in range(8)], writes=[("ps", b)])
                k = cx.r_i % 2
                cx.r_i += 1
                rt = cx.relu_tmp[k]
                P.op("act", lambda h, ps=ps, rt=rt: h.activation(rt[:], ps[:], AF.Relu),
                     reads=[("ps", b)], writes=[("ln_sq", k)])
                P.op("pool", lambda h, rt=rt, aT=aT, f=f: h.tensor_tensor(aT[:, f, :], rt[:], rt[:], ALU.mult),
                     reads=[("ln_sq", k)], writes=[("a", ai, f)])
            for oc in range(8):
                b, ps = cx.psum.next()
                fns = [(lambda h, f=f, oc=oc, ps=ps, wd=wd, aT=aT: h.matmul(
                    ps[:], wd[:, f, oc * 128:(oc + 1) * 128], aT[:, f, :], start=(f == 0), stop=(f == 7)))
                    for f in range(8)]
                P.mm_group(fns, reads=[("wd", bi)] + [("a", ai, f) for f in range(8)], writes=[("ps", b)])
                P.op("dve", lambda h, ps=ps, oc=oc: h.scalar_tensor_tensor(
                    xT[:, oc, sl], ps[:], mod[:, 16 + oc:17 + oc], xT[:, oc, sl], ALU.mult, ALU.add),
                    reads=[("ps", b), ("mod", tag), ("x", oc, tt)], writes=[("x", oc, tt)])
            if q == 3:
                emit_ln(P, cx, xT, tt, g_cols, b_cols)
                if after_ln is not None:
                    after_ln(tt)

        for tt in range(NT):
            mlp_tile(q, tt, bi, wu, wd)


def emit_load_xT(P, cx, x_ap, xT, key="x"):
    xv = x_ap.rearrange("(tb p) f -> p tb f", p=128)
    for tt in range(NT):
        si = cx.stg_i % 2
        cx.stg_i += 1
        stg = cx.stg[si]
        P.dma("sp", [(stg[:, 0:2, :], xv[:, tt * 4:tt * 4 + 2, :]),
                     (stg[:, 2:4, :], xv[:, tt * 4 + 2:tt * 4 + 4, :])], cx.stg_slot[si],
              writes=[("stg", si)])
        for c in range(8):
            b, ps = cx.psum.next()
            fns = [(lambda h, tb=tb, c=c, ps=ps, stg=stg: h.matmul(
                ps[:, tb * 128:(tb + 1) * 128], stg[:, tb, c * 128:(c + 1) * 128], cx.ident[:],
                start=True, stop=True)) for tb in range(4)]
            P.mm_group(fns, reads=[("stg", si), "ident"], writes=[("ps", b)])
            eng = "act" if c % 2 == 0 else "dve"
            sl = slice(tt * TT, (tt + 1) * TT)
            if eng == "act":
                P.op("act", lambda h, ps=ps, c=c, sl=sl: h.activation(xT[:, c, sl], ps[:], AF.Identity),
                     reads=[("ps", b)], writes=[(key, c, tt)])
            else:
                P.op("dve", lambda h, ps=ps, c=c, sl=sl: h.tensor_copy(xT[:, c, sl], ps[:]),
                     reads=[("ps", b)], writes=[(key, c, tt)])


def emit_store_x(P, cx, xT, out_ap, tt):
    ov = out_ap.rearrange("(tb p) f -> p tb f", p=128)
    for tb in range(4):
        si = cx.ost_i % 2
        cx.ost_i += 1
        stg = cx.ost[si]
        tsl = slice(tt * TT + tb * 128, tt * TT + (tb + 1) * 128)
        for half in range(2):
            b, ps = cx.psum.next()
            fns = [(lambda h, cc=cc, half=half, ps=ps, tsl=tsl: h.matmul(
                ps[:, cc * 128:(cc + 1) * 128], xT[:, half * 4 + cc, tsl], cx.ident[:],
                start=True, stop=True)) for cc in range(4)]
            P.mm_group(fns, reads=[("x", half * 4 + cc, tt) for cc in range(4)] + ["ident"],
                       writes=[("ps", b)])
            if half == 0:
                P.op("act", lambda h, ps=ps, stg=stg: h.activation(stg[:, 0:512], ps[:], AF.Identity),
                     reads=[("ps", b)], writes=[("ost", si, 0)])
            else:
                P.op("dve", lambda h, ps=ps, stg=stg: h.tensor_copy(stg[:, 512:1024], ps[:]),
                     reads=[("ps", b)], writes=[("ost", si, 1)])
        tok = P.dma("sp", [(ov[:, tt * 4 + tb, :], stg[:])], cx.out_slot[si],
                    reads=[("ost", si, 0), ("ost", si, 1)])
        cx.out_toks.append(tok)


def common_setup(nc, P, A, cx, ncols):
    cx.psum = Psum(nc, P)
    cx.cols, _, _ = A.alloc([128, ncols], F32)
    cx.ident_f, _, _ = A.alloc([128, 128], F32)
    cx.ident = cx.ident_f
    cx.onesm, _, _ = A.alloc([128, 128], F32)
    cx.sc, _, _ = A.alloc([128, 8], BF16)
    cx.mod = [A.alloc([128, 24], F32)[0] for _ in range(2)]
    cx.ln_s1, _, _ = A.alloc([128, TT], F32)
    cx.ln_s2, _, _ = A.alloc([128, TT], F32)
    cx.ln_sq = [A.alloc([128, TT], F32)[0] for _ in range(2)]
    cx.ln_mean, _, _ = A.alloc([128, TT], F32)
    cx.ln_rstd, _, _ = A.alloc([128, TT], F32)
    cx.ln_tmp = [A.alloc([128, TT], F32)[0] for _ in range(2)]
    cx.relu_tmp = cx.ln_sq
    cx.ada_buf = [A.alloc([128, 8, 128], BF16)[0] for _ in range(2)]
    cx.ada_slot = [P.slot() for _ in range(2)]
    cx.ada_i = 0
    cx.mlp_i = 0
    cx.a_i = 0
    cx.r_i = 0
    cx.stg_i = 0
    cx.ost_i = 0
    cx.out_toks = []
    cx.out_slot = [P.slot() for _ in range(2)]
    cx.const_slot = P.slot()
    cx.stg_slot = [P.slot() for _ in range(2)]
    cx.wu_slot = [P.slot() for _ in range(2)]
    cx.wd_slot = [P.slot() for _ in range(2)]
    cx.ps_ada = cx.psum.banks[7]


def fence_tokens(P, keys):
    toks = []
    for k in keys:
        t = P.last_w.get(k)
        if t is not None:
            toks.append(t)
        toks.extend(P.readers.get(k, {}).values())
    return toks


A_COLS = dict(c=0, adab0=8, adab1=32, lng0=56, lnb0=64, lng1=72, lnb1=80, binu=88)
A_NCOLS = 96


def body_A(nc, P, A, cx, io, xT, stage=None, fused=False):
    x_in, cols_in, rows_in, bb_in, consts_in = io["x"], io["cols"], io["rows"], io["bbias"], io["constsA"]
    ada_w0, ada_w1, w_in, w_s, w_out = io["ada_w00"], io["ada_w01"], io["a_w_in"], io["a_w_s"], io["a_w_out"]
    w_up, w_down, y_out = io["mlp_w_up0"], io["mlp_w_down0"], io["y"]
    mark = A.off
    win, _, _ = A.alloc([128, 8, 2048], BF16)
    wout, _, _ = A.alloc([128, 8, 1024], BF16)
    rows, _, _ = A.alloc([128, 3, 1024], F32)
    bbias, _, _ = A.alloc([128, 8, 128], F32)
    wcT, _, _ = A.alloc([128, 16, 128], BF16)
    vst, _, _ = A.alloc([128, 2, 6], F32)
    vmv, _, _ = A.alloc([128, 2], F32)
    vrs, _, _ = A.alloc([128, 1], F32)
    mark1b = A.off
    hT1 = [A.alloc([128, 8, TT], BF16)[0] for _ in range(2)]
    uT1, _, _ = A.alloc([128, 8, TT], BF16)
    uT = [uT1, uT1]
    vtok1, _, _ = A.alloc([128, 4, 1024], BF16)
    vtok = [vtok1, vtok1]
    vg = [A.alloc([128, 1024], F32)[0] for _ in range(2)]
    gt = [A.alloc([128, TT], F32)[0] for _ in range(2)]
    end1 = A.off
    A.off = mark1b
    cx.stg = [A.alloc([128, 4, 1024], F32)[0] for _ in range(2)]
    tril, _, _ = A.alloc([128, 128], F32)
    identb, _, _ = A.alloc([128, 128], BF16)
    ws_f, _, _ = A.alloc([128, 16, 128], F32)
    ws_b, _, _ = A.alloc([128, 16, 128], BF16)
    end1 = max(end1, A.off)
    A.off = mark
    h2T, _, _ = A.alloc([128, 8, T], BF16)
    cx.wu_buf = [A.alloc([128, 8, 1024], BF16)[0] for _ in range(2)]
    cx.wd_buf = [A.alloc([128, 8, 1024], BF16)[0] for _ in range(2)]
    cx.a_buf = [A.alloc([128, 8, TT], BF16)[0] for _ in range(2)]
    cx.ost = [A.alloc([128, 1024], F32)[0] for _ in range(2)]
    end2 = A.off
    A.off = max(end1, end2)
    print("SBUF plan A: persistent", mark, "phase1 end", end1, "phase2 end", end2)

    P.dma("sp", [(cx.cols[:], cols_in)], P.slot(), writes=["cols"])
    P.dma("sp", [(cx.ident_f[:], consts_in[:, 0, :]), (cx.onesm[:], consts_in[:, 1, :]),
                 (tril[:], consts_in[:, 2, :])], P.slot(), writes=["ident", "onesm", "tril"])
    P.dma("sp", [(rows[:], rows_in)], P.slot(), writes=["rows"])
    P.dma("sp", [(bbias[:], bb_in)], P.slot(), writes=["bbias"])
    P.dma("sp", [(ws_f[:], w_s.rearrange("g t s -> t g s"))], P.slot(), writes=["ws_f"])
    P.op("act", lambda h: h.activation(cx.sc[:], cx.cols[:, 0:8], AF.Silu), reads=["cols"], writes=["sc"])
    emit_ada(P, cx, ada_w0, cx.cols[:, 8:32], cx.mod[0], 0)
    wv = w_in.rearrange("(kc p) n -> p kc n", p=128)
    for j in range(4):
        P.dma("pool", [(win[:, :, j * 512:(j + 1) * 512], wv[:, :, j * 512:(j + 1) * 512])], P.slot(),
              writes=[("win", j)])
    wout_slot = P.slot()
    P.dma("pool", [(wout[:], w_out.rearrange("(kc p) n -> p kc n", p=128))], wout_slot, writes=["wout"])
    P.op("dve", lambda h: h.tensor_copy(identb[:], cx.ident_f[:]), reads=["ident"], writes=["identb"])
    P.op("dve", lambda h: h.tensor_tensor(ws_b[:], ws_f[:], tril[:, None, :].to_broadcast([128, 16, 128]), ALU.mult),
         reads=["ws_f", "tril"], writes=["ws_b"])
    for g4 in range(4):
        b, ps = cx.psum.next()
        fns = [(lambda h, gi=gi, g4=g4, ps=ps: h.matmul(
            ps[:, gi * 128:(gi + 1) * 128], ws_b[:, g4 * 4 + gi, :], identb[:], start=True, stop=True))
            for gi in range(4)]
        P.mm_group(fns, reads=["ws_b", "identb"], writes=[("ps", b)])
        P.op("dve", lambda h, g4=g4, ps=ps: h.tensor_copy(
            wcT[:, g4 * 4:(g4 + 1) * 4, :], ps[:].rearrange("p (g t) -> p g t", g=4)),
            reads=[("ps", b)], writes=[("wcT", g4)])
    emit_load_xT(P, cx, x_in, xT)
    if stage == "load":
        for tt in range(NT):
            emit_store_x(P, cx, xT, y_out, tt)
        return
    f1 = fence_tokens(P, [("stg", 0), ("stg", 1), "tril", "identb", "ws_f", "ws_b"])
    for e_ in ("dve", "act", "pool"):
        P.wait_all(e_, f1)

    def mix_tile(tt):
        sl = slice(tt * TT, (tt + 1) * TT)
        pi = tt % 2
        hT = hT1[pi]
        for c in range(8):
            P.op("dve", lambda h, c=c, hT=hT: h.tensor_scalar(
                hT[:, c, :], xT[:, c, sl], cx.mod[0][:, 8 + c:9 + c], cx.mod[0][:, c:c + 1], ALU.mult, ALU.add),
                reads=[("x", c, tt), ("mod", 0)], writes=[("h1", pi, c)])
        for fc in range(8):
            b, ps = cx.psum.next()
            fns = [(lambda h, kc=kc, fc=fc, ps=ps, hT=hT: h.matmul(
                ps[:], win[:, kc, fc * 128:(fc + 1) * 128], hT[:, kc, :], start=(kc == 0), stop=(kc == 7)))
                for kc in range(8)]
            P.mm_group(fns, reads=[("win", fc // 4)] + [("h1", pi, kc) for kc in range(8)], writes=[("ps", b)])
            P.op("act", lambda h, fc=fc, ps=ps: h.activation(
                uT[pi][:, fc, :], ps[:], GELU, bias=cx.cols[:, A_COLS["binu"] + fc:A_COLS["binu"] + fc + 1]),
                reads=[("ps", b), "cols"], writes=[("u", 0, fc)])
        for tb in range(4):
            vi = (tt * 4 + tb) % 2
            vgb = vg[vi]
            for half in range(2):
                b, ps = cx.psum.next()
                fns = [(lambda h, kc=kc, half=half, ps=ps, hT=hT, tb=tb: h.matmul(
                    ps[:], hT[:, kc, tb * 128:(tb + 1) * 128],
                    win[:, kc, 1024 + half * 512:1024 + (half + 1) * 512], start=(kc == 0), stop=(kc == 7)))
                    for kc in range(8)]
                P.mm_group(fns, reads=[("win", 2 + half)] + [("h1", pi, kc) for kc in range(8)],
                           writes=[("ps", b)])
                P.op("dve", lambda h, ps=ps, half=half, vgb=vgb: h.tensor_tensor(
                    vgb[:, half * 512:(half + 1) * 512], ps[:], rows[:, 0, half * 512:(half + 1) * 512], ALU.add),
                    reads=[("ps", b), "rows"], writes=[("vg", vi, half)])
                P.op("act", lambda h, half=half, vgb=vgb: h.activation(
                    vgb[:, half * 512:(half + 1) * 512], vgb[:, half * 512:(half + 1) * 512], GELU),
                    reads=[("vg", vi, half)], writes=[("vg", vi, half)])
                P.op("dve", lambda h, half=half, vgb=vgb: h.bn_stats(
                    vst[:, half, :], vgb[:, half * 512:(half + 1) * 512]),
                    reads=[("vg", vi, half)], writes=[("vst", half)])
            P.op("dve", lambda h: h.bn_aggr(vmv[:], vst[:].rearrange("p a b -> p (a b)")),
                 reads=[("vst", 0), ("vst", 1)], writes=["vmv"])
            P.op("dve", lambda h: h.tensor_scalar(vrs[:], vmv[:, 1:2], LN_EPS, None, ALU.add),
                 reads=["vmv"], writes=["vrs"])
            P.op("dve", lambda h: h.reciprocal(vrs[:], vrs[:]), reads=["vrs"], writes=["vrs"])
            P.op("act", lambda h: h.activation(vrs[:], vrs[:], AF.Sqrt), reads=["vrs"], writes=["vrs"])
            P.op("dve", lambda h, vgb=vgb: h.tensor_scalar(
                vgb[:], vgb[:], vmv[:, 0:1], vrs[:, 0:1], ALU.subtract, ALU.mult),
                reads=[("vg", vi, 0), ("vg", vi, 1), "vmv", "vrs"], writes=[("vg", vi, 0), ("vg", vi, 1)])
            P.op("pool", lambda h, vgb=vgb: h.tensor_tensor(vgb[:], vgb[:], rows[:, 1, :], ALU.mult),
                 reads=[("vg", vi, 0), ("vg", vi, 1), "rows"], writes=[("vg", vi, 0), ("vg", vi, 1)])
            P.op("pool", lambda h, vgb=vgb, tb=tb: h.tensor_tensor(vtok[pi][:, tb, :], vgb[:], rows[:, 2, :], ALU.add),
                 reads=[("vg", vi, 0), ("vg", vi, 1), "rows"], writes=[("vtok", 0, tb)])
        for fc in range(8):
            b, ps = cx.psum.next()
            fns = []
            for tb in range(4):
                for gi in range(2):
                    g = 2 * fc + gi
                    fns.append(lambda h, tb=tb, gi=gi, g=g, ps=ps: h.matmul(
                        ps[gi * 64:(gi + 1) * 64, tb * 128:(tb + 1) * 128],
                        vtok[pi][:, tb, g * 64:(g + 1) * 64], wcT[:, g, :], start=True, stop=True))
            P.mm_group(fns, reads=[("vtok", 0, tb) for tb in range(4)] + [("wcT", (2 * fc) // 4)],
                       writes=[("ps", b)])
            k = fc % 2
            P.op("dve", lambda h, ps=ps, fc=fc, k=k: h.tensor_tensor(
                gt[k][:].rearrange("p (a t) -> p a t", a=4), ps[:].rearrange("p (a t) -> p a t", a=4),
                bbias[:, fc, None, :].to_broadcast([128, 4, 128]), ALU.add),
                reads=[("ps", b), "bbias"], writes=[("gt", k)])
            P.op("pool", lambda h, fc=fc, k=k: h.tensor_tensor(uT[pi][:, fc, :], gt[k][:], uT[pi][:, fc, :], ALU.mult),
                 reads=[("gt", k), ("u", 0, fc)], writes=[("u", 0, fc)])
        for oc in range(8):
            b, ps = cx.psum.next()
            fns = [(lambda h, fc=fc, oc=oc, ps=ps: h.matmul(
                ps[:], wout[:, fc, oc * 128:(oc + 1) * 128], uT[pi][:, fc, :], start=(fc == 0), stop=(fc == 7)))
                for fc in range(8)]
            P.mm_group(fns, reads=["wout"] + [("u", 0, fc) for fc in range(8)], writes=[("ps", b)])
            P.op("dve", lambda h, ps=ps, oc=oc: h.scalar_tensor_tensor(
                xT[:, oc, sl], ps[:], cx.mod[0][:, 16 + oc:17 + oc], xT[:, oc, sl], ALU.mult, ALU.add),
                reads=[("ps", b), ("mod", 0), ("x", oc, tt)], writes=[("x", oc, tt)])
        emit_ln(P, cx, xT, tt, cx.cols[:, A_COLS["lng0"]:A_COLS["lng0"] + 8],
                cx.cols[:, A_COLS["lnb0"]:A_COLS["lnb0"] + 8])

    for tt in range(NT):
        mix_tile(tt)

    if stage == "mixA":
        dbg = nc.dram_tensor("dbg", [128, 16384], F32, kind="ExternalOutput").ap()
        dsl = P.slot()
        allk = list(P.last_w.keys())
        cx.out_toks.append(P.dma("sp", [(dbg[:, 0:24], cx.mod[0][:])], dsl, reads=allk))
        cx.out_toks.append(P.dma("pool", [(dbg[:, 1024:3072], wcT[:].rearrange("p g t -> p (g t)"))], dsl, reads=allk))
        cx.out_toks.append(P.dma("pool", [(dbg[:, 4096:8192], hT1[1][:].rearrange("p g t -> p (g t)"))], dsl, reads=allk))
        cx.out_toks.append(P.dma("pool", [(dbg[:, 8192:12288], uT1[:].rearrange("p g t -> p (g t)"))], dsl, reads=allk))
        cx.out_toks.append(P.dma("pool", [(dbg[:, 12288:16384], vtok1[:].rearrange("p g t -> p (g t)"))], dsl, reads=allk))
        for e_ in ("pe", "act", "dve", "pool"):
            P.wait_all(e_, list(cx.out_toks))
        for tt in range(NT):
            emit_store_x(P, cx, xT, y_out, tt)
        return
    emit_ada(P, cx, ada_w1, cx.cols[:, 32:56], cx.mod[1], 1)
    p1_keys = [("win", j) for j in range(4)] + ["wout", "rows", "bbias", "vst", "vmv", "vrs"] + \
              [("wcT", g4) for g4 in range(4)] + [("h1", p, c) for p in range(2) for c in range(8)] + \
              [("u", 0, c) for c in range(8)] + [("vtok", 0, tb) for tb in range(4)] + \
              [("vg", i, hf) for i in range(2) for hf in range(2)] + [("gt", 0), ("gt", 1)]
    f2 = fence_tokens(P, p1_keys)
    for e_ in ("dve", "act", "pool"):
        P.wait_all(e_, f2)
    for tt in range(NT):
        emit_modulate(P, cx, xT, h2T, cx.mod[1], 1, tt)

    def after_ln(tt):
        emit_store_x(P, cx, xT, y_out, tt)

    emit_mlp(P, cx, xT, h2T, cx.mod[1], 1, w_up, w_down,
             cx.cols[:, A_COLS["lng1"]:A_COLS["lng1"] + 8], cx.cols[:, A_COLS["lnb1"]:A_COLS["lnb1"] + 8],
             after_ln=(None if fused else after_ln))


def build_A(stage=None):
    nc = bass.Bass("TRN2", target_bir_lowering=False)
    dt = lambda n, s: nc.dram_tensor(n, s, F32, kind="ExternalInput").ap()
    x_in = dt("x", [T, D])
    cols_in = dt("cols", [128, A_NCOLS])
    rows_in = dt("rows", [128, 3, 1024])
    bb_in = dt("bbias", [128, 8, 128])
    consts_in = dt("consts", [128, 3, 128])
    ada_w0 = dt("ada_w0", [D, 3 * D])
    ada_w1 = dt("ada_w1", [D, 3 * D])
    w_in = dt("a_w_in", [D, 2 * D])
    w_s = dt("a_w_s", [16, 128, 128])
    w_out = dt("a_w_out", [D, D])
    w_up = dt("mlp_w_up", [D, 4 * D])
    w_down = dt("mlp_w_down", [4 * D, D])
    y_out = nc.dram_tensor("y", [T, D], F32, kind="ExternalOutput").ap()

    with contextlib.ExitStack() as es:
        P = Prog(nc, es)
        A = Arena(nc)
        cx = Ctx()
        common_setup(nc, P, A, cx, A_NCOLS)
        xT, _, _ = A.alloc([128, 8, T], F32)
        io = dict(x=x_in, cols=cols_in, rows=rows_in, bbias=bb_in, constsA=consts_in, ada_w00=ada_w0, ada_w01=ada_w1,
                  a_w_in=w_in, a_w_s=w_s, a_w_out=w_out, mlp_w_up0=w_up, mlp_w_down0=w_down, y=y_out)
        body_A(nc, P, A, cx, io, xT, stage=stage, fused=False)
        P.wait_all("sp", cx.out_toks)
        P.finish()
    return nc


def host_consts():
    ident = np.eye(128, dtype=np.float32)
    ones = np.full((128, 128), 1.0 / 1024.0, dtype=np.float32)
    tril = np.tril(np.ones((128, 128), dtype=np.float32))
    return np.ascontiguousarray(np.stack([ident, ones, tril], axis=1))


def prep_A(inp):
    x = np.asarray(inp["x"], dtype=np.float32)
    c = np.asarray(inp["c"], dtype=np.float32)
    maps = []
    consts = host_consts()
    rows = np.stack([inp["a_b_in"][0][1024:], inp["a_vn_g"][0], inp["a_vn_b"][0]], axis=0).astype(np.float32)
    rows = np.ascontiguousarray(np.broadcast_to(rows[None], (128, 3, 1024)))
    bs = np.asarray(inp["a_b_s"][0], dtype=np.float32)
    bb = np.zeros((128, 8, 128), dtype=np.float32)
    for fc in range(8):
        bb[0:64, fc, :] = bs[2 * fc][None, :]
        bb[64:128, fc, :] = bs[2 * fc + 1][None, :]
    for core in range(NCORES):
        b, half = core // 2, core % 2
        cols = np.zeros((128, A_NCOLS), dtype=np.float32)
        cols[:, 0:8] = col(c[b])
        cols[:, 8:32] = col(inp["ada_b"][0, 0])
        cols[:, 32:56] = col(inp["ada_b"][0, 1])
        cols[:, 56:64] = col(inp["ln_g"][0, 0])
        cols[:, 64:72] = col(inp["ln_b"][0, 0])
        cols[:, 72:80] = col(inp["ln_g"][0, 1])
        cols[:, 80:88] = col(inp["ln_b"][0, 1])
        cols[:, 88:96] = col(inp["a_b_in"][0][:1024])
        maps.append({
            "x": np.ascontiguousarray(x[b, half * T:(half + 1) * T]),
            "cols": cols, "rows": rows, "bbias": bb, "consts": consts,
            "ada_w0": np.ascontiguousarray(inp["ada_w"][0, 0]), "ada_w1": np.ascontiguousarray(inp["ada_w"][0, 1]),
            "a_w_in": np.ascontiguousarray(inp["a_w_in"][0]), "a_w_s": np.ascontiguousarray(inp["a_w_s"][0]),
            "a_w_out": np.ascontiguousarray(inp["a_w_out"][0]),
            "mlp_w_up": np.ascontiguousarray(inp["mlp_w_up"][0]), "mlp_w_down": np.ascontiguousarray(inp["mlp_w_down"][0]),
        })
    return maps


def run_A(inp, trace=False, stage=None, ncores=NCORES):
    nc = build_A(stage)
    maps = prep_A(inp)[:ncores]
    res = run_bass_kernel_spmd(nc, maps, core_ids=list(range(ncores)), trace=trace)
    x1 = np.zeros((4, 4096, D), dtype=np.float32)
    for core in range(ncores):
        b, half = core // 2, core % 2
        x1[b, half * T:(half + 1) * T] = res.results[core]["y"]
    if stage == "mixA":
        np.save("dbg0.npy", res.results[0]["dbg"])
    return x1, res


B_NCOLS = 96
PATTERNS = ((128, 1), (512, 4), (2048, 16))


def host_consts_B():
    ident = np.eye(128, dtype=np.float32)
    ones = np.full((128, 128), 1.0 / 1024.0, dtype=np.float32)
    k = np.arange(128)[:, None]
    q = np.arange(128)[None, :]
    d_prev = np.where(k >= q, 128.0 + q - k, 0.0).astype(np.float32)
    d_cur = np.where(k <= q, (q - k) * 1.0, 0.0).astype(np.float32)
    m_prev = np.where(k >= q, 0.0, -240000.0).astype(np.float32)
    m_cur = np.where(k <= q, 0.0, -240000.0).astype(np.float32)
    m = np.arange(128)[None, :]
    permA = ((k == m + 64) & (m < 64)).astype(np.float32)
    permB = ((k == m - 64) & (m >= 64)).astype(np.float32)
    return np.ascontiguousarray(np.stack([ident, ones, d_prev, d_cur, permA, permB, m_prev, m_cur], axis=1))


def body_B(nc, P, A, cx, io, xT, xT_at, stage=None, fused=False, cb=0):
    x_in, xp_in, cols_in, consts_in = io.get("x"), io.get("xp"), io["cols"], io["constsB"]
    ada_w0, ada_w1, w_qkv, w_out = io["ada_w10"], io["ada_w11"], io["b_w_qkv"], io["b_w_out"]
    w_up, w_down, y_out, xsp = io["mlp_w_up1"], io["mlp_w_down1"], io["y"], io["xsp"]
    mark = xT_at
    r1_end = xT_at + 65536
    save_off = A.off
    A.off = mark
    kt0, _, _ = A.alloc([128, 32 * 128], BF16)
    KT = [kt0, kt0]
    VT, _, _ = A.alloc([128, 32 * 128], BF16)
    VA, _, _ = A.alloc([128, 32, 2, 128], BF16)
    ACC, _, _ = A.alloc([128, 2, T], F32)
    QT = [A.alloc([128, T], BF16)[0] for _ in range(2)]
    tmpB = [A.alloc([128, 256], F32)[0] for _ in range(2)]
    Pt = [A.alloc([128, 512], BF16)[0] for _ in range(2)]
    assert A.off <= r1_end, (A.off, r1_end)
    A.off = max(r1_end, save_off)
    mark2 = A.off
    hT, _, _ = A.alloc([128, 8, 2 * T], BF16)
    oT, oT_at, _ = A.alloc([128, 8, T], BF16)
    wqkv = [A.alloc([128, 3, 8, 128], BF16)[0] for _ in range(2)]
    Bhl = [A.alloc([128, 2, 2, 256], BF16)[0] for _ in range(2)]
    rden, _, _ = A.alloc([128, 512], F32)
    identb, _, _ = A.alloc([128, 128], BF16)
    diffm, _, _ = A.alloc([128, 2, 128], F32)
    perm, _, _ = A.alloc([128, 2, 128], F32)
    mask01, _, _ = A.alloc([128, 2, 128], F32)
    endB1 = A.off
    A.off = oT_at
    cx.stg = [A.alloc([128, 4, 1024], F32)[0] for _ in range(2)]
    assert A.off <= oT_at + 32768
    A.off = mark2
    wout, _, _ = A.alloc([128, 8, 1024], BF16)
    A.off = mark2
    h2T, _, _ = A.alloc([128, 8, T], BF16)
    cx.wu_buf = [A.alloc([128, 8, 1024], BF16)[0] for _ in range(2)]
    cx.wd_buf = [A.alloc([128, 8, 1024], BF16)[0] for _ in range(2)]
    cx.a_buf = [A.alloc([128, 8, TT], BF16)[0] for _ in range(2)]
    cx.ost = [A.alloc([128, 1024], F32)[0] for _ in range(2)]
    endB2 = A.off
    print("SBUF plan B: mark", mark, "r1_end", r1_end, "attn end", endB1, "mlp end", endB2)
    assert max(endB1, endB2) <= A.cap

    flag = cx.cols[:, cb + 88:cb + 89]
    if not fused:
        P.dma("sp", [(cx.cols[:], cols_in)], P.slot(), writes=["cols"])
        P.dma("sp", [(cx.ident_f[:], consts_in[:, 0, :]), (cx.onesm[:], consts_in[:, 1, :])], P.slot(),
              writes=["ident", "onesm"])
        P.op("act", lambda h: h.activation(cx.sc[:], cx.cols[:, 0:8], AF.Silu), reads=["cols"], writes=["sc"])
    P.dma("sp", [(diffm[:], consts_in[:, 2:4, :]), (perm[:], consts_in[:, 4:6, :]),
                 (mask01[:], consts_in[:, 6:8, :])], P.slot(), writes=["diffm", "perm"])
    P.op("dve", lambda h: h.tensor_copy(identb[:], cx.ident_f[:]), reads=["ident"], writes=["identb"])
    emit_ada(P, cx, ada_w0, cx.cols[:, cb + 8:cb + 32], cx.mod[0], 0)
    mod0 = cx.mod[0]

    xpv = xp_in.rearrange("(tb p) f -> p tb f", p=128) if xp_in is not None else None

    def load_prev_tile(tt):
        si = cx.stg_i % 2
        cx.stg_i += 1
        stg = cx.stg[si]
        P.dma("sp", [(stg[:, 0:2, :], xpv[:, tt * 4:tt * 4 + 2, :]),
                     (stg[:, 2:4, :], xpv[:, tt * 4 + 2:tt * 4 + 4, :])], cx.stg_slot[si],
              writes=[("stg", si)])
        for c in range(8):
            b, ps = cx.psum.next()
            fns = [(lambda h, tb=tb, c=c, ps=ps, stg=stg: h.matmul(
                ps[:, tb * 128:(tb + 1) * 128], stg[:, tb, c * 128:(c + 1) * 128], cx.ident[:],
                start=True, stop=True)) for tb in range(4)]
            P.mm_group(fns, reads=[("stg", si), "ident"], writes=[("ps", b)])
            P.op("dve", lambda h, ps=ps, c=c, tt=tt: h.tensor_scalar(
                hT[:, c, tt * TT:(tt + 1) * TT], ps[:], mod0[:, 8 + c:9 + c], mod0[:, c:c + 1], ALU.mult, ALU.add),
                reads=[("ps", b), ("mod", 0)], writes=[("hall", c, tt)])

    if not fused:
        for tt in range(NT):
            load_prev_tile(tt)
        emit_load_xT(P, cx, x_in, xT)

    def mod_own(tt):
        for c in range(8):
            P.op("act" if c % 2 else "dve",
                 (lambda h, c=c, tt=tt: h.activation(hT[:, c, T + tt * TT:T + (tt + 1) * TT], xT[:, c, tt * TT:(tt + 1) * TT],
                                                     AF.Identity, bias=mod0[:, c:c + 1], scale=mod0[:, 8 + c:9 + c]))
                 if c % 2 else
                 (lambda h, c=c, tt=tt: h.tensor_scalar(hT[:, c, T + tt * TT:T + (tt + 1) * TT], xT[:, c, tt * TT:(tt + 1) * TT],
                                                        mod0[:, 8 + c:9 + c], mod0[:, c:c + 1], ALU.mult, ALU.add)),
                 reads=[("x", c, tt), ("mod", 0)], writes=[("hall", c, NT + tt)])

    for tt in range(NT):
        mod_own(tt)
    if fused:
        snd, rcv = io["snd"], io["rcv"]
        s_slot = P.slot()
        t_snd = P.dma("sp", [(snd[c], hT[:, c, T:2 * T]) for c in range(8)], s_slot,
                      reads=[("hall", c, NT + tt) for c in range(8) for tt in range(NT)])
        cc_sem = P.new_sem()
        ep = P.E["pool"]
        w_ = P._deps("pool", (), (), [t_snd])
        for c in range(8):
            ep.ops.append((w_ if c == 0 else [],
                           (lambda h, c=c: h.collective_compute(
                               "AllGather", ALU.bypass, replica_groups=[[0, 1], [2, 3], [4, 5], [6, 7]],
                               ins=[snd[c].opt()], outs=[rcv[c].opt()])),
                           (cc_sem, 1)))
        t_cc = (cc_sem, 8)
        r_slot = P.slot()
        P.dma("sp", [(hT[:, c, 0:T], rcv[c][0:128, :]) for c in range(8)], r_slot,
              writes=[("hall", c, tt) for c in range(8) for tt in range(NT)], extra=[t_cc])
    sp_slot = P.slot()
    xkeys = [("x", c, tt) for c in range(8) for tt in range(NT)]
    t_spill = P.dma("sp", [(xsp[:, c * T:(c + 1) * T], xT[:, c, :]) for c in range(8)], sp_slot, reads=xkeys)
    fsp = fence_tokens(P, xkeys) + [t_spill]
    for e_ in ("pe", "act", "dve", "pool"):
        P.wait_all(e_, fsp)
    P.op("pool", lambda h: h.memset(VA[:], 1.0), writes=["va_init"])
    P.op("pool", lambda h: h.tensor_scalar(VA[:, 0:16], VA[:, 0:16], flag, None, ALU.mult),
         reads=["va_init", "cols"], writes=["va_init"])
    va_tok = P.last_w["va_init"]

    wq_v = w_qkv.rearrange("(kc p) n -> p kc n", p=128)
    wq_slot = [P.slot() for _ in range(2)]
    st = {"i": 0, "e": 0, "p": 0}
    hall_keys_all = [("hall", c, t8) for c in range(8) for t8 in range(2 * NT)]

    def perm_view(buf2d, col0, d, tt, ps):
        if d == 1:
            return buf2d[:, col0 + tt * 512:col0 + (tt + 1) * 512], ps[:]
        if d == 4:
            dst = buf2d[:, col0 + tt * 512:col0 + (tt + 1) * 512].rearrange("p (r i) -> p i r", r=4)
            return dst, ps[:].rearrange("p (i r) -> p i r", r=4)
        dst = buf2d[:, col0:col0 + 2048].rearrange("p (r i) -> p i r", r=16)[:, tt * 32:(tt + 1) * 32, :]
        return dst, ps[:].rearrange("p (i r) -> p i r", r=16)

    import os as _os2
    _skip = set(_os2.environ.get("DBG_SKIP", "").split(","))

    def stage_prologue(hp, g, si):
        d = PATTERNS[g][1]
        wb = wqkv[si]
        bh = Bhl[si]
        pairs = []
        for t3 in range(3):
            c0 = ((g * 3 + t3) * 16 + 2 * hp) * 64
            pairs.append((wb[:, t3, :, :], wq_v[:, :, c0:c0 + 128]))
        P.dma("pool", pairs, wq_slot[si], writes=[("wqkv", si)])
        for hh in range(2):
            slope = 2.0 ** (-8.0 * (2 * hp + hh + 1) / 16.0)
            tb = tmpB[hh]
            dm = diffm[:].rearrange("p a q -> p (a q)")
            mk = mask01[:].rearrange("p a q -> p (a q)")
            P.op("pool", lambda h, tb=tb, slope=slope: h.tensor_scalar(tb[:], dm, -8.0 * slope * d, None, ALU.mult),
                 reads=["diffm"], writes=[("tmpB", hh)])
            P.op("pool", lambda h, tb=tb: h.tensor_tensor(tb[:], tb[:], mk, ALU.add),
                 reads=["diffm", ("tmpB", hh)], writes=[("tmpB", hh)])
            P.op("pool", lambda h, tb=tb, hh=hh: h.tensor_copy(bh[:, 0, hh, :], tb[:]),
                 reads=[("tmpB", hh)], writes=[("bhl", si, hh, 0)])
            P.op("pool", lambda h, tb=tb, hh=hh: h.tensor_tensor(bh[:, 1, hh, :], tb[:], bh[:, 0, hh, :], ALU.subtract),
                 reads=[("tmpB", hh), ("bhl", si, hh, 0)], writes=[("bhl", si, hh, 1)])

    def attn_stage(hp, g, si):
        d = PATTERNS[g][1]
        wb = wqkv[si]
        kt = KT[si]
        qt = QT[si]
        bh = Bhl[si]
        for tt in range(0 if "Q" in _skip else NT):
            b, ps = cx.psum.next()
            fns = [(lambda h, kc=kc, ps=ps, tt=tt: h.matmul(
                ps[:], wb[:, 0, kc, :], hT[:, kc, T + tt * TT:T + (tt + 1) * TT], start=(kc == 0), stop=(kc == 7)))
                for kc in range(8)]
            P.mm_group(fns, reads=[("wqkv", si)] + [("hall", kc, NT + tt) for kc in range(8)], writes=[("ps", b)])
            dst, src = perm_view(qt, 0, d, tt, ps)
            P.op("act", lambda h, dst=dst, src=src: h.activation(dst, src, AF.Identity),
                 reads=[("ps", b)], writes=[("qt", si)])
        for tt in range(0 if "K" in _skip else NT):
            b, ps = cx.psum.next()
            fns = [(lambda h, kc=kc, ps=ps, tt=tt: h.matmul(
                ps[:], wb[:, 1, kc, :], hT[:, kc, T + tt * TT:T + (tt + 1) * TT], start=(kc == 0), stop=(kc == 7)))
                for kc in range(8)]
            P.mm_group(fns, reads=[("wqkv", si)] + [("hall", kc, NT + tt) for kc in range(8)], writes=[("ps", b)])
            dst, src = perm_view(kt, 2048, d, tt, ps)
            P.op("dve", lambda h, dst=dst, src=src: h.tensor_copy(dst, src),
                 reads=[("ps", b)], writes=[("kt", 0)])
        if "K" in _skip:
            pass
        elif d == 1:
            b, ps = cx.psum.next()
            fns = [(lambda h, kc=kc, ps=ps: h.matmul(
                ps[:, 0:128], wb[:, 1, kc, :], hT[:, kc, T - 128:T], start=(kc == 0), stop=(kc == 7)))
                for kc in range(8)]
            P.mm_group(fns, reads=[("wqkv", si)] + [("hall", kc, NT - 1) for kc in range(8)], writes=[("ps", b)])
            P.op("dve", lambda h, ps=ps: h.tensor_copy(kt[:, 15 * 128:16 * 128], ps[:, 0:128]),
                 reads=[("ps", b)], writes=[("kt", 0)])
        elif d == 4:
            b, ps = cx.psum.next()
            fns = [(lambda h, kc=kc, ps=ps: h.matmul(
                ps[:], wb[:, 1, kc, :], hT[:, kc, T - 512:T], start=(kc == 0), stop=(kc == 7)))
                for kc in range(8)]
            P.mm_group(fns, reads=[("wqkv", si)] + [("hall", kc, NT - 1) for kc in range(8)], writes=[("ps", b)])
            dst = kt[:, 12 * 128:16 * 128].rearrange("p (r i) -> p i r", r=4)
            src = ps[:].rearrange("p (i r) -> p i r", r=4)
            P.op("dve", lambda h, dst=dst, src=src: h.tensor_copy(dst, src),
                 reads=[("ps", b)], writes=[("kt", 0)])
        else:
            for tt in range(NT):
                b, ps = cx.psum.next()
                fns = [(lambda h, kc=kc, ps=ps, tt=tt: h.matmul(
                    ps[:], wb[:, 1, kc, :], hT[:, kc, tt * TT:(tt + 1) * TT], start=(kc == 0), stop=(kc == 7)))
                    for kc in range(8)]
                P.mm_group(fns, reads=[("wqkv", si)] + [("hall", kc, tt) for kc in range(8)], writes=[("ps", b)])
                dst, src = perm_view(kt, 0, d, tt, ps)
                P.op("dve", lambda h, dst=dst, src=src: h.tensor_copy(dst, src),
                     reads=[("ps", b)], writes=[("kt", 0)])
        def vproj(tok0, ntok, col0, mode, tt, hkeys):
            b, ps = cx.psum.next()
            fns = [(lambda h, kc=kc, ps=ps: h.matmul(
                ps[:, 0:ntok], wb[:, 2, kc, :], hT[:, kc, tok0:tok0 + ntok], start=(kc == 0), stop=(kc == 7)))
                for kc in range(8)]
            P.mm_group(fns, reads=[("wqkv", si)] + hkeys, writes=[("ps", b)])
            if mode == "plain":
                dst, src = VT[:, col0:col0 + ntok], ps[:, 0:ntok]
            elif mode == "seg4":
                dst = VT[:, col0:col0 + 512].rearrange("p (r i) -> p i r", r=4)
                src = ps[:].rearrange("p (i r) -> p i r", r=4)
            else:
                dst, src = perm_view(VT, col0, d, tt, ps)
            P.op("dve", lambda h, dst=dst, src=src: h.tensor_copy(dst, src),
                 reads=[("ps", b)], writes=["vt"])

        for tt in range(0 if "V" in _skip else NT):
            hk = [("hall", kc, NT + tt) for kc in range(8)]
            if d == 1:
                vproj(T + tt * TT, TT, 2048 + tt * 512, "plain", tt, hk)
            elif d == 4:
                vproj(T + tt * TT, TT, 2048 + tt * 512, "seg4", tt, hk)
            else:
                vproj(T + tt * TT, TT, 2048, "perm", tt, hk)
        if "V" in _skip:
            pass
        elif d == 1:
            vproj(T - 128, 128, 15 * 128, "plain", 0, [("hall", kc, NT - 1) for kc in range(8)])
        elif d == 4:
            vproj(T - 512, 512, 12 * 128, "seg4", 0, [("hall", kc, NT - 1) for kc in range(8)])
        else:
            for tt in range(NT):
                vproj(tt * TT, TT, 0, "perm", tt, [("hall", kc, tt) for kc in range(8)])
        groups = []
        prev_slots = list(range(16 - d, 16))
        own_slots = list(range(16, 32))
        for i0 in range(0, len(prev_slots), 4):
            groups.append(prev_slots[i0:i0 + 4])
        for i0 in range(0, 16, 4):
            groups.append(own_slots[i0:i0 + 4])
        if "T" in _skip or "V" in _skip:
            groups = []
        for grp in groups:
            b, ps = cx.psum.next()
            fns = [(lambda h, gi=gi, s=s, ps=ps: h.matmul(
                ps[:, gi * 128:(gi + 1) * 128], VT[:, s * 128:(s + 1) * 128], identb[:], start=True, stop=True))
                for gi, s in enumerate(grp)]
            P.mm_group(fns, reads=["vt", "identb"], writes=[("ps", b)], extra=[va_tok])
            ng = len(grp)
            s0 = grp[0]
            srcv = ps[:, 0:ng * 128].rearrange("p (s c) -> p s c", c=128)
            if "X" in _skip:
                continue
            if "XP" in _skip and s0 < 16:
                continue
            if "XO" in _skip and s0 >= 16:
                continue
            if "ALTVA" in _skip:
                s0_ = min(grp[0], 30)
                P.op("dve", lambda h, ps=ps, s0_=s0_: h.tensor_copy(
                    VA[:, s0_:s0_ + 2, :, :].rearrange("p a b c -> p (a b c)"), ps[:]),
                    reads=[("ps", b)], writes=[("va", 0)])
                continue
            if "F2D" in _skip:
                for gi, sl_ in enumerate(grp):
                    if "NO20" in _skip and sl_ == 20:
                        continue
                    P.op("act", lambda h, gi=gi, sl_=sl_, ps=ps: h.activation(
                        VA[:, sl_, 0, :], ps[:, gi * 128:(gi + 1) * 128], AF.Identity),
                        reads=[("ps", b)], writes=[("va", 0)])
                    P.op("dve", lambda h, gi=gi, sl_=sl_, ps=ps: h.tensor_copy(
                        VA[:, sl_, 1, :], ps[:, gi * 128:(gi + 1) * 128]),
                        reads=[("ps", b)], writes=[("va", 1)])
                continue
            if "FULL" in _skip:
                P.op("act", lambda h, srcv=srcv, s0=s0, ng=ng: h.activation(
                    VA[:, s0:s0 + ng, 0, :], srcv, AF.Identity),
                    reads=[("ps", b)], writes=[("va", 0)])
                P.op("dve", lambda h, srcv=srcv, s0=s0, ng=ng: h.tensor_copy(
                    VA[:, s0:s0 + ng, 1, :], srcv),
                    reads=[("ps", b)], writes=[("va", 1)])
                continue
            if s0 < 16:
                P.op("act", lambda h, srcv=srcv, s0=s0, ng=ng: h.activation(
                    VA[:, s0:s0 + ng, 0, 0:64], srcv[:, :, 0:64], AF.Identity, scale=flag),
                    reads=[("ps", b), "cols"], writes=[("va", 0)])
                P.op("dve", lambda h, srcv=srcv, s0=s0, ng=ng: h.tensor_scalar(
                    VA[:, s0:s0 + ng, 1, 64:128], srcv[:, :, 64:128], flag, None, ALU.mult),
                    reads=[("ps", b), "cols"], writes=[("va", 1)])
            else:
                P.op("act", lambda h, srcv=srcv, s0=s0, ng=ng: h.activation(
                    VA[:, s0:s0 + ng, 0, 0:64], srcv[:, :, 0:64], AF.Identity),
                    reads=[("ps", b)], writes=[("va", 0)])
                P.op("dve", lambda h, srcv=srcv, s0=s0, ng=ng: h.tensor_copy(
                    VA[:, s0:s0 + ng, 1, 64:128], srcv[:, :, 64:128]),
                    reads=[("ps", b)], writes=[("va", 1)])
        nun = 0 if stage == 'proj' else 16
        ust = {}

        def unit_front(j):
            pi = st["p"] % 2
            st["p"] += 1
            pt = Pt[pi]
            for hh in range(2):
                b, ps = cx.psum.next()
                fns = [lambda h, ps=ps, hh=hh: h.matmul(ps[:, 0:256], identb[:], bh[:, 0, hh, :], start=True, stop=False),
                       lambda h, ps=ps, hh=hh: h.matmul(ps[:, 0:256], identb[:], bh[:, 1, hh, :], start=False, stop=False)]
                for kb in range(2):
                    slot = 16 + j - d if kb == 0 else 16 + j
                    fns.append(lambda h, hh=hh, kb=kb, slot=slot, ps=ps, j=j: h.matmul(
                        ps[:, kb * 128:(kb + 1) * 128],
                        kt[hh * 64:(hh + 1) * 64, slot * 128:(slot + 1) * 128],
                        qt[hh * 64:(hh + 1) * 64, j * 128:(j + 1) * 128], start=False, stop=(kb == 1)))
                P.mm_group(fns, reads=[("kt", 0), ("qt", si), "identb", ("bhl", si, hh, 0), ("bhl", si, hh, 1)],
                           writes=[("ps", b)])
                P.op("act", lambda h, ps=ps, pt=pt, hh=hh: h.activation(
                    pt[:, hh * 256:(hh + 1) * 256], ps[:, 0:256], AF.Exp, scale=0.125),
                    reads=[("ps", b)], writes=[("pt", pi, hh)])
            ust[j] = (pi, pt)

        def unit_back(j):
            pi, pt = ust[j]
            n, r = j // d, j % d
            b2, ps2 = cx.psum.next()
            fns = []
            for hh in range(2):
                for kb in range(2):
                    slot = 16 + j - d if kb == 0 else 16 + j
                    fns.append(lambda h, hh=hh, kb=kb, slot=slot, ps2=ps2, pt=pt: h.matmul(
                        ps2[:, hh * 128:(hh + 1) * 128], VA[:, slot, hh, :],
                        pt[:, (hh * 2 + kb) * 128:(hh * 2 + kb + 1) * 128], start=(kb == 0), stop=(kb == 1)))
            P.mm_group(fns, reads=[("pt", pi, 0), ("pt", pi, 1), ("va", 0), ("va", 1)], writes=[("ps", b2)])
            start = n * 128 * d + r
            accv = ACC[:, :, start:start + 127 * d + 1:d]
            src2 = ps2[:, 0:256].rearrange("p (a q) -> p a q", a=2)
            wr = ["acc"] if j == nun - 1 else []
            if g == 0:
                P.op("dve", lambda h, accv=accv, src2=src2: h.tensor_copy(accv, src2),
                     reads=[("ps", b2)], writes=wr, extra=acc_dep)
            else:
                P.op("dve", lambda h, accv=accv, src2=src2: h.tensor_tensor(accv, src2, accv, ALU.add),
                     reads=[("ps", b2)], writes=wr, extra=acc_dep)

        acc_dep = fence_tokens(P, ["acc"])
        if nun:
            unit_front(0)
        for j in range(nun):
            if j + 1 < nun:
                unit_front(j + 1)
            unit_back(j)

    def finalize(hp):
        for tt in range(NT):
            sl = slice(tt * TT, (tt + 1) * TT)
            b, ps = cx.psum.next()
            fns = [lambda h, ps=ps, sl=sl: h.matmul(ps[:], perm[:, 0, :], ACC[:, 0, sl], start=True, stop=False),
                   lambda h, ps=ps, sl=sl: h.matmul(ps[:], perm[:, 1, :], ACC[:, 1, sl], start=False, stop=True)]
            P.mm_group(fns, reads=["acc", "perm"], writes=[("ps", b)])
            P.op("dve", lambda h, ps=ps: h.reciprocal(rden[:], ps[:]), reads=[("ps", b)], writes=["rden"])
            P.op("pool", lambda h, sl=sl: h.tensor_tensor(oT[0:64, hp, sl], ACC[0:64, 0, sl], rden[0:64, :], ALU.mult),
                 reads=["acc", "rden"], writes=[("o", hp, tt, 0)])
            P.op("pool", lambda h, sl=sl: h.tensor_tensor(oT[64:128, hp, sl], ACC[64:128, 1, sl], rden[64:128, :], ALU.mult),
                 reads=["acc", "rden"], writes=[("o", hp, tt, 1)])

    P.wait_all("pool", fence_tokens(P, [("stg", 0), ("stg", 1)]))
    nhp = {"pre": 0, "one": 1, "proj": 1}.get(stage, 8)
    import os as _os
    _gl = [int(x) for x in _os.environ.get("DBG_G", "0,1,2").split(",")]
    stages = [(hp, g) for hp in range(nhp) for g in _gl]
    if stages:
        stage_prologue(stages[0][0], stages[0][1], 0)
    for i_s, (hp, g) in enumerate(stages):
        if i_s + 1 < len(stages):
            stage_prologue(stages[i_s + 1][0], stages[i_s + 1][1], (i_s + 1) % 2)
        attn_stage(hp, g, i_s % 2)
        if stage != "proj" and g == _gl[-1]:
            finalize(hp)
    if stage in ("pre", "one", "proj"):
        fx = fence_tokens(P, ["vt", ("kt", 0), ("kt", 1), ("qt", 0), ("qt", 1), ("va", 0), ("va", 1), "acc", "va_init",
                              ("tmpB", 0), ("tmpB", 1), ("pt", 0, 0), ("pt", 0, 1), ("pt", 1, 0), ("pt", 1, 1)])
        P.wait_all("sp", fx)
        t_re = P.dma("sp", [(xT[:, c, :], xsp[:, c * T:(c + 1) * T]) for c in range(8)], P.slot(), writes=xkeys)
        for e_ in ("pe", "act", "dve", "pool"):
            P.wait_all(e_, fence_tokens(P, hall_keys_all + [("o", hp, tt, hf) for hp in range(nhp) for tt in range(NT) for hf in range(2)] + [("wqkv", 0), ("wqkv", 1)] + [("bhl", i, hh, z) for i in range(2) for hh in range(2) for z in range(2)]))
        for tt in range(NT):
            emit_store_x(P, cx, xT, y_out, tt)
        return

    att_keys = ["vt", ("kt", 0), ("kt", 1), ("qt", 0), ("qt", 1), ("va", 0), ("va", 1), "acc", "va_init",
                ("tmpB", 0), ("tmpB", 1), ("pt", 0, 0), ("pt", 0, 1), ("pt", 1, 0), ("pt", 1, 1)]
    fx = fence_tokens(P, att_keys)
    P.wait_all("sp", fx)
    x_slot = P.slot()
    P.dma("sp", [(xT[:, c, :], xsp[:, c * T:(c + 1) * T]) for c in range(8)], x_slot, writes=xkeys)
    fo = fence_tokens(P, hall_keys_all + [("wqkv", 0), ("wqkv", 1)])
    P.wait_all("pool", fo)
    P.dma("pool", [(wout[:], w_out.rearrange("(kc p) n -> p kc n", p=128))], P.slot(), writes=["wout"])

    def outproj_tile(tt):
        sl = slice(tt * TT, (tt + 1) * TT)
        for oc in range(8):
            b, ps = cx.psum.next()
            fns = [(lambda h, hp=hp, oc=oc, ps=ps: h.matmul(
                ps[:], wout[:, hp, oc * 128:(oc + 1) * 128], oT[:, hp, sl], start=(hp == 0), stop=(hp == 7)))
                for hp in range(8)]
            P.mm_group(fns, reads=["wout"] + [("o", hp, tt, hf) for hp in range(8) for hf in range(2)],
                       writes=[("ps", b)])
            P.op("dve", lambda h, ps=ps, oc=oc: h.scalar_tensor_tensor(
                xT[:, oc, sl], ps[:], mod0[:, 16 + oc:17 + oc], xT[:, oc, sl], ALU.mult, ALU.add),
                reads=[("ps", b), ("mod", 0), ("x", oc, tt)], writes=[("x", oc, tt)])
        emit_ln(P, cx, xT, tt, cx.cols[:, cb + 56:cb + 64], cx.cols[:, cb + 64:cb + 72])

    for tt in range(NT):
        outproj_tile(tt)

    if stage == "attn":
        for e_ in ("pe", "act", "dve", "pool"):
            P.wait_all(e_, fence_tokens(P, ["wout"] + [("o", hp, tt, hf) for hp in range(8) for tt in range(NT) for hf in range(2)]))
        for tt in range(NT):
            emit_store_x(P, cx, xT, y_out, tt)
        return

    emit_ada(P, cx, ada_w1, cx.cols[:, cb + 32:cb + 56], cx.mod[1], 1)
    f2 = fence_tokens(P, ["wout"] + [("o", hp, tt, hf) for hp in range(8) for tt in range(NT) for hf in range(2)]
                      + [("bhl", i, hh, z) for i in range(2) for hh in range(2) for z in range(2)] + ["rden", "diffm", "perm", "identb", "vt"])
    for e_ in ("dve", "act", "pool"):
        P.wait_all(e_, f2)
    for tt in range(NT):
        emit_modulate(P, cx, xT, h2T, cx.mod[1], 1, tt)

    def after_ln(tt):
        emit_store_x(P, cx, xT, y_out, tt)

    emit_mlp(P, cx, xT, h2T, cx.mod[1], 1, w_up, w_down, cx.cols[:, cb + 72:cb + 80], cx.cols[:, cb + 80:cb + 88], after_ln=after_ln)


def build_B(stage=None):
    nc = bass.Bass("TRN2", target_bir_lowering=False)
    dt = lambda n, s: nc.dram_tensor(n, s, F32, kind="ExternalInput").ap()
    x_in = dt("x", [T, D])
    xp_in = dt("xp", [T, D])
    cols_in = dt("cols", [128, B_NCOLS])
    consts_in = dt("consts", [128, 8, 128])
    ada_w0 = dt("ada_w0", [D, 3 * D])
    ada_w1 = dt("ada_w1", [D, 3 * D])
    w_qkv = dt("b_w_qkv", [D, 9 * D])
    w_out = dt("b_w_out", [D, D])
    w_up = dt("mlp_w_up", [D, 4 * D])
    w_down = dt("mlp_w_down", [4 * D, D])
    y_out = nc.dram_tensor("y", [T, D], F32, kind="ExternalOutput").ap()
    xsp = nc.dram_tensor("xsp", [128, 8 * T], F32).ap()

    with contextlib.ExitStack() as es:
        P = Prog(nc, es)
        A = Arena(nc)
        cx = Ctx()
        common_setup(nc, P, A, cx, B_NCOLS)
        xT, xT_at, _ = A.alloc([128, 8, T], F32)
        io = dict(x=x_in, xp=xp_in, cols=cols_in, constsB=consts_in, ada_w10=ada_w0, ada_w11=ada_w1, b_w_qkv=w_qkv,
                  b_w_out=w_out, mlp_w_up1=w_up, mlp_w_down1=w_down, y=y_out, xsp=xsp)
        body_B(nc, P, A, cx, io, xT, xT_at, stage=stage, fused=False, cb=0)
        P.wait_all("sp", cx.out_toks)
        P.finish()
    return nc


def prep_B(inp, x1):
    c = np.asarray(inp["c"], dtype=np.float32)
    consts = host_consts_B()
    maps = []
    zeros = np.zeros((T, D), dtype=np.float32)
    for core in range(NCORES):
        b, half = core // 2, core % 2
        cols = np.zeros((128, B_NCOLS), dtype=np.float32)
        cols[:, 0:8] = col(c[b])
        cols[:, 8:32] = col(inp["ada_b"][1, 0])
        cols[:, 32:56] = col(inp["ada_b"][1, 1])
        cols[:, 56:64] = col(inp["ln_g"][1, 0])
        cols[:, 64:72] = col(inp["ln_b"][1, 0])
        cols[:, 72:80] = col(inp["ln_g"][1, 1])
        cols[:, 80:88] = col(inp["ln_b"][1, 1])
        cols[:, 88] = float(half)
        maps.append({
            "x": np.ascontiguousarray(x1[b, half * T:(half + 1) * T]) if x1 is not None else None,
            "xp": (np.ascontiguousarray(x1[b, 0:T]) if half == 1 else zeros) if x1 is not None else None,
            "cols": cols, "consts": consts,
            "ada_w0": np.ascontiguousarray(inp["ada_w"][1, 0]), "ada_w1": np.ascontiguousarray(inp["ada_w"][1, 1]),
            "b_w_qkv": np.ascontiguousarray(inp["b_w_qkv"][0]), "b_w_out": np.ascontiguousarray(inp["b_w_out"][0]),
            "mlp_w_up": np.ascontiguousarray(inp["mlp_w_up"][1]), "mlp_w_down": np.ascontiguousarray(inp["mlp_w_down"][1]),
        })
    return maps


def run_B(inp, x1, trace=False, stage=None, ncores=NCORES):
    nc = build_B(stage)
    maps = prep_B(inp, x1)[:ncores]
    res = run_bass_kernel_spmd(nc, maps, core_ids=list(range(ncores)), trace=trace)
    out = np.zeros((4, 4096, D), dtype=np.float32)
    for core in range(ncores):
        b, half = core // 2, core % 2
        out[b, half * T:(half + 1) * T] = res.results[core]["y"]
    return out, res


F_NCOLS = A_NCOLS + B_NCOLS


def build_F():
    nc = bass.Bass("TRN2", target_bir_lowering=False)
    dt = lambda n, s: nc.dram_tensor(n, s, F32, kind="ExternalInput").ap()
    io = dict(
        x=dt("x", [T, D]), cols=dt("cols", [128, F_NCOLS]), rows=dt("rows", [128, 3, 1024]),
        bbias=dt("bbias", [128, 8, 128]), constsA=dt("constsA", [128, 3, 128]), constsB=dt("constsB", [128, 8, 128]),
        ada_w00=dt("ada_w00", [D, 3 * D]), ada_w01=dt("ada_w01", [D, 3 * D]),
        ada_w10=dt("ada_w10", [D, 3 * D]), ada_w11=dt("ada_w11", [D, 3 * D]),
        a_w_in=dt("a_w_in", [D, 2 * D]), a_w_s=dt("a_w_s", [16, 128, 128]), a_w_out=dt("a_w_out", [D, D]),
        mlp_w_up0=dt("mlp_w_up0", [D, 4 * D]), mlp_w_down0=dt("mlp_w_down0", [4 * D, D]),
        mlp_w_up1=dt("mlp_w_up1", [D, 4 * D]), mlp_w_down1=dt("mlp_w_down1", [4 * D, D]),
        b_w_qkv=dt("b_w_qkv", [D, 9 * D]), b_w_out=dt("b_w_out", [D, D]))
    io["y"] = nc.dram_tensor("y", [T, D], F32, kind="ExternalOutput").ap()
    io["xsp"] = nc.dram_tensor("xsp", [128, 8 * T], F32).ap()
    io["snd"] = [nc.dram_tensor(f"snd{c}", [128, T], BF16).ap() for c in range(8)]
    io["rcv"] = [nc.dram_tensor(f"rcv{c}", [256, T], BF16).ap() for c in range(8)]
    with contextlib.ExitStack() as es:
        P = Prog(nc, es)
        A = Arena(nc)
        cx = Ctx()
        common_setup(nc, P, A, cx, F_NCOLS)
        xT, xT_at, _ = A.alloc([128, 8, T], F32)
        mark = A.off
        body_A(nc, P, A, cx, io, xT, stage=None, fused=True)
        P.barrier()
        A.off = mark
        body_B(nc, P, A, cx, io, xT, xT_at, stage=None, fused=True, cb=A_NCOLS)
        P.wait_all("sp", cx.out_toks)
        P.finish()
    return nc


def prep_F(inp):
    mA = prep_A(inp)
    mB = prep_B(inp, None)
    maps = []
    for core in range(NCORES):
        a, b = mA[core], mB[core]
        maps.append({
            "x": a["x"], "cols": np.ascontiguousarray(np.concatenate([a["cols"], b["cols"]], axis=1)),
            "rows": a["rows"], "bbias": a["bbias"], "constsA": a["consts"], "constsB": b["consts"],
            "ada_w00": a["ada_w0"], "ada_w01": a["ada_w1"], "ada_w10": b["ada_w0"], "ada_w11": b["ada_w1"],
            "a_w_in": a["a_w_in"], "a_w_s": a["a_w_s"], "a_w_out": a["a_w_out"],
            "mlp_w_up0": a["mlp_w_up"], "mlp_w_down0": a["mlp_w_down"],
            "mlp_w_up1": b["mlp_w_up"], "mlp_w_down1": b["mlp_w_down"],
            "b_w_qkv": b["b_w_qkv"], "b_w_out": b["b_w_out"],
        })
    return maps


def run_F(inp, trace=False, ncores=NCORES):
    nc = build_F()
    maps = prep_F(inp)[:ncores]
    res = run_bass_kernel_spmd(nc, maps, core_ids=list(range(ncores)), trace=trace)
    out = np.zeros((4, 4096, D), dtype=np.float32)
    for core in range(ncores):
        b, half = core // 2, core % 2
        out[b, half * T:(half + 1) * T] = res.results[core]["y"]
    return out, res


def kernel(**inputs):
    inp = {k: np.asarray(v) for k, v in inputs.items()}
    out, _ = run_F(inp)
    return out
```

```python
import contextlib
import numpy as np
import concourse.bass as bass
import concourse.mybir as mybir
from concourse.bass_utils import run_bass_kernel_spmd

F32 = mybir.dt.float32
BF16 = mybir.dt.bfloat16
AF = mybir.ActivationFunctionType
ALU = mybir.AluOpType
AX = mybir.AxisListType

D = 1024
T = 2048
NT = 4
TT = 512
NCORES = 8
ALPHA = 4.0 ** 0.25
INV_ALPHA = 1.0 / ALPHA
LN_EPS = 1e-5
EPS_P = LN_EPS / (ALPHA * ALPHA)
GELU = AF.Gelu_apprx_tanh


class Eng:
    def __init__(self, name, sem):
        self.name = name
        self.sem = sem
        self.count = 0
        self.ops = []
        self.waited = {}


class Prog:
    def __init__(self, nc, es, nsem=100):
        self.nc = nc
        self.free_sems = [es.enter_context(nc.semaphore(f"s{i}")) for i in range(nsem)]
        self.E = {}
        for n in ["pe", "act", "dve", "pool", "sp"]:
            self.E[n] = Eng(n, self.free_sems.pop())
        self.last_w = {}
        self.readers = {}
        self.sem_id = {}

    def new_sem(self):
        return self.free_sems.pop()

    def slot(self):
        sl = {"sem": self.new_sem(), "count": 0}
        self.slots = getattr(self, "slots", [])
        self.slots.append(sl)
        return sl

    def barrier(self, extra=()):
        toks = [(e.sem, e.count) for e in self.E.values() if e.count > 0]
        toks += [(sl["sem"], sl["count"]) for sl in getattr(self, "slots", []) if sl["count"] > 0]
        toks += list(extra)
        for eng in self.E:
            self.wait_all(eng, toks)

    def _sid(self, s):
        return id(s)

    @staticmethod
    def _norm(reads, writes):
        r2, w2 = [], list(writes)
        for k in reads:
            if isinstance(k, tuple) and k and k[0] == "ps":
                if k not in w2:
                    w2.append(k)
            else:
                r2.append(k)
        return r2, w2

    def _deps(self, eng, reads, writes, extra):
        toks = []
        for r in reads:
            t = self.last_w.get(r)
            if t is not None:
                toks.append(t)
        for w in writes:
            t = self.last_w.get(w)
            if t is not None:
                toks.append(t)
            for t in self.readers.get(w, {}).values():
                toks.append(t)
        for t in extra:
            if t is not None:
                toks.append(t)
        e = self.E[eng]
        waits = []
        for (s, v) in toks:
            if eng == "pe" and s is e.sem:
                continue
            k = self._sid(s)
            if e.waited.get(k, 0) < v:
                e.waited[k] = v
                waits.append((s, v))
        return waits

    def _record(self, tok, reads, writes):
        for r in reads:
            d = self.readers.setdefault(r, {})
            k = self._sid(tok[0])
            if k not in d or d[k][1] < tok[1]:
                d[k] = tok
        for w in writes:
            self.last_w[w] = tok
            self.readers[w] = {}

    def op(self, eng, fn, reads=(), writes=(), extra=()):
        reads, writes = self._norm(reads, writes)
        e = self.E[eng]
        waits = self._deps(eng, reads, writes, extra)
        e.count += 1
        tok = (e.sem, e.count)
        e.ops.append((waits, fn, (e.sem, 1)))
        self._record(tok, reads, writes)
        return tok

    def mm_group(self, fns, reads=(), writes=(), extra=()):
        reads, writes = self._norm(reads, writes)
        e = self.E["pe"]
        waits = self._deps("pe", reads, writes, extra)
        for i, fn in enumerate(fns):
            last = i == len(fns) - 1
            e.ops.append((waits if i == 0 else [], fn, (e.sem, 1) if last else None))
        e.count += 1
        tok = (e.sem, e.count)
        self._record(tok, reads, writes)
        return tok

    def dma(self, eng, pairs, slot, reads=(), writes=(), extra=()):
        e = self.E[eng]
        waits = self._deps(eng, reads, writes, extra)
        for i, (o, i_) in enumerate(pairs):
            e.ops.append((waits if i == 0 else [],
                          (lambda h, o=o, i_=i_: h.dma_start(out=o, in_=i_)),
                          (slot["sem"], 16)))
            slot["count"] += 16
        tok = (slot["sem"], slot["count"])
        self._record(tok, reads, writes)
        return tok

    def wait_all(self, eng, toks):
        e = self.E[eng]
        waits = self._deps(eng, (), (), toks)
        e.ops.append((waits, None, None))

    def replay(self, eng, h):
        for waits, fn, inc in self.E[eng].ops:
            for s, v in waits:
                h.wait_ge(s, v)
            if fn is None:
                continue
            inst = fn(h)
            if inc is not None:
                inst.then_inc(inc[0], inc[1])

    def finish(self):
        nc = self.nc
        with nc.Block() as block:
            @block.tensor
            def _(h):
                self.replay("pe", h)

            @block.scalar
            def _(h):
                self.replay("act", h)

            @block.vector
            def _(h):
                self.replay("dve", h)

            @block.gpsimd
            def _(h):
                self.replay("pool", h)

            @block.sync
            def _(h):
                self.replay("sp", h)


class Arena:
    def __init__(self, nc, base=16512, cap=229376):
        self.nc = nc
        self.cap = cap
        self.off = base
        self.n = 0

    def alloc(self, shape, dtype, at=None, name=None):
        nbytes = int(np.prod(shape[1:])) * (4 if dtype == F32 else 2)
        nbytes = (nbytes + 63) // 64 * 64
        if at is None:
            at = self.off
            self.off += nbytes
            assert self.off <= self.cap, f"SBUF overflow {self.off}"
        else:
            assert at + nbytes <= self.cap, "SBUF overflow (at)"
        self.n += 1
        t = self.nc.alloc_sbuf_tensor_at(name or f"sb{self.n}", list(shape), dtype, offset=at, align_bytes=64)
        return t, at, nbytes


class Psum:
    def __init__(self, nc, P):
        self.banks = [nc.alloc_psum_tensor(f"psb{i}", [128, 512], F32) for i in range(8)]
        self.i = 0

    def next(self):
        b = self.i
        self.i = (self.i + 1) % 7
        return b, self.banks[b]


def col(v):
    v = np.asarray(v, dtype=np.float32)
    return np.ascontiguousarray(v.reshape(-1, 128).T)


class Ctx:
    pass


class AdaStepper:
    NP = 24

    def __init__(self, P, cx, wada_ap, bias_cols, out_mod, tag):
        self.P, self.cx = P, cx
        self.wv = wada_ap.rearrange("(kc p) n -> p kc n", p=128)
        self.bias_cols, self.out_mod, self.tag = bias_cols, out_mod, tag
        self.k = 0
        self.base = cx.ada_i
        cx.ada_i += self.NP
        self._dma(0)

    def _dma(self, pc):
        bi = (self.base + pc) % 2
        self.P.dma("pool", [(self.cx.ada_buf[bi][:], self.wv[:, :, pc * 128:(pc + 1) * 128])], self.cx.ada_slot[bi],
                   writes=[("adabuf", bi)])

    def tick(self):
        if self.k >= self.NP:
            return False
        P, cx, pc = self.P, self.cx, self.k
        if pc + 1 < self.NP:
            self._dma(pc + 1)
        bi = (self.base + pc) % 2
        buf = cx.ada_buf[bi]
        ps = cx.ps_ada
        fns = [(lambda h, kc=kc, pc=pc, buf=buf: h.matmul(
            ps[:, pc:pc + 1], buf[:, kc, :], cx.sc[:, kc:kc + 1], start=(kc == 0), stop=(kc == 7)))
            for kc in range(8)]
        P.mm_group(fns, reads=[("adabuf", bi), "sc"], writes=[("ps_ada", pc)])
        self.k += 1
        if self.k == self.NP:
            self._finish()
        return True

    def flush(self):
        while self.tick():
            pass

    def _finish(self):
        P, ps, out_mod, tag = self.P, self.cx.ps_ada, self.out_mod, self.tag
        rd = [("ps_ada", f) for f in range(24)]
        P.op("dve", lambda h: h.tensor_tensor(out_mod[:, :], ps[:, 0:24], self.bias_cols, ALU.add),
             reads=rd + ["cols"], writes=[("mod", tag)])
        P.op("dve", lambda h: h.tensor_scalar(out_mod[:, 8:16], out_mod[:, 8:16], 1.0, None, ALU.add),
             reads=[("mod", tag)], writes=[("mod", tag)])
        P.op("dve", lambda h: h.tensor_scalar(out_mod[:, 16:24], out_mod[:, 16:24], 1.0, INV_ALPHA, ALU.add, ALU.mult),
             reads=[("mod", tag)], writes=[("mod", tag)])


def ada_tick(cx):
    st_ = getattr(cx, "ada_bg", None)
    if st_ is not None:
        st_.tick()


def ada_flush(cx):
    st_ = getattr(cx, "ada_bg", None)
    if st_ is not None:
        st_.flush()
        cx.ada_bg = None


def emit_modulate(P, cx, xT, hT, mod, tag, tt, eng="dve"):
    sl = slice(tt * TT, (tt + 1) * TT)
    for c in range(8):
        if eng == "act":
            P.op("act", lambda h, c=c: h.activation(hT[:, c, sl], xT[:, c, sl], AF.Identity,
                                                    bias=mod[:, c:c + 1], scale=mod[:, 8 + c:9 + c]),
                 reads=[("x", c, tt), ("mod", tag)], writes=[("h", c, tt)])
        else:
            P.op(eng, lambda h, c=c: h.tensor_scalar(hT[:, c, sl], xT[:, c, sl], mod[:, 8 + c:9 + c],
                                                     mod[:, c:c + 1], ALU.mult, ALU.add),
                 reads=[("x", c, tt), ("mod", tag)], writes=[("h", c, tt)])


def emit_ln_p1(P, cx, xT, tt):
    sl = slice(tt * TT, (tt + 1) * TT)
    s1, s2, sq = cx.ln_s1, cx.ln_s2, cx.ln_sq
    P.op("pool", lambda h: h.tensor_tensor(s1[:], xT[:, 0, sl], xT[:, 1, sl], ALU.add),
         reads=[("x", 0, tt), ("x", 1, tt)], writes=["ln_s1"])
    for c in range(2, 8):
        P.op("pool", lambda h, c=c: h.tensor_tensor(s1[:], s1[:], xT[:, c, sl], ALU.add),
             reads=[("x", c, tt), "ln_s1"], writes=["ln_s1"])
    for c in range(8):
        if c == 0:
            P.op("act", lambda h: h.activation(s2[:], xT[:, 0, sl], AF.Square),
                 reads=[("x", 0, tt)], writes=["ln_s2"])
        else:
            k = c % 2
            P.op("act", lambda h, c=c, k=k: h.activation(sq[k][:], xT[:, c, sl], AF.Square),
                 reads=[("x", c, tt)], writes=[("ln_sq", k)])
            P.op("dve", lambda h, k=k: h.tensor_tensor(s2[:], s2[:], sq[k][:], ALU.add),
                 reads=[("ln_sq", k), "ln_s2"], writes=["ln_s2"])


def emit_ln_p2(P, cx, xT, tt, g_cols, b_cols):
    sl = slice(tt * TT, (tt + 1) * TT)
    s1, s2, sq = cx.ln_s1, cx.ln_s2, cx.ln_sq
    b1, ps1 = cx.psum.next()
    P.mm_group([lambda h: h.matmul(ps1[:], cx.onesm[:], s1[:], start=True, stop=True)],
               reads=["ln_s1", "onesm"], writes=[("ps", b1)])
    b2, ps2 = cx.psum.next()
    P.mm_group([lambda h: h.matmul(ps2[:], cx.onesm[:], s2[:], start=True, stop=True)],
               reads=["ln_s2", "onesm"], writes=[("ps", b2)])
    mean, msq, rstd = cx.ln_mean, cx.ln_tmp[1], cx.ln_rstd
    P.op("act", lambda h: h.activation(mean[:], ps1[:], AF.Identity), reads=[("ps", b1)], writes=["ln_mean"])
    P.op("act", lambda h: h.activation(msq[:], ps1[:], AF.Square), reads=[("ps", b1)], writes=[("ln_tmp", 1)])
    P.op("dve", lambda h: h.tensor_tensor(rstd[:], ps2[:], msq[:], ALU.subtract),
         reads=[("ps", b2), ("ln_tmp", 1)], writes=["ln_rstd"])
    P.op("dve", lambda h: h.tensor_scalar(rstd[:], rstd[:], EPS_P, None, ALU.add),
         reads=["ln_rstd"], writes=["ln_rstd"])
    P.op("dve", lambda h: h.reciprocal(rstd[:], rstd[:]), reads=["ln_rstd"], writes=["ln_rstd"])
    P.op("act", lambda h: h.activation(rstd[:], rstd[:], AF.Sqrt), reads=["ln_rstd"], writes=["ln_rstd"])
    for c in range(8):
        k = c % 2
        tmp = cx.ln_tmp[k]
        P.op("dve", lambda h, c=c, tmp=tmp: h.tensor_tensor(tmp[:], xT[:, c, sl], mean[:], ALU.subtract),
             reads=[("x", c, tt), "ln_mean"], writes=[("ln_tmp", k)])
        P.op("pool", lambda h, tmp=tmp: h.tensor_tensor(tmp[:], tmp[:], rstd[:], ALU.mult),
             reads=["ln_rstd", ("ln_tmp", k)], writes=[("ln_tmp", k)])
        P.op("act", lambda h, c=c, tmp=tmp: h.activation(xT[:, c, sl], tmp[:], AF.Identity,
                                                         bias=b_cols[:, c:c + 1], scale=g_cols[:, c:c + 1]),
             reads=[("ln_tmp", k), "cols"], writes=[("x", c, tt)])


def emit_ln(P, cx, xT, tt, g_cols, b_cols):
    emit_ln_p1(P, cx, xT, tt)
    emit_ln_p2(P, cx, xT, tt, g_cols, b_cols)


def emit_mlp(P, cx, xT, hT, mod, tag, w_up_ap, w_down_ap, g_cols, b_cols, after_ln=None):
    wu_v = w_up_ap.rearrange("(kc p) n -> p kc n", p=128)
    wd_v = w_down_ap.rearrange("(fc p) n -> p fc n", p=128)
    for q in range(4):
        bi = cx.mlp_i % 2
        cx.mlp_i += 1
        wu, wd = cx.wu_buf[bi], cx.wd_buf[bi]
        P.dma("pool", [(wu[:, 0:4, :], wu_v[:, 0:4, q * 1024:(q + 1) * 1024]),
                       (wu[:, 4:8, :], wu_v[:, 4:8, q * 1024:(q + 1) * 1024])],
              cx.wu_slot[bi], writes=[("wu", bi)])
        P.dma("pool", [(wd[:, 0:4, :], wd_v[:, q * 8:q * 8 + 4, :]),
                       (wd[:, 4:8, :], wd_v[:, q * 8 + 4:q * 8 + 8, :])],
              cx.wd_slot[bi], writes=[("wd", bi)])
        def mlp_tile(q, tt, bi, wu, wd):
            sl = slice(tt * TT, (tt + 1) * TT)
            ai = cx.a_i % 2
            cx.a_i += 1
            aT = cx.a_buf[ai]
            for f in range(8):
                b, ps = cx.psum.next()
                fns = [(lambda h, kc=kc, f=f, ps=ps, wu=wu: h.matmul(
                    ps[:], wu[:, kc, f * 128:(f + 1) * 128], hT[:, kc, sl], start=(kc == 0), stop=(kc == 7)))
                    for kc in range(8)]
                P.mm_group(fns, reads=[("wu", bi)] + [("h", kc, tt) for kc in range(8)], writes=[("ps", b)])
                if f % 4 == 0:
                    ada_tick(cx)
                k = cx.r_i % 2
                cx.r_i += 1
                rt = cx.relu_tmp[k]
                P.op("act", lambda h, ps=ps, rt=rt: h.activation(rt[:], ps[:], AF.Relu),
                     reads=[("ps", b)], writes=[("ln_sq", k)])
                P.op("pool", lambda h, rt=rt, aT=aT, f=f: h.tensor_tensor(aT[:, f, :], rt[:], rt[:], ALU.mult),
                     reads=[("ln_sq", k)], writes=[("a", ai, f)])
            for oc in range(8):
                b, ps = cx.psum.next()
                fns = [(lambda h, f=f, oc=oc, ps=ps, wd=wd, aT=aT: h.matmul(
                    ps[:], wd[:, f, oc * 128:(oc + 1) * 128], aT[:, f, :], start=(f == 0), stop=(f == 7)))
                    for f in range(8)]
                P.mm_group(fns, reads=[("wd", bi)] + [("a", ai, f) for f in range(8)], writes=[("ps", b)])
                P.op("dve", lambda h, ps=ps, oc=oc: h.scalar_tensor_tensor(
                    xT[:, oc, sl], ps[:], mod[:, 16 + oc:17 + oc], xT[:, oc, sl], ALU.mult, ALU.add),
                    reads=[("ps", b), ("mod", tag), ("x", oc, tt)], writes=[("x", oc, tt)])
        for tt in range(NT):
            mlp_tile(q, tt, bi, wu, wd)
            if q == 3 and tt >= 1:
                emit_ln_p2(P, cx, xT, tt - 1, g_cols, b_cols)
                if after_ln is not None:
                    after_ln(tt - 1)
            if q == 3:
                emit_ln_p1(P, cx, xT, tt)
        if q == 3:
            emit_ln_p2(P, cx, xT, NT - 1, g_cols, b_cols)
            if after_ln is not None:
                after_ln(NT - 1)


def emit_load_xT(P, cx, x_ap, xT, key="x"):
    xv = x_ap.rearrange("(tb p) f -> p tb f", p=128)
    for tt in range(NT):
        si = cx.stg_i % 2
        cx.stg_i += 1
        stg = cx.stg[si]
        P.dma("sp", [(stg[:, 0:2, :], xv[:, tt * 4:tt * 4 + 2, :]),
                     (stg[:, 2:4, :], xv[:, tt * 4 + 2:tt * 4 + 4, :])], cx.stg_slot[si],
              writes=[("stg", si)])
        for c in range(8):
            b, ps = cx.psum.next()
            fns = [(lambda h, tb=tb, c=c, ps=ps, stg=stg: h.matmul(
                ps[:, tb * 128:(tb + 1) * 128], stg[:, tb, c * 128:(c + 1) * 128], cx.ident[:],
                start=True, stop=True)) for tb in range(4)]
            P.mm_group(fns, reads=[("stg", si), "ident"], writes=[("ps", b)])
            ada_tick(cx)
            eng = "act" if c % 2 == 0 else "dve"
            sl = slice(tt * TT, (tt + 1) * TT)
            if eng == "act":
                P.op("act", lambda h, ps=ps, c=c, sl=sl: h.activation(xT[:, c, sl], ps[:], AF.Identity),
                     reads=[("ps", b)], writes=[(key, c, tt)])
            else:
                P.op("dve", lambda h, ps=ps, c=c, sl=sl: h.tensor_copy(xT[:, c, sl], ps[:]),
                     reads=[("ps", b)], writes=[(key, c, tt)])


def emit_store_x(P, cx, xT, out_ap, tt):
    ov = out_ap.rearrange("(tb p) f -> p tb f", p=128)
    for tb in range(4):
        si = cx.ost_i % 2
        cx.ost_i += 1
        stg = cx.ost[si]
        tsl = slice(tt * TT + tb * 128, tt * TT + (tb + 1) * 128)
        for half in range(2):
            b, ps = cx.psum.next()
            fns = [(lambda h, cc=cc, half=half, ps=ps, tsl=tsl: h.matmul(
                ps[:, cc * 128:(cc + 1) * 128], xT[:, half * 4 + cc, tsl], cx.ident[:],
                start=True, stop=True)) for cc in range(4)]
            P.mm_group(fns, reads=[("x", half * 4 + cc, tt) for cc in range(4)] + ["ident"],
                       writes=[("ps", b)])
            if half == 0:
                P.op("act", lambda h, ps=ps, stg=stg: h.activation(stg[:, 0:512], ps[:], AF.Identity),
                     reads=[("ps", b)], writes=[("ost", si, 0)])
            else:
                P.op("dve", lambda h, ps=ps, stg=stg: h.tensor_copy(stg[:, 512:1024], ps[:]),
                     reads=[("ps", b)], writes=[("ost", si, 1)])
        tok = P.dma("sp", [(ov[:, tt * 4 + tb, :], stg[:])], cx.out_slot[si],
                    reads=[("ost", si, 0), ("ost", si, 1)])
        cx.out_toks.append(tok)


def common_setup(nc, P, A, cx, ncols):
    cx.psum = Psum(nc, P)
    cx.cols, _, _ = A.alloc([128, ncols], F32)
    cx.ident_f, _, _ = A.alloc([128, 128], F32)
    cx.ident = cx.ident_f
    cx.onesm, _, _ = A.alloc([128, 128], F32)
    cx.sc, _, _ = A.alloc([128, 8], BF16)
    cx.mod = [A.alloc([128, 24], F32)[0] for _ in range(2)]
    cx.ln_s1, _, _ = A.alloc([128, TT], F32)
    cx.ln_s2, _, _ = A.alloc([128, TT], F32)
    cx.ln_sq = [A.alloc([128, TT], F32)[0] for _ in range(2)]
    cx.ln_mean, _, _ = A.alloc([128, TT], F32)
    cx.ln_rstd, _, _ = A.alloc([128, TT], F32)
    cx.ln_tmp = [A.alloc([128, TT], F32)[0] for _ in range(2)]
    cx.relu_tmp = cx.ln_sq
    cx.ada_buf = [A.alloc([128, 8, 128], BF16)[0] for _ in range(2)]
    cx.ada_slot = [P.slot() for _ in range(2)]
    cx.ada_i = 0
    cx.mlp_i = 0
    cx.a_i = 0
    cx.r_i = 0
    cx.stg_i = 0
    cx.ost_i = 0
    cx.out_toks = []
    cx.out_slot = [P.slot() for _ in range(2)]
    cx.const_slot = P.slot()
    cx.stg_slot = [P.slot() for _ in range(2)]
    cx.wu_slot = [P.slot() for _ in range(2)]
    cx.wd_slot = [P.slot() for _ in range(2)]
    cx.ps_ada = cx.psum.banks[7]


def fence_tokens(P, keys):
    toks = []
    for k in keys:
        t = P.last_w.get(k)
        if t is not None:
            toks.append(t)
        toks.extend(P.readers.get(k, {}).values())
    return toks


A_COLS = dict(c=0, adab0=8, adab1=32, lng0=56, lnb0=64, lng1=72, lnb1=80, binu=88)
A_NCOLS = 96


def body_A(nc, P, A, cx, io, xT, stage=None, fused=False):
    x_in, cols_in, rows_in, bb_in, consts_in = io["x"], io["cols"], io["rows"], io["bbias"], io["constsA"]
    ada_w0, ada_w1, w_in, w_s, w_out = io["ada_w00"], io["ada_w01"], io["a_w_in"], io["a_w_s"], io["a_w_out"]
    w_up, w_down, y_out = io["mlp_w_up0"], io["mlp_w_down0"], io["y"]
    mark = A.off
    win, _, _ = A.alloc([128, 8, 2048], BF16)
    wout, _, _ = A.alloc([128, 8, 1024], BF16)
    rows, _, _ = A.alloc([128, 3, 1024], F32)
    bbias, _, _ = A.alloc([128, 8, 128], F32)
    wcT, _, _ = A.alloc([128, 16, 128], BF16)
    vst, _, _ = A.alloc([128, 2, 6], F32)
    vmv, _, _ = A.alloc([128, 2], F32)
    vrs, _, _ = A.alloc([128, 1], F32)
    mark1b = A.off
    hT1 = [A.alloc([128, 8, TT], BF16)[0] for _ in range(2)]
    uT1, _, _ = A.alloc([128, 8, TT], BF16)
    uT = [uT1, uT1]
    vtok1, _, _ = A.alloc([128, 4, 1024], BF16)
    vtok = [vtok1, vtok1]
    vg = [A.alloc([128, 1024], F32)[0] for _ in range(2)]
    gt = [A.alloc([128, TT], F32)[0] for _ in range(2)]
    end1 = A.off
    A.off = mark1b
    cx.stg = [A.alloc([128, 4, 1024], F32)[0] for _ in range(2)]
    tril, _, _ = A.alloc([128, 128], F32)
    identb, _, _ = A.alloc([128, 128], BF16)
    ws_f, _, _ = A.alloc([128, 16, 128], F32)
    ws_b, _, _ = A.alloc([128, 16, 128], BF16)
    end1 = max(end1, A.off)
    A.off = mark
    h2T, _, _ = A.alloc([128, 8, T], BF16)
    cx.wu_buf = [A.alloc([128, 8, 1024], BF16)[0] for _ in range(2)]
    cx.wd_buf = [A.alloc([128, 8, 1024], BF16)[0] for _ in range(2)]
    cx.a_buf = [A.alloc([128, 8, TT], BF16)[0] for _ in range(2)]
    ost_at = A.off
    cx.ost = [A.alloc([128, 1024], F32)[0] for _ in range(2)]
    end2 = A.off
    hst = A.alloc([128, 8, TT], BF16, at=ost_at)[0] if fused else None
    A.off = max(end1, end2)
    print("SBUF plan A: persistent", mark, "phase1 end", end1, "phase2 end", end2)

    P.dma("sp", [(cx.cols[:], cols_in)], P.slot(), writes=["cols"])
    P.dma("sp", [(cx.ident_f[:], consts_in[:, 0, :]), (cx.onesm[:], consts_in[:, 1, :]),
                 (tril[:], consts_in[:, 2, :])], P.slot(), writes=["ident", "onesm", "tril"])
    P.dma("sp", [(rows[:], rows_in)], P.slot(), writes=["rows"])
    P.dma("sp", [(bbias[:], bb_in)], P.slot(), writes=["bbias"])
    P.dma("sp", [(ws_f[:], w_s.rearrange("g t s -> t g s"))], P.slot(), writes=["ws_f"])
    P.op("act", lambda h: h.activation(cx.sc[:], cx.cols[:, 0:8], AF.Silu), reads=["cols"], writes=["sc"])
    cx.ada_bg = AdaStepper(P, cx, ada_w0, cx.cols[:, 8:32], cx.mod[0], 0)
    wv = w_in.rearrange("(kc p) n -> p kc n", p=128)
    for j in range(4):
        P.dma("pool", [(win[:, :, j * 512:(j + 1) * 512], wv[:, :, j * 512:(j + 1) * 512])], P.slot(),
              writes=[("win", j)])
    wout_slot = P.slot()
    P.dma("pool", [(wout[:], w_out.rearrange("(kc p) n -> p kc n", p=128))], wout_slot, writes=["wout"])
    P.op("dve", lambda h: h.tensor_copy(identb[:], cx.ident_f[:]), reads=["ident"], writes=["identb"])
    P.op("dve", lambda h: h.tensor_tensor(ws_b[:], ws_f[:], tril[:, None, :].to_broadcast([128, 16, 128]), ALU.mult),
         reads=["ws_f", "tril"], writes=["ws_b"])
    for g4 in range(4):
        b, ps = cx.psum.next()
        fns = [(lambda h, gi=gi, g4=g4, ps=ps: h.matmul(
            ps[:, gi * 128:(gi + 1) * 128], ws_b[:, g4 * 4 + gi, :], identb[:], start=True, stop=True))
            for gi in range(4)]
        P.mm_group(fns, reads=["ws_b", "identb"], writes=[("ps", b)])
        P.op("dve", lambda h, g4=g4, ps=ps: h.tensor_copy(
            wcT[:, g4 * 4:(g4 + 1) * 4, :], ps[:].rearrange("p (g t) -> p g t", g=4)),
            reads=[("ps", b)], writes=[("wcT", g4)])
    emit_load_xT(P, cx, x_in, xT)
    if stage == "load":
        for tt in range(NT):
            emit_store_x(P, cx, xT, y_out, tt)
        return
    ada_flush(cx)
    cx.ada_bg = AdaStepper(P, cx, ada_w1, cx.cols[:, 32:56], cx.mod[1], 1)
    f1 = fence_tokens(P, [("stg", 0), ("stg", 1), "tril", "identb", "ws_f", "ws_b"])
    for e_ in ("dve", "act", "pool"):
        P.wait_all(e_, f1)

    def mix_tile(tt):
        sl = slice(tt * TT, (tt + 1) * TT)
        pi = tt % 2
        hT = hT1[pi]
        for c in range(8):
            P.op("dve", lambda h, c=c, hT=hT: h.tensor_scalar(
                hT[:, c, :], xT[:, c, sl], cx.mod[0][:, 8 + c:9 + c], cx.mod[0][:, c:c + 1], ALU.mult, ALU.add),
                reads=[("x", c, tt), ("mod", 0)], writes=[("h1", pi, c)])
        for tb in range(4):
            vi = (tt * 4 + tb) % 2
            vgb = vg[vi]
            for half in range(2):
                b, ps = cx.psum.next()
                fns = [(lambda h, kc=kc, half=half, ps=ps, hT=hT, tb=tb: h.matmul(
                    ps[:], hT[:, kc, tb * 128:(tb + 1) * 128],
                    win[:, kc, 1024 + half * 512:1024 + (half + 1) * 512], start=(kc == 0), stop=(kc == 7)))
                    for kc in range(8)]
                P.mm_group(fns, reads=[("win", 2 + half)] + [("h1", pi, kc) for kc in range(8)],
                           writes=[("ps", b)])
                if half == 0:
                    ada_tick(cx)
                P.op("dve", lambda h, ps=ps, half=half, vgb=vgb: h.tensor_tensor(
                    vgb[:, half * 512:(half + 1) * 512], ps[:], rows[:, 0, half * 512:(half + 1) * 512], ALU.add),
                    reads=[("ps", b), "rows"], writes=[("vg", vi, half)])
                P.op("act", lambda h, half=half, vgb=vgb: h.activation(
                    vgb[:, half * 512:(half + 1) * 512], vgb[:, half * 512:(half + 1) * 512], GELU),
                    reads=[("vg", vi, half)], writes=[("vg", vi, half)])
                P.op("dve", lambda h, half=half, vgb=vgb: h.bn_stats(
                    vst[:, half, :], vgb[:, half * 512:(half + 1) * 512]),
                    reads=[("vg", vi, half)], writes=[("vst", half)])
            P.op("dve", lambda h: h.bn_aggr(vmv[:], vst[:].rearrange("p a b -> p (a b)")),
                 reads=[("vst", 0), ("vst", 1)], writes=["vmv"])
            P.op("dve", lambda h: h.tensor_scalar(vrs[:], vmv[:, 1:2], LN_EPS, None, ALU.add),
                 reads=["vmv"], writes=["vrs"])
            P.op("dve", lambda h: h.reciprocal(vrs[:], vrs[:]), reads=["vrs"], writes=["vrs"])
            P.op("act", lambda h: h.activation(vrs[:], vrs[:], AF.Sqrt), reads=["vrs"], writes=["vrs"])
            P.op("dve", lambda h, vgb=vgb: h.tensor_scalar(
                vgb[:], vgb[:], vmv[:, 0:1], vrs[:, 0:1], ALU.subtract, ALU.mult),
                reads=[("vg", vi, 0), ("vg", vi, 1), "vmv", "vrs"], writes=[("vg", vi, 0), ("vg", vi, 1)])
            P.op("pool", lambda h, vgb=vgb: h.tensor_tensor(vgb[:], vgb[:], rows[:, 1, :], ALU.mult),
                 reads=[("vg", vi, 0), ("vg", vi, 1), "rows"], writes=[("vg", vi, 0), ("vg", vi, 1)])
            P.op("pool", lambda h, vgb=vgb, tb=tb: h.tensor_tensor(vtok[pi][:, tb, :], vgb[:], rows[:, 2, :], ALU.add),
                 reads=[("vg", vi, 0), ("vg", vi, 1), "rows"], writes=[("vtok", 0, tb)])
        for fc in range(8):
            b, ps = cx.psum.next()
            fns = [(lambda h, kc=kc, fc=fc, ps=ps, hT=hT: h.matmul(
                ps[:], win[:, kc, fc * 128:(fc + 1) * 128], hT[:, kc, :], start=(kc == 0), stop=(kc == 7)))
                for kc in range(8)]
            P.mm_group(fns, reads=[("win", fc // 4)] + [("h1", pi, kc) for kc in range(8)], writes=[("ps", b)])
            if fc % 2 == 0:
                ada_tick(cx)
            P.op("act", lambda h, fc=fc, ps=ps: h.activation(
                uT[pi][:, fc, :], ps[:], GELU, bias=cx.cols[:, A_COLS["binu"] + fc:A_COLS["binu"] + fc + 1]),
                reads=[("ps", b), "cols"], writes=[("u", 0, fc)])
        if tt >= 1:
            emit_ln_p2(P, cx, xT, tt - 1, cx.cols[:, A_COLS["lng0"]:A_COLS["lng0"] + 8],
                       cx.cols[:, A_COLS["lnb0"]:A_COLS["lnb0"] + 8])
        for fc in range(8):
            b, ps = cx.psum.next()
            fns = []
            for tb in range(4):
                for gi in range(2):
                    g = 2 * fc + gi
                    fns.append(lambda h, tb=tb, gi=gi, g=g, ps=ps: h.matmul(
                        ps[gi * 64:(gi + 1) * 64, tb * 128:(tb + 1) * 128],
                        vtok[pi][:, tb, g * 64:(g + 1) * 64], wcT[:, g, :], start=True, stop=True))
            P.mm_group(fns, reads=[("vtok", 0, tb) for tb in range(4)] + [("wcT", (2 * fc) // 4)],
                       writes=[("ps", b)])
            k = fc % 2
            P.op("dve", lambda h, ps=ps, fc=fc, k=k: h.tensor_tensor(
                gt[k][:].rearrange("p (a t) -> p a t", a=4), ps[:].rearrange("p (a t) -> p a t", a=4),
                bbias[:, fc, None, :].to_broadcast([128, 4, 128]), ALU.add),
                reads=[("ps", b), "bbias"], writes=[("gt", k)])
            P.op("pool", lambda h, fc=fc, k=k: h.tensor_tensor(uT[pi][:, fc, :], gt[k][:], uT[pi][:, fc, :], ALU.mult),
                 reads=[("gt", k), ("u", 0, fc)], writes=[("u", 0, fc)])
        for oc in range(8):
            b, ps = cx.psum.next()
            fns = [(lambda h, fc=fc, oc=oc, ps=ps: h.matmul(
                ps[:], wout[:, fc, oc * 128:(oc + 1) * 128], uT[pi][:, fc, :], start=(fc == 0), stop=(fc == 7)))
                for fc in range(8)]
            P.mm_group(fns, reads=["wout"] + [("u", 0, fc) for fc in range(8)], writes=[("ps", b)])
            P.op("dve", lambda h, ps=ps, oc=oc: h.scalar_tensor_tensor(
                xT[:, oc, sl], ps[:], cx.mod[0][:, 16 + oc:17 + oc], xT[:, oc, sl], ALU.mult, ALU.add),
                reads=[("ps", b), ("mod", 0), ("x", oc, tt)], writes=[("x", oc, tt)])
        emit_ln_p1(P, cx, xT, tt)

    for tt in range(NT):
        mix_tile(tt)
    emit_ln_p2(P, cx, xT, NT - 1, cx.cols[:, A_COLS["lng0"]:A_COLS["lng0"] + 8],
               cx.cols[:, A_COLS["lnb0"]:A_COLS["lnb0"] + 8])

    if stage == "mixA":
        dbg = nc.dram_tensor("dbg", [128, 16384], F32, kind="ExternalOutput").ap()
        dsl = P.slot()
        allk = list(P.last_w.keys())
        cx.out_toks.append(P.dma("sp", [(dbg[:, 0:24], cx.mod[0][:])], dsl, reads=allk))
        cx.out_toks.append(P.dma("pool", [(dbg[:, 1024:3072], wcT[:].rearrange("p g t -> p (g t)"))], dsl, reads=allk))
        cx.out_toks.append(P.dma("pool", [(dbg[:, 4096:8192], hT1[1][:].rearrange("p g t -> p (g t)"))], dsl, reads=allk))
        cx.out_toks.append(P.dma("pool", [(dbg[:, 8192:12288], uT1[:].rearrange("p g t -> p (g t)"))], dsl, reads=allk))
        cx.out_toks.append(P.dma("pool", [(dbg[:, 12288:16384], vtok1[:].rearrange("p g t -> p (g t)"))], dsl, reads=allk))
        for e_ in ("pe", "act", "dve", "pool"):
            P.wait_all(e_, list(cx.out_toks))
        for tt in range(NT):
            emit_store_x(P, cx, xT, y_out, tt)
        return
    ada_flush(cx)
    if fused:
        cx.ada_bg = AdaStepper(P, cx, io["ada_w10"], cx.cols[:, A_NCOLS + 8:A_NCOLS + 32], cx.mod[0], 0)
        cx.pre_ada_b0 = True
    p1_keys = [("win", j) for j in range(4)] + ["wout", "rows", "bbias", "vst", "vmv", "vrs"] + \
              [("wcT", g4) for g4 in range(4)] + [("h1", p, c) for p in range(2) for c in range(8)] + \
              [("u", 0, c) for c in range(8)] + [("vtok", 0, tb) for tb in range(4)] + \
              [("vg", i, hf) for i in range(2) for hf in range(2)] + [("gt", 0), ("gt", 1)]
    f2 = fence_tokens(P, p1_keys)
    for e_ in ("dve", "act", "pool"):
        P.wait_all(e_, f2)
    for tt in range(NT):
        emit_modulate(P, cx, xT, h2T, cx.mod[1], 1, tt)

    def after_ln(tt):
        emit_store_x(P, cx, xT, y_out, tt)

    if fused:
        cx.cc_sem = P.new_sem()
        snd_slot = P.slot()

    def send_tile(tt):
        ada_flush(cx)
        modb = cx.mod[0]
        sl = slice(tt * TT, (tt + 1) * TT)
        for c in range(8):
            if c % 2:
                P.op("act", lambda h, c=c: h.activation(hst[:, c, :], xT[:, c, sl], AF.Identity,
                                                        bias=modb[:, c:c + 1], scale=modb[:, 8 + c:9 + c]),
                     reads=[("x", c, tt), ("mod", 0)], writes=[("hst", c)])
            else:
                P.op("dve", lambda h, c=c: h.tensor_scalar(hst[:, c, :], xT[:, c, sl], modb[:, 8 + c:9 + c],
                                                           modb[:, c:c + 1], ALU.mult, ALU.add),
                     reads=[("x", c, tt), ("mod", 0)], writes=[("hst", c)])
        snd, rcv = io["snd"], io["rcv"]
        t_snd = P.dma("sp", [(snd[tt * 2 + hf], hst[:, hf * 4:(hf + 1) * 4, :].rearrange("p c t -> p (c t)"))
                             for hf in range(2)], snd_slot, reads=[("hst", c) for c in range(8)])
        ep = P.E["pool"]
        w_ = P._deps("pool", (), (), [t_snd])
        for hf in range(2):
            ep.ops.append((w_ if hf == 0 else [],
                           (lambda h, i=tt * 2 + hf: h.collective_compute(
                               "AllGather", ALU.bypass, replica_groups=[[0, 1], [2, 3], [4, 5], [6, 7]],
                               ins=[snd[i].opt()], outs=[rcv[i].opt()])),
                           (cx.cc_sem, 1)))

    emit_mlp(P, cx, xT, h2T, cx.mod[1], 1, w_up, w_down,
             cx.cols[:, A_COLS["lng1"]:A_COLS["lng1"] + 8], cx.cols[:, A_COLS["lnb1"]:A_COLS["lnb1"] + 8],
             after_ln=(send_tile if fused else after_ln))
    ada_flush(cx)


def build_A(stage=None):
    nc = bass.Bass("TRN2", target_bir_lowering=False)
    dt = lambda n, s: nc.dram_tensor(n, s, F32, kind="ExternalInput").ap()
    x_in = dt("x", [T, D])
    cols_in = dt("cols", [128, A_NCOLS])
    rows_in = dt("rows", [128, 3, 1024])
    bb_in = dt("bbias", [128, 8, 128])
    consts_in = dt("consts", [128, 3, 128])
    ada_w0 = dt("ada_w0", [D, 3 * D])
    ada_w1 = dt("ada_w1", [D, 3 * D])
    w_in = dt("a_w_in", [D, 2 * D])
    w_s = dt("a_w_s", [16, 128, 128])
    w_out = dt("a_w_out", [D, D])
    w_up = dt("mlp_w_up", [D, 4 * D])
    w_down = dt("mlp_w_down", [4 * D, D])
    y_out = nc.dram_tensor("y", [T, D], F32, kind="ExternalOutput").ap()

    with contextlib.ExitStack() as es:
        P = Prog(nc, es)
        A = Arena(nc)
        cx = Ctx()
        common_setup(nc, P, A, cx, A_NCOLS)
        xT, _, _ = A.alloc([128, 8, T], F32)
        io = dict(x=x_in, cols=cols_in, rows=rows_in, bbias=bb_in, constsA=consts_in, ada_w00=ada_w0, ada_w01=ada_w1,
                  a_w_in=w_in, a_w_s=w_s, a_w_out=w_out, mlp_w_up0=w_up, mlp_w_down0=w_down, y=y_out)
        body_A(nc, P, A, cx, io, xT, stage=stage, fused=False)
        P.wait_all("sp", cx.out_toks)
        P.finish()
    return nc


def host_consts():
    ident = np.eye(128, dtype=np.float32)
    ones = np.full((128, 128), 1.0 / 1024.0, dtype=np.float32)
    tril = np.tril(np.ones((128, 128), dtype=np.float32))
    return np.ascontiguousarray(np.stack([ident, ones, tril], axis=1))


def prep_A(inp):
    x = np.asarray(inp["x"], dtype=np.float32)
    c = np.asarray(inp["c"], dtype=np.float32)
    maps = []
    consts = host_consts()
    rows = np.stack([inp["a_b_in"][0][1024:], inp["a_vn_g"][0], inp["a_vn_b"][0]], axis=0).astype(np.float32)
    rows = np.ascontiguousarray(np.broadcast_to(rows[None], (128, 3, 1024)))
    bs = np.asarray(inp["a_b_s"][0], dtype=np.float32)
    bb = np.zeros((128, 8, 128), dtype=np.float32)
    for fc in range(8):
        bb[0:64, fc, :] = bs[2 * fc][None, :]
        bb[64:128, fc, :] = bs[2 * fc + 1][None, :]
    for core in range(NCORES):
        b, half = core // 2, core % 2
        cols = np.zeros((128, A_NCOLS), dtype=np.float32)
        cols[:, 0:8] = col(c[b])
        cols[:, 8:32] = col(inp["ada_b"][0, 0])
        cols[:, 32:56] = col(inp["ada_b"][0, 1])
        cols[:, 56:64] = col(inp["ln_g"][0, 0])
        cols[:, 64:72] = col(inp["ln_b"][0, 0])
        cols[:, 72:80] = col(inp["ln_g"][0, 1])
        cols[:, 80:88] = col(inp["ln_b"][0, 1])
        cols[:, 88:96] = col(inp["a_b_in"][0][:1024])
        maps.append({
            "x": np.ascontiguousarray(x[b, half * T:(half + 1) * T]),
            "cols": cols, "rows": rows, "bbias": bb, "consts": consts,
            "ada_w0": np.ascontiguousarray(inp["ada_w"][0, 0]), "ada_w1": np.ascontiguousarray(inp["ada_w"][0, 1]),
            "a_w_in": np.ascontiguousarray(inp["a_w_in"][0]), "a_w_s": np.ascontiguousarray(inp["a_w_s"][0]),
            "a_w_out": np.ascontiguousarray(inp["a_w_out"][0]),
            "mlp_w_up": np.ascontiguousarray(inp["mlp_w_up"][0]), "mlp_w_down": np.ascontiguousarray(inp["mlp_w_down"][0]),
        })
    return maps


def run_A(inp, trace=False, stage=None, ncores=NCORES):
    nc = build_A(stage)
    maps = prep_A(inp)[:ncores]
    res = run_bass_kernel_spmd(nc, maps, core_ids=list(range(ncores)), trace=trace)
    x1 = np.zeros((4, 4096, D), dtype=np.float32)
    for core in range(ncores):
        b, half = core // 2, core % 2
        x1[b, half * T:(half + 1) * T] = res.results[core]["y"]
    if stage == "mixA":
        np.save("dbg0.npy", res.results[0]["dbg"])
    return x1, res


B_NCOLS = 96
PATTERNS = ((128, 1), (512, 4), (2048, 16))


def host_consts_B():
    ident = np.eye(128, dtype=np.float32)
    ones = np.full((128, 128), 1.0 / 1024.0, dtype=np.float32)
    k = np.arange(128)[:, None]
    q = np.arange(128)[None, :]
    d_prev = np.where(k >= q, 128.0 + q - k, 0.0).astype(np.float32)
    d_cur = np.where(k <= q, (q - k) * 1.0, 0.0).astype(np.float32)
    m_prev = np.where(k >= q, 0.0, -240000.0).astype(np.float32)
    m_cur = np.where(k <= q, 0.0, -240000.0).astype(np.float32)
    m = np.arange(128)[None, :]
    permA = ((k == m + 64) & (m < 64)).astype(np.float32)
    permB = ((k == m - 64) & (m >= 64)).astype(np.float32)
    return np.ascontiguousarray(np.stack([ident, ones, d_prev, d_cur, permA, permB, m_prev, m_cur], axis=1))


def body_B(nc, P, A, cx, io, xT, xT_at, stage=None, fused=False, cb=0):
    x_in, xp_in, cols_in, consts_in = io.get("x"), io.get("xp"), io["cols"], io["constsB"]
    ada_w0, ada_w1, w_qkv, w_out = io["ada_w10"], io["ada_w11"], io["b_w_qkv"], io["b_w_out"]
    w_up, w_down, y_out, xsp = io["mlp_w_up1"], io["mlp_w_down1"], io["y"], io["xsp"]
    mark = xT_at
    r1_end = xT_at + 65536
    save_off = A.off
    A.off = mark
    kt0, _, _ = A.alloc([128, 32 * 128], BF16)
    KT = [kt0, kt0]
    VT, _, _ = A.alloc([128, 32 * 128], BF16)
    VA, _, _ = A.alloc([128, 32, 2, 128], BF16)
    ACC, _, _ = A.alloc([128, 2, T], F32)
    QT = [A.alloc([128, T], BF16)[0] for _ in range(2)]
    tmpB = [A.alloc([128, 256], F32)[0] for _ in range(2)]
    Pt = [A.alloc([128, 512], BF16)[0] for _ in range(2)]
    assert A.off <= r1_end, (A.off, r1_end)
    A.off = max(r1_end, save_off)
    mark2 = A.off
    hT, _, _ = A.alloc([128, 8, 2 * T], BF16)
    oT, oT_at, _ = A.alloc([128, 8, T], BF16)
    wqkv = [A.alloc([128, 3, 8, 128], BF16)[0] for _ in range(2)]
    Bhl = [A.alloc([128, 2, 2, 256], BF16)[0] for _ in range(2)]
    rden, _, _ = A.alloc([128, 512], F32)
    identb, _, _ = A.alloc([128, 128], BF16)
    diffm, _, _ = A.alloc([128, 2, 128], F32)
    perm, _, _ = A.alloc([128, 2, 128], F32)
    mask01, _, _ = A.alloc([128, 2, 128], F32)
    endB1 = A.off
    A.off = oT_at
    cx.stg = [A.alloc([128, 4, 1024], F32)[0] for _ in range(2)]
    assert A.off <= oT_at + 32768
    A.off = mark2
    wout, _, _ = A.alloc([128, 8, 1024], BF16)
    A.off = mark2
    h2T, _, _ = A.alloc([128, 8, T], BF16)
    cx.wu_buf = [A.alloc([128, 8, 1024], BF16)[0] for _ in range(2)]
    cx.wd_buf = [A.alloc([128, 8, 1024], BF16)[0] for _ in range(2)]
    cx.a_buf = [A.alloc([128, 8, TT], BF16)[0] for _ in range(2)]
    cx.ost = [A.alloc([128, 1024], F32)[0] for _ in range(2)]
    endB2 = A.off
    print("SBUF plan B: mark", mark, "r1_end", r1_end, "attn end", endB1, "mlp end", endB2)
    assert max(endB1, endB2) <= A.cap

    flag = cx.cols[:, cb + 88:cb + 89]
    if not fused:
        P.dma("sp", [(cx.cols[:], cols_in)], P.slot(), writes=["cols"])
        P.dma("sp", [(cx.ident_f[:], consts_in[:, 0, :]), (cx.onesm[:], consts_in[:, 1, :])], P.slot(),
              writes=["ident", "onesm"])
        P.op("act", lambda h: h.activation(cx.sc[:], cx.cols[:, 0:8], AF.Silu), reads=["cols"], writes=["sc"])
    P.dma("sp", [(diffm[:], consts_in[:, 2:4, :]), (perm[:], consts_in[:, 4:6, :]),
                 (mask01[:], consts_in[:, 6:8, :])], P.slot(), writes=["diffm", "perm"])
    P.op("dve", lambda h: h.tensor_copy(identb[:], cx.ident_f[:]), reads=["ident"], writes=["identb"])
    if not getattr(cx, "pre_ada_b0", False):
        cx.ada_bg = AdaStepper(P, cx, ada_w0, cx.cols[:, cb + 8:cb + 32], cx.mod[0], 0)
        ada_flush(cx)
    mod0 = cx.mod[0]
    cx.ada_bg = AdaStepper(P, cx, ada_w1, cx.cols[:, cb + 32:cb + 56], cx.mod[1], 1)

    xpv = xp_in.rearrange("(tb p) f -> p tb f", p=128) if xp_in is not None else None

    def load_prev_tile(tt):
        si = cx.stg_i % 2
        cx.stg_i += 1
        stg = cx.stg[si]
        P.dma("sp", [(stg[:, 0:2, :], xpv[:, tt * 4:tt * 4 + 2, :]),
                     (stg[:, 2:4, :], xpv[:, tt * 4 + 2:tt * 4 + 4, :])], cx.stg_slot[si],
              writes=[("stg", si)])
        for c in range(8):
            b, ps = cx.psum.next()
            fns = [(lambda h, tb=tb, c=c, ps=ps, stg=stg: h.matmul(
                ps[:, tb * 128:(tb + 1) * 128], stg[:, tb, c * 128:(c + 1) * 128], cx.ident[:],
                start=True, stop=True)) for tb in range(4)]
            P.mm_group(fns, reads=[("stg", si), "ident"], writes=[("ps", b)])
            P.op("dve", lambda h, ps=ps, c=c, tt=tt: h.tensor_scalar(
                hT[:, c, tt * TT:(tt + 1) * TT], ps[:], mod0[:, 8 + c:9 + c], mod0[:, c:c + 1], ALU.mult, ALU.add),
                reads=[("ps", b), ("mod", 0)], writes=[("hall", c, tt)])

    if not fused:
        for tt in range(NT):
            load_prev_tile(tt)
        emit_load_xT(P, cx, x_in, xT)

    def mod_own(tt):
        for c in range(8):
            P.op("act" if c % 2 else "dve",
                 (lambda h, c=c, tt=tt: h.activation(hT[:, c, T + tt * TT:T + (tt + 1) * TT], xT[:, c, tt * TT:(tt + 1) * TT],
                                                     AF.Identity, bias=mod0[:, c:c + 1], scale=mod0[:, 8 + c:9 + c]))
                 if c % 2 else
                 (lambda h, c=c, tt=tt: h.tensor_scalar(hT[:, c, T + tt * TT:T + (tt + 1) * TT], xT[:, c, tt * TT:(tt + 1) * TT],
                                                        mod0[:, 8 + c:9 + c], mod0[:, c:c + 1], ALU.mult, ALU.add)),
                 reads=[("x", c, tt), ("mod", 0)], writes=[("hall", c, NT + tt)])

    for tt in range(NT):
        mod_own(tt)
    if fused:
        rcv = io["rcv"]
        r_slot = P.slot()
        P.dma("sp", [(hT[:, hf * 4:(hf + 1) * 4, tt * TT:(tt + 1) * TT],
                      rcv[tt * 2 + hf][0:128, :].rearrange("p (c t) -> p c t", c=4))
                     for tt in range(NT) for hf in range(2)], r_slot,
              writes=[("hall", c, tt) for c in range(8) for tt in range(NT)], extra=[(cx.cc_sem, 8)])
    sp_slot = P.slot()
    xkeys = [("x", c, tt) for c in range(8) for tt in range(NT)]
    t_spill = P.dma("sp", [(xsp[:, c * T:(c + 1) * T], xT[:, c, :]) for c in range(8)], sp_slot, reads=xkeys)
    fsp = fence_tokens(P, xkeys) + [t_spill]
    for e_ in ("pe", "act", "dve", "pool"):
        P.wait_all(e_, fsp)
    P.op("pool", lambda h: h.memset(VA[:], 1.0), writes=["va_init"])
    P.op("pool", lambda h: h.tensor_scalar(VA[:, 0:16], VA[:, 0:16], flag, None, ALU.mult),
         reads=["va_init", "cols"], writes=["va_init"])
    va_tok = P.last_w["va_init"]

    wq_v = w_qkv.rearrange("(kc p) n -> p kc n", p=128)
    wq_slot = [P.slot() for _ in range(2)]
    st = {"i": 0, "e": 0, "p": 0}
    hall_keys_all = [("hall", c, t8) for c in range(8) for t8 in range(2 * NT)]

    def perm_view(buf2d, col0, d, tt, ps):
        if d == 1:
            return buf2d[:, col0 + tt * 512:col0 + (tt + 1) * 512], ps[:]
        if d == 4:
            dst = buf2d[:, col0 + tt * 512:col0 + (tt + 1) * 512].rearrange("p (r i) -> p i r", r=4)
            return dst, ps[:].rearrange("p (i r) -> p i r", r=4)
        dst = buf2d[:, col0:col0 + 2048].rearrange("p (r i) -> p i r", r=16)[:, tt * 32:(tt + 1) * 32, :]
        return dst, ps[:].rearrange("p (i r) -> p i r", r=16)

    import os as _os2
    _skip = set(_os2.environ.get("DBG_SKIP", "").split(","))

    def stage_prologue(hp, g, si):
        d = PATTERNS[g][1]
        wb = wqkv[si]
        bh = Bhl[si]
        pairs = []
        for t3 in range(3):
            c0 = ((g * 3 + t3) * 16 + 2 * hp) * 64
            pairs.append((wb[:, t3, :, :], wq_v[:, :, c0:c0 + 128]))
        P.dma("pool", pairs, wq_slot[si], writes=[("wqkv", si)])
        for hh in range(2):
            slope = 2.0 ** (-8.0 * (2 * hp + hh + 1) / 16.0)
            tb = tmpB[hh]
            dm = diffm[:].rearrange("p a q -> p (a q)")
            mk = mask01[:].rearrange("p a q -> p (a q)")
            P.op("pool", lambda h, tb=tb, slope=slope: h.tensor_scalar(tb[:], dm, -8.0 * slope * d, None, ALU.mult),
                 reads=["diffm"], writes=[("tmpB", hh)])
            P.op("pool", lambda h, tb=tb: h.tensor_tensor(tb[:], tb[:], mk, ALU.add),
                 reads=["diffm", ("tmpB", hh)], writes=[("tmpB", hh)])
            P.op("pool", lambda h, tb=tb, hh=hh: h.tensor_copy(bh[:, 0, hh, :], tb[:]),
                 reads=[("tmpB", hh)], writes=[("bhl", si, hh, 0)])
            P.op("pool", lambda h, tb=tb, hh=hh: h.tensor_tensor(bh[:, 1, hh, :], tb[:], bh[:, 0, hh, :], ALU.subtract),
                 reads=[("tmpB", hh), ("bhl", si, hh, 0)], writes=[("bhl", si, hh, 1)])

    def attn_stage(hp, g, si):
        d = PATTERNS[g][1]
        wb = wqkv[si]
        kt = KT[si]
        qt = QT[si]
        bh = Bhl[si]
        for tt in range(0 if "Q" in _skip else NT):
            b, ps = cx.psum.next()
            fns = [(lambda h, kc=kc, ps=ps, tt=tt: h.matmul(
                ps[:], wb[:, 0, kc, :], hT[:, kc, T + tt * TT:T + (tt + 1) * TT], start=(kc == 0), stop=(kc == 7)))
                for kc in range(8)]
            P.mm_group(fns, reads=[("wqkv", si)] + [("hall", kc, NT + tt) for kc in range(8)], writes=[("ps", b)])
            if tt % 2 == 0:
                ada_tick(cx)
            dst, src = perm_view(qt, 0, d, tt, ps)
            P.op("act", lambda h, dst=dst, src=src: h.activation(dst, src, AF.Identity),
                 reads=[("ps", b)], writes=[("qt", si)])
        for tt in range(0 if "K" in _skip else NT):
            b, ps = cx.psum.next()
            fns = [(lambda h, kc=kc, ps=ps, tt=tt: h.matmul(
                ps[:], wb[:, 1, kc, :], hT[:, kc, T + tt * TT:T + (tt + 1) * TT], start=(kc == 0), stop=(kc == 7)))
                for kc in range(8)]
            P.mm_group(fns, reads=[("wqkv", si)] + [("hall", kc, NT + tt) for kc in range(8)], writes=[("ps", b)])
            dst, src = perm_view(kt, 2048, d, tt, ps)
            P.op("dve", lambda h, dst=dst, src=src: h.tensor_copy(dst, src),
                 reads=[("ps", b)], writes=[("kt", 0)])
        if "K" in _skip:
            pass
        elif d == 1:
            b, ps = cx.psum.next()
            fns = [(lambda h, kc=kc, ps=ps: h.matmul(
                ps[:, 0:128], wb[:, 1, kc, :], hT[:, kc, T - 128:T], start=(kc == 0), stop=(kc == 7)))
                for kc in range(8)]
            P.mm_group(fns, reads=[("wqkv", si)] + [("hall", kc, NT - 1) for kc in range(8)], writes=[("ps", b)])
            P.op("dve", lambda h, ps=ps: h.tensor_copy(kt[:, 15 * 128:16 * 128], ps[:, 0:128]),
                 reads=[("ps", b)], writes=[("kt", 0)])
        elif d == 4:
            b, ps = cx.psum.next()
            fns = [(lambda h, kc=kc, ps=ps: h.matmul(
                ps[:], wb[:, 1, kc, :], hT[:, kc, T - 512:T], start=(kc == 0), stop=(kc == 7)))
                for kc in range(8)]
            P.mm_group(fns, reads=[("wqkv", si)] + [("hall", kc, NT - 1) for kc in range(8)], writes=[("ps", b)])
            dst = kt[:, 12 * 128:16 * 128].rearrange("p (r i) -> p i r", r=4)
            src = ps[:].rearrange("p (i r) -> p i r", r=4)
            P.op("dve", lambda h, dst=dst, src=src: h.tensor_copy(dst, src),
                 reads=[("ps", b)], writes=[("kt", 0)])
        else:
            for tt in range(NT):
                b, ps = cx.psum.next()
                fns = [(lambda h, kc=kc, ps=ps, tt=tt: h.matmul(
                    ps[:], wb[:, 1, kc, :], hT[:, kc, tt * TT:(tt + 1) * TT], start=(kc == 0), stop=(kc == 7)))
                    for kc in range(8)]
                P.mm_group(fns, reads=[("wqkv", si)] + [("hall", kc, tt) for kc in range(8)], writes=[("ps", b)])
                dst, src = perm_view(kt, 0, d, tt, ps)
                P.op("dve", lambda h, dst=dst, src=src: h.tensor_copy(dst, src),
                     reads=[("ps", b)], writes=[("kt", 0)])
        def vproj(tok0, ntok, col0, mode, tt, hkeys):
            b, ps = cx.psum.next()
            fns = [(lambda h, kc=kc, ps=ps: h.matmul(
                ps[:, 0:ntok], wb[:, 2, kc, :], hT[:, kc, tok0:tok0 + ntok], start=(kc == 0), stop=(kc == 7)))
                for kc in range(8)]
            P.mm_group(fns, reads=[("wqkv", si)] + hkeys, writes=[("ps", b)])
            if mode == "plain":
                dst, src = VT[:, col0:col0 + ntok], ps[:, 0:ntok]
            elif mode == "seg4":
                dst = VT[:, col0:col0 + 512].rearrange("p (r i) -> p i r", r=4)
                src = ps[:].rearrange("p (i r) -> p i r", r=4)
            else:
                dst, src = perm_view(VT, col0, d, tt, ps)
            P.op("dve", lambda h, dst=dst, src=src: h.tensor_copy(dst, src),
                 reads=[("ps", b)], writes=["vt"])

        for tt in range(0 if "V" in _skip else NT):
            hk = [("hall", kc, NT + tt) for kc in range(8)]
            if d == 1:
                vproj(T + tt * TT, TT, 2048 + tt * 512, "plain", tt, hk)
            elif d == 4:
                vproj(T + tt * TT, TT, 2048 + tt * 512, "seg4", tt, hk)
            else:
                vproj(T + tt * TT, TT, 2048, "perm", tt, hk)
        if "V" in _skip:
            pass
        elif d == 1:
            vproj(T - 128, 128, 15 * 128, "plain", 0, [("hall", kc, NT - 1) for kc in range(8)])
        elif d == 4:
            vproj(T - 512, 512, 12 * 128, "seg4", 0, [("hall", kc, NT - 1) for kc in range(8)])
        else:
            for tt in range(NT):
                vproj(tt * TT, TT, 0, "perm", tt, [("hall", kc, tt) for kc in range(8)])
        groups = []
        prev_slots = list(range(16 - d, 16))
        own_slots = list(range(16, 32))
        for i0 in range(0, len(prev_slots), 4):
            groups.append(prev_slots[i0:i0 + 4])
        for i0 in range(0, 16, 4):
            groups.append(own_slots[i0:i0 + 4])
        if "T" in _skip or "V" in _skip:
            groups = []
        for grp in groups:
            b, ps = cx.psum.next()
            fns = [(lambda h, gi=gi, s=s, ps=ps: h.matmul(
                ps[:, gi * 128:(gi + 1) * 128], VT[:, s * 128:(s + 1) * 128], identb[:], start=True, stop=True))
                for gi, s in enumerate(grp)]
            P.mm_group(fns, reads=["vt", "identb"], writes=[("ps", b)], extra=[va_tok])
            ng = len(grp)
            s0 = grp[0]
            srcv = ps[:, 0:ng * 128].rearrange("p (s c) -> p s c", c=128)
            if "X" in _skip:
                continue
            if "XP" in _skip and s0 < 16:
                continue
            if "XO" in _skip and s0 >= 16:
                continue
            if "ALTVA" in _skip:
                s0_ = min(grp[0], 30)
                P.op("dve", lambda h, ps=ps, s0_=s0_: h.tensor_copy(
                    VA[:, s0_:s0_ + 2, :, :].rearrange("p a b c -> p (a b c)"), ps[:]),
                    reads=[("ps", b)], writes=[("va", 0)])
                continue
            if "F2D" in _skip:
                for gi, sl_ in enumerate(grp):
                    if "NO20" in _skip and sl_ == 20:
                        continue
                    P.op("act", lambda h, gi=gi, sl_=sl_, ps=ps: h.activation(
                        VA[:, sl_, 0, :], ps[:, gi * 128:(gi + 1) * 128], AF.Identity),
                        reads=[("ps", b)], writes=[("va", 0)])
                    P.op("dve", lambda h, gi=gi, sl_=sl_, ps=ps: h.tensor_copy(
                        VA[:, sl_, 1, :], ps[:, gi * 128:(gi + 1) * 128]),
                        reads=[("ps", b)], writes=[("va", 1)])
                continue
            if "FULL" in _skip:
                P.op("act", lambda h, srcv=srcv, s0=s0, ng=ng: h.activation(
                    VA[:, s0:s0 + ng, 0, :], srcv, AF.Identity),
                    reads=[("ps", b)], writes=[("va", 0)])
                P.op("dve", lambda h, srcv=srcv, s0=s0, ng=ng: h.tensor_copy(
                    VA[:, s0:s0 + ng, 1, :], srcv),
                    reads=[("ps", b)], writes=[("va", 1)])
                continue
            if s0 < 16:
                P.op("act", lambda h, srcv=srcv, s0=s0, ng=ng: h.activation(
                    VA[:, s0:s0 + ng, 0, 0:64], srcv[:, :, 0:64], AF.Identity, scale=flag),
                    reads=[("ps", b), "cols"], writes=[("va", 0)])
                P.op("dve", lambda h, srcv=srcv, s0=s0, ng=ng: h.tensor_scalar(
                    VA[:, s0:s0 + ng, 1, 64:128], srcv[:, :, 64:128], flag, None, ALU.mult),
                    reads=[("ps", b), "cols"], writes=[("va", 1)])
            else:
                P.op("act", lambda h, srcv=srcv, s0=s0, ng=ng: h.activation(
                    VA[:, s0:s0 + ng, 0, 0:64], srcv[:, :, 0:64], AF.Identity),
                    reads=[("ps", b)], writes=[("va", 0)])
                P.op("dve", lambda h, srcv=srcv, s0=s0, ng=ng: h.tensor_copy(
                    VA[:, s0:s0 + ng, 1, 64:128], srcv[:, :, 64:128]),
                    reads=[("ps", b)], writes=[("va", 1)])
        nun = 0 if stage == 'proj' else 16
        ust = {}

        def unit_front(j):
            pi = st["p"] % 2
            st["p"] += 1
            pt = Pt[pi]
            for hh in range(2):
                b, ps = cx.psum.next()
                fns = [lambda h, ps=ps, hh=hh: h.matmul(ps[:, 0:256], identb[:], bh[:, 0, hh, :], start=True, stop=False),
                       lambda h, ps=ps, hh=hh: h.matmul(ps[:, 0:256], identb[:], bh[:, 1, hh, :], start=False, stop=False)]
                for kb in range(2):
                    slot = 16 + j - d if kb == 0 else 16 + j
                    fns.append(lambda h, hh=hh, kb=kb, slot=slot, ps=ps, j=j: h.matmul(
                        ps[:, kb * 128:(kb + 1) * 128],
                        kt[hh * 64:(hh + 1) * 64, slot * 128:(slot + 1) * 128],
                        qt[hh * 64:(hh + 1) * 64, j * 128:(j + 1) * 128], start=False, stop=(kb == 1)))
                P.mm_group(fns, reads=[("kt", 0), ("qt", si), "identb", ("bhl", si, hh, 0), ("bhl", si, hh, 1)],
                           writes=[("ps", b)])
                P.op("act", lambda h, ps=ps, pt=pt, hh=hh: h.activation(
                    pt[:, hh * 256:(hh + 1) * 256], ps[:, 0:256], AF.Exp, scale=0.125),
                    reads=[("ps", b)], writes=[("pt", pi, hh)])
            ust[j] = (pi, pt)

        def unit_back(j):
            pi, pt = ust[j]
            n, r = j // d, j % d
            b2, ps2 = cx.psum.next()
            fns = []
            for hh in range(2):
                for kb in range(2):
                    slot = 16 + j - d if kb == 0 else 16 + j
                    fns.append(lambda h, hh=hh, kb=kb, slot=slot, ps2=ps2, pt=pt: h.matmul(
                        ps2[:, hh * 128:(hh + 1) * 128], VA[:, slot, hh, :],
                        pt[:, (hh * 2 + kb) * 128:(hh * 2 + kb + 1) * 128], start=(kb == 0), stop=(kb == 1)))
            P.mm_group(fns, reads=[("pt", pi, 0), ("pt", pi, 1), ("va", 0), ("va", 1)], writes=[("ps", b2)])
            start = n * 128 * d + r
            accv = ACC[:, :, start:start + 127 * d + 1:d]
            src2 = ps2[:, 0:256].rearrange("p (a q) -> p a q", a=2)
            wr = ["acc"] if j == nun - 1 else []
            if g == 0:
                P.op("dve", lambda h, accv=accv, src2=src2: h.tensor_copy(accv, src2),
                     reads=[("ps", b2)], writes=wr, extra=acc_dep)
            else:
                P.op("dve", lambda h, accv=accv, src2=src2: h.tensor_tensor(accv, src2, accv, ALU.add),
                     reads=[("ps", b2)], writes=wr, extra=acc_dep)

        acc_dep = fence_tokens(P, ["acc"])
        if nun:
            unit_front(0)
        for j in range(nun):
            if j + 1 < nun:
                unit_front(j + 1)
            unit_back(j)

    def finalize(hp):
        for tt in range(NT):
            sl = slice(tt * TT, (tt + 1) * TT)
            b, ps = cx.psum.next()
            fns = [lambda h, ps=ps, sl=sl: h.matmul(ps[:], perm[:, 0, :], ACC[:, 0, sl], start=True, stop=False),
                   lambda h, ps=ps, sl=sl: h.matmul(ps[:], perm[:, 1, :], ACC[:, 1, sl], start=False, stop=True)]
            P.mm_group(fns, reads=["acc", "perm"], writes=[("ps", b)])
            P.op("dve", lambda h, ps=ps: h.reciprocal(rden[:], ps[:]), reads=[("ps", b)], writes=["rden"])
            P.op("pool", lambda h, sl=sl: h.tensor_tensor(oT[0:64, hp, sl], ACC[0:64, 0, sl], rden[0:64, :], ALU.mult),
                 reads=["acc", "rden"], writes=[("o", hp, tt, 0)])
            P.op("pool", lambda h, sl=sl: h.tensor_tensor(oT[64:128, hp, sl], ACC[64:128, 1, sl], rden[64:128, :], ALU.mult),
                 reads=["acc", "rden"], writes=[("o", hp, tt, 1)])

    P.wait_all("pool", fence_tokens(P, [("stg", 0), ("stg", 1)]))
    nhp = {"pre": 0, "one": 1, "proj": 1}.get(stage, 8)
    import os as _os
    _gl = [int(x) for x in _os.environ.get("DBG_G", "0,1,2").split(",")]
    stages = [(hp, g) for hp in range(nhp) for g in _gl]
    if stages:
        stage_prologue(stages[0][0], stages[0][1], 0)
    for i_s, (hp, g) in enumerate(stages):
        if i_s + 1 < len(stages):
            stage_prologue(stages[i_s + 1][0], stages[i_s + 1][1], (i_s + 1) % 2)
        attn_stage(hp, g, i_s % 2)
        if stage != "proj" and g == _gl[-1]:
            finalize(hp)
    if stage in ("pre", "one", "proj"):
        fx = fence_tokens(P, ["vt", ("kt", 0), ("kt", 1), ("qt", 0), ("qt", 1), ("va", 0), ("va", 1), "acc", "va_init",
                              ("tmpB", 0), ("tmpB", 1), ("pt", 0, 0), ("pt", 0, 1), ("pt", 1, 0), ("pt", 1, 1)])
        P.wait_all("sp", fx)
        t_re = P.dma("sp", [(xT[:, c, :], xsp[:, c * T:(c + 1) * T]) for c in range(8)], P.slot(), writes=xkeys)
        for e_ in ("pe", "act", "dve", "pool"):
            P.wait_all(e_, fence_tokens(P, hall_keys_all + [("o", hp, tt, hf) for hp in range(nhp) for tt in range(NT) for hf in range(2)] + [("wqkv", 0), ("wqkv", 1)] + [("bhl", i, hh, z) for i in range(2) for hh in range(2) for z in range(2)]))
        for tt in range(NT):
            emit_store_x(P, cx, xT, y_out, tt)
        return

    att_keys = ["vt", ("kt", 0), ("kt", 1), ("qt", 0), ("qt", 1), ("va", 0), ("va", 1), "acc", "va_init",
                ("tmpB", 0), ("tmpB", 1), ("pt", 0, 0), ("pt", 0, 1), ("pt", 1, 0), ("pt", 1, 1)]
    fx = fence_tokens(P, att_keys)
    P.wait_all("sp", fx)
    x_slot = P.slot()
    P.dma("sp", [(xT[:, c, :], xsp[:, c * T:(c + 1) * T]) for c in range(8)], x_slot, writes=xkeys)
    fo = fence_tokens(P, hall_keys_all + [("wqkv", 0), ("wqkv", 1)])
    P.wait_all("pool", fo)
    P.dma("pool", [(wout[:], w_out.rearrange("(kc p) n -> p kc n", p=128))], P.slot(), writes=["wout"])

    def outproj_tile(tt):
        sl = slice(tt * TT, (tt + 1) * TT)
        for oc in range(8):
            b, ps = cx.psum.next()
            fns = [(lambda h, hp=hp, oc=oc, ps=ps: h.matmul(
                ps[:], wout[:, hp, oc * 128:(oc + 1) * 128], oT[:, hp, sl], start=(hp == 0), stop=(hp == 7)))
                for hp in range(8)]
            P.mm_group(fns, reads=["wout"] + [("o", hp, tt, hf) for hp in range(8) for hf in range(2)],
                       writes=[("ps", b)])
            P.op("dve", lambda h, ps=ps, oc=oc: h.scalar_tensor_tensor(
                xT[:, oc, sl], ps[:], mod0[:, 16 + oc:17 + oc], xT[:, oc, sl], ALU.mult, ALU.add),
                reads=[("ps", b), ("mod", 0), ("x", oc, tt)], writes=[("x", oc, tt)])
        emit_ln(P, cx, xT, tt, cx.cols[:, cb + 56:cb + 64], cx.cols[:, cb + 64:cb + 72])

    for tt in range(NT):
        outproj_tile(tt)

    if stage == "attn":
        for e_ in ("pe", "act", "dve", "pool"):
            P.wait_all(e_, fence_tokens(P, ["wout"] + [("o", hp, tt, hf) for hp in range(8) for tt in range(NT) for hf in range(2)]))
        for tt in range(NT):
            emit_store_x(P, cx, xT, y_out, tt)
        return

    ada_flush(cx)
    f2 = fence_tokens(P, ["wout"] + [("o", hp, tt, hf) for hp in range(8) for tt in range(NT) for hf in range(2)]
                      + [("bhl", i, hh, z) for i in range(2) for hh in range(2) for z in range(2)] + ["rden", "diffm", "perm", "identb", "vt"])
    for e_ in ("dve", "act", "pool"):
        P.wait_all(e_, f2)
    for tt in range(NT):
        emit_modulate(P, cx, xT, h2T, cx.mod[1], 1, tt)

    def after_ln(tt):
        emit_store_x(P, cx, xT, y_out, tt)

    emit_mlp(P, cx, xT, h2T, cx.mod[1], 1, w_up, w_down, cx.cols[:, cb + 72:cb + 80], cx.cols[:, cb + 80:cb + 88], after_ln=after_ln)


def build_B(stage=None):
    nc = bass.Bass("TRN2", target_bir_lowering=False)
    dt = lambda n, s: nc.dram_tensor(n, s, F32, kind="ExternalInput").ap()
    x_in = dt("x", [T, D])
    xp_in = dt("xp", [T, D])
    cols_in = dt("cols", [128, B_NCOLS])
    consts_in = dt("consts", [128, 8, 128])
    ada_w0 = dt("ada_w0", [D, 3 * D])
    ada_w1 = dt("ada_w1", [D, 3 * D])
    w_qkv = dt("b_w_qkv", [D, 9 * D])
    w_out = dt("b_w_out", [D, D])
    w_up = dt("mlp_w_up", [D, 4 * D])
    w_down = dt("mlp_w_down", [4 * D, D])
    y_out = nc.dram_tensor("y", [T, D], F32, kind="ExternalOutput").ap()
    xsp = nc.dram_tensor("xsp", [128, 8 * T], F32).ap()

    with contextlib.ExitStack() as es:
        P = Prog(nc, es)
        A = Arena(nc)
        cx = Ctx()
        common_setup(nc, P, A, cx, B_NCOLS)
        xT, xT_at, _ = A.alloc([128, 8, T], F32)
        io = dict(x=x_in, xp=xp_in, cols=cols_in, constsB=consts_in, ada_w10=ada_w0, ada_w11=ada_w1, b_w_qkv=w_qkv,
                  b_w_out=w_out, mlp_w_up1=w_up, mlp_w_down1=w_down, y=y_out, xsp=xsp)
        body_B(nc, P, A, cx, io, xT, xT_at, stage=stage, fused=False, cb=0)
        P.wait_all("sp", cx.out_toks)
        P.finish()
    return nc


def prep_B(inp, x1):
    c = np.asarray(inp["c"], dtype=np.float32)
    consts = host_consts_B()
    maps = []
    zeros = np.zeros((T, D), dtype=np.float32)
    for core in range(NCORES):
        b, half = core // 2, core % 2
        cols = np.zeros((128, B_NCOLS), dtype=np.float32)
        cols[:, 0:8] = col(c[b])
        cols[:, 8:32] = col(inp["ada_b"][1, 0])
        cols[:, 32:56] = col(inp["ada_b"][1, 1])
        cols[:, 56:64] = col(inp["ln_g"][1, 0])
        cols[:, 64:72] = col(inp["ln_b"][1, 0])
        cols[:, 72:80] = col(inp["ln_g"][1, 1])
        cols[:, 80:88] = col(inp["ln_b"][1, 1])
        cols[:, 88] = float(half)
        maps.append({
            "x": np.ascontiguousarray(x1[b, half * T:(half + 1) * T]) if x1 is not None else None,
            "xp": (np.ascontiguousarray(x1[b, 0:T]) if half == 1 else zeros) if x1 is not None else None,
            "cols": cols, "consts": consts,
            "ada_w0": np.ascontiguousarray(inp["ada_w"][1, 0]), "ada_w1": np.ascontiguousarray(inp["ada_w"][1, 1]),
            "b_w_qkv": np.ascontiguousarray(inp["b_w_qkv"][0]), "b_w_out": np.ascontiguousarray(inp["b_w_out"][0]),
            "mlp_w_up": np.ascontiguousarray(inp["mlp_w_up"][1]), "mlp_w_down": np.ascontiguousarray(inp["mlp_w_down"][1]),
        })
    return maps


def run_B(inp, x1, trace=False, stage=None, ncores=NCORES):
    nc = build_B(stage)
    maps = prep_B(inp, x1)[:ncores]
    res = run_bass_kernel_spmd(nc, maps, core_ids=list(range(ncores)), trace=trace)
    out = np.zeros((4, 4096, D), dtype=np.float32)
    for core in range(ncores):
        b, half = core // 2, core % 2
        out[b, half * T:(half + 1) * T] = res.results[core]["y"]
    return out, res


F_NCOLS = A_NCOLS + B_NCOLS


def build_F():
    nc = bass.Bass("TRN2", target_bir_lowering=False)
    dt = lambda n, s: nc.dram_tensor(n, s, F32, kind="ExternalInput").ap()
    io = dict(
        x=dt("x", [T, D]), cols=dt("cols", [128, F_NCOLS]), rows=dt("rows", [128, 3, 1024]),
        bbias=dt("bbias", [128, 8, 128]), constsA=dt("constsA", [128, 3, 128]), constsB=dt("constsB", [128, 8, 128]),
        ada_w00=dt("ada_w00", [D, 3 * D]), ada_w01=dt("ada_w01", [D, 3 * D]),
        ada_w10=dt("ada_w10", [D, 3 * D]), ada_w11=dt("ada_w11", [D, 3 * D]),
        a_w_in=dt("a_w_in", [D, 2 * D]), a_w_s=dt("a_w_s", [16, 128, 128]), a_w_out=dt("a_w_out", [D, D]),
        mlp_w_up0=dt("mlp_w_up0", [D, 4 * D]), mlp_w_down0=dt("mlp_w_down0", [4 * D, D]),
        mlp_w_up1=dt("mlp_w_up1", [D, 4 * D]), mlp_w_down1=dt("mlp_w_down1", [4 * D, D]),
        b_w_qkv=dt("b_w_qkv", [D, 9 * D]), b_w_out=dt("b_w_out", [D, D]))
    io["y"] = nc.dram_tensor("y", [T, D], F32, kind="ExternalOutput").ap()
    io["xsp"] = nc.dram_tensor("xsp", [128, 8 * T], F32).ap()
    io["snd"] = [nc.dram_tensor(f"snd{c}", [128, T], BF16).ap() for c in range(8)]
    io["rcv"] = [nc.dram_tensor(f"rcv{c}", [256, T], BF16).ap() for c in range(8)]
    with contextlib.ExitStack() as es:
        P = Prog(nc, es)
        A = Arena(nc)
        cx = Ctx()
        common_setup(nc, P, A, cx, F_NCOLS)
        xT, xT_at, _ = A.alloc([128, 8, T], F32)
        mark = A.off
        body_A(nc, P, A, cx, io, xT, stage=None, fused=True)
        P.barrier()
        A.off = mark
        body_B(nc, P, A, cx, io, xT, xT_at, stage=None, fused=True, cb=A_NCOLS)
        P.wait_all("sp", cx.out_toks)
        P.finish()
    return nc


def prep_F(inp):
    mA = prep_A(inp)
    mB = prep_B(inp, None)
    maps = []
    for core in range(NCORES):
        a, b = mA[core], mB[core]
        maps.append({
            "x": a["x"], "cols": np.ascontiguousarray(np.concatenate([a["cols"], b["cols"]], axis=1)),
            "rows": a["rows"], "bbias": a["bbias"], "constsA": a["consts"], "constsB": b["consts"],
            "ada_w00": a["ada_w0"], "ada_w01": a["ada_w1"], "ada_w10": b["ada_w0"], "ada_w11": b["ada_w1"],
            "a_w_in": a["a_w_in"], "a_w_s": a["a_w_s"], "a_w_out": a["a_w_out"],
            "mlp_w_up0": a["mlp_w_up"], "mlp_w_down0": a["mlp_w_down"],
            "mlp_w_up1": b["mlp_w_up"], "mlp_w_down1": b["mlp_w_down"],
            "b_w_qkv": b["b_w_qkv"], "b_w_out": b["b_w_out"],
        })
    return maps


def run_F(inp, trace=False, ncores=NCORES):
    nc = build_F()
    maps = prep_F(inp)[:ncores]
    res = run_bass_kernel_spmd(nc, maps, core_ids=list(range(ncores)), trace=trace)
    out = np.zeros((4, 4096, D), dtype=np.float32)
    for core in range(ncores):
        b, half = core // 2, core % 2
        out[b, half * T:(half + 1) * T] = res.results[core]["y"]
    return out, res


def kernel(**inputs):
    inp = {k: np.asarray(v) for k, v in inputs.items()}
    out, _ = run_F(inp)
    return out
```

```python
import contextlib
import numpy as np
import concourse.bass as bass
import concourse.mybir as mybir
from concourse.bass_utils import run_bass_kernel_spmd

F32 = mybir.dt.float32
BF16 = mybir.dt.bfloat16
AF = mybir.ActivationFunctionType
ALU = mybir.AluOpType
AX = mybir.AxisListType

D = 1024
T = 2048
NT = 4
TT = 512
NCORES = 8
ALPHA = 4.0 ** 0.25
INV_ALPHA = 1.0 / ALPHA
LN_EPS = 1e-5
EPS_P = LN_EPS / (ALPHA * ALPHA)
GELU = AF.Gelu_apprx_tanh


class Eng:
    def __init__(self, name, sem):
        self.name = name
        self.sem = sem
        self.count = 0
        self.ops = []
        self.waited = {}


class Prog:
    def __init__(self, nc, es, nsem=100):
        self.nc = nc
        self.free_sems = [es.enter_context(nc.semaphore(f"s{i}")) for i in range(nsem)]
        self.E = {}
        for n in ["pe", "act", "dve", "pool", "sp"]:
            self.E[n] = Eng(n, self.free_sems.pop())
        self.last_w = {}
        self.readers = {}
        self.sem_id = {}

    def new_sem(self):
        return self.free_sems.pop()

    def slot(self):
        sl = {"sem": self.new_sem(), "count": 0}
        self.slots = getattr(self, "slots", [])
        self.slots.append(sl)
        return sl

    def barrier(self, extra=()):
        toks = [(e.sem, e.count) for e in self.E.values() if e.count > 0]
        toks += [(sl["sem"], sl["count"]) for sl in getattr(self, "slots", []) if sl["count"] > 0]
        toks += list(extra)
        for eng in self.E:
            self.wait_all(eng, toks)

    def _sid(self, s):
        return id(s)

    @staticmethod
    def _norm(reads, writes):
        r2, w2 = [], list(writes)
        for k in reads:
            if isinstance(k, tuple) and k and k[0] == "ps":
                if k not in w2:
                    w2.append(k)
            else:
                r2.append(k)
        return r2, w2

    def _deps(self, eng, reads, writes, extra):
        toks = []
        for r in reads:
            t = self.last_w.get(r)
            if t is not None:
                toks.append(t)
        for w in writes:
            t = self.last_w.get(w)
            if t is not None:
                toks.append(t)
            for t in self.readers.get(w, {}).values():
                toks.append(t)
        for t in extra:
            if t is not None:
                toks.append(t)
        e = self.E[eng]
        waits = []
        for (s, v) in toks:
            if eng == "pe" and s is e.sem:
                continue
            k = self._sid(s)
            if e.waited.get(k, 0) < v:
                e.waited[k] = v
                waits.append((s, v))
        return waits

    def _record(self, tok, reads, writes):
        for r in reads:
            d = self.readers.setdefault(r, {})
            k = self._sid(tok[0])
            if k not in d or d[k][1] < tok[1]:
                d[k] = tok
        for w in writes:
            self.last_w[w] = tok
            self.readers[w] = {}

    def op(self, eng, fn, reads=(), writes=(), extra=()):
        reads, writes = self._norm(reads, writes)
        e = self.E[eng]
        waits = self._deps(eng, reads, writes, extra)
        e.count += 1
        tok = (e.sem, e.count)
        e.ops.append((waits, fn, (e.sem, 1)))
        self._record(tok, reads, writes)
        return tok

    def mm_group(self, fns, reads=(), writes=(), extra=()):
        reads, writes = self._norm(reads, writes)
        e = self.E["pe"]
        waits = self._deps("pe", reads, writes, extra)
        for i, fn in enumerate(fns):
            last = i == len(fns) - 1
            e.ops.append((waits if i == 0 else [], fn, (e.sem, 1) if last else None))
        e.count += 1
        tok = (e.sem, e.count)
        self._record(tok, reads, writes)
        return tok

    def dma(self, eng, pairs, slot, reads=(), writes=(), extra=()):
        e = self.E[eng]
        waits = self._deps(eng, reads, writes, extra)
        for i, (o, i_) in enumerate(pairs):
            e.ops.append((waits if i == 0 else [],
                          (lambda h, o=o, i_=i_: h.dma_start(out=o, in_=i_)),
                          (slot["sem"], 16)))
            slot["count"] += 16
        tok = (slot["sem"], slot["count"])
        self._record(tok, reads, writes)
        return tok

    def wait_all(self, eng, toks):
        e = self.E[eng]
        waits = self._deps(eng, (), (), toks)
        e.ops.append((waits, None, None))

    def replay(self, eng, h):
        for waits, fn, inc in self.E[eng].ops:
            for s, v in waits:
                h.wait_ge(s, v)
            if fn is None:
                continue
            inst = fn(h)
            if inc is not None:
                inst.then_inc(inc[0], inc[1])

    def finish(self):
        nc = self.nc
        with nc.Block() as block:
            @block.tensor
            def _(h):
                self.replay("pe", h)

            @block.scalar
            def _(h):
                self.replay("act", h)

            @block.vector
            def _(h):
                self.replay("dve", h)

            @block.gpsimd
            def _(h):
                self.replay("pool", h)

            @block.sync
            def _(h):
                self.replay("sp", h)


class Arena:
    def __init__(self, nc, base=16512, cap=229376):
        self.nc = nc
        self.cap = cap
        self.off = base
        self.n = 0

    def alloc(self, shape, dtype, at=None, name=None):
        nbytes = int(np.prod(shape[1:])) * (4 if dtype == F32 else 2)
        nbytes = (nbytes + 63) // 64 * 64
        if at is None:
            at = self.off
            self.off += nbytes
            assert self.off <= self.cap, f"SBUF overflow {self.off}"
        else:
            assert at + nbytes <= self.cap, "SBUF overflow (at)"
        self.n += 1
        t = self.nc.alloc_sbuf_tensor_at(name or f"sb{self.n}", list(shape), dtype, offset=at, align_bytes=64)
        return t, at, nbytes


class Psum:
    def __init__(self, nc, P):
        self.banks = [nc.alloc_psum_tensor(f"psb{i}", [128, 512], F32) for i in range(8)]
        self.i = 0

    def next(self):
        b = self.i
        self.i = (self.i + 1) % 7
        return b, self.banks[b]


def col(v):
    v = np.asarray(v, dtype=np.float32)
    return np.ascontiguousarray(v.reshape(-1, 128).T)


class Ctx:
    pass


class AdaStepper:
    NP = 24

    def __init__(self, P, cx, wada_ap, bias_cols, out_mod, tag):
        self.P, self.cx = P, cx
        self.wv = wada_ap.rearrange("(kc p) n -> p kc n", p=128)
        self.bias_cols, self.out_mod, self.tag = bias_cols, out_mod, tag
        self.k = 0
        self.base = cx.ada_i
        cx.ada_i += self.NP
        self._dma(0)

    def _dma(self, pc):
        bi = (self.base + pc) % 2
        self.P.dma("pool", [(self.cx.ada_buf[bi][:], self.wv[:, :, pc * 128:(pc + 1) * 128])], self.cx.ada_slot[bi],
                   writes=[("adabuf", bi)])

    def tick(self):
        if self.k >= self.NP:
            return False
        P, cx, pc = self.P, self.cx, self.k
        if pc + 1 < self.NP:
            self._dma(pc + 1)
        bi = (self.base + pc) % 2
        buf = cx.ada_buf[bi]
        ps = cx.ps_ada
        fns = [(lambda h, kc=kc, pc=pc, buf=buf: h.matmul(
            ps[:, pc:pc + 1], buf[:, kc, :], cx.sc[:, kc:kc + 1], start=(kc == 0), stop=(kc == 7)))
            for kc in range(8)]
        P.mm_group(fns, reads=[("adabuf", bi), "sc"], writes=[("ps_ada", pc)])
        self.k += 1
        if self.k == self.NP:
            self._finish()
        return True

    def flush(self):
        while self.tick():
            pass

    def _finish(self):
        P, ps, out_mod, tag = self.P, self.cx.ps_ada, self.out_mod, self.tag
        rd = [("ps_ada", f) for f in range(24)]
        P.op("dve", lambda h: h.tensor_tensor(out_mod[:, :], ps[:, 0:24], self.bias_cols, ALU.add),
             reads=rd + ["cols"], writes=[("mod", tag)])
        P.op("dve", lambda h: h.tensor_scalar(out_mod[:, 8:16], out_mod[:, 8:16], 1.0, None, ALU.add),
             reads=[("mod", tag)], writes=[("mod", tag)])
        P.op("dve", lambda h: h.tensor_scalar(out_mod[:, 16:24], out_mod[:, 16:24], 1.0, INV_ALPHA, ALU.add, ALU.mult),
             reads=[("mod", tag)], writes=[("mod", tag)])


def ada_tick(cx):
    st_ = getattr(cx, "ada_bg", None)
    if st_ is not None:
        st_.tick()


def ada_flush(cx):
    st_ = getattr(cx, "ada_bg", None)
    if st_ is not None:
        st_.flush()
        cx.ada_bg = None


def emit_modulate(P, cx, xT, hT, mod, tag, tt, eng="dve"):
    sl = slice(tt * TT, (tt + 1) * TT)
    for c in range(8):
        if eng == "act":
            P.op("act", lambda h, c=c: h.activation(hT[:, c, sl], xT[:, c, sl], AF.Identity,
                                                    bias=mod[:, c:c + 1], scale=mod[:, 8 + c:9 + c]),
                 reads=[("x", c, tt), ("mod", tag)], writes=[("h", c, tt)])
        else:
            P.op(eng, lambda h, c=c: h.tensor_scalar(hT[:, c, sl], xT[:, c, sl], mod[:, 8 + c:9 + c],
                                                     mod[:, c:c + 1], ALU.mult, ALU.add),
                 reads=[("x", c, tt), ("mod", tag)], writes=[("h", c, tt)])


def emit_ln_p1(P, cx, xT, tt):
    sl = slice(tt * TT, (tt + 1) * TT)
    s1, s2, sq = cx.ln_s1, cx.ln_s2, cx.ln_sq
    P.op("pool", lambda h: h.tensor_tensor(s1[:], xT[:, 0, sl], xT[:, 1, sl], ALU.add),
         reads=[("x", 0, tt), ("x", 1, tt)], writes=["ln_s1"])
    for c in range(2, 8):
        P.op("pool", lambda h, c=c: h.tensor_tensor(s1[:], s1[:], xT[:, c, sl], ALU.add),
             reads=[("x", c, tt), "ln_s1"], writes=["ln_s1"])
    for c in range(8):
        if c == 0:
            P.op("act", lambda h: h.activation(s2[:], xT[:, 0, sl], AF.Square),
                 reads=[("x", 0, tt)], writes=["ln_s2"])
        else:
            k = c % 2
            P.op("act", lambda h, c=c, k=k: h.activation(sq[k][:], xT[:, c, sl], AF.Square),
                 reads=[("x", c, tt)], writes=[("ln_sq", k)])
            P.op("dve", lambda h, k=k: h.tensor_tensor(s2[:], s2[:], sq[k][:], ALU.add),
                 reads=[("ln_sq", k), "ln_s2"], writes=["ln_s2"])


def emit_ln_p2(P, cx, xT, tt, g_cols, b_cols):
    sl = slice(tt * TT, (tt + 1) * TT)
    s1, s2, sq = cx.ln_s1, cx.ln_s2, cx.ln_sq
    b1, ps1 = cx.psum.next()
    P.mm_group([lambda h: h.matmul(ps1[:], cx.onesm[:], s1[:], start=True, stop=True)],
               reads=["ln_s1", "onesm"], writes=[("ps", b1)])
    b2, ps2 = cx.psum.next()
    P.mm_group([lambda h: h.matmul(ps2[:], cx.onesm[:], s2[:], start=True, stop=True)],
               reads=["ln_s2", "onesm"], writes=[("ps", b2)])
    mean, msq, rstd = cx.ln_mean, cx.ln_tmp[1], cx.ln_rstd
    P.op("act", lambda h: h.activation(mean[:], ps1[:], AF.Identity), reads=[("ps", b1)], writes=["ln_mean"])
    P.op("act", lambda h: h.activation(msq[:], ps1[:], AF.Square), reads=[("ps", b1)], writes=[("ln_tmp", 1)])
    P.op("dve", lambda h: h.tensor_tensor(rstd[:], ps2[:], msq[:], ALU.subtract),
         reads=[("ps", b2), ("ln_tmp", 1)], writes=["ln_rstd"])
    P.op("dve", lambda h: h.tensor_scalar(rstd[:], rstd[:], EPS_P, None, ALU.add),
         reads=["ln_rstd"], writes=["ln_rstd"])
    P.op("dve", lambda h: h.reciprocal(rstd[:], rstd[:]), reads=["ln_rstd"], writes=["ln_rstd"])
    P.op("act", lambda h: h.activation(rstd[:], rstd[:], AF.Sqrt), reads=["ln_rstd"], writes=["ln_rstd"])
    for c in range(8):
        k = c % 2
        tmp = cx.ln_tmp[k]
        P.op("dve", lambda h, c=c, tmp=tmp: h.tensor_tensor(tmp[:], xT[:, c, sl], mean[:], ALU.subtract),
             reads=[("x", c, tt), "ln_mean"], writes=[("ln_tmp", k)])
        P.op("pool", lambda h, tmp=tmp: h.tensor_tensor(tmp[:], tmp[:], rstd[:], ALU.mult),
             reads=["ln_rstd", ("ln_tmp", k)], writes=[("ln_tmp", k)])
        P.op("act", lambda h, c=c, tmp=tmp: h.activation(xT[:, c, sl], tmp[:], AF.Identity,
                                                         bias=b_cols[:, c:c + 1], scale=g_cols[:, c:c + 1]),
             reads=[("ln_tmp", k), "cols"], writes=[("x", c, tt)])


def emit_ln(P, cx, xT, tt, g_cols, b_cols):
    emit_ln_p1(P, cx, xT, tt)
    emit_ln_p2(P, cx, xT, tt, g_cols, b_cols)


def emit_mlp(P, cx, xT, hT, mod, tag, w_up_ap, w_down_ap, g_cols, b_cols, after_ln=None):
    wu_v = w_up_ap.rearrange("(kc p) n -> p kc n", p=128)
    wd_v = w_down_ap.rearrange("(fc p) n -> p fc n", p=128)
    for q in range(4):
        bi = cx.mlp_i % 2
        cx.mlp_i += 1
        wu, wd = cx.wu_buf[bi], cx.wd_buf[bi]
        P.dma("pool", [(wu[:, 0:4, :], wu_v[:, 0:4, q * 1024:(q + 1) * 1024]),
                       (wu[:, 4:8, :], wu_v[:, 4:8, q * 1024:(q + 1) * 1024])],
              cx.wu_slot[bi], writes=[("wu", bi)])
        P.dma("pool", [(wd[:, 0:4, :], wd_v[:, q * 8:q * 8 + 4, :]),
                       (wd[:, 4:8, :], wd_v[:, q * 8 + 4:q * 8 + 8, :])],
              cx.wd_slot[bi], writes=[("wd", bi)])
        def mlp_tile(q, tt, bi, wu, wd):
            sl = slice(tt * TT, (tt + 1) * TT)
            ai = cx.a_i % 2
            cx.a_i += 1
            aT = cx.a_buf[ai]
            for f in range(8):
                b, ps = cx.psum.next()
                fns = [(lambda h, kc=kc, f=f, ps=ps, wu=wu: h.matmul(
                    ps[:], wu[:, kc, f * 128:(f + 1) * 128], hT[:, kc, sl], start=(kc == 0), stop=(kc == 7)))
                    for kc in range(8)]
                P.mm_group(fns, reads=[("wu", bi)] + [("h", kc, tt) for kc in range(8)], writes=[("ps", b)])
                if f % 4 == 0:
                    ada_tick(cx)
                k = cx.r_i % 2
                cx.r_i += 1
                rt = cx.relu_tmp[k]
                P.op("act", lambda h, ps=ps, rt=rt: h.activation(rt[:], ps[:], AF.Relu),
                     reads=[("ps", b)], writes=[("ln_sq", k)])
                P.op("pool", lambda h, rt=rt, aT=aT, f=f: h.tensor_tensor(aT[:, f, :], rt[:], rt[:], ALU.mult),
                     reads=[("ln_sq", k)], writes=[("a", ai, f)])
            for oc in range(8):
                b, ps = cx.psum.next()
                fns = [(lambda h, f=f, oc=oc, ps=ps, wd=wd, aT=aT: h.matmul(
                    ps[:], wd[:, f, oc * 128:(oc + 1) * 128], aT[:, f, :], start=(f == 0), stop=(f == 7)))
                    for f in range(8)]
                P.mm_group(fns, reads=[("wd", bi)] + [("a", ai, f) for f in range(8)], writes=[("ps", b)])
                P.op("dve", lambda h, ps=ps, oc=oc: h.scalar_tensor_tensor(
                    xT[:, oc, sl], ps[:], mod[:, 16 + oc:17 + oc], xT[:, oc, sl], ALU.mult, ALU.add),
                    reads=[("ps", b), ("mod", tag), ("x", oc, tt)], writes=[("x", oc, tt)])
        for tt in range(NT):
            mlp_tile(q, tt, bi, wu, wd)
            if q == 3 and tt >= 1:
                emit_ln_p2(P, cx, xT, tt - 1, g_cols, b_cols)
                if after_ln is not None:
                    after_ln(tt - 1)
            if q == 3:
                emit_ln_p1(P, cx, xT, tt)
        if q == 3:
            emit_ln_p2(P, cx, xT, NT - 1, g_cols, b_cols)
            if after_ln is not None:
                after_ln(NT - 1)


def emit_load_xT(P, cx, x_ap, xT, key="x"):
    xv = x_ap.rearrange("(tb p) f -> p tb f", p=128)
    for tt in range(NT):
        si = cx.stg_i % 2
        cx.stg_i += 1
        stg = cx.stg[si]
        P.dma("sp", [(stg[:, 0:2, :], xv[:, tt * 4:tt * 4 + 2, :]),
                     (stg[:, 2:4, :], xv[:, tt * 4 + 2:tt * 4 + 4, :])], cx.stg_slot[si],
              writes=[("stg", si)])
        for c in range(8):
            b, ps = cx.psum.next()
            fns = [(lambda h, tb=tb, c=c, ps=ps, stg=stg: h.matmul(
                ps[:, tb * 128:(tb + 1) * 128], stg[:, tb, c * 128:(c + 1) * 128], cx.ident[:],
                start=True, stop=True)) for tb in range(4)]
            P.mm_group(fns, reads=[("stg", si), "ident"], writes=[("ps", b)])
            ada_tick(cx)
            eng = "act" if c % 2 == 0 else "dve"
            sl = slice(tt * TT, (tt + 1) * TT)
            if eng == "act":
                P.op("act", lambda h, ps=ps, c=c, sl=sl: h.activation(xT[:, c, sl], ps[:], AF.Identity),
                     reads=[("ps", b)], writes=[(key, c, tt)])
            else:
                P.op("dve", lambda h, ps=ps, c=c, sl=sl: h.tensor_copy(xT[:, c, sl], ps[:]),
                     reads=[("ps", b)], writes=[(key, c, tt)])


def emit_store_x(P, cx, xT, out_ap, tt):
    ov = out_ap.rearrange("(tb p) f -> p tb f", p=128)
    for tb in range(4):
        si = cx.ost_i % 2
        cx.ost_i += 1
        stg = cx.ost[si]
        tsl = slice(tt * TT + tb * 128, tt * TT + (tb + 1) * 128)
        for half in range(2):
            b, ps = cx.psum.next()
            fns = [(lambda h, cc=cc, half=half, ps=ps, tsl=tsl: h.matmul(
                ps[:, cc * 128:(cc + 1) * 128], xT[:, half * 4 + cc, tsl], cx.ident[:],
                start=True, stop=True)) for cc in range(4)]
            P.mm_group(fns, reads=[("x", half * 4 + cc, tt) for cc in range(4)] + ["ident"],
                       writes=[("ps", b)])
            if half == 0:
                P.op("act", lambda h, ps=ps, stg=stg: h.activation(stg[:, 0:512], ps[:], AF.Identity),
                     reads=[("ps", b)], writes=[("ost", si, 0)])
            else:
                P.op("dve", lambda h, ps=ps, stg=stg: h.tensor_copy(stg[:, 512:1024], ps[:]),
                     reads=[("ps", b)], writes=[("ost", si, 1)])
        tok = P.dma("sp", [(ov[:, tt * 4 + tb, :], stg[:])], cx.out_slot[si],
                    reads=[("ost", si, 0), ("ost", si, 1)])
        cx.out_toks.append(tok)


def common_setup(nc, P, A, cx, ncols):
    cx.psum = Psum(nc, P)
    cx.cols, _, _ = A.alloc([128, ncols], F32)
    cx.ident_f, _, _ = A.alloc([128, 128], F32)
    cx.ident = cx.ident_f
    cx.onesm, _, _ = A.alloc([128, 128], F32)
    cx.sc, _, _ = A.alloc([128, 8], BF16)
    cx.mod = [A.alloc([128, 24], F32)[0] for _ in range(2)]
    cx.ln_s1, _, _ = A.alloc([128, TT], F32)
    cx.ln_s2, _, _ = A.alloc([128, TT], F32)
    cx.ln_sq = [A.alloc([128, TT], F32)[0] for _ in range(2)]
    cx.ln_mean, _, _ = A.alloc([128, TT], F32)
    cx.ln_rstd, _, _ = A.alloc([128, TT], F32)
    cx.ln_tmp = [A.alloc([128, TT], F32)[0] for _ in range(2)]
    cx.relu_tmp = cx.ln_sq
    cx.ada_buf = [A.alloc([128, 8, 128], BF16)[0] for _ in range(2)]
    cx.ada_slot = [P.slot() for _ in range(2)]
    cx.ada_i = 0
    cx.mlp_i = 0
    cx.a_i = 0
    cx.r_i = 0
    cx.stg_i = 0
    cx.ost_i = 0
    cx.out_toks = []
    cx.out_slot = [P.slot() for _ in range(2)]
    cx.const_slot = P.slot()
    cx.stg_slot = [P.slot() for _ in range(2)]
    cx.wu_slot = [P.slot() for _ in range(2)]
    cx.wd_slot = [P.slot() for _ in range(2)]
    cx.ps_ada = cx.psum.banks[7]


def fence_tokens(P, keys):
    toks = []
    for k in keys:
        t = P.last_w.get(k)
        if t is not None:
            toks.append(t)
        toks.extend(P.readers.get(k, {}).values())
    return toks


A_COLS = dict(c=0, adab0=8, adab1=32, lng0=56, lnb0=64, lng1=72, lnb1=80, binu=88)
A_NCOLS = 96


def body_A(nc, P, A, cx, io, xT, stage=None, fused=False):
    x_in, cols_in, rows_in, bb_in, consts_in = io["x"], io["cols"], io["rows"], io["bbias"], io["constsA"]
    ada_w0, ada_w1, w_in, w_s, w_out = io["ada_w00"], io["ada_w01"], io["a_w_in"], io["a_w_s"], io["a_w_out"]
    w_up, w_down, y_out = io["mlp_w_up0"], io["mlp_w_down0"], io["y"]
    mark = A.off
    win, _, _ = A.alloc([128, 8, 2048], BF16)
    wout, _, _ = A.alloc([128, 8, 1024], BF16)
    rows, _, _ = A.alloc([128, 3, 1024], F32)
    bbias, _, _ = A.alloc([128, 8, 128], F32)
    wcT, _, _ = A.alloc([128, 16, 128], BF16)
    vst, _, _ = A.alloc([128, 2, 6], F32)
    vmv, _, _ = A.alloc([128, 2], F32)
    vrs, _, _ = A.alloc([128, 1], F32)
    mark1b = A.off
    hT1 = [A.alloc([128, 8, TT], BF16)[0] for _ in range(2)]
    uT1, _, _ = A.alloc([128, 8, TT], BF16)
    uT = [uT1, uT1]
    vtok1, _, _ = A.alloc([128, 4, 1024], BF16)
    vtok = [vtok1, vtok1]
    vg = [A.alloc([128, 1024], F32)[0] for _ in range(2)]
    gt = [A.alloc([128, TT], F32)[0] for _ in range(2)]
    end1 = A.off
    A.off = mark1b
    cx.stg = [A.alloc([128, 4, 1024], F32)[0] for _ in range(2)]
    tril, _, _ = A.alloc([128, 128], F32)
    identb, _, _ = A.alloc([128, 128], BF16)
    ws_f, _, _ = A.alloc([128, 16, 128], F32)
    ws_b, _, _ = A.alloc([128, 16, 128], BF16)
    end1 = max(end1, A.off)
    A.off = mark
    h2T, _, _ = A.alloc([128, 8, T], BF16)
    cx.wu_buf = [A.alloc([128, 8, 1024], BF16)[0] for _ in range(2)]
    cx.wd_buf = [A.alloc([128, 8, 1024], BF16)[0] for _ in range(2)]
    cx.a_buf = [A.alloc([128, 8, TT], BF16)[0] for _ in range(2)]
    ost_at = A.off
    cx.ost = [A.alloc([128, 1024], F32)[0] for _ in range(2)]
    end2 = A.off
    hst = A.alloc([128, 8, TT], BF16, at=ost_at)[0] if fused else None
    A.off = max(end1, end2)
    print("SBUF plan A: persistent", mark, "phase1 end", end1, "phase2 end", end2)

    P.dma("sp", [(cx.cols[:], cols_in)], P.slot(), writes=["cols"])
    P.dma("sp", [(cx.ident_f[:], consts_in[:, 0, :]), (cx.onesm[:], consts_in[:, 1, :]),
                 (tril[:], consts_in[:, 2, :])], P.slot(), writes=["ident", "onesm", "tril"])
    P.dma("sp", [(rows[:], rows_in)], P.slot(), writes=["rows"])
    P.dma("sp", [(bbias[:], bb_in)], P.slot(), writes=["bbias"])
    P.dma("sp", [(ws_f[:], w_s.rearrange("g t s -> t g s"))], P.slot(), writes=["ws_f"])
    P.op("act", lambda h: h.activation(cx.sc[:], cx.cols[:, 0:8], AF.Silu), reads=["cols"], writes=["sc"])
    cx.ada_bg = AdaStepper(P, cx, ada_w0, cx.cols[:, 8:32], cx.mod[0], 0)
    wv = w_in.rearrange("(kc p) n -> p kc n", p=128)
    for j in range(4):
        P.dma("pool", [(win[:, :, j * 512:(j + 1) * 512], wv[:, :, j * 512:(j + 1) * 512])], P.slot(),
              writes=[("win", j)])
    wout_slot = P.slot()
    P.dma("pool", [(wout[:], w_out.rearrange("(kc p) n -> p kc n", p=128))], wout_slot, writes=["wout"])
    P.op("dve", lambda h: h.tensor_copy(identb[:], cx.ident_f[:]), reads=["ident"], writes=["identb"])
    P.op("dve", lambda h: h.tensor_tensor(ws_b[:], ws_f[:], tril[:, None, :].to_broadcast([128, 16, 128]), ALU.mult),
         reads=["ws_f", "tril"], writes=["ws_b"])
    for g4 in range(4):
        b, ps = cx.psum.next()
        fns = [(lambda h, gi=gi, g4=g4, ps=ps: h.matmul(
            ps[:, gi * 128:(gi + 1) * 128], ws_b[:, g4 * 4 + gi, :], identb[:], start=True, stop=True))
            for gi in range(4)]
        P.mm_group(fns, reads=["ws_b", "identb"], writes=[("ps", b)])
        P.op("dve", lambda h, g4=g4, ps=ps: h.tensor_copy(
            wcT[:, g4 * 4:(g4 + 1) * 4, :], ps[:].rearrange("p (g t) -> p g t", g=4)),
            reads=[("ps", b)], writes=[("wcT", g4)])
    emit_load_xT(P, cx, x_in, xT)
    if stage == "load":
        for tt in range(NT):
            emit_store_x(P, cx, xT, y_out, tt)
        return
    ada_flush(cx)
    cx.ada_bg = AdaStepper(P, cx, ada_w1, cx.cols[:, 32:56], cx.mod[1], 1)
    f1 = fence_tokens(P, [("stg", 0), ("stg", 1), "tril", "identb", "ws_f", "ws_b"])
    for e_ in ("dve", "act", "pool"):
        P.wait_all(e_, f1)

    def mix_tile(tt):
        sl = slice(tt * TT, (tt + 1) * TT)
        pi = tt % 2
        hT = hT1[pi]
        for c in range(8):
            P.op("dve", lambda h, c=c, hT=hT: h.tensor_scalar(
                hT[:, c, :], xT[:, c, sl], cx.mod[0][:, 8 + c:9 + c], cx.mod[0][:, c:c + 1], ALU.mult, ALU.add),
                reads=[("x", c, tt), ("mod", 0)], writes=[("h1", pi, c)])
        for tb in range(4):
            vi = (tt * 4 + tb) % 2
            vgb = vg[vi]
            for half in range(2):
                b, ps = cx.psum.next()
                fns = [(lambda h, kc=kc, half=half, ps=ps, hT=hT, tb=tb: h.matmul(
                    ps[:], hT[:, kc, tb * 128:(tb + 1) * 128],
                    win[:, kc, 1024 + half * 512:1024 + (half + 1) * 512], start=(kc == 0), stop=(kc == 7)))
                    for kc in range(8)]
                P.mm_group(fns, reads=[("win", 2 + half)] + [("h1", pi, kc) for kc in range(8)],
                           writes=[("ps", b)])
                if half == 0:
                    ada_tick(cx)
                P.op("dve", lambda h, ps=ps, half=half, vgb=vgb: h.tensor_tensor(
                    vgb[:, half * 512:(half + 1) * 512], ps[:], rows[:, 0, half * 512:(half + 1) * 512], ALU.add),
                    reads=[("ps", b), "rows"], writes=[("vg", vi, half)])
                P.op("act", lambda h, half=half, vgb=vgb: h.activation(
                    vgb[:, half * 512:(half + 1) * 512], vgb[:, half * 512:(half + 1) * 512], GELU),
                    reads=[("vg", vi, half)], writes=[("vg", vi, half)])
                P.op("dve", lambda h, half=half, vgb=vgb: h.bn_stats(
                    vst[:, half, :], vgb[:, half * 512:(half + 1) * 512]),
                    reads=[("vg", vi, half)], writes=[("vst", half)])
            P.op("dve", lambda h: h.bn_aggr(vmv[:], vst[:].rearrange("p a b -> p (a b)")),
                 reads=[("vst", 0), ("vst", 1)], writes=["vmv"])
            P.op("dve", lambda h: h.tensor_scalar(vrs[:], vmv[:, 1:2], LN_EPS, None, ALU.add),
                 reads=["vmv"], writes=["vrs"])
            P.op("dve", lambda h: h.reciprocal(vrs[:], vrs[:]), reads=["vrs"], writes=["vrs"])
            P.op("act", lambda h: h.activation(vrs[:], vrs[:], AF.Sqrt), reads=["vrs"], writes=["vrs"])
            P.op("dve", lambda h, vgb=vgb: h.tensor_scalar(
                vgb[:], vgb[:], vmv[:, 0:1], vrs[:, 0:1], ALU.subtract, ALU.mult),
                reads=[("vg", vi, 0), ("vg", vi, 1), "vmv", "vrs"], writes=[("vg", vi, 0), ("vg", vi, 1)])
            P.op("pool", lambda h, vgb=vgb: h.tensor_tensor(vgb[:], vgb[:], rows[:, 1, :], ALU.mult),
                 reads=[("vg", vi, 0), ("vg", vi, 1), "rows"], writes=[("vg", vi, 0), ("vg", vi, 1)])
            P.op("pool", lambda h, vgb=vgb, tb=tb: h.tensor_tensor(vtok[pi][:, tb, :], vgb[:], rows[:, 2, :], ALU.add),
                 reads=[("vg", vi, 0), ("vg", vi, 1), "rows"], writes=[("vtok", 0, tb)])
        for fc in range(8):
            b, ps = cx.psum.next()
            fns = [(lambda h, kc=kc, fc=fc, ps=ps, hT=hT: h.matmul(
                ps[:], win[:, kc, fc * 128:(fc + 1) * 128], hT[:, kc, :], start=(kc == 0), stop=(kc == 7)))
                for kc in range(8)]
            P.mm_group(fns, reads=[("win", fc // 4)] + [("h1", pi, kc) for kc in range(8)], writes=[("ps", b)])
            if fc % 2 == 0:
                ada_tick(cx)
            P.op("act", lambda h, fc=fc, ps=ps: h.activation(
                uT[pi][:, fc, :], ps[:], GELU, bias=cx.cols[:, A_COLS["binu"] + fc:A_COLS["binu"] + fc + 1]),
                reads=[("ps", b), "cols"], writes=[("u", 0, fc)])
        if tt >= 1:
            emit_ln_p2(P, cx, xT, tt - 1, cx.cols[:, A_COLS["lng0"]:A_COLS["lng0"] + 8],
                       cx.cols[:, A_COLS["lnb0"]:A_COLS["lnb0"] + 8])
        for fc in range(8):
            b, ps = cx.psum.next()
            fns = []
            for tb in range(4):
                for gi in range(2):
                    g = 2 * fc + gi
                    fns.append(lambda h, tb=tb, gi=gi, g=g, ps=ps: h.matmul(
                        ps[gi * 64:(gi + 1) * 64, tb * 128:(tb + 1) * 128],
                        vtok[pi][:, tb, g * 64:(g + 1) * 64], wcT[:, g, :], start=True, stop=True))
            P.mm_group(fns, reads=[("vtok", 0, tb) for tb in range(4)] + [("wcT", (2 * fc) // 4)],
                       writes=[("ps", b)])
            k = fc % 2
            P.op("dve", lambda h, ps=ps, fc=fc, k=k: h.tensor_tensor(
                gt[k][:].rearrange("p (a t) -> p a t", a=4), ps[:].rearrange("p (a t) -> p a t", a=4),
                bbias[:, fc, None, :].to_broadcast([128, 4, 128]), ALU.add),
                reads=[("ps", b), "bbias"], writes=[("gt", k)])
            P.op("pool", lambda h, fc=fc, k=k: h.tensor_tensor(uT[pi][:, fc, :], gt[k][:], uT[pi][:, fc, :], ALU.mult),
                 reads=[("gt", k), ("u", 0, fc)], writes=[("u", 0, fc)])
        for oc in range(8):
            b, ps = cx.psum.next()
            fns = [(lambda h, fc=fc, oc=oc, ps=ps: h.matmul(
                ps[:], wout[:, fc, oc * 128:(oc + 1) * 128], uT[pi][:, fc, :], start=(fc == 0), stop=(fc == 7)))
                for fc in range(8)]
            P.mm_group(fns, reads=["wout"] + [("u", 0, fc) for fc in range(8)], writes=[("ps", b)])
            P.op("dve", lambda h, ps=ps, oc=oc: h.scalar_tensor_tensor(
                xT[:, oc, sl], ps[:], cx.mod[0][:, 16 + oc:17 + oc], xT[:, oc, sl], ALU.mult, ALU.add),
                reads=[("ps", b), ("mod", 0), ("x", oc, tt)], writes=[("x", oc, tt)])
        emit_ln_p1(P, cx, xT, tt)

    for tt in range(NT):
        mix_tile(tt)
    emit_ln_p2(P, cx, xT, NT - 1, cx.cols[:, A_COLS["lng0"]:A_COLS["lng0"] + 8],
               cx.cols[:, A_COLS["lnb0"]:A_COLS["lnb0"] + 8])

    if stage == "mixA":
        dbg = nc.dram_tensor("dbg", [128, 16384], F32, kind="ExternalOutput").ap()
        dsl = P.slot()
        allk = list(P.last_w.keys())
        cx.out_toks.append(P.dma("sp", [(dbg[:, 0:24], cx.mod[0][:])], dsl, reads=allk))
        cx.out_toks.append(P.dma("pool", [(dbg[:, 1024:3072], wcT[:].rearrange("p g t -> p (g t)"))], dsl, reads=allk))
        cx.out_toks.append(P.dma("pool", [(dbg[:, 4096:8192], hT1[1][:].rearrange("p g t -> p (g t)"))], dsl, reads=allk))
        cx.out_toks.append(P.dma("pool", [(dbg[:, 8192:12288], uT1[:].rearrange("p g t -> p (g t)"))], dsl, reads=allk))
        cx.out_toks.append(P.dma("pool", [(dbg[:, 12288:16384], vtok1[:].rearrange("p g t -> p (g t)"))], dsl, reads=allk))
        for e_ in ("pe", "act", "dve", "pool"):
            P.wait_all(e_, list(cx.out_toks))
        for tt in range(NT):
            emit_store_x(P, cx, xT, y_out, tt)
        return
    ada_flush(cx)
    if fused:
        cx.ada_bg = AdaStepper(P, cx, io["ada_w10"], cx.cols[:, A_NCOLS + 8:A_NCOLS + 32], cx.mod[0], 0)
        cx.pre_ada_b0 = True
    p1_keys = [("win", j) for j in range(4)] + ["wout", "rows", "bbias", "vst", "vmv", "vrs"] + \
              [("wcT", g4) for g4 in range(4)] + [("h1", p, c) for p in range(2) for c in range(8)] + \
              [("u", 0, c) for c in range(8)] + [("vtok", 0, tb) for tb in range(4)] + \
              [("vg", i, hf) for i in range(2) for hf in range(2)] + [("gt", 0), ("gt", 1)]
    f2 = fence_tokens(P, p1_keys)
    for e_ in ("dve", "act", "pool"):
        P.wait_all(e_, f2)
    for tt in range(NT):
        emit_modulate(P, cx, xT, h2T, cx.mod[1], 1, tt)

    def after_ln(tt):
        emit_store_x(P, cx, xT, y_out, tt)

    if fused:
        cx.cc_sem = P.new_sem()
        snd_slot = P.slot()
        cx.spill_slot = P.slot()
        cx.spill_toks = []

    def send_tile(tt):
        ada_flush(cx)
        modb = cx.mod[0]
        sl = slice(tt * TT, (tt + 1) * TT)
        for c in range(8):
            if c % 2:
                P.op("act", lambda h, c=c: h.activation(hst[:, c, :], xT[:, c, sl], AF.Identity,
                                                        bias=modb[:, c:c + 1], scale=modb[:, 8 + c:9 + c]),
                     reads=[("x", c, tt), ("mod", 0)], writes=[("hst", c)])
            else:
                P.op("dve", lambda h, c=c: h.tensor_scalar(hst[:, c, :], xT[:, c, sl], modb[:, 8 + c:9 + c],
                                                           modb[:, c:c + 1], ALU.mult, ALU.add),
                     reads=[("x", c, tt), ("mod", 0)], writes=[("hst", c)])
        snd, rcv = io["snd"], io["rcv"]
        t_snd = P.dma("sp", [(snd[tt * 2 + hf], hst[:, hf * 4:(hf + 1) * 4, :].rearrange("p c t -> p (c t)"))
                             for hf in range(2)], snd_slot, reads=[("hst", c) for c in range(8)])
        xsp_ = io["xsp"]
        cx.spill_toks.append(P.dma("sp", [(xsp_[:, c * T + tt * TT:c * T + (tt + 1) * TT], xT[:, c, sl]) for c in range(8)],
                                   cx.spill_slot, reads=[("x", c, tt) for c in range(8)]))
        ep = P.E["pool"]
        w_ = P._deps("pool", (), (), [t_snd])
        for hf in range(2):
            ep.ops.append((w_ if hf == 0 else [],
                           (lambda h, i=tt * 2 + hf: h.collective_compute(
                               "AllGather", ALU.bypass, replica_groups=[[0, 1], [2, 3], [4, 5], [6, 7]],
                               ins=[snd[i].opt()], outs=[rcv[i].opt()])),
                           (cx.cc_sem, 1)))

    emit_mlp(P, cx, xT, h2T, cx.mod[1], 1, w_up, w_down,
             cx.cols[:, A_COLS["lng1"]:A_COLS["lng1"] + 8], cx.cols[:, A_COLS["lnb1"]:A_COLS["lnb1"] + 8],
             after_ln=(send_tile if fused else after_ln))
    ada_flush(cx)


def build_A(stage=None):
    nc = bass.Bass("TRN2", target_bir_lowering=False)
    dt = lambda n, s: nc.dram_tensor(n, s, F32, kind="ExternalInput").ap()
    x_in = dt("x", [T, D])
    cols_in = dt("cols", [128, A_NCOLS])
    rows_in = dt("rows", [128, 3, 1024])
    bb_in = dt("bbias", [128, 8, 128])
    consts_in = dt("consts", [128, 3, 128])
    ada_w0 = dt("ada_w0", [D, 3 * D])
    ada_w1 = dt("ada_w1", [D, 3 * D])
    w_in = dt("a_w_in", [D, 2 * D])
    w_s = dt("a_w_s", [16, 128, 128])
    w_out = dt("a_w_out", [D, D])
    w_up = dt("mlp_w_up", [D, 4 * D])
    w_down = dt("mlp_w_down", [4 * D, D])
    y_out = nc.dram_tensor("y", [T, D], F32, kind="ExternalOutput").ap()

    with contextlib.ExitStack() as es:
        P = Prog(nc, es)
        A = Arena(nc)
        cx = Ctx()
        common_setup(nc, P, A, cx, A_NCOLS)
        xT, _, _ = A.alloc([128, 8, T], F32)
        io = dict(x=x_in, cols=cols_in, rows=rows_in, bbias=bb_in, constsA=consts_in, ada_w00=ada_w0, ada_w01=ada_w1,
                  a_w_in=w_in, a_w_s=w_s, a_w_out=w_out, mlp_w_up0=w_up, mlp_w_down0=w_down, y=y_out)
        body_A(nc, P, A, cx, io, xT, stage=stage, fused=False)
        P.wait_all("sp", cx.out_toks)
        P.finish()
    return nc


def host_consts():
    ident = np.eye(128, dtype=np.float32)
    ones = np.full((128, 128), 1.0 / 1024.0, dtype=np.float32)
    tril = np.tril(np.ones((128, 128), dtype=np.float32))
    return np.ascontiguousarray(np.stack([ident, ones, tril], axis=1))


def prep_A(inp):
    x = np.asarray(inp["x"], dtype=np.float32)
    c = np.asarray(inp["c"], dtype=np.float32)
    maps = []
    consts = host_consts()
    rows = np.stack([inp["a_b_in"][0][1024:], inp["a_vn_g"][0], inp["a_vn_b"][0]], axis=0).astype(np.float32)
    rows = np.ascontiguousarray(np.broadcast_to(rows[None], (128, 3, 1024)))
    bs = np.asarray(inp["a_b_s"][0], dtype=np.float32)
    bb = np.zeros((128, 8, 128), dtype=np.float32)
    for fc in range(8):
        bb[0:64, fc, :] = bs[2 * fc][None, :]
        bb[64:128, fc, :] = bs[2 * fc + 1][None, :]
    for core in range(NCORES):
        b, half = core // 2, core % 2
        cols = np.zeros((128, A_NCOLS), dtype=np.float32)
        cols[:, 0:8] = col(c[b])
        cols[:, 8:32] = col(inp["ada_b"][0, 0])
        cols[:, 32:56] = col(inp["ada_b"][0, 1])
        cols[:, 56:64] = col(inp["ln_g"][0, 0])
        cols[:, 64:72] = col(inp["ln_b"][0, 0])
        cols[:, 72:80] = col(inp["ln_g"][0, 1])
        cols[:, 80:88] = col(inp["ln_b"][0, 1])
        cols[:, 88:96] = col(inp["a_b_in"][0][:1024])
        maps.append({
            "x": np.ascontiguousarray(x[b, half * T:(half + 1) * T]),
            "cols": cols, "rows": rows, "bbias": bb, "consts": consts,
            "ada_w0": np.ascontiguousarray(inp["ada_w"][0, 0]), "ada_w1": np.ascontiguousarray(inp["ada_w"][0, 1]),
            "a_w_in": np.ascontiguousarray(inp["a_w_in"][0]), "a_w_s": np.ascontiguousarray(inp["a_w_s"][0]),
            "a_w_out": np.ascontiguousarray(inp["a_w_out"][0]),
            "mlp_w_up": np.ascontiguousarray(inp["mlp_w_up"][0]), "mlp_w_down": np.ascontiguousarray(inp["mlp_w_down"][0]),
        })
    return maps


def run_A(inp, trace=False, stage=None, ncores=NCORES):
    nc = build_A(stage)
    maps = prep_A(inp)[:ncores]
    res = run_bass_kernel_spmd(nc, maps, core_ids=list(range(ncores)), trace=trace)
    x1 = np.zeros((4, 4096, D), dtype=np.float32)
    for core in range(ncores):
        b, half = core // 2, core % 2
        x1[b, half * T:(half + 1) * T] = res.results[core]["y"]
    if stage == "mixA":
        np.save("dbg0.npy", res.results[0]["dbg"])
    return x1, res


B_NCOLS = 96
PATTERNS = ((128, 1), (512, 4), (2048, 16))


def host_consts_B():
    ident = np.eye(128, dtype=np.float32)
    ones = np.full((128, 128), 1.0 / 1024.0, dtype=np.float32)
    k = np.arange(128)[:, None]
    q = np.arange(128)[None, :]
    d_prev = np.where(k >= q, 128.0 + q - k, 0.0).astype(np.float32)
    d_cur = np.where(k <= q, (q - k) * 1.0, 0.0).astype(np.float32)
    m_prev = np.where(k >= q, 0.0, -240000.0).astype(np.float32)
    m_cur = np.where(k <= q, 0.0, -240000.0).astype(np.float32)
    m = np.arange(128)[None, :]
    permA = ((k == m + 64) & (m < 64)).astype(np.float32)
    permB = ((k == m - 64) & (m >= 64)).astype(np.float32)
    return np.ascontiguousarray(np.stack([ident, ones, d_prev, d_cur, permA, permB, m_prev, m_cur], axis=1))


def body_B(nc, P, A, cx, io, xT, xT_at, stage=None, fused=False, cb=0):
    x_in, xp_in, cols_in, consts_in = io.get("x"), io.get("xp"), io["cols"], io["constsB"]
    ada_w0, ada_w1, w_qkv, w_out = io["ada_w10"], io["ada_w11"], io["b_w_qkv"], io["b_w_out"]
    w_up, w_down, y_out, xsp = io["mlp_w_up1"], io["mlp_w_down1"], io["y"], io["xsp"]
    mark = xT_at
    r1_end = xT_at + 65536
    save_off = A.off
    A.off = mark
    kt0, _, _ = A.alloc([128, 32 * 128], BF16)
    KT = [kt0, kt0]
    VT, _, _ = A.alloc([128, 32 * 128], BF16)
    VA, _, _ = A.alloc([128, 32, 2, 128], BF16)
    ACC, _, _ = A.alloc([128, 2, T], F32)
    QT = [A.alloc([128, T], BF16)[0] for _ in range(2)]
    tmpB = [A.alloc([128, 256], F32)[0] for _ in range(2)]
    Pt = [A.alloc([128, 512], BF16)[0] for _ in range(2)]
    assert A.off <= r1_end, (A.off, r1_end)
    A.off = max(r1_end, save_off)
    mark2 = A.off
    hT, _, _ = A.alloc([128, 8, 2 * T], BF16)
    oT, oT_at, _ = A.alloc([128, 8, T], BF16)
    wqkv = [A.alloc([128, 3, 8, 128], BF16)[0] for _ in range(2)]
    Bhl = [A.alloc([128, 2, 2, 256], BF16)[0] for _ in range(2)]
    rden, _, _ = A.alloc([128, 512], F32)
    identb, _, _ = A.alloc([128, 128], BF16)
    diffm, _, _ = A.alloc([128, 2, 128], F32)
    perm, _, _ = A.alloc([128, 2, 128], F32)
    mask01, _, _ = A.alloc([128, 2, 128], F32)
    endB1 = A.off
    A.off = oT_at
    cx.stg = [A.alloc([128, 4, 1024], F32)[0] for _ in range(2)]
    assert A.off <= oT_at + 32768
    A.off = mark2
    wout, _, _ = A.alloc([128, 8, 1024], BF16)
    A.off = mark2
    h2T, _, _ = A.alloc([128, 8, T], BF16)
    cx.wu_buf = [A.alloc([128, 8, 1024], BF16)[0] for _ in range(2)]
    cx.wd_buf = [A.alloc([128, 8, 1024], BF16)[0] for _ in range(2)]
    cx.a_buf = [A.alloc([128, 8, TT], BF16)[0] for _ in range(2)]
    cx.ost = [A.alloc([128, 1024], F32)[0] for _ in range(2)]
    endB2 = A.off
    print("SBUF plan B: mark", mark, "r1_end", r1_end, "attn end", endB1, "mlp end", endB2)
    assert max(endB1, endB2) <= A.cap

    flag = cx.cols[:, cb + 88:cb + 89]
    if not fused:
        P.dma("sp", [(cx.cols[:], cols_in)], P.slot(), writes=["cols"])
        P.dma("sp", [(cx.ident_f[:], consts_in[:, 0, :]), (cx.onesm[:], consts_in[:, 1, :])], P.slot(),
              writes=["ident", "onesm"])
        P.op("act", lambda h: h.activation(cx.sc[:], cx.cols[:, 0:8], AF.Silu), reads=["cols"], writes=["sc"])
    P.dma("sp", [(diffm[:], consts_in[:, 2:4, :]), (perm[:], consts_in[:, 4:6, :]),
                 (mask01[:], consts_in[:, 6:8, :])], P.slot(), writes=["diffm", "perm"])
    P.op("dve", lambda h: h.tensor_copy(identb[:], cx.ident_f[:]), reads=["ident"], writes=["identb"])
    if not getattr(cx, "pre_ada_b0", False):
        cx.ada_bg = AdaStepper(P, cx, ada_w0, cx.cols[:, cb + 8:cb + 32], cx.mod[0], 0)
        ada_flush(cx)
    mod0 = cx.mod[0]
    cx.ada_bg = AdaStepper(P, cx, ada_w1, cx.cols[:, cb + 32:cb + 56], cx.mod[1], 1)

    xpv = xp_in.rearrange("(tb p) f -> p tb f", p=128) if xp_in is not None else None

    def load_prev_tile(tt):
        si = cx.stg_i % 2
        cx.stg_i += 1
        stg = cx.stg[si]
        P.dma("sp", [(stg[:, 0:2, :], xpv[:, tt * 4:tt * 4 + 2, :]),
                     (stg[:, 2:4, :], xpv[:, tt * 4 + 2:tt * 4 + 4, :])], cx.stg_slot[si],
              writes=[("stg", si)])
        for c in range(8):
            b, ps = cx.psum.next()
            fns = [(lambda h, tb=tb, c=c, ps=ps, stg=stg: h.matmul(
                ps[:, tb * 128:(tb + 1) * 128], stg[:, tb, c * 128:(c + 1) * 128], cx.ident[:],
                start=True, stop=True)) for tb in range(4)]
            P.mm_group(fns, reads=[("stg", si), "ident"], writes=[("ps", b)])
            P.op("dve", lambda h, ps=ps, c=c, tt=tt: h.tensor_scalar(
                hT[:, c, tt * TT:(tt + 1) * TT], ps[:], mod0[:, 8 + c:9 + c], mod0[:, c:c + 1], ALU.mult, ALU.add),
                reads=[("ps", b), ("mod", 0)], writes=[("hall", c, tt)])

    if not fused:
        for tt in range(NT):
            load_prev_tile(tt)
        emit_load_xT(P, cx, x_in, xT)

    def mod_own(tt):
        for c in range(8):
            P.op("act" if c % 2 else "dve",
                 (lambda h, c=c, tt=tt: h.activation(hT[:, c, T + tt * TT:T + (tt + 1) * TT], xT[:, c, tt * TT:(tt + 1) * TT],
                                                     AF.Identity, bias=mod0[:, c:c + 1], scale=mod0[:, 8 + c:9 + c]))
                 if c % 2 else
                 (lambda h, c=c, tt=tt: h.tensor_scalar(hT[:, c, T + tt * TT:T + (tt + 1) * TT], xT[:, c, tt * TT:(tt + 1) * TT],
                                                        mod0[:, 8 + c:9 + c], mod0[:, c:c + 1], ALU.mult, ALU.add)),
                 reads=[("x", c, tt), ("mod", 0)], writes=[("hall", c, NT + tt)])

    for tt in range(NT):
        mod_own(tt)
    if fused:
        rcv = io["rcv"]
        r_slot = P.slot()
        P.dma("sp", [(hT[:, hf * 4:(hf + 1) * 4, tt * TT:(tt + 1) * TT],
                      rcv[tt * 2 + hf][0:128, :].rearrange("p (c t) -> p c t", c=4))
                     for tt in range(NT) for hf in range(2)], r_slot,
              writes=[("hall", c, tt) for c in range(8) for tt in range(NT)], extra=[(cx.cc_sem, 8)])
    sp_slot = P.slot()
    xkeys = [("x", c, tt) for c in range(8) for tt in range(NT)]
    if fused:
        t_spills = list(cx.spill_toks)
    else:
        t_spills = [P.dma("sp", [(xsp[:, c * T:(c + 1) * T], xT[:, c, :]) for c in range(8)], sp_slot, reads=xkeys)]
    fsp = fence_tokens(P, xkeys) + t_spills
    for e_ in ("pe", "act", "dve", "pool"):
        P.wait_all(e_, fsp)
    P.op("pool", lambda h: h.memset(VA[:], 1.0), writes=["va_init"])
    P.op("act", lambda h: h.activation(VA[:, 0:16, 0, 64:128], VA[:, 0:16, 0, 64:128], AF.Identity, scale=flag),
         reads=["va_init", "cols"], writes=["va_init"])
    P.op("act", lambda h: h.activation(VA[:, 0:16, 1, 0:64], VA[:, 0:16, 1, 0:64], AF.Identity, scale=flag),
         reads=["va_init", "cols"], writes=["va_init"])
    va_tok = P.last_w["va_init"]

    wq_v = w_qkv.rearrange("(kc p) n -> p kc n", p=128)
    wq_slot = [P.slot() for _ in range(2)]
    st = {"i": 0, "e": 0, "p": 0}
    hall_keys_all = [("hall", c, t8) for c in range(8) for t8 in range(2 * NT)]

    def perm_view(buf2d, col0, d, tt, ps):
        if d == 1:
            return buf2d[:, col0 + tt * 512:col0 + (tt + 1) * 512], ps[:]
        if d == 4:
            dst = buf2d[:, col0 + tt * 512:col0 + (tt + 1) * 512].rearrange("p (r i) -> p i r", r=4)
            return dst, ps[:].rearrange("p (i r) -> p i r", r=4)
        dst = buf2d[:, col0:col0 + 2048].rearrange("p (r i) -> p i r", r=16)[:, tt * 32:(tt + 1) * 32, :]
        return dst, ps[:].rearrange("p (i r) -> p i r", r=16)

    import os as _os2
    _skip = set(_os2.environ.get("DBG_SKIP", "").split(","))

    def stage_prologue(hp, g, si):
        d = PATTERNS[g][1]
        wb = wqkv[si]
        bh = Bhl[si]
        pairs = []
        for t3 in range(3):
            c0 = ((g * 3 + t3) * 16 + 2 * hp) * 64
            pairs.append((wb[:, t3, :, :], wq_v[:, :, c0:c0 + 128]))
        P.dma("pool", pairs, wq_slot[si], writes=[("wqkv", si)])
        for hh in range(2):
            slope = 2.0 ** (-8.0 * (2 * hp + hh + 1) / 16.0)
            tb = tmpB[hh]
            dm = diffm[:].rearrange("p a q -> p (a q)")
            mk = mask01[:].rearrange("p a q -> p (a q)")
            P.op("pool", lambda h, tb=tb, slope=slope: h.tensor_scalar(tb[:], dm, -8.0 * slope * d, None, ALU.mult),
                 reads=["diffm"], writes=[("tmpB", hh)])
            P.op("pool", lambda h, tb=tb: h.tensor_tensor(tb[:], tb[:], mk, ALU.add),
                 reads=["diffm", ("tmpB", hh)], writes=[("tmpB", hh)])
            P.op("pool", lambda h, tb=tb, hh=hh: h.tensor_copy(bh[:, 0, hh, :], tb[:]),
                 reads=[("tmpB", hh)], writes=[("bhl", si, hh, 0)])
            P.op("pool", lambda h, tb=tb, hh=hh: h.tensor_tensor(bh[:, 1, hh, :], tb[:], bh[:, 0, hh, :], ALU.subtract),
                 reads=[("tmpB", hh), ("bhl", si, hh, 0)], writes=[("bhl", si, hh, 1)])

    def attn_stage(hp, g, si):
        d = PATTERNS[g][1]
        wb = wqkv[si]
        kt = KT[si]
        qt = QT[si]
        bh = Bhl[si]
        for tt in range(0 if "Q" in _skip else NT):
            b, ps = cx.psum.next()
            fns = [(lambda h, kc=kc, ps=ps, tt=tt: h.matmul(
                ps[:], wb[:, 0, kc, :], hT[:, kc, T + tt * TT:T + (tt + 1) * TT], start=(kc == 0), stop=(kc == 7)))
                for kc in range(8)]
            P.mm_group(fns, reads=[("wqkv", si)] + [("hall", kc, NT + tt) for kc in range(8)], writes=[("ps", b)])
            if tt % 2 == 0:
                ada_tick(cx)
            dst, src = perm_view(qt, 0, d, tt, ps)
            P.op("act", lambda h, dst=dst, src=src: h.activation(dst, src, AF.Identity),
                 reads=[("ps", b)], writes=[("qt", si)])
        for tt in range(0 if "K" in _skip else NT):
            b, ps = cx.psum.next()
            fns = [(lambda h, kc=kc, ps=ps, tt=tt: h.matmul(
                ps[:], wb[:, 1, kc, :], hT[:, kc, T + tt * TT:T + (tt + 1) * TT], start=(kc == 0), stop=(kc == 7)))
                for kc in range(8)]
            P.mm_group(fns, reads=[("wqkv", si)] + [("hall", kc, NT + tt) for kc in range(8)], writes=[("ps", b)])
            dst, src = perm_view(kt, 2048, d, tt, ps)
            P.op("dve", lambda h, dst=dst, src=src: h.tensor_copy(dst, src),
                 reads=[("ps", b)], writes=[("kt", 0)])
        if "K" in _skip:
            pass
        elif d == 1:
            b, ps = cx.psum.next()
            fns = [(lambda h, kc=kc, ps=ps: h.matmul(
                ps[:, 0:128], wb[:, 1, kc, :], hT[:, kc, T - 128:T], start=(kc == 0), stop=(kc == 7)))
                for kc in range(8)]
            P.mm_group(fns, reads=[("wqkv", si)] + [("hall", kc, NT - 1) for kc in range(8)], writes=[("ps", b)])
            P.op("dve", lambda h, ps=ps: h.tensor_copy(kt[:, 15 * 128:16 * 128], ps[:, 0:128]),
                 reads=[("ps", b)], writes=[("kt", 0)])
        elif d == 4:
            b, ps = cx.psum.next()
            fns = [(lambda h, kc=kc, ps=ps: h.matmul(
                ps[:], wb[:, 1, kc, :], hT[:, kc, T - 512:T], start=(kc == 0), stop=(kc == 7)))
                for kc in range(8)]
            P.mm_group(fns, reads=[("wqkv", si)] + [("hall", kc, NT - 1) for kc in range(8)], writes=[("ps", b)])
            dst = kt[:, 12 * 128:16 * 128].rearrange("p (r i) -> p i r", r=4)
            src = ps[:].rearrange("p (i r) -> p i r", r=4)
            P.op("dve", lambda h, dst=dst, src=src: h.tensor_copy(dst, src),
                 reads=[("ps", b)], writes=[("kt", 0)])
        else:
            for tt in range(NT):
                b, ps = cx.psum.next()
                fns = [(lambda h, kc=kc, ps=ps, tt=tt: h.matmul(
                    ps[:], wb[:, 1, kc, :], hT[:, kc, tt * TT:(tt + 1) * TT], start=(kc == 0), stop=(kc == 7)))
                    for kc in range(8)]
                P.mm_group(fns, reads=[("wqkv", si)] + [("hall", kc, tt) for kc in range(8)], writes=[("ps", b)])
                dst, src = perm_view(kt, 0, d, tt, ps)
                P.op("dve", lambda h, dst=dst, src=src: h.tensor_copy(dst, src),
                     reads=[("ps", b)], writes=[("kt", 0)])
        def vproj(tok0, ntok, col0, mode, tt, hkeys):
            b, ps = cx.psum.next()
            fns = [(lambda h, kc=kc, ps=ps: h.matmul(
                ps[:, 0:ntok], wb[:, 2, kc, :], hT[:, kc, tok0:tok0 + ntok], start=(kc == 0), stop=(kc == 7)))
                for kc in range(8)]
            P.mm_group(fns, reads=[("wqkv", si)] + hkeys, writes=[("ps", b)])
            if mode == "plain":
                dst, src = VT[:, col0:col0 + ntok], ps[:, 0:ntok]
            elif mode == "seg4":
                dst = VT[:, col0:col0 + 512].rearrange("p (r i) -> p i r", r=4)
                src = ps[:].rearrange("p (i r) -> p i r", r=4)
            else:
                dst, src = perm_view(VT, col0, d, tt, ps)
            P.op("dve", lambda h, dst=dst, src=src: h.tensor_copy(dst, src),
                 reads=[("ps", b)], writes=["vt"])

        for tt in range(0 if "V" in _skip else NT):
            hk = [("hall", kc, NT + tt) for kc in range(8)]
            if d == 1:
                vproj(T + tt * TT, TT, 2048 + tt * 512, "plain", tt, hk)
            elif d == 4:
                vproj(T + tt * TT, TT, 2048 + tt * 512, "seg4", tt, hk)
            else:
                vproj(T + tt * TT, TT, 2048, "perm", tt, hk)
        if "V" in _skip:
            pass
        elif d == 1:
            vproj(T - 128, 128, 15 * 128, "plain", 0, [("hall", kc, NT - 1) for kc in range(8)])
        elif d == 4:
            vproj(T - 512, 512, 12 * 128, "seg4", 0, [("hall", kc, NT - 1) for kc in range(8)])
        else:
            for tt in range(NT):
                vproj(tt * TT, TT, 0, "perm", tt, [("hall", kc, tt) for kc in range(8)])
        groups = []
        prev_slots = list(range(16 - d, 16))
        own_slots = list(range(16, 32))
        for i0 in range(0, len(prev_slots), 4):
            groups.append(prev_slots[i0:i0 + 4])
        for i0 in range(0, 16, 4):
            groups.append(own_slots[i0:i0 + 4])
        if "T" in _skip or "V" in _skip:
            groups = []
        for grp in groups:
            b, ps = cx.psum.next()
            fns = [(lambda h, gi=gi, s=s, ps=ps: h.matmul(
                ps[:, gi * 128:(gi + 1) * 128], VT[:, s * 128:(s + 1) * 128], identb[:], start=True, stop=True))
                for gi, s in enumerate(grp)]
            P.mm_group(fns, reads=["vt", "identb"], writes=[("ps", b)], extra=[va_tok])
            ng = len(grp)
            s0 = grp[0]
            srcv = ps[:, 0:ng * 128].rearrange("p (s c) -> p s c", c=128)
            if "X" in _skip:
                continue
            if "XP" in _skip and s0 < 16:
                continue
            if "XO" in _skip and s0 >= 16:
                continue
            if "ALTVA" in _skip:
                s0_ = min(grp[0], 30)
                P.op("dve", lambda h, ps=ps, s0_=s0_: h.tensor_copy(
                    VA[:, s0_:s0_ + 2, :, :].rearrange("p a b c -> p (a b c)"), ps[:]),
                    reads=[("ps", b)], writes=[("va", 0)])
                continue
            if "F2D" in _skip:
                for gi, sl_ in enumerate(grp):
                    if "NO20" in _skip and sl_ == 20:
                        continue
                    P.op("act", lambda h, gi=gi, sl_=sl_, ps=ps: h.activation(
                        VA[:, sl_, 0, :], ps[:, gi * 128:(gi + 1) * 128], AF.Identity),
                        reads=[("ps", b)], writes=[("va", 0)])
                    P.op("dve", lambda h, gi=gi, sl_=sl_, ps=ps: h.tensor_copy(
                        VA[:, sl_, 1, :], ps[:, gi * 128:(gi + 1) * 128]),
                        reads=[("ps", b)], writes=[("va", 1)])
                continue
            if "FULL" in _skip:
                P.op("act", lambda h, srcv=srcv, s0=s0, ng=ng: h.activation(
                    VA[:, s0:s0 + ng, 0, :], srcv, AF.Identity),
                    reads=[("ps", b)], writes=[("va", 0)])
                P.op("dve", lambda h, srcv=srcv, s0=s0, ng=ng: h.tensor_copy(
                    VA[:, s0:s0 + ng, 1, :], srcv),
                    reads=[("ps", b)], writes=[("va", 1)])
                continue
            if s0 < 16:
                P.op("act", lambda h, srcv=srcv, s0=s0, ng=ng: h.activation(
                    VA[:, s0:s0 + ng, 0, 0:64], srcv[:, :, 0:64], AF.Identity, scale=flag),
                    reads=[("ps", b), "cols"], writes=[("va", 0)])
                P.op("dve", lambda h, srcv=srcv, s0=s0, ng=ng: h.tensor_scalar(
                    VA[:, s0:s0 + ng, 1, 64:128], srcv[:, :, 64:128], flag, None, ALU.mult),
                    reads=[("ps", b), "cols"], writes=[("va", 1)])
            else:
                P.op("act", lambda h, srcv=srcv, s0=s0, ng=ng: h.activation(
                    VA[:, s0:s0 + ng, 0, 0:64], srcv[:, :, 0:64], AF.Identity),
                    reads=[("ps", b)], writes=[("va", 0)])
                P.op("dve", lambda h, srcv=srcv, s0=s0, ng=ng: h.tensor_copy(
                    VA[:, s0:s0 + ng, 1, 64:128], srcv[:, :, 64:128]),
                    reads=[("ps", b)], writes=[("va", 1)])
        nun = 0 if stage == 'proj' else 16
        ust = {}

        def unit_front(j):
            pi = st["p"] % 2
            st["p"] += 1
            pt = Pt[pi]
            for hh in range(2):
                b, ps = cx.psum.next()
                fns = [lambda h, ps=ps, hh=hh: h.matmul(ps[:, 0:256], identb[:], bh[:, 0, hh, :], start=True, stop=False),
                       lambda h, ps=ps, hh=hh: h.matmul(ps[:, 0:256], identb[:], bh[:, 1, hh, :], start=False, stop=False)]
                for kb in range(2):
                    slot = 16 + j - d if kb == 0 else 16 + j
                    fns.append(lambda h, hh=hh, kb=kb, slot=slot, ps=ps, j=j: h.matmul(
                        ps[:, kb * 128:(kb + 1) * 128],
                        kt[hh * 64:(hh + 1) * 64, slot * 128:(slot + 1) * 128],
                        qt[hh * 64:(hh + 1) * 64, j * 128:(j + 1) * 128], start=False, stop=(kb == 1)))
                P.mm_group(fns, reads=[("kt", 0), ("qt", si), "identb", ("bhl", si, hh, 0), ("bhl", si, hh, 1)],
                           writes=[("ps", b)])
                P.op("act", lambda h, ps=ps, pt=pt, hh=hh: h.activation(
                    pt[:, hh * 256:(hh + 1) * 256], ps[:, 0:256], AF.Exp, scale=0.125),
                    reads=[("ps", b)], writes=[("pt", pi, hh)])
            ust[j] = (pi, pt)

        def unit_back(j):
            pi, pt = ust[j]
            n, r = j // d, j % d
            b2, ps2 = cx.psum.next()
            fns = []
            for hh in range(2):
                for kb in range(2):
                    slot = 16 + j - d if kb == 0 else 16 + j
                    fns.append(lambda h, hh=hh, kb=kb, slot=slot, ps2=ps2, pt=pt: h.matmul(
                        ps2[:, hh * 128:(hh + 1) * 128], VA[:, slot, hh, :],
                        pt[:, (hh * 2 + kb) * 128:(hh * 2 + kb + 1) * 128], start=(kb == 0), stop=(kb == 1)))
            P.mm_group(fns, reads=[("pt", pi, 0), ("pt", pi, 1), ("va", 0), ("va", 1)], writes=[("ps", b2)])
            start = n * 128 * d + r
            accv = ACC[:, :, start:start + 127 * d + 1:d]
            src2 = ps2[:, 0:256].rearrange("p (a q) -> p a q", a=2)
            wr = ["acc"] if j == nun - 1 else []
            if g == 0:
                P.op("dve", lambda h, accv=accv, src2=src2: h.tensor_copy(accv, src2),
                     reads=[("ps", b2)], writes=wr, extra=acc_dep)
            else:
                P.op("dve", lambda h, accv=accv, src2=src2: h.tensor_tensor(accv, src2, accv, ALU.add),
                     reads=[("ps", b2)], writes=wr, extra=acc_dep)

        acc_dep = fence_tokens(P, ["acc"])
        if nun:
            unit_front(0)
        for j in range(nun):
            if j + 1 < nun:
                unit_front(j + 1)
            unit_back(j)

    def finalize(hp):
        for tt in range(NT):
            sl = slice(tt * TT, (tt + 1) * TT)
            b, ps = cx.psum.next()
            fns = [lambda h, ps=ps, sl=sl: h.matmul(ps[:], perm[:, 0, :], ACC[:, 0, sl], start=True, stop=False),
                   lambda h, ps=ps, sl=sl: h.matmul(ps[:], perm[:, 1, :], ACC[:, 1, sl], start=False, stop=True)]
            P.mm_group(fns, reads=["acc", "perm"], writes=[("ps", b)])
            P.op("dve", lambda h, ps=ps: h.reciprocal(rden[:], ps[:]), reads=[("ps", b)], writes=["rden"])
            P.op("pool", lambda h, sl=sl: h.tensor_tensor(oT[0:64, hp, sl], ACC[0:64, 0, sl], rden[0:64, :], ALU.mult),
                 reads=["acc", "rden"], writes=[("o", hp, tt, 0)])
            P.op("pool", lambda h, sl=sl: h.tensor_tensor(oT[64:128, hp, sl], ACC[64:128, 1, sl], rden[64:128, :], ALU.mult),
                 reads=["acc", "rden"], writes=[("o", hp, tt, 1)])

    P.wait_all("pool", fence_tokens(P, [("stg", 0), ("stg", 1)]))
    nhp = {"pre": 0, "one": 1, "proj": 1}.get(stage, 8)
    import os as _os
    _gl = [int(x) for x in _os.environ.get("DBG_G", "0,1,2").split(",")]
    stages = [(hp, g) for hp in range(nhp) for g in _gl]
    if stages:
        stage_prologue(stages[0][0], stages[0][1], 0)
    for i_s, (hp, g) in enumerate(stages):
        if i_s + 1 < len(stages):
            stage_prologue(stages[i_s + 1][0], stages[i_s + 1][1], (i_s + 1) % 2)
        attn_stage(hp, g, i_s % 2)
        if stage != "proj" and g == _gl[-1]:
            finalize(hp)
    if stage in ("pre", "one", "proj"):
        fx = fence_tokens(P, ["vt", ("kt", 0), ("kt", 1), ("qt", 0), ("qt", 1), ("va", 0), ("va", 1), "acc", "va_init",
                              ("tmpB", 0), ("tmpB", 1), ("pt", 0, 0), ("pt", 0, 1), ("pt", 1, 0), ("pt", 1, 1)])
        P.wait_all("sp", fx)
        t_re = P.dma("sp", [(xT[:, c, :], xsp[:, c * T:(c + 1) * T]) for c in range(8)], P.slot(), writes=xkeys)
        for e_ in ("pe", "act", "dve", "pool"):
            P.wait_all(e_, fence_tokens(P, hall_keys_all + [("o", hp, tt, hf) for hp in range(nhp) for tt in range(NT) for hf in range(2)] + [("wqkv", 0), ("wqkv", 1)] + [("bhl", i, hh, z) for i in range(2) for hh in range(2) for z in range(2)]))
        for tt in range(NT):
            emit_store_x(P, cx, xT, y_out, tt)
        return

    att_keys = ["vt", ("kt", 0), ("kt", 1), ("qt", 0), ("qt", 1), ("va", 0), ("va", 1), "acc", "va_init",
                ("tmpB", 0), ("tmpB", 1), ("pt", 0, 0), ("pt", 0, 1), ("pt", 1, 0), ("pt", 1, 1)]
    fx = fence_tokens(P, att_keys)
    P.wait_all("sp", fx)
    x_slot = P.slot()
    P.dma("sp", [(xT[:, c, :], xsp[:, c * T:(c + 1) * T]) for c in range(8)], x_slot, writes=xkeys)
    fo = fence_tokens(P, hall_keys_all + [("wqkv", 0), ("wqkv", 1)])
    P.wait_all("pool", fo)
    P.dma("pool", [(wout[:], w_out.rearrange("(kc p) n -> p kc n", p=128))], P.slot(), writes=["wout"])

    def outproj_tile(tt):
        sl = slice(tt * TT, (tt + 1) * TT)
        for oc in range(8):
            b, ps = cx.psum.next()
            fns = [(lambda h, hp=hp, oc=oc, ps=ps: h.matmul(
                ps[:], wout[:, hp, oc * 128:(oc + 1) * 128], oT[:, hp, sl], start=(hp == 0), stop=(hp == 7)))
                for hp in range(8)]
            P.mm_group(fns, reads=["wout"] + [("o", hp, tt, hf) for hp in range(8) for hf in range(2)],
                       writes=[("ps", b)])
            P.op("dve", lambda h, ps=ps, oc=oc: h.scalar_tensor_tensor(
                xT[:, oc, sl], ps[:], mod0[:, 16 + oc:17 + oc], xT[:, oc, sl], ALU.mult, ALU.add),
                reads=[("ps", b), ("mod", 0), ("x", oc, tt)], writes=[("x", oc, tt)])
        emit_ln(P, cx, xT, tt, cx.cols[:, cb + 56:cb + 64], cx.cols[:, cb + 64:cb + 72])

    for tt in range(NT):
        outproj_tile(tt)

    if stage == "attn":
        for e_ in ("pe", "act", "dve", "pool"):
            P.wait_all(e_, fence_tokens(P, ["wout"] + [("o", hp, tt, hf) for hp in range(8) for tt in range(NT) for hf in range(2)]))
        for tt in range(NT):
            emit_store_x(P, cx, xT, y_out, tt)
        return

    ada_flush(cx)
    f2 = fence_tokens(P, ["wout"] + [("o", hp, tt, hf) for hp in range(8) for tt in range(NT) for hf in range(2)]
                      + [("bhl", i, hh, z) for i in range(2) for hh in range(2) for z in range(2)] + ["rden", "diffm", "perm", "identb", "vt"])
    for e_ in ("dve", "act", "pool"):
        P.wait_all(e_, f2)
    for tt in range(NT):
        emit_modulate(P, cx, xT, h2T, cx.mod[1], 1, tt)

    def after_ln(tt):
        emit_store_x(P, cx, xT, y_out, tt)

    emit_mlp(P, cx, xT, h2T, cx.mod[1], 1, w_up, w_down, cx.cols[:, cb + 72:cb + 80], cx.cols[:, cb + 80:cb + 88], after_ln=after_ln)


def build_B(stage=None):
    nc = bass.Bass("TRN2", target_bir_lowering=False)
    dt = lambda n, s: nc.dram_tensor(n, s, F32, kind="ExternalInput").ap()
    x_in = dt("x", [T, D])
    xp_in = dt("xp", [T, D])
    cols_in = dt("cols", [128, B_NCOLS])
    consts_in = dt("consts", [128, 8, 128])
    ada_w0 = dt("ada_w0", [D, 3 * D])
    ada_w1 = dt("ada_w1", [D, 3 * D])
    w_qkv = dt("b_w_qkv", [D, 9 * D])
    w_out = dt("b_w_out", [D, D])
    w_up = dt("mlp_w_up", [D, 4 * D])
    w_down = dt("mlp_w_down", [4 * D, D])
    y_out = nc.dram_tensor("y", [T, D], F32, kind="ExternalOutput").ap()
    xsp = nc.dram_tensor("xsp", [128, 8 * T], F32).ap()

    with contextlib.ExitStack() as es:
        P = Prog(nc, es)
        A = Arena(nc)
        cx = Ctx()
        common_setup(nc, P, A, cx, B_NCOLS)
        xT, xT_at, _ = A.alloc([128, 8, T], F32)
        io = dict(x=x_in, xp=xp_in, cols=cols_in, constsB=consts_in, ada_w10=ada_w0, ada_w11=ada_w1, b_w_qkv=w_qkv,
                  b_w_out=w_out, mlp_w_up1=w_up, mlp_w_down1=w_down, y=y_out, xsp=xsp)
        body_B(nc, P, A, cx, io, xT, xT_at, stage=stage, fused=False, cb=0)
        P.wait_all("sp", cx.out_toks)
        P.finish()
    return nc


def prep_B(inp, x1):
    c = np.asarray(inp["c"], dtype=np.float32)
    consts = host_consts_B()
    maps = []
    zeros = np.zeros((T, D), dtype=np.float32)
    for core in range(NCORES):
        b, half = core // 2, core % 2
        cols = np.zeros((128, B_NCOLS), dtype=np.float32)
        cols[:, 0:8] = col(c[b])
        cols[:, 8:32] = col(inp["ada_b"][1, 0])
        cols[:, 32:56] = col(inp["ada_b"][1, 1])
        cols[:, 56:64] = col(inp["ln_g"][1, 0])
        cols[:, 64:72] = col(inp["ln_b"][1, 0])
        cols[:, 72:80] = col(inp["ln_g"][1, 1])
        cols[:, 80:88] = col(inp["ln_b"][1, 1])
        cols[:, 88] = float(half)
        maps.append({
            "x": np.ascontiguousarray(x1[b, half * T:(half + 1) * T]) if x1 is not None else None,
            "xp": (np.ascontiguousarray(x1[b, 0:T]) if half == 1 else zeros) if x1 is not None else None,
            "cols": cols, "consts": consts,
            "ada_w0": np.ascontiguousarray(inp["ada_w"][1, 0]), "ada_w1": np.ascontiguousarray(inp["ada_w"][1, 1]),
            "b_w_qkv": np.ascontiguousarray(inp["b_w_qkv"][0]), "b_w_out": np.ascontiguousarray(inp["b_w_out"][0]),
            "mlp_w_up": np.ascontiguousarray(inp["mlp_w_up"][1]), "mlp_w_down": np.ascontiguousarray(inp["mlp_w_down"][1]),
        })
    return maps


def run_B(inp, x1, trace=False, stage=None, ncores=NCORES):
    nc = build_B(stage)
    maps = prep_B(inp, x1)[:ncores]
    res = run_bass_kernel_spmd(nc, maps, core_ids=list(range(ncores)), trace=trace)
    out = np.zeros((4, 4096, D), dtype=np.float32)
    for core in range(ncores):
        b, half = core // 2, core % 2
        out[b, half * T:(half + 1) * T] = res.results[core]["y"]
    return out, res


F_NCOLS = A_NCOLS + B_NCOLS


def build_F():
    nc = bass.Bass("TRN2", target_bir_lowering=False)
    dt = lambda n, s: nc.dram_tensor(n, s, F32, kind="ExternalInput").ap()
    io = dict(
        x=dt("x", [T, D]), cols=dt("cols", [128, F_NCOLS]), rows=dt("rows", [128, 3, 1024]),
        bbias=dt("bbias", [128, 8, 128]), constsA=dt("constsA", [128, 3, 128]), constsB=dt("constsB", [128, 8, 128]),
        ada_w00=dt("ada_w00", [D, 3 * D]), ada_w01=dt("ada_w01", [D, 3 * D]),
        ada_w10=dt("ada_w10", [D, 3 * D]), ada_w11=dt("ada_w11", [D, 3 * D]),
        a_w_in=dt("a_w_in", [D, 2 * D]), a_w_s=dt("a_w_s", [16, 128, 128]), a_w_out=dt("a_w_out", [D, D]),
        mlp_w_up0=dt("mlp_w_up0", [D, 4 * D]), mlp_w_down0=dt("mlp_w_down0", [4 * D, D]),
        mlp_w_up1=dt("mlp_w_up1", [D, 4 * D]), mlp_w_down1=dt("mlp_w_down1", [4 * D, D]),
        b_w_qkv=dt("b_w_qkv", [D, 9 * D]), b_w_out=dt("b_w_out", [D, D]))
    io["y"] = nc.dram_tensor("y", [T, D], F32, kind="ExternalOutput").ap()
    io["xsp"] = nc.dram_tensor("xsp", [128, 8 * T], F32).ap()
    io["snd"] = [nc.dram_tensor(f"snd{c}", [128, T], BF16).ap() for c in range(8)]
    io["rcv"] = [nc.dram_tensor(f"rcv{c}", [256, T], BF16).ap() for c in range(8)]
    with contextlib.ExitStack() as es:
        P = Prog(nc, es)
        A = Arena(nc)
        cx = Ctx()
        common_setup(nc, P, A, cx, F_NCOLS)
        xT, xT_at, _ = A.alloc([128, 8, T], F32)
        mark = A.off
        body_A(nc, P, A, cx, io, xT, stage=None, fused=True)
        P.barrier()
        A.off = mark
        body_B(nc, P, A, cx, io, xT, xT_at, stage=None, fused=True, cb=A_NCOLS)
        P.wait_all("sp", cx.out_toks)
        P.finish()
    return nc


def prep_F(inp):
    mA = prep_A(inp)
    mB = prep_B(inp, None)
    maps = []
    for core in range(NCORES):
        a, b = mA[core], mB[core]
        maps.append({
            "x": a["x"], "cols": np.ascontiguousarray(np.concatenate([a["cols"], b["cols"]], axis=1)),
            "rows": a["rows"], "bbias": a["bbias"], "constsA": a["consts"], "constsB": b["consts"],
            "ada_w00": a["ada_w0"], "ada_w01": a["ada_w1"], "ada_w10": b["ada_w0"], "ada_w11": b["ada_w1"],
            "a_w_in": a["a_w_in"], "a_w_s": a["a_w_s"], "a_w_out": a["a_w_out"],
            "mlp_w_up0": a["mlp_w_up"], "mlp_w_down0": a["mlp_w_down"],
            "mlp_w_up1": b["mlp_w_up"], "mlp_w_down1": b["mlp_w_down"],
            "b_w_qkv": b["b_w_qkv"], "b_w_out": b["b_w_out"],
        })
    return maps


def run_F(inp, trace=False, ncores=NCORES):
    nc = build_F()
    maps = prep_F(inp)[:ncores]
    res = run_bass_kernel_spmd(nc, maps, core_ids=list(range(ncores)), trace=trace)
    out = np.zeros((4, 4096, D), dtype=np.float32)
    for core in range(ncores):
        b, half = core // 2, core % 2
        out[b, half * T:(half + 1) * T] = res.results[core]["y"]
    return out, res


def kernel(**inputs):
    inp = {k: np.asarray(v) for k, v in inputs.items()}
    out, _ = run_F(inp)
    return out
```

```python
import contextlib
import numpy as np
import concourse.bass as bass
import concourse.mybir as mybir
from concourse.bass_utils import run_bass_kernel_spmd

F32 = mybir.dt.float32
BF16 = mybir.dt.bfloat16
AF = mybir.ActivationFunctionType
ALU = mybir.AluOpType
AX = mybir.AxisListType

D = 1024
T = 2048
NT = 4
TT = 512
NCORES = 8
ALPHA = 4.0 ** 0.25
INV_ALPHA = 1.0 / ALPHA
LN_EPS = 1e-5
EPS_P = LN_EPS / (ALPHA * ALPHA)
GELU = AF.Gelu_apprx_tanh


class Eng:
    def __init__(self, name, sem):
        self.name = name
        self.sem = sem
        self.count = 0
        self.ops = []
        self.waited = {}


class Prog:
    def __init__(self, nc, es, nsem=100):
        self.nc = nc
        self.free_sems = [es.enter_context(nc.semaphore(f"s{i}")) for i in range(nsem)]
        self.E = {}
        for n in ["pe", "act", "dve", "pool", "sp"]:
            self.E[n] = Eng(n, self.free_sems.pop())
        self.last_w = {}
        self.readers = {}
        self.sem_id = {}

    def new_sem(self):
        return self.free_sems.pop()

    def slot(self):
        sl = {"sem": self.new_sem(), "count": 0}
        self.slots = getattr(self, "slots", [])
        self.slots.append(sl)
        return sl

    def barrier(self, extra=()):
        toks = [(e.sem, e.count) for e in self.E.values() if e.count > 0]
        toks += [(sl["sem"], sl["count"]) for sl in getattr(self, "slots", []) if sl["count"] > 0]
        toks += list(extra)
        for eng in self.E:
            self.wait_all(eng, toks)

    def _sid(self, s):
        return id(s)

    @staticmethod
    def _norm(reads, writes):
        r2, w2 = [], list(writes)
        for k in reads:
            if isinstance(k, tuple) and k and k[0] == "ps":
                if k not in w2:
                    w2.append(k)
            else:
                r2.append(k)
        return r2, w2

    def _deps(self, eng, reads, writes, extra):
        toks = []
        for r in reads:
            t = self.last_w.get(r)
            if t is not None:
                toks.append(t)
        for w in writes:
            t = self.last_w.get(w)
            if t is not None:
                toks.append(t)
            for t in self.readers.get(w, {}).values():
                toks.append(t)
        for t in extra:
            if t is not None:
                toks.append(t)
        e = self.E[eng]
        waits = []
        for (s, v) in toks:
            if eng == "pe" and s is e.sem:
                continue
            k = self._sid(s)
            if e.waited.get(k, 0) < v:
                e.waited[k] = v
                waits.append((s, v))
        return waits

    def _record(self, tok, reads, writes):
        for r in reads:
            d = self.readers.setdefault(r, {})
            k = self._sid(tok[0])
            if k not in d or d[k][1] < tok[1]:
                d[k] = tok
        for w in writes:
            self.last_w[w] = tok
            self.readers[w] = {}

    def op(self, eng, fn, reads=(), writes=(), extra=()):
        reads, writes = self._norm(reads, writes)
        e = self.E[eng]
        waits = self._deps(eng, reads, writes, extra)
        e.count += 1
        tok = (e.sem, e.count)
        e.ops.append((waits, fn, (e.sem, 1)))
        self._record(tok, reads, writes)
        return tok

    def mm_group(self, fns, reads=(), writes=(), extra=()):
        reads, writes = self._norm(reads, writes)
        e = self.E["pe"]
        waits = self._deps("pe", reads, writes, extra)
        for i, fn in enumerate(fns):
            last = i == len(fns) - 1
            e.ops.append((waits if i == 0 else [], fn, (e.sem, 1) if last else None))
        e.count += 1
        tok = (e.sem, e.count)
        self._record(tok, reads, writes)
        return tok

    def dma(self, eng, pairs, slot, reads=(), writes=(), extra=()):
        e = self.E[eng]
        waits = self._deps(eng, reads, writes, extra)
        for i, (o, i_) in enumerate(pairs):
            e.ops.append((waits if i == 0 else [],
                          (lambda h, o=o, i_=i_: h.dma_start(out=o, in_=i_)),
                          (slot["sem"], 16)))
            slot["count"] += 16
        tok = (slot["sem"], slot["count"])
        self._record(tok, reads, writes)
        return tok

    def wait_all(self, eng, toks):
        e = self.E[eng]
        waits = self._deps(eng, (), (), toks)
        e.ops.append((waits, None, None))

    def replay(self, eng, h):
        for waits, fn, inc in self.E[eng].ops:
            for s, v in waits:
                h.wait_ge(s, v)
            if fn is None:
                continue
            inst = fn(h)
            if inc is not None:
                inst.then_inc(inc[0], inc[1])

    def finish(self):
        nc = self.nc
        with nc.Block() as block:
            @block.tensor
            def _(h):
                self.replay("pe", h)

            @block.scalar
            def _(h):
                self.replay("act", h)

            @block.vector
            def _(h):
                self.replay("dve", h)

            @block.gpsimd
            def _(h):
                self.replay("pool", h)

            @block.sync
            def _(h):
                self.replay("sp", h)


class Arena:
    def __init__(self, nc, base=16512, cap=229376):
        self.nc = nc
        self.cap = cap
        self.off = base
        self.n = 0

    def alloc(self, shape, dtype, at=None, name=None):
        nbytes = int(np.prod(shape[1:])) * (4 if dtype == F32 else 2)
        nbytes = (nbytes + 63) // 64 * 64
        if at is None:
            at = self.off
            self.off += nbytes
            assert self.off <= self.cap, f"SBUF overflow {self.off}"
        else:
            assert at + nbytes <= self.cap, "SBUF overflow (at)"
        self.n += 1
        t = self.nc.alloc_sbuf_tensor_at(name or f"sb{self.n}", list(shape), dtype, offset=at, align_bytes=64)
        return t, at, nbytes


class Psum:
    def __init__(self, nc, P):
        self.banks = [nc.alloc_psum_tensor(f"psb{i}", [128, 512], F32) for i in range(8)]
        self.i = 0

    def next(self):
        b = self.i
        self.i = (self.i + 1) % 7
        return b, self.banks[b]


def col(v):
    v = np.asarray(v, dtype=np.float32)
    return np.ascontiguousarray(v.reshape(-1, 128).T)


class Ctx:
    pass


class AdaStepper:
    NP = 24

    def __init__(self, P, cx, wada_ap, bias_cols, out_mod, tag):
        self.P, self.cx = P, cx
        self.wv = wada_ap.rearrange("(kc p) n -> p kc n", p=128)
        self.bias_cols, self.out_mod, self.tag = bias_cols, out_mod, tag
        self.k = 0
        self.base = cx.ada_i
        cx.ada_i += self.NP
        self._dma(0)

    def _dma(self, pc):
        bi = (self.base + pc) % 2
        self.P.dma("pool", [(self.cx.ada_buf[bi][:], self.wv[:, :, pc * 128:(pc + 1) * 128])], self.cx.ada_slot[bi],
                   writes=[("adabuf", bi)])

    def tick(self):
        if self.k >= self.NP:
            return False
        P, cx, pc = self.P, self.cx, self.k
        if pc + 1 < self.NP:
            self._dma(pc + 1)
        bi = (self.base + pc) % 2
        buf = cx.ada_buf[bi]
        ps = cx.ps_ada
        fns = [(lambda h, kc=kc, pc=pc, buf=buf: h.matmul(
            ps[:, pc:pc + 1], buf[:, kc, :], cx.sc[:, kc:kc + 1], start=(kc == 0), stop=(kc == 7)))
            for kc in range(8)]
        P.mm_group(fns, reads=[("adabuf", bi), "sc"], writes=[("ps_ada", pc)])
        self.k += 1
        if self.k == self.NP:
            self._finish()
        return True

    def flush(self):
        while self.tick():
            pass

    def _finish(self):
        P, ps, out_mod, tag = self.P, self.cx.ps_ada, self.out_mod, self.tag
        rd = [("ps_ada", f) for f in range(24)]
        P.op("dve", lambda h: h.tensor_tensor(out_mod[:, :], ps[:, 0:24], self.bias_cols, ALU.add),
             reads=rd + ["cols"], writes=[("mod", tag)])
        P.op("dve", lambda h: h.tensor_scalar(out_mod[:, 8:16], out_mod[:, 8:16], 1.0, None, ALU.add),
             reads=[("mod", tag)], writes=[("mod", tag)])
        P.op("dve", lambda h: h.tensor_scalar(out_mod[:, 16:24], out_mod[:, 16:24], 1.0, INV_ALPHA, ALU.add, ALU.mult),
             reads=[("mod", tag)], writes=[("mod", tag)])


def ada_tick(cx):
    st_ = getattr(cx, "ada_bg", None)
    if st_ is not None:
        st_.tick()


def ada_flush(cx):
    st_ = getattr(cx, "ada_bg", None)
    if st_ is not None:
        st_.flush()
        cx.ada_bg = None


def emit_modulate(P, cx, xT, hT, mod, tag, tt, eng="dve"):
    sl = slice(tt * TT, (tt + 1) * TT)
    for c in range(8):
        if eng == "act":
            P.op("act", lambda h, c=c: h.activation(hT[:, c, sl], xT[:, c, sl], AF.Identity,
                                                    bias=mod[:, c:c + 1], scale=mod[:, 8 + c:9 + c]),
                 reads=[("x", c, tt), ("mod", tag)], writes=[("h", c, tt)])
        else:
            P.op(eng, lambda h, c=c: h.tensor_scalar(hT[:, c, sl], xT[:, c, sl], mod[:, 8 + c:9 + c],
                                                     mod[:, c:c + 1], ALU.mult, ALU.add),
                 reads=[("x", c, tt), ("mod", tag)], writes=[("h", c, tt)])


def emit_ln_p1(P, cx, xT, tt):
    sl = slice(tt * TT, (tt + 1) * TT)
    s1, s2, sq = cx.ln_s1, cx.ln_s2, cx.ln_sq
    P.op("pool", lambda h: h.tensor_tensor(s1[:], xT[:, 0, sl], xT[:, 1, sl], ALU.add),
         reads=[("x", 0, tt), ("x", 1, tt)], writes=["ln_s1"])
    for c in range(2, 8):
        P.op("pool", lambda h, c=c: h.tensor_tensor(s1[:], s1[:], xT[:, c, sl], ALU.add),
             reads=[("x", c, tt), "ln_s1"], writes=["ln_s1"])
    for c in range(8):
        if c == 0:
            P.op("act", lambda h: h.activation(s2[:], xT[:, 0, sl], AF.Square),
                 reads=[("x", 0, tt)], writes=["ln_s2"])
        else:
            k = c % 2
            P.op("act", lambda h, c=c, k=k: h.activation(sq[k][:], xT[:, c, sl], AF.Square),
                 reads=[("x", c, tt)], writes=[("ln_sq", k)])
            P.op("dve", lambda h, k=k: h.tensor_tensor(s2[:], s2[:], sq[k][:], ALU.add),
                 reads=[("ln_sq", k), "ln_s2"], writes=["ln_s2"])


def emit_ln_p2(P, cx, xT, tt, g_cols, b_cols):
    sl = slice(tt * TT, (tt + 1) * TT)
    s1, s2, sq = cx.ln_s1, cx.ln_s2, cx.ln_sq
    b1, ps1 = cx.psum.next()
    P.mm_group([lambda h: h.matmul(ps1[:], cx.onesm[:], s1[:], start=True, stop=True)],
               reads=["ln_s1", "onesm"], writes=[("ps", b1)])
    b2, ps2 = cx.psum.next()
    P.mm_group([lambda h: h.matmul(ps2[:], cx.onesm[:], s2[:], start=True, stop=True)],
               reads=["ln_s2", "onesm"], writes=[("ps", b2)])
    mean, msq, rstd = cx.ln_mean, cx.ln_tmp[1], cx.ln_rstd
    P.op("act", lambda h: h.activation(mean[:], ps1[:], AF.Identity), reads=[("ps", b1)], writes=["ln_mean"])
    P.op("act", lambda h: h.activation(msq[:], ps1[:], AF.Square), reads=[("ps", b1)], writes=[("ln_tmp", 1)])
    P.op("dve", lambda h: h.tensor_tensor(rstd[:], ps2[:], msq[:], ALU.subtract),
         reads=[("ps", b2), ("ln_tmp", 1)], writes=["ln_rstd"])
    P.op("dve", lambda h: h.tensor_scalar(rstd[:], rstd[:], EPS_P, None, ALU.add),
         reads=["ln_rstd"], writes=["ln_rstd"])
    P.op("dve", lambda h: h.reciprocal(rstd[:], rstd[:]), reads=["ln_rstd"], writes=["ln_rstd"])
    P.op("act", lambda h: h.activation(rstd[:], rstd[:], AF.Sqrt), reads=["ln_rstd"], writes=["ln_rstd"])
    for c in range(8):
        k = c % 2
        tmp = cx.ln_tmp[k]
        P.op("dve", lambda h, c=c, tmp=tmp: h.tensor_tensor(tmp[:], xT[:, c, sl], mean[:], ALU.subtract),
             reads=[("x", c, tt), "ln_mean"], writes=[("ln_tmp", k)])
        P.op("pool", lambda h, tmp=tmp: h.tensor_tensor(tmp[:], tmp[:], rstd[:], ALU.mult),
             reads=["ln_rstd", ("ln_tmp", k)], writes=[("ln_tmp", k)])
        P.op("act", lambda h, c=c, tmp=tmp: h.activation(xT[:, c, sl], tmp[:], AF.Identity,
                                                         bias=b_cols[:, c:c + 1], scale=g_cols[:, c:c + 1]),
             reads=[("ln_tmp", k), "cols"], writes=[("x", c, tt)])


def emit_ln(P, cx, xT, tt, g_cols, b_cols):
    emit_ln_p1(P, cx, xT, tt)
    emit_ln_p2(P, cx, xT, tt, g_cols, b_cols)


def emit_mlp(P, cx, xT, hT, mod, tag, w_up_ap, w_down_ap, g_cols, b_cols, after_ln=None):
    wu_v = w_up_ap.rearrange("(kc p) n -> p kc n", p=128)
    wd_v = w_down_ap.rearrange("(fc p) n -> p fc n", p=128)
    for q in range(4):
        bi = cx.mlp_i % 2
        cx.mlp_i += 1
        wu, wd = cx.wu_buf[bi], cx.wd_buf[bi]
        P.dma("pool", [(wu[:, 0:4, :], wu_v[:, 0:4, q * 1024:(q + 1) * 1024]),
                       (wu[:, 4:8, :], wu_v[:, 4:8, q * 1024:(q + 1) * 1024])],
              cx.wu_slot[bi], writes=[("wu", bi)])
        P.dma("pool", [(wd[:, 0:4, :], wd_v[:, q * 8:q * 8 + 4, :]),
                       (wd[:, 4:8, :], wd_v[:, q * 8 + 4:q * 8 + 8, :])],
              cx.wd_slot[bi], writes=[("wd", bi)])
        def mlp_tile(q, tt, bi, wu, wd):
            sl = slice(tt * TT, (tt + 1) * TT)
            ai = cx.a_i % 2
            cx.a_i += 1
            aT = cx.a_buf[ai]
            for f in range(8):
                b, ps = cx.psum.next()
                fns = [(lambda h, kc=kc, f=f, ps=ps, wu=wu: h.matmul(
                    ps[:], wu[:, kc, f * 128:(f + 1) * 128], hT[:, kc, sl], start=(kc == 0), stop=(kc == 7)))
                    for kc in range(8)]
                P.mm_group(fns, reads=[("wu", bi)] + [("h", kc, tt) for kc in range(8)], writes=[("ps", b)])
                if f % 4 == 0:
                    ada_tick(cx)
                k = cx.r_i % 2
                cx.r_i += 1
                rt = cx.relu_tmp[k]
                P.op("act", lambda h, ps=ps, rt=rt: h.activation(rt[:], ps[:], AF.Relu),
                     reads=[("ps", b)], writes=[("ln_sq", k)])
                P.op("pool", lambda h, rt=rt, aT=aT, f=f: h.tensor_tensor(aT[:, f, :], rt[:], rt[:], ALU.mult),
                     reads=[("ln_sq", k)], writes=[("a", ai, f)])
            for oc in range(8):
                b, ps = cx.psum.next()
                fns = [(lambda h, f=f, oc=oc, ps=ps, wd=wd, aT=aT: h.matmul(
                    ps[:], wd[:, f, oc * 128:(oc + 1) * 128], aT[:, f, :], start=(f == 0), stop=(f == 7)))
                    for f in range(8)]
                P.mm_group(fns, reads=[("wd", bi)] + [("a", ai, f) for f in range(8)], writes=[("ps", b)])
                P.op("dve", lambda h, ps=ps, oc=oc: h.scalar_tensor_tensor(
                    xT[:, oc, sl], ps[:], mod[:, 16 + oc:17 + oc], xT[:, oc, sl], ALU.mult, ALU.add),
                    reads=[("ps", b), ("mod", tag), ("x", oc, tt)], writes=[("x", oc, tt)])
        for tt in range(NT):
            mlp_tile(q, tt, bi, wu, wd)
            if q == 3 and tt >= 1:
                emit_ln_p2(P, cx, xT, tt - 1, g_cols, b_cols)
                if after_ln is not None:
                    after_ln(tt - 1)
            if q == 3:
                emit_ln_p1(P, cx, xT, tt)
        if q == 3:
            emit_ln_p2(P, cx, xT, NT - 1, g_cols, b_cols)
            if after_ln is not None:
                after_ln(NT - 1)


def emit_load_xT(P, cx, x_ap, xT, key="x"):
    xv = x_ap.rearrange("(tb p) f -> p tb f", p=128)
    for tt in range(NT):
        si = cx.stg_i % 2
        cx.stg_i += 1
        stg = cx.stg[si]
        P.dma("sp", [(stg[:, 0:2, :], xv[:, tt * 4:tt * 4 + 2, :]),
                     (stg[:, 2:4, :], xv[:, tt * 4 + 2:tt * 4 + 4, :])], cx.stg_slot[si],
              writes=[("stg", si)])
        for c in range(8):
            b, ps = cx.psum.next()
            fns = [(lambda h, tb=tb, c=c, ps=ps, stg=stg: h.matmul(
                ps[:, tb * 128:(tb + 1) * 128], stg[:, tb, c * 128:(c + 1) * 128], cx.ident[:],
                start=True, stop=True)) for tb in range(4)]
            P.mm_group(fns, reads=[("stg", si), "ident"], writes=[("ps", b)])
            ada_tick(cx)
            eng = "act" if c % 2 == 0 else "dve"
            sl = slice(tt * TT, (tt + 1) * TT)
            if eng == "act":
                P.op("act", lambda h, ps=ps, c=c, sl=sl: h.activation(xT[:, c, sl], ps[:], AF.Identity),
                     reads=[("ps", b)], writes=[(key, c, tt)])
            else:
                P.op("dve", lambda h, ps=ps, c=c, sl=sl: h.tensor_copy(xT[:, c, sl], ps[:]),
                     reads=[("ps", b)], writes=[(key, c, tt)])


def emit_store_x(P, cx, xT, out_ap, tt):
    ov = out_ap.rearrange("(tb p) f -> p tb f", p=128)
    for tb in range(4):
        si = cx.ost_i % 2
        cx.ost_i += 1
        stg = cx.ost[si]
        tsl = slice(tt * TT + tb * 128, tt * TT + (tb + 1) * 128)
        for half in range(2):
            b, ps = cx.psum.next()
            fns = [(lambda h, cc=cc, half=half, ps=ps, tsl=tsl: h.matmul(
                ps[:, cc * 128:(cc + 1) * 128], xT[:, half * 4 + cc, tsl], cx.ident[:],
                start=True, stop=True)) for cc in range(4)]
            P.mm_group(fns, reads=[("x", half * 4 + cc, tt) for cc in range(4)] + ["ident"],
                       writes=[("ps", b)])
            if half == 0:
                P.op("act", lambda h, ps=ps, stg=stg: h.activation(stg[:, 0:512], ps[:], AF.Identity),
                     reads=[("ps", b)], writes=[("ost", si, 0)])
            else:
                P.op("dve", lambda h, ps=ps, stg=stg: h.tensor_copy(stg[:, 512:1024], ps[:]),
                     reads=[("ps", b)], writes=[("ost", si, 1)])
        tok = P.dma("sp", [(ov[:, tt * 4 + tb, :], stg[:])], cx.out_slot[si],
                    reads=[("ost", si, 0), ("ost", si, 1)])
        cx.out_toks.append(tok)


def common_setup(nc, P, A, cx, ncols):
    cx.psum = Psum(nc, P)
    cx.cols, _, _ = A.alloc([128, ncols], F32)
    cx.ident_f, _, _ = A.alloc([128, 128], F32)
    cx.ident = cx.ident_f
    cx.onesm, _, _ = A.alloc([128, 128], F32)
    cx.sc, _, _ = A.alloc([128, 8], BF16)
    cx.mod = [A.alloc([128, 24], F32)[0] for _ in range(2)]
    cx.ln_s1, _, _ = A.alloc([128, TT], F32)
    cx.ln_s2, _, _ = A.alloc([128, TT], F32)
    cx.ln_sq = [A.alloc([128, TT], F32)[0] for _ in range(2)]
    cx.ln_mean, _, _ = A.alloc([128, TT], F32)
    cx.ln_rstd, _, _ = A.alloc([128, TT], F32)
    cx.ln_tmp = [A.alloc([128, TT], F32)[0] for _ in range(2)]
    cx.relu_tmp = cx.ln_sq
    cx.ada_buf = [A.alloc([128, 8, 128], BF16)[0] for _ in range(2)]
    cx.ada_slot = [P.slot() for _ in range(2)]
    cx.ada_i = 0
    cx.mlp_i = 0
    cx.a_i = 0
    cx.r_i = 0
    cx.stg_i = 0
    cx.ost_i = 0
    cx.out_toks = []
    cx.out_slot = [P.slot() for _ in range(2)]
    cx.const_slot = P.slot()
    cx.stg_slot = [P.slot() for _ in range(2)]
    cx.wu_slot = [P.slot() for _ in range(2)]
    cx.wd_slot = [P.slot() for _ in range(2)]
    cx.ps_ada = cx.psum.banks[7]


def fence_tokens(P, keys):
    toks = []
    for k in keys:
        t = P.last_w.get(k)
        if t is not None:
            toks.append(t)
        toks.extend(P.readers.get(k, {}).values())
    return toks


A_COLS = dict(c=0, adab0=8, adab1=32, lng0=56, lnb0=64, lng1=72, lnb1=80, binu=88)
A_NCOLS = 96


def body_A(nc, P, A, cx, io, xT, stage=None, fused=False):
    x_in, cols_in, rows_in, bb_in, consts_in = io["x"], io["cols"], io["rows"], io["bbias"], io["constsA"]
    ada_w0, ada_w1, w_in, w_s, w_out = io["ada_w00"], io["ada_w01"], io["a_w_in"], io["a_w_s"], io["a_w_out"]
    w_up, w_down, y_out = io["mlp_w_up0"], io["mlp_w_down0"], io["y"]
    mark = A.off
    win, _, _ = A.alloc([128, 8, 2048], BF16)
    wout, _, _ = A.alloc([128, 8, 1024], BF16)
    rows, _, _ = A.alloc([128, 3, 1024], F32)
    bbias, _, _ = A.alloc([128, 8, 128], F32)
    wcT, _, _ = A.alloc([128, 16, 128], BF16)
    vst, _, _ = A.alloc([128, 2, 6], F32)
    vmv, _, _ = A.alloc([128, 2], F32)
    vrs, _, _ = A.alloc([128, 1], F32)
    mark1b = A.off
    hT1 = [A.alloc([128, 8, TT], BF16)[0] for _ in range(2)]
    uT1, _, _ = A.alloc([128, 8, TT], BF16)
    uT = [uT1, uT1]
    vtok1, _, _ = A.alloc([128, 4, 1024], BF16)
    vtok = [vtok1, vtok1]
    vg = [A.alloc([128, 1024], F32)[0] for _ in range(2)]
    gt = [A.alloc([128, TT], F32)[0] for _ in range(2)]
    end1 = A.off
    A.off = mark1b
    cx.stg = [A.alloc([128, 4, 1024], F32)[0] for _ in range(2)]
    tril, _, _ = A.alloc([128, 128], F32)
    identb, _, _ = A.alloc([128, 128], BF16)
    ws_f, _, _ = A.alloc([128, 16, 128], F32)
    ws_b, _, _ = A.alloc([128, 16, 128], BF16)
    end1 = max(end1, A.off)
    A.off = mark
    h2T, _, _ = A.alloc([128, 8, T], BF16)
    cx.wu_buf = [A.alloc([128, 8, 1024], BF16)[0] for _ in range(2)]
    cx.wd_buf = [A.alloc([128, 8, 1024], BF16)[0] for _ in range(2)]
    cx.a_buf = [A.alloc([128, 8, TT], BF16)[0] for _ in range(2)]
    ost_at = A.off
    cx.ost = [A.alloc([128, 1024], F32)[0] for _ in range(2)]
    end2 = A.off
    hst = A.alloc([128, 8, TT], BF16, at=ost_at)[0] if fused else None
    A.off = max(end1, end2)
    print("SBUF plan A: persistent", mark, "phase1 end", end1, "phase2 end", end2)

    P.dma("sp", [(cx.cols[:], cols_in)], P.slot(), writes=["cols"])
    P.dma("sp", [(cx.ident_f[:], consts_in[:, 0, :]), (cx.onesm[:], consts_in[:, 1, :]),
                 (tril[:], consts_in[:, 2, :])], P.slot(), writes=["ident", "onesm", "tril"])
    P.dma("sp", [(rows[:], rows_in)], P.slot(), writes=["rows"])
    P.dma("sp", [(bbias[:], bb_in)], P.slot(), writes=["bbias"])
    P.dma("sp", [(ws_f[:], w_s.rearrange("g t s -> t g s"))], P.slot(), writes=["ws_f"])
    P.op("act", lambda h: h.activation(cx.sc[:], cx.cols[:, 0:8], AF.Silu), reads=["cols"], writes=["sc"])
    cx.ada_bg = AdaStepper(P, cx, ada_w0, cx.cols[:, 8:32], cx.mod[0], 0)
    wv = w_in.rearrange("(kc p) n -> p kc n", p=128)
    for j in range(4):
        P.dma("pool", [(win[:, :, j * 512:(j + 1) * 512], wv[:, :, j * 512:(j + 1) * 512])], P.slot(),
              writes=[("win", j)])
    wout_slot = P.slot()
    P.dma("pool", [(wout[:], w_out.rearrange("(kc p) n -> p kc n", p=128))], wout_slot, writes=["wout"])
    P.op("dve", lambda h: h.tensor_copy(identb[:], cx.ident_f[:]), reads=["ident"], writes=["identb"])
    P.op("dve", lambda h: h.tensor_tensor(ws_b[:], ws_f[:], tril[:, None, :].to_broadcast([128, 16, 128]), ALU.mult),
         reads=["ws_f", "tril"], writes=["ws_b"])
    for g4 in range(4):
        b, ps = cx.psum.next()
        fns = [(lambda h, gi=gi, g4=g4, ps=ps: h.matmul(
            ps[:, gi * 128:(gi + 1) * 128], ws_b[:, g4 * 4 + gi, :], identb[:], start=True, stop=True))
            for gi in range(4)]
        P.mm_group(fns, reads=["ws_b", "identb"], writes=[("ps", b)])
        P.op("dve", lambda h, g4=g4, ps=ps: h.tensor_copy(
            wcT[:, g4 * 4:(g4 + 1) * 4, :], ps[:].rearrange("p (g t) -> p g t", g=4)),
            reads=[("ps", b)], writes=[("wcT", g4)])
    emit_load_xT(P, cx, x_in, xT)
    if stage == "load":
        for tt in range(NT):
            emit_store_x(P, cx, xT, y_out, tt)
        return
    ada_flush(cx)
    cx.ada_bg = AdaStepper(P, cx, ada_w1, cx.cols[:, 32:56], cx.mod[1], 1)
    f1 = fence_tokens(P, [("stg", 0), ("stg", 1), "tril", "identb", "ws_f", "ws_b"])
    for e_ in ("dve", "act", "pool"):
        P.wait_all(e_, f1)

    def mix_tile(tt):
        sl = slice(tt * TT, (tt + 1) * TT)
        pi = tt % 2
        hT = hT1[pi]
        def modulate(t2):
            p2 = t2 % 2
            sl2 = slice(t2 * TT, (t2 + 1) * TT)
            for c in range(8):
                P.op("dve", lambda h, c=c, p2=p2, sl2=sl2: h.tensor_scalar(
                    hT1[p2][:, c, :], xT[:, c, sl2], cx.mod[0][:, 8 + c:9 + c], cx.mod[0][:, c:c + 1], ALU.mult, ALU.add),
                    reads=[("x", c, t2), ("mod", 0)], writes=[("h1", p2, c)])
        if tt == 0:
            modulate(0)
        for tb in range(4):
            vi = (tt * 4 + tb) % 2
            vgb = vg[vi]
            for half in range(2):
                b, ps = cx.psum.next()
                fns = [(lambda h, kc=kc, half=half, ps=ps, hT=hT, tb=tb: h.matmul(
                    ps[:], hT[:, kc, tb * 128:(tb + 1) * 128],
                    win[:, kc, 1024 + half * 512:1024 + (half + 1) * 512], start=(kc == 0), stop=(kc == 7)))
                    for kc in range(8)]
                P.mm_group(fns, reads=[("win", 2 + half)] + [("h1", pi, kc) for kc in range(8)],
                           writes=[("ps", b)])
                if half == 0:
                    ada_tick(cx)
                P.op("dve", lambda h, ps=ps, half=half, vgb=vgb: h.tensor_tensor(
                    vgb[:, half * 512:(half + 1) * 512], ps[:], rows[:, 0, half * 512:(half + 1) * 512], ALU.add),
                    reads=[("ps", b), "rows"], writes=[("vg", vi, half)])
                P.op("act", lambda h, half=half, vgb=vgb: h.activation(
                    vgb[:, half * 512:(half + 1) * 512], vgb[:, half * 512:(half + 1) * 512], GELU),
                    reads=[("vg", vi, half)], writes=[("vg", vi, half)])
                P.op("dve", lambda h, half=half, vgb=vgb: h.bn_stats(
                    vst[:, half, :], vgb[:, half * 512:(half + 1) * 512]),
                    reads=[("vg", vi, half)], writes=[("vst", half)])
            P.op("dve", lambda h: h.bn_aggr(vmv[:], vst[:].rearrange("p a b -> p (a b)")),
                 reads=[("vst", 0), ("vst", 1)], writes=["vmv"])
            P.op("dve", lambda h: h.tensor_scalar(vrs[:], vmv[:, 1:2], LN_EPS, None, ALU.add),
                 reads=["vmv"], writes=["vrs"])
            P.op("dve", lambda h: h.reciprocal(vrs[:], vrs[:]), reads=["vrs"], writes=["vrs"])
            P.op("act", lambda h: h.activation(vrs[:], vrs[:], AF.Sqrt), reads=["vrs"], writes=["vrs"])
            P.op("dve", lambda h, vgb=vgb: h.tensor_scalar(
                vgb[:], vgb[:], vmv[:, 0:1], vrs[:, 0:1], ALU.subtract, ALU.mult),
                reads=[("vg", vi, 0), ("vg", vi, 1), "vmv", "vrs"], writes=[("vg", vi, 0), ("vg", vi, 1)])
            P.op("pool", lambda h, vgb=vgb: h.tensor_tensor(vgb[:], vgb[:], rows[:, 1, :], ALU.mult),
                 reads=[("vg", vi, 0), ("vg", vi, 1), "rows"], writes=[("vg", vi, 0), ("vg", vi, 1)])
            P.op("pool", lambda h, vgb=vgb, tb=tb: h.tensor_tensor(vtok[pi][:, tb, :], vgb[:], rows[:, 2, :], ALU.add),
                 reads=[("vg", vi, 0), ("vg", vi, 1), "rows"], writes=[("vtok", 0, tb)])
        if tt + 1 < NT:
            modulate(tt + 1)
        for fc in range(8):
            b, ps = cx.psum.next()
            fns = [(lambda h, kc=kc, fc=fc, ps=ps, hT=hT: h.matmul(
                ps[:], win[:, kc, fc * 128:(fc + 1) * 128], hT[:, kc, :], start=(kc == 0), stop=(kc == 7)))
                for kc in range(8)]
            P.mm_group(fns, reads=[("win", fc // 4)] + [("h1", pi, kc) for kc in range(8)], writes=[("ps", b)])
            if fc % 2 == 0:
                ada_tick(cx)
            P.op("act", lambda h, fc=fc, ps=ps: h.activation(
                uT[pi][:, fc, :], ps[:], GELU, bias=cx.cols[:, A_COLS["binu"] + fc:A_COLS["binu"] + fc + 1]),
                reads=[("ps", b), "cols"], writes=[("u", 0, fc)])
        for fc in range(8):
            b, ps = cx.psum.next()
            fns = []
            for tb in range(4):
                for gi in range(2):
                    g = 2 * fc + gi
                    fns.append(lambda h, tb=tb, gi=gi, g=g, ps=ps: h.matmul(
                        ps[gi * 64:(gi + 1) * 64, tb * 128:(tb + 1) * 128],
                        vtok[pi][:, tb, g * 64:(g + 1) * 64], wcT[:, g, :], start=True, stop=True))
            P.mm_group(fns, reads=[("vtok", 0, tb) for tb in range(4)] + [("wcT", (2 * fc) // 4)],
                       writes=[("ps", b)])
            k = fc % 2
            P.op("dve", lambda h, ps=ps, fc=fc, k=k: h.tensor_tensor(
                gt[k][:].rearrange("p (a t) -> p a t", a=4), ps[:].rearrange("p (a t) -> p a t", a=4),
                bbias[:, fc, None, :].to_broadcast([128, 4, 128]), ALU.add),
                reads=[("ps", b), "bbias"], writes=[("gt", k)])
            P.op("pool", lambda h, fc=fc, k=k: h.tensor_tensor(uT[pi][:, fc, :], gt[k][:], uT[pi][:, fc, :], ALU.mult),
                 reads=[("gt", k), ("u", 0, fc)], writes=[("u", 0, fc)])
        if tt >= 1:
            emit_ln_p2(P, cx, xT, tt - 1, cx.cols[:, A_COLS["lng0"]:A_COLS["lng0"] + 8],
                       cx.cols[:, A_COLS["lnb0"]:A_COLS["lnb0"] + 8])
        for oc in range(8):
            b, ps = cx.psum.next()
            fns = [(lambda h, fc=fc, oc=oc, ps=ps: h.matmul(
                ps[:], wout[:, fc, oc * 128:(oc + 1) * 128], uT[pi][:, fc, :], start=(fc == 0), stop=(fc == 7)))
                for fc in range(8)]
            P.mm_group(fns, reads=["wout"] + [("u", 0, fc) for fc in range(8)], writes=[("ps", b)])
            P.op("dve", lambda h, ps=ps, oc=oc: h.scalar_tensor_tensor(
                xT[:, oc, sl], ps[:], cx.mod[0][:, 16 + oc:17 + oc], xT[:, oc, sl], ALU.mult, ALU.add),
                reads=[("ps", b), ("mod", 0), ("x", oc, tt)], writes=[("x", oc, tt)])
        emit_ln_p1(P, cx, xT, tt)

    for tt in range(NT):
        mix_tile(tt)
    emit_ln_p2(P, cx, xT, NT - 1, cx.cols[:, A_COLS["lng0"]:A_COLS["lng0"] + 8],
               cx.cols[:, A_COLS["lnb0"]:A_COLS["lnb0"] + 8])

    if stage == "mixA":
        dbg = nc.dram_tensor("dbg", [128, 16384], F32, kind="ExternalOutput").ap()
        dsl = P.slot()
        allk = list(P.last_w.keys())
        cx.out_toks.append(P.dma("sp", [(dbg[:, 0:24], cx.mod[0][:])], dsl, reads=allk))
        cx.out_toks.append(P.dma("pool", [(dbg[:, 1024:3072], wcT[:].rearrange("p g t -> p (g t)"))], dsl, reads=allk))
        cx.out_toks.append(P.dma("pool", [(dbg[:, 4096:8192], hT1[1][:].rearrange("p g t -> p (g t)"))], dsl, reads=allk))
        cx.out_toks.append(P.dma("pool", [(dbg[:, 8192:12288], uT1[:].rearrange("p g t -> p (g t)"))], dsl, reads=allk))
        cx.out_toks.append(P.dma("pool", [(dbg[:, 12288:16384], vtok1[:].rearrange("p g t -> p (g t)"))], dsl, reads=allk))
        for e_ in ("pe", "act", "dve", "pool"):
            P.wait_all(e_, list(cx.out_toks))
        for tt in range(NT):
            emit_store_x(P, cx, xT, y_out, tt)
        return
    ada_flush(cx)
    if fused:
        cx.ada_bg = AdaStepper(P, cx, io["ada_w10"], cx.cols[:, A_NCOLS + 8:A_NCOLS + 32], cx.mod[0], 0)
        cx.pre_ada_b0 = True
    p1_keys = [("win", j) for j in range(4)] + ["wout", "rows", "bbias", "vst", "vmv", "vrs"] + \
              [("wcT", g4) for g4 in range(4)] + [("h1", p, c) for p in range(2) for c in range(8)] + \
              [("u", 0, c) for c in range(8)] + [("vtok", 0, tb) for tb in range(4)] + \
              [("vg", i, hf) for i in range(2) for hf in range(2)] + [("gt", 0), ("gt", 1)]
    f2 = fence_tokens(P, p1_keys)
    for e_ in ("dve", "act", "pool"):
        P.wait_all(e_, f2)
    for tt in range(NT):
        emit_modulate(P, cx, xT, h2T, cx.mod[1], 1, tt)

    def after_ln(tt):
        emit_store_x(P, cx, xT, y_out, tt)

    if fused:
        cx.cc_sem = P.new_sem()
        snd_slot = P.slot()
        cx.spill_slot = P.slot()
        cx.spill_toks = []

    def send_tile(tt):
        ada_flush(cx)
        modb = cx.mod[0]
        sl = slice(tt * TT, (tt + 1) * TT)
        for c in range(8):
            if c % 2:
                P.op("act", lambda h, c=c: h.activation(hst[:, c, :], xT[:, c, sl], AF.Identity,
                                                        bias=modb[:, c:c + 1], scale=modb[:, 8 + c:9 + c]),
                     reads=[("x", c, tt), ("mod", 0)], writes=[("hst", c)])
            else:
                P.op("dve", lambda h, c=c: h.tensor_scalar(hst[:, c, :], xT[:, c, sl], modb[:, 8 + c:9 + c],
                                                           modb[:, c:c + 1], ALU.mult, ALU.add),
                     reads=[("x", c, tt), ("mod", 0)], writes=[("hst", c)])
        snd, rcv = io["snd"], io["rcv"]
        t_snd = P.dma("sp", [(snd[tt * 2 + hf], hst[:, hf * 4:(hf + 1) * 4, :].rearrange("p c t -> p (c t)"))
                             for hf in range(2)], snd_slot, reads=[("hst", c) for c in range(8)])
        xsp_ = io["xsp"]
        cx.spill_toks.append(P.dma("sp", [(xsp_[:, c * T + tt * TT:c * T + (tt + 1) * TT], xT[:, c, sl]) for c in range(8)],
                                   cx.spill_slot, reads=[("x", c, tt) for c in range(8)]))
        ep = P.E["pool"]
        w_ = P._deps("pool", (), (), [t_snd])
        for hf in range(2):
            ep.ops.append((w_ if hf == 0 else [],
                           (lambda h, i=tt * 2 + hf: h.collective_compute(
                               "AllGather", ALU.bypass, replica_groups=[[0, 1], [2, 3], [4, 5], [6, 7]],
                               ins=[snd[i].opt()], outs=[rcv[i].opt()])),
                           (cx.cc_sem, 1)))

    emit_mlp(P, cx, xT, h2T, cx.mod[1], 1, w_up, w_down,
             cx.cols[:, A_COLS["lng1"]:A_COLS["lng1"] + 8], cx.cols[:, A_COLS["lnb1"]:A_COLS["lnb1"] + 8],
             after_ln=(send_tile if fused else after_ln))
    ada_flush(cx)


def build_A(stage=None):
    nc = bass.Bass("TRN2", target_bir_lowering=False)
    dt = lambda n, s: nc.dram_tensor(n, s, F32, kind="ExternalInput").ap()
    x_in = dt("x", [T, D])
    cols_in = dt("cols", [128, A_NCOLS])
    rows_in = dt("rows", [128, 3, 1024])
    bb_in = dt("bbias", [128, 8, 128])
    consts_in = dt("consts", [128, 3, 128])
    ada_w0 = dt("ada_w0", [D, 3 * D])
    ada_w1 = dt("ada_w1", [D, 3 * D])
    w_in = dt("a_w_in", [D, 2 * D])
    w_s = dt("a_w_s", [16, 128, 128])
    w_out = dt("a_w_out", [D, D])
    w_up = dt("mlp_w_up", [D, 4 * D])
    w_down = dt("mlp_w_down", [4 * D, D])
    y_out = nc.dram_tensor("y", [T, D], F32, kind="ExternalOutput").ap()

    with contextlib.ExitStack() as es:
        P = Prog(nc, es)
        A = Arena(nc)
        cx = Ctx()
        common_setup(nc, P, A, cx, A_NCOLS)
        xT, _, _ = A.alloc([128, 8, T], F32)
        io = dict(x=x_in, cols=cols_in, rows=rows_in, bbias=bb_in, constsA=consts_in, ada_w00=ada_w0, ada_w01=ada_w1,
                  a_w_in=w_in, a_w_s=w_s, a_w_out=w_out, mlp_w_up0=w_up, mlp_w_down0=w_down, y=y_out)
        body_A(nc, P, A, cx, io, xT, stage=stage, fused=False)
        P.wait_all("sp", cx.out_toks)
        P.finish()
    return nc


def host_consts():
    ident = np.eye(128, dtype=np.float32)
    ones = np.full((128, 128), 1.0 / 1024.0, dtype=np.float32)
    tril = np.tril(np.ones((128, 128), dtype=np.float32))
    return np.ascontiguousarray(np.stack([ident, ones, tril], axis=1))


def prep_A(inp):
    x = np.asarray(inp["x"], dtype=np.float32)
    c = np.asarray(inp["c"], dtype=np.float32)
    maps = []
    consts = host_consts()
    rows = np.stack([inp["a_b_in"][0][1024:], inp["a_vn_g"][0], inp["a_vn_b"][0]], axis=0).astype(np.float32)
    rows = np.ascontiguousarray(np.broadcast_to(rows[None], (128, 3, 1024)))
    bs = np.asarray(inp["a_b_s"][0], dtype=np.float32)
    bb = np.zeros((128, 8, 128), dtype=np.float32)
    for fc in range(8):
        bb[0:64, fc, :] = bs[2 * fc][None, :]
        bb[64:128, fc, :] = bs[2 * fc + 1][None, :]
    for core in range(NCORES):
        b, half = core // 2, core % 2
        cols = np.zeros((128, A_NCOLS), dtype=np.float32)
        cols[:, 0:8] = col(c[b])
        cols[:, 8:32] = col(inp["ada_b"][0, 0])
        cols[:, 32:56] = col(inp["ada_b"][0, 1])
        cols[:, 56:64] = col(inp["ln_g"][0, 0])
        cols[:, 64:72] = col(inp["ln_b"][0, 0])
        cols[:, 72:80] = col(inp["ln_g"][0, 1])
        cols[:, 80:88] = col(inp["ln_b"][0, 1])
        cols[:, 88:96] = col(inp["a_b_in"][0][:1024])
        maps.append({
            "x": np.ascontiguousarray(x[b, half * T:(half + 1) * T]),
            "cols": cols, "rows": rows, "bbias": bb, "consts": consts,
            "ada_w0": np.ascontiguousarray(inp["ada_w"][0, 0]), "ada_w1": np.ascontiguousarray(inp["ada_w"][0, 1]),
            "a_w_in": np.ascontiguousarray(inp["a_w_in"][0]), "a_w_s": np.ascontiguousarray(inp["a_w_s"][0]),
            "a_w_out": np.ascontiguousarray(inp["a_w_out"][0]),
            "mlp_w_up": np.ascontiguousarray(inp["mlp_w_up"][0]), "mlp_w_down": np.ascontiguousarray(inp["mlp_w_down"][0]),
        })
    return maps


def run_A(inp, trace=False, stage=None, ncores=NCORES):
    nc = build_A(stage)
    maps = prep_A(inp)[:ncores]
    res = run_bass_kernel_spmd(nc, maps, core_ids=list(range(ncores)), trace=trace)
    x1 = np.zeros((4, 4096, D), dtype=np.float32)
    for core in range(ncores):
        b, half = core // 2, core % 2
        x1[b, half * T:(half + 1) * T] = res.results[core]["y"]
    if stage == "mixA":
        np.save("dbg0.npy", res.results[0]["dbg"])
    return x1, res


B_NCOLS = 96
PATTERNS = ((128, 1), (512, 4), (2048, 16))


def host_consts_B():
    ident = np.eye(128, dtype=np.float32)
    ones = np.full((128, 128), 1.0 / 1024.0, dtype=np.float32)
    k = np.arange(128)[:, None]
    q = np.arange(128)[None, :]
    d_prev = np.where(k >= q, 128.0 + q - k, 0.0).astype(np.float32)
    d_cur = np.where(k <= q, (q - k) * 1.0, 0.0).astype(np.float32)
    m_prev = np.where(k >= q, 0.0, -240000.0).astype(np.float32)
    m_cur = np.where(k <= q, 0.0, -240000.0).astype(np.float32)
    m = np.arange(128)[None, :]
    permA = ((k == m + 64) & (m < 64)).astype(np.float32)
    permB = ((k == m - 64) & (m >= 64)).astype(np.float32)
    return np.ascontiguousarray(np.stack([ident, ones, d_prev, d_cur, permA, permB, m_prev, m_cur], axis=1))


def body_B(nc, P, A, cx, io, xT, xT_at, stage=None, fused=False, cb=0):
    x_in, xp_in, cols_in, consts_in = io.get("x"), io.get("xp"), io["cols"], io["constsB"]
    ada_w0, ada_w1, w_qkv, w_out = io["ada_w10"], io["ada_w11"], io["b_w_qkv"], io["b_w_out"]
    w_up, w_down, y_out, xsp = io["mlp_w_up1"], io["mlp_w_down1"], io["y"], io["xsp"]
    mark = xT_at
    r1_end = xT_at + 65536
    save_off = A.off
    A.off = mark
    kt0, _, _ = A.alloc([128, 32 * 128], BF16)
    KT = [kt0, kt0]
    VT, _, _ = A.alloc([128, 32 * 128], BF16)
    VA, _, _ = A.alloc([128, 32, 2, 128], BF16)
    ACC, _, _ = A.alloc([128, 2, T], F32)
    QT = [A.alloc([128, T], BF16)[0] for _ in range(2)]
    tmpB = [A.alloc([128, 256], F32)[0] for _ in range(2)]
    Pt = [A.alloc([128, 512], BF16)[0] for _ in range(2)]
    assert A.off <= r1_end, (A.off, r1_end)
    A.off = max(r1_end, save_off)
    mark2 = A.off
    hT, _, _ = A.alloc([128, 8, 2 * T], BF16)
    oT, oT_at, _ = A.alloc([128, 8, T], BF16)
    wqkv = [A.alloc([128, 3, 8, 128], BF16)[0] for _ in range(2)]
    Bhl = [A.alloc([128, 2, 2, 256], BF16)[0] for _ in range(2)]
    rden, _, _ = A.alloc([128, 512], F32)
    identb, _, _ = A.alloc([128, 128], BF16)
    diffm, _, _ = A.alloc([128, 2, 128], F32)
    perm, _, _ = A.alloc([128, 2, 128], F32)
    mask01, _, _ = A.alloc([128, 2, 128], F32)
    endB1 = A.off
    A.off = oT_at
    cx.stg = [A.alloc([128, 4, 1024], F32)[0] for _ in range(2)]
    assert A.off <= oT_at + 32768
    A.off = mark2
    wout, _, _ = A.alloc([128, 8, 1024], BF16)
    A.off = mark2
    h2T, _, _ = A.alloc([128, 8, T], BF16)
    cx.wu_buf = [A.alloc([128, 8, 1024], BF16)[0] for _ in range(2)]
    cx.wd_buf = [A.alloc([128, 8, 1024], BF16)[0] for _ in range(2)]
    cx.a_buf = [A.alloc([128, 8, TT], BF16)[0] for _ in range(2)]
    cx.ost = [A.alloc([128, 1024], F32)[0] for _ in range(2)]
    endB2 = A.off
    print("SBUF plan B: mark", mark, "r1_end", r1_end, "attn end", endB1, "mlp end", endB2)
    assert max(endB1, endB2) <= A.cap

    flag = cx.cols[:, cb + 88:cb + 89]
    if not fused:
        P.dma("sp", [(cx.cols[:], cols_in)], P.slot(), writes=["cols"])
        P.dma("sp", [(cx.ident_f[:], consts_in[:, 0, :]), (cx.onesm[:], consts_in[:, 1, :])], P.slot(),
              writes=["ident", "onesm"])
        P.op("act", lambda h: h.activation(cx.sc[:], cx.cols[:, 0:8], AF.Silu), reads=["cols"], writes=["sc"])
    P.dma("sp", [(diffm[:], consts_in[:, 2:4, :]), (perm[:], consts_in[:, 4:6, :]),
                 (mask01[:], consts_in[:, 6:8, :])], P.slot(), writes=["diffm", "perm"])
    P.op("dve", lambda h: h.tensor_copy(identb[:], cx.ident_f[:]), reads=["ident"], writes=["identb"])
    if not getattr(cx, "pre_ada_b0", False):
        cx.ada_bg = AdaStepper(P, cx, ada_w0, cx.cols[:, cb + 8:cb + 32], cx.mod[0], 0)
        ada_flush(cx)
    mod0 = cx.mod[0]
    cx.ada_bg = AdaStepper(P, cx, ada_w1, cx.cols[:, cb + 32:cb + 56], cx.mod[1], 1)

    xpv = xp_in.rearrange("(tb p) f -> p tb f", p=128) if xp_in is not None else None

    def load_prev_tile(tt):
        si = cx.stg_i % 2
        cx.stg_i += 1
        stg = cx.stg[si]
        P.dma("sp", [(stg[:, 0:2, :], xpv[:, tt * 4:tt * 4 + 2, :]),
                     (stg[:, 2:4, :], xpv[:, tt * 4 + 2:tt * 4 + 4, :])], cx.stg_slot[si],
              writes=[("stg", si)])
        for c in range(8):
            b, ps = cx.psum.next()
            fns = [(lambda h, tb=tb, c=c, ps=ps, stg=stg: h.matmul(
                ps[:, tb * 128:(tb + 1) * 128], stg[:, tb, c * 128:(c + 1) * 128], cx.ident[:],
                start=True, stop=True)) for tb in range(4)]
            P.mm_group(fns, reads=[("stg", si), "ident"], writes=[("ps", b)])
            P.op("dve", lambda h, ps=ps, c=c, tt=tt: h.tensor_scalar(
                hT[:, c, tt * TT:(tt + 1) * TT], ps[:], mod0[:, 8 + c:9 + c], mod0[:, c:c + 1], ALU.mult, ALU.add),
                reads=[("ps", b), ("mod", 0)], writes=[("hall", c, tt)])

    if not fused:
        for tt in range(NT):
            load_prev_tile(tt)
        emit_load_xT(P, cx, x_in, xT)

    def mod_own(tt):
        for c in range(8):
            P.op("act" if c % 2 else "dve",
                 (lambda h, c=c, tt=tt: h.activation(hT[:, c, T + tt * TT:T + (tt + 1) * TT], xT[:, c, tt * TT:(tt + 1) * TT],
                                                     AF.Identity, bias=mod0[:, c:c + 1], scale=mod0[:, 8 + c:9 + c]))
                 if c % 2 else
                 (lambda h, c=c, tt=tt: h.tensor_scalar(hT[:, c, T + tt * TT:T + (tt + 1) * TT], xT[:, c, tt * TT:(tt + 1) * TT],
                                                        mod0[:, 8 + c:9 + c], mod0[:, c:c + 1], ALU.mult, ALU.add)),
                 reads=[("x", c, tt), ("mod", 0)], writes=[("hall", c, NT + tt)])

    for tt in range(NT):
        mod_own(tt)
    if fused:
        rcv = io["rcv"]
        r_slot = P.slot()
        P.dma("sp", [(hT[:, hf * 4:(hf + 1) * 4, tt * TT:(tt + 1) * TT],
                      rcv[tt * 2 + hf][0:128, :].rearrange("p (c t) -> p c t", c=4))
                     for tt in range(NT) for hf in range(2)], r_slot,
              writes=[("hall", c, tt) for c in range(8) for tt in range(NT)], extra=[(cx.cc_sem, 8)])
    sp_slot = P.slot()
    xkeys = [("x", c, tt) for c in range(8) for tt in range(NT)]
    if fused:
        t_spills = list(cx.spill_toks)
    else:
        t_spills = [P.dma("sp", [(xsp[:, c * T:(c + 1) * T], xT[:, c, :]) for c in range(8)], sp_slot, reads=xkeys)]
    fsp = fence_tokens(P, xkeys) + t_spills
    for e_ in ("pe", "act", "dve", "pool"):
        P.wait_all(e_, fsp)
    P.op("pool", lambda h: h.memset(VA[:], 1.0), writes=["va_init"])
    P.op("act", lambda h: h.activation(VA[:, 0:16, 0, 64:128], VA[:, 0:16, 0, 64:128], AF.Identity, scale=flag),
         reads=["va_init", "cols"], writes=["va_init"])
    P.op("act", lambda h: h.activation(VA[:, 0:16, 1, 0:64], VA[:, 0:16, 1, 0:64], AF.Identity, scale=flag),
         reads=["va_init", "cols"], writes=["va_init"])
    va_tok = P.last_w["va_init"]

    wq_v = w_qkv.rearrange("(kc p) n -> p kc n", p=128)
    wq_slot = [P.slot() for _ in range(2)]
    st = {"i": 0, "e": 0, "p": 0}
    hall_keys_all = [("hall", c, t8) for c in range(8) for t8 in range(2 * NT)]

    def perm_view(buf2d, col0, d, tt, ps):
        if d == 1:
            return buf2d[:, col0 + tt * 512:col0 + (tt + 1) * 512], ps[:]
        if d == 4:
            dst = buf2d[:, col0 + tt * 512:col0 + (tt + 1) * 512].rearrange("p (r i) -> p i r", r=4)
            return dst, ps[:].rearrange("p (i r) -> p i r", r=4)
        dst = buf2d[:, col0:col0 + 2048].rearrange("p (r i) -> p i r", r=16)[:, tt * 32:(tt + 1) * 32, :]
        return dst, ps[:].rearrange("p (i r) -> p i r", r=16)

    import os as _os2
    _skip = set(_os2.environ.get("DBG_SKIP", "").split(","))

    def stage_prologue(hp, g, si):
        d = PATTERNS[g][1]
        wb = wqkv[si]
        bh = Bhl[si]
        pairs = []
        for t3 in range(3):
            c0 = ((g * 3 + t3) * 16 + 2 * hp) * 64
            pairs.append((wb[:, t3, :, :], wq_v[:, :, c0:c0 + 128]))
        P.dma("pool", pairs, wq_slot[si], writes=[("wqkv", si)])
        for hh in range(2):
            slope = 2.0 ** (-8.0 * (2 * hp + hh + 1) / 16.0)
            tb = tmpB[hh]
            dm = diffm[:].rearrange("p a q -> p (a q)")
            mk = mask01[:].rearrange("p a q -> p (a q)")
            P.op("pool", lambda h, tb=tb, slope=slope: h.tensor_scalar(tb[:], dm, -8.0 * slope * d, None, ALU.mult),
                 reads=["diffm"], writes=[("tmpB", hh)])
            P.op("pool", lambda h, tb=tb: h.tensor_tensor(tb[:], tb[:], mk, ALU.add),
                 reads=["diffm", ("tmpB", hh)], writes=[("tmpB", hh)])
            P.op("pool", lambda h, tb=tb, hh=hh: h.tensor_copy(bh[:, 0, hh, :], tb[:]),
                 reads=[("tmpB", hh)], writes=[("bhl", si, hh, 0)])
            P.op("pool", lambda h, tb=tb, hh=hh: h.tensor_tensor(bh[:, 1, hh, :], tb[:], bh[:, 0, hh, :], ALU.subtract),
                 reads=[("tmpB", hh), ("bhl", si, hh, 0)], writes=[("bhl", si, hh, 1)])

    def attn_stage(hp, g, si):
        d = PATTERNS[g][1]
        wb = wqkv[si]
        kt = KT[si]
        qt = QT[si]
        bh = Bhl[si]
        for tt in range(0 if "Q" in _skip else NT):
            b, ps = cx.psum.next()
            fns = [(lambda h, kc=kc, ps=ps, tt=tt: h.matmul(
                ps[:], wb[:, 0, kc, :], hT[:, kc, T + tt * TT:T + (tt + 1) * TT], start=(kc == 0), stop=(kc == 7)))
                for kc in range(8)]
            P.mm_group(fns, reads=[("wqkv", si)] + [("hall", kc, NT + tt) for kc in range(8)], writes=[("ps", b)])
            if tt % 2 == 0:
                ada_tick(cx)
            dst, src = perm_view(qt, 0, d, tt, ps)
            P.op("act", lambda h, dst=dst, src=src: h.activation(dst, src, AF.Identity),
                 reads=[("ps", b)], writes=[("qt", si)])
        for tt in range(0 if "K" in _skip else NT):
            b, ps = cx.psum.next()
            fns = [(lambda h, kc=kc, ps=ps, tt=tt: h.matmul(
                ps[:], wb[:, 1, kc, :], hT[:, kc, T + tt * TT:T + (tt + 1) * TT], start=(kc == 0), stop=(kc == 7)))
                for kc in range(8)]
            P.mm_group(fns, reads=[("wqkv", si)] + [("hall", kc, NT + tt) for kc in range(8)], writes=[("ps", b)])
            dst, src = perm_view(kt, 2048, d, tt, ps)
            P.op("dve", lambda h, dst=dst, src=src: h.tensor_copy(dst, src),
                 reads=[("ps", b)], writes=[("kt", 0)])
        if "K" in _skip:
            pass
        elif d == 1:
            b, ps = cx.psum.next()
            fns = [(lambda h, kc=kc, ps=ps: h.matmul(
                ps[:, 0:128], wb[:, 1, kc, :], hT[:, kc, T - 128:T], start=(kc == 0), stop=(kc == 7)))
                for kc in range(8)]
            P.mm_group(fns, reads=[("wqkv", si)] + [("hall", kc, NT - 1) for kc in range(8)], writes=[("ps", b)])
            P.op("dve", lambda h, ps=ps: h.tensor_copy(kt[:, 15 * 128:16 * 128], ps[:, 0:128]),
                 reads=[("ps", b)], writes=[("kt", 0)])
        elif d == 4:
            b, ps = cx.psum.next()
            fns = [(lambda h, kc=kc, ps=ps: h.matmul(
                ps[:], wb[:, 1, kc, :], hT[:, kc, T - 512:T], start=(kc == 0), stop=(kc == 7)))
                for kc in range(8)]
            P.mm_group(fns, reads=[("wqkv", si)] + [("hall", kc, NT - 1) for kc in range(8)], writes=[("ps", b)])
            dst = kt[:, 12 * 128:16 * 128].rearrange("p (r i) -> p i r", r=4)
            src = ps[:].rearrange("p (i r) -> p i r", r=4)
            P.op("dve", lambda h, dst=dst, src=src: h.tensor_copy(dst, src),
                 reads=[("ps", b)], writes=[("kt", 0)])
        else:
            for tt in range(NT):
                b, ps = cx.psum.next()
                fns = [(lambda h, kc=kc, ps=ps, tt=tt: h.matmul(
                    ps[:], wb[:, 1, kc, :], hT[:, kc, tt * TT:(tt + 1) * TT], start=(kc == 0), stop=(kc == 7)))
                    for kc in range(8)]
                P.mm_group(fns, reads=[("wqkv", si)] + [("hall", kc, tt) for kc in range(8)], writes=[("ps", b)])
                dst, src = perm_view(kt, 0, d, tt, ps)
                P.op("dve", lambda h, dst=dst, src=src: h.tensor_copy(dst, src),
                     reads=[("ps", b)], writes=[("kt", 0)])
        def vproj(tok0, ntok, col0, mode, tt, hkeys):
            b, ps = cx.psum.next()
            fns = [(lambda h, kc=kc, ps=ps: h.matmul(
                ps[:, 0:ntok], wb[:, 2, kc, :], hT[:, kc, tok0:tok0 + ntok], start=(kc == 0), stop=(kc == 7)))
                for kc in range(8)]
            P.mm_group(fns, reads=[("wqkv", si)] + hkeys, writes=[("ps", b)])
            if mode == "plain":
                dst, src = VT[:, col0:col0 + ntok], ps[:, 0:ntok]
            elif mode == "seg4":
                dst = VT[:, col0:col0 + 512].rearrange("p (r i) -> p i r", r=4)
                src = ps[:].rearrange("p (i r) -> p i r", r=4)
            else:
                dst, src = perm_view(VT, col0, d, tt, ps)
            P.op("act", lambda h, dst=dst, src=src: h.activation(dst, src, AF.Identity),
                 reads=[("ps", b)], writes=["vt"])

        for tt in range(0 if "V" in _skip else NT):
            hk = [("hall", kc, NT + tt) for kc in range(8)]
            if d == 1:
                vproj(T + tt * TT, TT, 2048 + tt * 512, "plain", tt, hk)
            elif d == 4:
                vproj(T + tt * TT, TT, 2048 + tt * 512, "seg4", tt, hk)
            else:
                vproj(T + tt * TT, TT, 2048, "perm", tt, hk)
        if "V" in _skip:
            pass
        elif d == 1:
            vproj(T - 128, 128, 15 * 128, "plain", 0, [("hall", kc, NT - 1) for kc in range(8)])
        elif d == 4:
            vproj(T - 512, 512, 12 * 128, "seg4", 0, [("hall", kc, NT - 1) for kc in range(8)])
        else:
            for tt in range(NT):
                vproj(tt * TT, TT, 0, "perm", tt, [("hall", kc, tt) for kc in range(8)])
        groups = []
        prev_slots = list(range(16 - d, 16))
        own_slots = list(range(16, 32))
        for i0 in range(0, len(prev_slots), 4):
            groups.append(prev_slots[i0:i0 + 4])
        for i0 in range(0, 16, 4):
            groups.append(own_slots[i0:i0 + 4])
        if "T" in _skip or "V" in _skip:
            groups = []
        for grp in groups:
            b, ps = cx.psum.next()
            fns = [(lambda h, gi=gi, s=s, ps=ps: h.matmul(
                ps[:, gi * 128:(gi + 1) * 128], VT[:, s * 128:(s + 1) * 128], identb[:], start=True, stop=True))
                for gi, s in enumerate(grp)]
            P.mm_group(fns, reads=["vt", "identb"], writes=[("ps", b)], extra=[va_tok])
            ng = len(grp)
            s0 = grp[0]
            srcv = ps[:, 0:ng * 128].rearrange("p (s c) -> p s c", c=128)
            if "X" in _skip:
                continue
            if "XP" in _skip and s0 < 16:
                continue
            if "XO" in _skip and s0 >= 16:
                continue
            if "ALTVA" in _skip:
                s0_ = min(grp[0], 30)
                P.op("dve", lambda h, ps=ps, s0_=s0_: h.tensor_copy(
                    VA[:, s0_:s0_ + 2, :, :].rearrange("p a b c -> p (a b c)"), ps[:]),
                    reads=[("ps", b)], writes=[("va", 0)])
                continue
            if "F2D" in _skip:
                for gi, sl_ in enumerate(grp):
                    if "NO20" in _skip and sl_ == 20:
                        continue
                    P.op("act", lambda h, gi=gi, sl_=sl_, ps=ps: h.activation(
                        VA[:, sl_, 0, :], ps[:, gi * 128:(gi + 1) * 128], AF.Identity),
                        reads=[("ps", b)], writes=[("va", 0)])
                    P.op("dve", lambda h, gi=gi, sl_=sl_, ps=ps: h.tensor_copy(
                        VA[:, sl_, 1, :], ps[:, gi * 128:(gi + 1) * 128]),
                        reads=[("ps", b)], writes=[("va", 1)])
                continue
            if "FULL" in _skip:
                P.op("act", lambda h, srcv=srcv, s0=s0, ng=ng: h.activation(
                    VA[:, s0:s0 + ng, 0, :], srcv, AF.Identity),
                    reads=[("ps", b)], writes=[("va", 0)])
                P.op("dve", lambda h, srcv=srcv, s0=s0, ng=ng: h.tensor_copy(
                    VA[:, s0:s0 + ng, 1, :], srcv),
                    reads=[("ps", b)], writes=[("va", 1)])
                continue
            if s0 < 16:
                P.op("act", lambda h, srcv=srcv, s0=s0, ng=ng: h.activation(
                    VA[:, s0:s0 + ng, 0, 0:64], srcv[:, :, 0:64], AF.Identity, scale=flag),
                    reads=[("ps", b), "cols"], writes=[("va", 0)])
                P.op("dve", lambda h, srcv=srcv, s0=s0, ng=ng: h.tensor_scalar(
                    VA[:, s0:s0 + ng, 1, 64:128], srcv[:, :, 64:128], flag, None, ALU.mult),
                    reads=[("ps", b), "cols"], writes=[("va", 1)])
            else:
                P.op("act", lambda h, srcv=srcv, s0=s0, ng=ng: h.activation(
                    VA[:, s0:s0 + ng, 0, 0:64], srcv[:, :, 0:64], AF.Identity),
                    reads=[("ps", b)], writes=[("va", 0)])
                P.op("dve", lambda h, srcv=srcv, s0=s0, ng=ng: h.tensor_copy(
                    VA[:, s0:s0 + ng, 1, 64:128], srcv[:, :, 64:128]),
                    reads=[("ps", b)], writes=[("va", 1)])
        nun = 0 if stage == 'proj' else 16
        ust = {}

        def unit_front(j):
            pi = st["p"] % 2
            st["p"] += 1
            pt = Pt[pi]
            for hh in range(2):
                b, ps = cx.psum.next()
                fns = [lambda h, ps=ps, hh=hh: h.matmul(ps[:, 0:256], identb[:], bh[:, 0, hh, :], start=True, stop=False),
                       lambda h, ps=ps, hh=hh: h.matmul(ps[:, 0:256], identb[:], bh[:, 1, hh, :], start=False, stop=False)]
                for kb in range(2):
                    slot = 16 + j - d if kb == 0 else 16 + j
                    fns.append(lambda h, hh=hh, kb=kb, slot=slot, ps=ps, j=j: h.matmul(
                        ps[:, kb * 128:(kb + 1) * 128],
                        kt[hh * 64:(hh + 1) * 64, slot * 128:(slot + 1) * 128],
                        qt[hh * 64:(hh + 1) * 64, j * 128:(j + 1) * 128], start=False, stop=(kb == 1)))
                P.mm_group(fns, reads=[("kt", 0), ("qt", si), "identb", ("bhl", si, hh, 0), ("bhl", si, hh, 1)],
                           writes=[("ps", b)])
                P.op("act", lambda h, ps=ps, pt=pt, hh=hh: h.activation(
                    pt[:, hh * 256:(hh + 1) * 256], ps[:, 0:256], AF.Exp, scale=0.125),
                    reads=[("ps", b)], writes=[("pt", pi, hh)])
            ust[j] = (pi, pt)

        def unit_back(j):
            pi, pt = ust[j]
            n, r = j // d, j % d
            b2, ps2 = cx.psum.next()
            fns = []
            for hh in range(2):
                for kb in range(2):
                    slot = 16 + j - d if kb == 0 else 16 + j
                    fns.append(lambda h, hh=hh, kb=kb, slot=slot, ps2=ps2, pt=pt: h.matmul(
                        ps2[:, hh * 128:(hh + 1) * 128], VA[:, slot, hh, :],
                        pt[:, (hh * 2 + kb) * 128:(hh * 2 + kb + 1) * 128], start=(kb == 0), stop=(kb == 1)))
            P.mm_group(fns, reads=[("pt", pi, 0), ("pt", pi, 1), ("va", 0), ("va", 1)], writes=[("ps", b2)])
            start = n * 128 * d + r
            accv = ACC[:, :, start:start + 127 * d + 1:d]
            src2 = ps2[:, 0:256].rearrange("p (a q) -> p a q", a=2)
            wr = ["acc"] if j == nun - 1 else []
            if g == 0:
                P.op("dve", lambda h, accv=accv, src2=src2: h.tensor_copy(accv, src2),
                     reads=[("ps", b2)], writes=wr, extra=acc_dep)
            else:
                P.op("dve", lambda h, accv=accv, src2=src2: h.tensor_tensor(accv, src2, accv, ALU.add),
                     reads=[("ps", b2)], writes=wr, extra=acc_dep)

        acc_dep = fence_tokens(P, ["acc"])
        if nun:
            unit_front(0)
        for j in range(nun):
            if j + 1 < nun:
                unit_front(j + 1)
            unit_back(j)

    def finalize(hp):
        for tt in range(NT):
            sl = slice(tt * TT, (tt + 1) * TT)
            b, ps = cx.psum.next()
            fns = [lambda h, ps=ps, sl=sl: h.matmul(ps[:], perm[:, 0, :], ACC[:, 0, sl], start=True, stop=False),
                   lambda h, ps=ps, sl=sl: h.matmul(ps[:], perm[:, 1, :], ACC[:, 1, sl], start=False, stop=True)]
            P.mm_group(fns, reads=["acc", "perm"], writes=[("ps", b)])
            P.op("dve", lambda h, ps=ps: h.reciprocal(rden[:], ps[:]), reads=[("ps", b)], writes=["rden"])
            P.op("pool", lambda h, sl=sl: h.tensor_tensor(oT[0:64, hp, sl], ACC[0:64, 0, sl], rden[0:64, :], ALU.mult),
                 reads=["acc", "rden"], writes=[("o", hp, tt, 0)])
            P.op("pool", lambda h, sl=sl: h.tensor_tensor(oT[64:128, hp, sl], ACC[64:128, 1, sl], rden[64:128, :], ALU.mult),
                 reads=["acc", "rden"], writes=[("o", hp, tt, 1)])

    P.wait_all("pool", fence_tokens(P, [("stg", 0), ("stg", 1)]))
    nhp = {"pre": 0, "one": 1, "proj": 1}.get(stage, 8)
    import os as _os
    _gl = [int(x) for x in _os.environ.get("DBG_G", "0,1,2").split(",")]
    stages = [(hp, g) for hp in range(nhp) for g in _gl]
    if stages:
        stage_prologue(stages[0][0], stages[0][1], 0)
    for i_s, (hp, g) in enumerate(stages):
        if i_s + 1 < len(stages):
            stage_prologue(stages[i_s + 1][0], stages[i_s + 1][1], (i_s + 1) % 2)
        attn_stage(hp, g, i_s % 2)
        if stage != "proj" and g == _gl[-1]:
            finalize(hp)
    if stage in ("pre", "one", "proj"):
        fx = fence_tokens(P, ["vt", ("kt", 0), ("kt", 1), ("qt", 0), ("qt", 1), ("va", 0), ("va", 1), "acc", "va_init",
                              ("tmpB", 0), ("tmpB", 1), ("pt", 0, 0), ("pt", 0, 1), ("pt", 1, 0), ("pt", 1, 1)])
        P.wait_all("sp", fx)
        t_re = P.dma("sp", [(xT[:, c, :], xsp[:, c * T:(c + 1) * T]) for c in range(8)], P.slot(), writes=xkeys)
        for e_ in ("pe", "act", "dve", "pool"):
            P.wait_all(e_, fence_tokens(P, hall_keys_all + [("o", hp, tt, hf) for hp in range(nhp) for tt in range(NT) for hf in range(2)] + [("wqkv", 0), ("wqkv", 1)] + [("bhl", i, hh, z) for i in range(2) for hh in range(2) for z in range(2)]))
        for tt in range(NT):
            emit_store_x(P, cx, xT, y_out, tt)
        return

    att_keys = ["vt", ("kt", 0), ("kt", 1), ("qt", 0), ("qt", 1), ("va", 0), ("va", 1), "acc", "va_init",
                ("tmpB", 0), ("tmpB", 1), ("pt", 0, 0), ("pt", 0, 1), ("pt", 1, 0), ("pt", 1, 1)]
    fx = fence_tokens(P, att_keys)
    P.wait_all("sp", fx)
    x_slot = P.slot()
    P.dma("sp", [(xT[:, c, :], xsp[:, c * T:(c + 1) * T]) for c in range(8)], x_slot, writes=xkeys)
    fo = fence_tokens(P, hall_keys_all + [("wqkv", 0), ("wqkv", 1)])
    P.wait_all("pool", fo)
    P.dma("pool", [(wout[:], w_out.rearrange("(kc p) n -> p kc n", p=128))], P.slot(), writes=["wout"])

    def outproj_tile(tt):
        sl = slice(tt * TT, (tt + 1) * TT)
        for oc in range(8):
            b, ps = cx.psum.next()
            fns = [(lambda h, hp=hp, oc=oc, ps=ps: h.matmul(
                ps[:], wout[:, hp, oc * 128:(oc + 1) * 128], oT[:, hp, sl], start=(hp == 0), stop=(hp == 7)))
                for hp in range(8)]
            P.mm_group(fns, reads=["wout"] + [("o", hp, tt, hf) for hp in range(8) for hf in range(2)],
                       writes=[("ps", b)])
            P.op("dve", lambda h, ps=ps, oc=oc: h.scalar_tensor_tensor(
                xT[:, oc, sl], ps[:], mod0[:, 16 + oc:17 + oc], xT[:, oc, sl], ALU.mult, ALU.add),
                reads=[("ps", b), ("mod", 0), ("x", oc, tt)], writes=[("x", oc, tt)])
    for tt in range(NT):
        outproj_tile(tt)
        if tt >= 1:
            emit_ln_p2(P, cx, xT, tt - 1, cx.cols[:, cb + 56:cb + 64], cx.cols[:, cb + 64:cb + 72])
        emit_ln_p1(P, cx, xT, tt)
    emit_ln_p2(P, cx, xT, NT - 1, cx.cols[:, cb + 56:cb + 64], cx.cols[:, cb + 64:cb + 72])

    if stage == "attn":
        for e_ in ("pe", "act", "dve", "pool"):
            P.wait_all(e_, fence_tokens(P, ["wout"] + [("o", hp, tt, hf) for hp in range(8) for tt in range(NT) for hf in range(2)]))
        for tt in range(NT):
            emit_store_x(P, cx, xT, y_out, tt)
        return

    ada_flush(cx)
    f2 = fence_tokens(P, ["wout"] + [("o", hp, tt, hf) for hp in range(8) for tt in range(NT) for hf in range(2)]
                      + [("bhl", i, hh, z) for i in range(2) for hh in range(2) for z in range(2)] + ["rden", "diffm", "perm", "identb", "vt"])
    for e_ in ("dve", "act", "pool"):
        P.wait_all(e_, f2)
    for tt in range(NT):
        emit_modulate(P, cx, xT, h2T, cx.mod[1], 1, tt)

    def after_ln(tt):
        emit_store_x(P, cx, xT, y_out, tt)

    emit_mlp(P, cx, xT, h2T, cx.mod[1], 1, w_up, w_down, cx.cols[:, cb + 72:cb + 80], cx.cols[:, cb + 80:cb + 88], after_ln=after_ln)


def build_B(stage=None):
    nc = bass.Bass("TRN2", target_bir_lowering=False)
    dt = lambda n, s: nc.dram_tensor(n, s, F32, kind="ExternalInput").ap()
    x_in = dt("x", [T, D])
    xp_in = dt("xp", [T, D])
    cols_in = dt("cols", [128, B_NCOLS])
    consts_in = dt("consts", [128, 8, 128])
    ada_w0 = dt("ada_w0", [D, 3 * D])
    ada_w1 = dt("ada_w1", [D, 3 * D])
    w_qkv = dt("b_w_qkv", [D, 9 * D])
    w_out = dt("b_w_out", [D, D])
    w_up = dt("mlp_w_up", [D, 4 * D])
    w_down = dt("mlp_w_down", [4 * D, D])
    y_out = nc.dram_tensor("y", [T, D], F32, kind="ExternalOutput").ap()
    xsp = nc.dram_tensor("xsp", [128, 8 * T], F32).ap()

    with contextlib.ExitStack() as es:
        P = Prog(nc, es)
        A = Arena(nc)
        cx = Ctx()
        common_setup(nc, P, A, cx, B_NCOLS)
        xT, xT_at, _ = A.alloc([128, 8, T], F32)
        io = dict(x=x_in, xp=xp_in, cols=cols_in, constsB=consts_in, ada_w10=ada_w0, ada_w11=ada_w1, b_w_qkv=w_qkv,
                  b_w_out=w_out, mlp_w_up1=w_up, mlp_w_down1=w_down, y=y_out, xsp=xsp)
        body_B(nc, P, A, cx, io, xT, xT_at, stage=stage, fused=False, cb=0)
        P.wait_all("sp", cx.out_toks)
        P.finish()
    return nc


def prep_B(inp, x1):
    c = np.asarray(inp["c"], dtype=np.float32)
    consts = host_consts_B()
    maps = []
    zeros = np.zeros((T, D), dtype=np.float32)
    for core in range(NCORES):
        b, half = core // 2, core % 2
        cols = np.zeros((128, B_NCOLS), dtype=np.float32)
        cols[:, 0:8] = col(c[b])
        cols[:, 8:32] = col(inp["ada_b"][1, 0])
        cols[:, 32:56] = col(inp["ada_b"][1, 1])
        cols[:, 56:64] = col(inp["ln_g"][1, 0])
        cols[:, 64:72] = col(inp["ln_b"][1, 0])
        cols[:, 72:80] = col(inp["ln_g"][1, 1])
        cols[:, 80:88] = col(inp["ln_b"][1, 1])
        cols[:, 88] = float(half)
        maps.append({
            "x": np.ascontiguousarray(x1[b, half * T:(half + 1) * T]) if x1 is not None else None,
            "xp": (np.ascontiguousarray(x1[b, 0:T]) if half == 1 else zeros) if x1 is not None else None,
            "cols": cols, "consts": consts,
            "ada_w0": np.ascontiguousarray(inp["ada_w"][1, 0]), "ada_w1": np.ascontiguousarray(inp["ada_w"][1, 1]),
            "b_w_qkv": np.ascontiguousarray(inp["b_w_qkv"][0]), "b_w_out": np.ascontiguousarray(inp["b_w_out"][0]),
            "mlp_w_up": np.ascontiguousarray(inp["mlp_w_up"][1]), "mlp_w_down": np.ascontiguousarray(inp["mlp_w_down"][1]),
        })
    return maps


def run_B(inp, x1, trace=False, stage=None, ncores=NCORES):
    nc = build_B(stage)
    maps = prep_B(inp, x1)[:ncores]
    res = run_bass_kernel_spmd(nc, maps, core_ids=list(range(ncores)), trace=trace)
    out = np.zeros((4, 4096, D), dtype=np.float32)
    for core in range(ncores):
        b, half = core // 2, core % 2
        out[b, half * T:(half + 1) * T] = res.results[core]["y"]
    return out, res


F_NCOLS = A_NCOLS + B_NCOLS


def build_F():
    nc = bass.Bass("TRN2", target_bir_lowering=False)
    dt = lambda n, s: nc.dram_tensor(n, s, F32, kind="ExternalInput").ap()
    io = dict(
        x=dt("x", [T, D]), cols=dt("cols", [128, F_NCOLS]), rows=dt("rows", [128, 3, 1024]),
        bbias=dt("bbias", [128, 8, 128]), constsA=dt("constsA", [128, 3, 128]), constsB=dt("constsB", [128, 8, 128]),
        ada_w00=dt("ada_w00", [D, 3 * D]), ada_w01=dt("ada_w01", [D, 3 * D]),
        ada_w10=dt("ada_w10", [D, 3 * D]), ada_w11=dt("ada_w11", [D, 3 * D]),
        a_w_in=dt("a_w_in", [D, 2 * D]), a_w_s=dt("a_w_s", [16, 128, 128]), a_w_out=dt("a_w_out", [D, D]),
        mlp_w_up0=dt("mlp_w_up0", [D, 4 * D]), mlp_w_down0=dt("mlp_w_down0", [4 * D, D]),
        mlp_w_up1=dt("mlp_w_up1", [D, 4 * D]), mlp_w_down1=dt("mlp_w_down1", [4 * D, D]),
        b_w_qkv=dt("b_w_qkv", [D, 9 * D]), b_w_out=dt("b_w_out", [D, D]))
    io["y"] = nc.dram_tensor("y", [T, D], F32, kind="ExternalOutput").ap()
    io["xsp"] = nc.dram_tensor("xsp", [128, 8 * T], F32).ap()
    io["snd"] = [nc.dram_tensor(f"snd{c}", [128, T], BF16).ap() for c in range(8)]
    io["rcv"] = [nc.dram_tensor(f"rcv{c}", [256, T], BF16).ap() for c in range(8)]
    with contextlib.ExitStack() as es:
        P = Prog(nc, es)
        A = Arena(nc)
        cx = Ctx()
        common_setup(nc, P, A, cx, F_NCOLS)
        xT, xT_at, _ = A.alloc([128, 8, T], F32)
        mark = A.off
        body_A(nc, P, A, cx, io, xT, stage=None, fused=True)
        P.barrier()
        A.off = mark
        body_B(nc, P, A, cx, io, xT, xT_at, stage=None, fused=True, cb=A_NCOLS)
        P.wait_all("sp", cx.out_toks)
        P.finish()
    return nc


def prep_F(inp):
    mA = prep_A(inp)
    mB = prep_B(inp, None)
    maps = []
    for core in range(NCORES):
        a, b = mA[core], mB[core]
        maps.append({
            "x": a["x"], "cols": np.ascontiguousarray(np.concatenate([a["cols"], b["cols"]], axis=1)),
            "rows": a["rows"], "bbias": a["bbias"], "constsA": a["consts"], "constsB": b["consts"],
            "ada_w00": a["ada_w0"], "ada_w01": a["ada_w1"], "ada_w10": b["ada_w0"], "ada_w11": b["ada_w1"],
            "a_w_in": a["a_w_in"], "a_w_s": a["a_w_s"], "a_w_out": a["a_w_out"],
            "mlp_w_up0": a["mlp_w_up"], "mlp_w_down0": a["mlp_w_down"],
            "mlp_w_up1": b["mlp_w_up"], "mlp_w_down1": b["mlp_w_down"],
            "b_w_qkv": b["b_w_qkv"], "b_w_out": b["b_w_out"],
        })
    return maps


def run_F(inp, trace=False, ncores=NCORES):
    nc = build_F()
    maps = prep_F(inp)[:ncores]
    res = run_bass_kernel_spmd(nc, maps, core_ids=list(range(ncores)), trace=trace)
    out = np.zeros((4, 4096, D), dtype=np.float32)
    for core in range(ncores):
        b, half = core // 2, core % 2
        out[b, half * T:(half + 1) * T] = res.results[core]["y"]
    return out, res


def kernel(**inputs):
    inp = {k: np.asarray(v) for k, v in inputs.items()}
    out, _ = run_F(inp)
    return out
```

```python
import contextlib
import numpy as np
import concourse.bass as bass
import concourse.mybir as mybir
from concourse.bass_utils import run_bass_kernel_spmd

F32 = mybir.dt.float32
BF16 = mybir.dt.bfloat16
AF = mybir.ActivationFunctionType
ALU = mybir.AluOpType
AX = mybir.AxisListType

D = 1024
T = 2048
NT = 4
TT = 512
NCORES = 8
ALPHA = 4.0 ** 0.25
INV_ALPHA = 1.0 / ALPHA
LN_EPS = 1e-5
EPS_P = LN_EPS / (ALPHA * ALPHA)
GELU = AF.Gelu_apprx_tanh


class Eng:
    def __init__(self, name, sem):
        self.name = name
        self.sem = sem
        self.count = 0
        self.ops = []
        self.waited = {}


class Prog:
    def __init__(self, nc, es, nsem=100):
        self.nc = nc
        self.free_sems = [es.enter_context(nc.semaphore(f"s{i}")) for i in range(nsem)]
        self.E = {}
        for n in ["pe", "act", "dve", "pool", "sp"]:
            self.E[n] = Eng(n, self.free_sems.pop())
        self.last_w = {}
        self.readers = {}
        self.sem_id = {}

    def new_sem(self):
        return self.free_sems.pop()

    def slot(self):
        sl = {"sem": self.new_sem(), "count": 0}
        self.slots = getattr(self, "slots", [])
        self.slots.append(sl)
        return sl

    def barrier(self, extra=()):
        toks = [(e.sem, e.count) for e in self.E.values() if e.count > 0]
        toks += [(sl["sem"], sl["count"]) for sl in getattr(self, "slots", []) if sl["count"] > 0]
        toks += list(extra)
        for eng in self.E:
            self.wait_all(eng, toks)

    def _sid(self, s):
        return id(s)

    @staticmethod
    def _norm(reads, writes):
        r2, w2 = [], list(writes)
        for k in reads:
            if isinstance(k, tuple) and k and k[0] == "ps":
                if k not in w2:
                    w2.append(k)
            else:
                r2.append(k)
        return r2, w2

    def _deps(self, eng, reads, writes, extra):
        toks = []
        for r in reads:
            t = self.last_w.get(r)
            if t is not None:
                toks.append(t)
        for w in writes:
            t = self.last_w.get(w)
            if t is not None:
                toks.append(t)
            for t in self.readers.get(w, {}).values():
                toks.append(t)
        for t in extra:
            if t is not None:
                toks.append(t)
        e = self.E[eng]
        waits = []
        for (s, v) in toks:
            if eng == "pe" and s is e.sem:
                continue
            k = self._sid(s)
            if e.waited.get(k, 0) < v:
                e.waited[k] = v
                waits.append((s, v))
        return waits

    def _record(self, tok, reads, writes):
        for r in reads:
            d = self.readers.setdefault(r, {})
            k = self._sid(tok[0])
            if k not in d or d[k][1] < tok[1]:
                d[k] = tok
        for w in writes:
            self.last_w[w] = tok
            self.readers[w] = {}

    def op(self, eng, fn, reads=(), writes=(), extra=()):
        reads, writes = self._norm(reads, writes)
        e = self.E[eng]
        waits = self._deps(eng, reads, writes, extra)
        e.count += 1
        tok = (e.sem, e.count)
        e.ops.append((waits, fn, (e.sem, 1)))
        self._record(tok, reads, writes)
        return tok

    def mm_group(self, fns, reads=(), writes=(), extra=()):
        reads, writes = self._norm(reads, writes)
        e = self.E["pe"]
        waits = self._deps("pe", reads, writes, extra)
        for i, fn in enumerate(fns):
            last = i == len(fns) - 1
            e.ops.append((waits if i == 0 else [], fn, (e.sem, 1) if last else None))
        e.count += 1
        tok = (e.sem, e.count)
        self._record(tok, reads, writes)
        return tok

    def dma(self, eng, pairs, slot, reads=(), writes=(), extra=()):
        e = self.E[eng]
        waits = self._deps(eng, reads, writes, extra)
        for i, (o, i_) in enumerate(pairs):
            e.ops.append((waits if i == 0 else [],
                          (lambda h, o=o, i_=i_: h.dma_start(out=o, in_=i_)),
                          (slot["sem"], 16)))
            slot["count"] += 16
        tok = (slot["sem"], slot["count"])
        self._record(tok, reads, writes)
        return tok

    def wait_all(self, eng, toks):
        e = self.E[eng]
        waits = self._deps(eng, (), (), toks)
        e.ops.append((waits, None, None))

    def replay(self, eng, h):
        for waits, fn, inc in self.E[eng].ops:
            for s, v in waits:
                h.wait_ge(s, v)
            if fn is None:
                continue
            inst = fn(h)
            if inc is not None:
                inst.then_inc(inc[0], inc[1])

    def finish(self):
        nc = self.nc
        with nc.Block() as block:
            @block.tensor
            def _(h):
                self.replay("pe", h)

            @block.scalar
            def _(h):
                self.replay("act", h)

            @block.vector
            def _(h):
                self.replay("dve", h)

            @block.gpsimd
            def _(h):
                self.replay("pool", h)

            @block.sync
            def _(h):
                self.replay("sp", h)


class Arena:
    def __init__(self, nc, base=16512, cap=229376):
        self.nc = nc
        self.cap = cap
        self.off = base
        self.n = 0

    def alloc(self, shape, dtype, at=None, name=None):
        nbytes = int(np.prod(shape[1:])) * (4 if dtype == F32 else 2)
        nbytes = (nbytes + 63) // 64 * 64
        if at is None:
            at = self.off
            self.off += nbytes
            assert self.off <= self.cap, f"SBUF overflow {self.off}"
        else:
            assert at + nbytes <= self.cap, "SBUF overflow (at)"
        self.n += 1
        t = self.nc.alloc_sbuf_tensor_at(name or f"sb{self.n}", list(shape), dtype, offset=at, align_bytes=64)
        return t, at, nbytes


class Psum:
    def __init__(self, nc, P):
        self.banks = [nc.alloc_psum_tensor(f"psb{i}", [128, 512], F32) for i in range(8)]
        self.i = 0

    def next(self):
        b = self.i
        self.i = (self.i + 1) % 7
        return b, self.banks[b]


def col(v):
    v = np.asarray(v, dtype=np.float32)
    return np.ascontiguousarray(v.reshape(-1, 128).T)


class Ctx:
    pass


class AdaStepper:
    NP = 24

    def __init__(self, P, cx, wada_ap, bias_cols, out_mod, tag):
        self.P, self.cx = P, cx
        self.wv = wada_ap.rearrange("(kc p) n -> p kc n", p=128)
        self.bias_cols, self.out_mod, self.tag = bias_cols, out_mod, tag
        self.k = 0
        self.base = cx.ada_i
        cx.ada_i += self.NP
        self._dma(0)

    def _dma(self, pc):
        bi = (self.base + pc) % 2
        self.P.dma("pool", [(self.cx.ada_buf[bi][:], self.wv[:, :, pc * 128:(pc + 1) * 128])], self.cx.ada_slot[bi],
                   writes=[("adabuf", bi)])

    def tick(self):
        if self.k >= self.NP:
            return False
        P, cx, pc = self.P, self.cx, self.k
        if pc + 1 < self.NP:
            self._dma(pc + 1)
        bi = (self.base + pc) % 2
        buf = cx.ada_buf[bi]
        ps = cx.ps_ada
        fns = [(lambda h, kc=kc, pc=pc, buf=buf: h.matmul(
            ps[:, pc:pc + 1], buf[:, kc, :], cx.sc[:, kc:kc + 1], start=(kc == 0), stop=(kc == 7)))
            for kc in range(8)]
        P.mm_group(fns, reads=[("adabuf", bi), "sc"], writes=[("ps_ada", pc)])
        self.k += 1
        if self.k == self.NP:
            self._finish()
        return True

    def flush(self):
        while self.tick():
            pass

    def _finish(self):
        P, ps, out_mod, tag = self.P, self.cx.ps_ada, self.out_mod, self.tag
        rd = [("ps_ada", f) for f in range(24)]
        P.op("dve", lambda h: h.tensor_tensor(out_mod[:, :], ps[:, 0:24], self.bias_cols, ALU.add),
             reads=rd + ["cols"], writes=[("mod", tag)])
        P.op("dve", lambda h: h.tensor_scalar(out_mod[:, 8:16], out_mod[:, 8:16], 1.0, None, ALU.add),
             reads=[("mod", tag)], writes=[("mod", tag)])
        P.op("dve", lambda h: h.tensor_scalar(out_mod[:, 16:24], out_mod[:, 16:24], 1.0, INV_ALPHA, ALU.add, ALU.mult),
             reads=[("mod", tag)], writes=[("mod", tag)])


def ada_tick(cx):
    st_ = getattr(cx, "ada_bg", None)
    if st_ is not None:
        st_.tick()


def ada_flush(cx):
    st_ = getattr(cx, "ada_bg", None)
    if st_ is not None:
        st_.flush()
        cx.ada_bg = None


def emit_modulate(P, cx, xT, hT, mod, tag, tt, eng="dve"):
    sl = slice(tt * TT, (tt + 1) * TT)
    for c in range(8):
        if eng == "act":
            P.op("act", lambda h, c=c: h.activation(hT[:, c, sl], xT[:, c, sl], AF.Identity,
                                                    bias=mod[:, c:c + 1], scale=mod[:, 8 + c:9 + c]),
                 reads=[("x", c, tt), ("mod", tag)], writes=[("h", c, tt)])
        else:
            P.op(eng, lambda h, c=c: h.tensor_scalar(hT[:, c, sl], xT[:, c, sl], mod[:, 8 + c:9 + c],
                                                     mod[:, c:c + 1], ALU.mult, ALU.add),
                 reads=[("x", c, tt), ("mod", tag)], writes=[("h", c, tt)])


def emit_ln_p1(P, cx, xT, tt):
    sl = slice(tt * TT, (tt + 1) * TT)
    s1, s2, sq = cx.ln_s1, cx.ln_s2, cx.ln_sq
    P.op("pool", lambda h: h.tensor_tensor(s1[:], xT[:, 0, sl], xT[:, 1, sl], ALU.add),
         reads=[("x", 0, tt), ("x", 1, tt)], writes=["ln_s1"])
    for c in range(2, 8):
        P.op("pool", lambda h, c=c: h.tensor_tensor(s1[:], s1[:], xT[:, c, sl], ALU.add),
             reads=[("x", c, tt), "ln_s1"], writes=["ln_s1"])
    for c in range(8):
        if c == 0:
            P.op("act", lambda h: h.activation(s2[:], xT[:, 0, sl], AF.Square),
                 reads=[("x", 0, tt)], writes=["ln_s2"])
        else:
            k = c % 2
            P.op("act", lambda h, c=c, k=k: h.activation(sq[k][:], xT[:, c, sl], AF.Square),
                 reads=[("x", c, tt)], writes=[("ln_sq", k)])
            P.op("dve", lambda h, k=k: h.tensor_tensor(s2[:], s2[:], sq[k][:], ALU.add),
                 reads=[("ln_sq", k), "ln_s2"], writes=["ln_s2"])


def emit_ln_p2(P, cx, xT, tt, g_cols, b_cols):
    sl = slice(tt * TT, (tt + 1) * TT)
    s1, s2, sq = cx.ln_s1, cx.ln_s2, cx.ln_sq
    b1, ps1 = cx.psum.next()
    P.mm_group([lambda h: h.matmul(ps1[:], cx.onesm[:], s1[:], start=True, stop=True)],
               reads=["ln_s1", "onesm"], writes=[("ps", b1)])
    b2, ps2 = cx.psum.next()
    P.mm_group([lambda h: h.matmul(ps2[:], cx.onesm[:], s2[:], start=True, stop=True)],
               reads=["ln_s2", "onesm"], writes=[("ps", b2)])
    mean, msq, rstd = cx.ln_mean, cx.ln_tmp[1], cx.ln_rstd
    P.op("act", lambda h: h.activation(mean[:], ps1[:], AF.Identity), reads=[("ps", b1)], writes=["ln_mean"])
    P.op("act", lambda h: h.activation(msq[:], ps1[:], AF.Square), reads=[("ps", b1)], writes=[("ln_tmp", 1)])
    P.op("dve", lambda h: h.tensor_tensor(rstd[:], ps2[:], msq[:], ALU.subtract),
         reads=[("ps", b2), ("ln_tmp", 1)], writes=["ln_rstd"])
    P.op("dve", lambda h: h.tensor_scalar(rstd[:], rstd[:], EPS_P, None, ALU.add),
         reads=["ln_rstd"], writes=["ln_rstd"])
    P.op("dve", lambda h: h.reciprocal(rstd[:], rstd[:]), reads=["ln_rstd"], writes=["ln_rstd"])
    P.op("act", lambda h: h.activation(rstd[:], rstd[:], AF.Sqrt), reads=["ln_rstd"], writes=["ln_rstd"])
    for c in range(8):
        k = c % 2
        tmp = cx.ln_tmp[k]
        P.op("dve", lambda h, c=c, tmp=tmp: h.tensor_tensor(tmp[:], xT[:, c, sl], mean[:], ALU.subtract),
             reads=[("x", c, tt), "ln_mean"], writes=[("ln_tmp", k)])
        P.op("pool", lambda h, tmp=tmp: h.tensor_tensor(tmp[:], tmp[:], rstd[:], ALU.mult),
             reads=["ln_rstd", ("ln_tmp", k)], writes=[("ln_tmp", k)])
        P.op("act", lambda h, c=c, tmp=tmp: h.activation(xT[:, c, sl], tmp[:], AF.Identity,
                                                         bias=b_cols[:, c:c + 1], scale=g_cols[:, c:c + 1]),
             reads=[("ln_tmp", k), "cols"], writes=[("x", c, tt)])


def emit_ln(P, cx, xT, tt, g_cols, b_cols):
    emit_ln_p1(P, cx, xT, tt)
    emit_ln_p2(P, cx, xT, tt, g_cols, b_cols)


def emit_mlp(P, cx, xT, hT, mod, tag, w_up_ap, w_down_ap, g_cols, b_cols, after_ln=None):
    wu_v = w_up_ap.rearrange("(kc p) n -> p kc n", p=128)
    wd_v = w_down_ap.rearrange("(fc p) n -> p fc n", p=128)
    for q in range(4):
        bi = cx.mlp_i % 2
        cx.mlp_i += 1
        wu, wd = cx.wu_buf[bi], cx.wd_buf[bi]
        P.dma("pool", [(wu[:, 0:4, :], wu_v[:, 0:4, q * 1024:(q + 1) * 1024]),
                       (wu[:, 4:8, :], wu_v[:, 4:8, q * 1024:(q + 1) * 1024])],
              cx.wu_slot[bi], writes=[("wu", bi)])
        P.dma("pool", [(wd[:, 0:4, :], wd_v[:, q * 8:q * 8 + 4, :]),
                       (wd[:, 4:8, :], wd_v[:, q * 8 + 4:q * 8 + 8, :])],
              cx.wd_slot[bi], writes=[("wd", bi)])
        def mlp_tile(q, tt, bi, wu, wd):
            sl = slice(tt * TT, (tt + 1) * TT)
            ai = cx.a_i % 2
            cx.a_i += 1
            aT = cx.a_buf[ai]
            for f in range(8):
                b, ps = cx.psum.next()
                fns = [(lambda h, kc=kc, f=f, ps=ps, wu=wu: h.matmul(
                    ps[:], wu[:, kc, f * 128:(f + 1) * 128], hT[:, kc, sl], start=(kc == 0), stop=(kc == 7)))
                    for kc in range(8)]
                P.mm_group(fns, reads=[("wu", bi)] + [("h", kc, tt) for kc in range(8)], writes=[("ps", b)])
                if f % 4 == 0:
                    ada_tick(cx)
                k = cx.r_i % 2
                cx.r_i += 1
                rt = cx.relu_tmp[k]
                P.op("act", lambda h, ps=ps, rt=rt: h.activation(rt[:], ps[:], AF.Relu),
                     reads=[("ps", b)], writes=[("ln_sq", k)])
                P.op("pool", lambda h, rt=rt, aT=aT, f=f: h.tensor_tensor(aT[:, f, :], rt[:], rt[:], ALU.mult),
                     reads=[("ln_sq", k)], writes=[("a", ai, f)])
            for oc in range(8):
                b, ps = cx.psum.next()
                fns = [(lambda h, f=f, oc=oc, ps=ps, wd=wd, aT=aT: h.matmul(
                    ps[:], wd[:, f, oc * 128:(oc + 1) * 128], aT[:, f, :], start=(f == 0), stop=(f == 7)))
                    for f in range(8)]
                P.mm_group(fns, reads=[("wd", bi)] + [("a", ai, f) for f in range(8)], writes=[("ps", b)])
                P.op("dve", lambda h, ps=ps, oc=oc: h.scalar_tensor_tensor(
                    xT[:, oc, sl], ps[:], mod[:, 16 + oc:17 + oc], xT[:, oc, sl], ALU.mult, ALU.add),
                    reads=[("ps", b), ("mod", tag), ("x", oc, tt)], writes=[("x", oc, tt)])
        for tt in range(NT):
            mlp_tile(q, tt, bi, wu, wd)
            if q == 3 and tt >= 1:
                emit_ln_p2(P, cx, xT, tt - 1, g_cols, b_cols)
                if after_ln is not None:
                    after_ln(tt - 1)
            if q == 3:
                emit_ln_p1(P, cx, xT, tt)
        if q == 3:
            emit_ln_p2(P, cx, xT, NT - 1, g_cols, b_cols)
            if after_ln is not None:
                after_ln(NT - 1)


def emit_load_xT(P, cx, x_ap, xT, key="x"):
    xv = x_ap.rearrange("(tb p) f -> p tb f", p=128)
    for tt in range(NT):
        si = cx.stg_i % 2
        cx.stg_i += 1
        stg = cx.stg[si]
        P.dma("sp", [(stg[:, 0:2, :], xv[:, tt * 4:tt * 4 + 2, :]),
                     (stg[:, 2:4, :], xv[:, tt * 4 + 2:tt * 4 + 4, :])], cx.stg_slot[si],
              writes=[("stg", si)])
        for c in range(8):
            b, ps = cx.psum.next()
            fns = [(lambda h, tb=tb, c=c, ps=ps, stg=stg: h.matmul(
                ps[:, tb * 128:(tb + 1) * 128], stg[:, tb, c * 128:(c + 1) * 128], cx.ident[:],
                start=True, stop=True)) for tb in range(4)]
            P.mm_group(fns, reads=[("stg", si), "ident"], writes=[("ps", b)])
            ada_tick(cx)
            eng = "act" if c % 2 == 0 else "dve"
            sl = slice(tt * TT, (tt + 1) * TT)
            if eng == "act":
                P.op("act", lambda h, ps=ps, c=c, sl=sl: h.activation(xT[:, c, sl], ps[:], AF.Identity),
                     reads=[("ps", b)], writes=[(key, c, tt)])
            else:
                P.op("dve", lambda h, ps=ps, c=c, sl=sl: h.tensor_copy(xT[:, c, sl], ps[:]),
                     reads=[("ps", b)], writes=[(key, c, tt)])


def emit_store_x(P, cx, xT, out_ap, tt):
    ov = out_ap.rearrange("(tb p) f -> p tb f", p=128)
    for tb in range(4):
        si = cx.ost_i % 2
        cx.ost_i += 1
        stg = cx.ost[si]
        tsl = slice(tt * TT + tb * 128, tt * TT + (tb + 1) * 128)
        for half in range(2):
            b, ps = cx.psum.next()
            fns = [(lambda h, cc=cc, half=half, ps=ps, tsl=tsl: h.matmul(
                ps[:, cc * 128:(cc + 1) * 128], xT[:, half * 4 + cc, tsl], cx.ident[:],
                start=True, stop=True)) for cc in range(4)]
            P.mm_group(fns, reads=[("x", half * 4 + cc, tt) for cc in range(4)] + ["ident"],
                       writes=[("ps", b)])
            if half == 0:
                P.op("act", lambda h, ps=ps, stg=stg: h.activation(stg[:, 0:512], ps[:], AF.Identity),
                     reads=[("ps", b)], writes=[("ost", si, 0)])
            else:
                P.op("dve", lambda h, ps=ps, stg=stg: h.tensor_copy(stg[:, 512:1024], ps[:]),
                     reads=[("ps", b)], writes=[("ost", si, 1)])
        tok = P.dma("sp", [(ov[:, tt * 4 + tb, :], stg[:])], cx.out_slot[si],
                    reads=[("ost", si, 0), ("ost", si, 1)])
        cx.out_toks.append(tok)


def common_setup(nc, P, A, cx, ncols):
    cx.psum = Psum(nc, P)
    cx.cols, _, _ = A.alloc([128, ncols], F32)
    cx.ident_f, _, _ = A.alloc([128, 128], F32)
    cx.ident = cx.ident_f
    cx.onesm, _, _ = A.alloc([128, 128], F32)
    cx.sc, _, _ = A.alloc([128, 8], BF16)
    cx.mod = [A.alloc([128, 24], F32)[0] for _ in range(2)]
    cx.ln_s1, _, _ = A.alloc([128, TT], F32)
    cx.ln_s2, _, _ = A.alloc([128, TT], F32)
    cx.ln_sq = [A.alloc([128, TT], F32)[0] for _ in range(2)]
    cx.ln_mean, _, _ = A.alloc([128, TT], F32)
    cx.ln_rstd, _, _ = A.alloc([128, TT], F32)
    cx.ln_tmp = [A.alloc([128, TT], F32)[0] for _ in range(2)]
    cx.relu_tmp = cx.ln_sq
    cx.ada_buf = [A.alloc([128, 8, 128], BF16)[0] for _ in range(2)]
    cx.ada_slot = [P.slot() for _ in range(2)]
    cx.ada_i = 0
    cx.mlp_i = 0
    cx.a_i = 0
    cx.r_i = 0
    cx.stg_i = 0
    cx.ost_i = 0
    cx.out_toks = []
    cx.out_slot = [P.slot() for _ in range(2)]
    cx.const_slot = P.slot()
    cx.stg_slot = [P.slot() for _ in range(2)]
    cx.wu_slot = [P.slot() for _ in range(2)]
    cx.wd_slot = [P.slot() for _ in range(2)]
    cx.ps_ada = cx.psum.banks[7]


def fence_tokens(P, keys):
    toks = []
    for k in keys:
        t = P.last_w.get(k)
        if t is not None:
            toks.append(t)
        toks.extend(P.readers.get(k, {}).values())
    return toks


A_COLS = dict(c=0, adab0=8, adab1=32, lng0=56, lnb0=64, lng1=72, lnb1=80, binu=88)
A_NCOLS = 96


def body_A(nc, P, A, cx, io, xT, stage=None, fused=False):
    x_in, cols_in, rows_in, bb_in, consts_in = io["x"], io["cols"], io["rows"], io["bbias"], io["constsA"]
    ada_w0, ada_w1, w_in, w_s, w_out = io["ada_w00"], io["ada_w01"], io["a_w_in"], io["a_w_s"], io["a_w_out"]
    w_up, w_down, y_out = io["mlp_w_up0"], io["mlp_w_down0"], io["y"]
    mark = A.off
    win, _, _ = A.alloc([128, 8, 2048], BF16)
    wout, _, _ = A.alloc([128, 8, 1024], BF16)
    rows, _, _ = A.alloc([128, 3, 1024], F32)
    bbias, _, _ = A.alloc([128, 8, 128], F32)
    wcT, _, _ = A.alloc([128, 16, 128], BF16)
    vst, _, _ = A.alloc([128, 2, 6], F32)
    vmv, _, _ = A.alloc([128, 2], F32)
    vrs, _, _ = A.alloc([128, 1], F32)
    mark1b = A.off
    hT1 = [A.alloc([128, 8, TT], BF16)[0] for _ in range(2)]
    uT1, _, _ = A.alloc([128, 8, TT], BF16)
    uT = [uT1, uT1]
    vtok1, _, _ = A.alloc([128, 4, 1024], BF16)
    vtok = [vtok1, vtok1]
    vg = [A.alloc([128, 1024], F32)[0] for _ in range(2)]
    gt = [A.alloc([128, TT], F32)[0] for _ in range(2)]
    end1 = A.off
    A.off = mark1b
    cx.stg = [A.alloc([128, 4, 1024], F32)[0] for _ in range(2)]
    tril, _, _ = A.alloc([128, 128], F32)
    identb, _, _ = A.alloc([128, 128], BF16)
    ws_f, _, _ = A.alloc([128, 16, 128], F32)
    ws_b, _, _ = A.alloc([128, 16, 128], BF16)
    end1 = max(end1, A.off)
    A.off = mark
    h2T, _, _ = A.alloc([128, 8, T], BF16)
    cx.wu_buf = [A.alloc([128, 8, 1024], BF16)[0] for _ in range(2)]
    cx.wd_buf = [A.alloc([128, 8, 1024], BF16)[0] for _ in range(2)]
    cx.a_buf = [A.alloc([128, 8, TT], BF16)[0] for _ in range(2)]
    ost_at = A.off
    cx.ost = [A.alloc([128, 1024], F32)[0] for _ in range(2)]
    end2 = A.off
    hst = A.alloc([128, 8, TT], BF16, at=ost_at)[0] if fused else None
    A.off = max(end1, end2)
    print("SBUF plan A: persistent", mark, "phase1 end", end1, "phase2 end", end2)

    P.dma("sp", [(cx.cols[:], cols_in)], P.slot(), writes=["cols"])
    P.dma("sp", [(cx.ident_f[:], consts_in[:, 0, :]), (cx.onesm[:], consts_in[:, 1, :]),
                 (tril[:], consts_in[:, 2, :])], P.slot(), writes=["ident", "onesm", "tril"])
    P.dma("sp", [(rows[:], rows_in)], P.slot(), writes=["rows"])
    P.dma("sp", [(bbias[:], bb_in)], P.slot(), writes=["bbias"])
    P.dma("sp", [(ws_f[:], w_s.rearrange("g t s -> t g s"))], P.slot(), writes=["ws_f"])
    P.op("act", lambda h: h.activation(cx.sc[:], cx.cols[:, 0:8], AF.Silu), reads=["cols"], writes=["sc"])
    cx.ada_bg = AdaStepper(P, cx, ada_w0, cx.cols[:, 8:32], cx.mod[0], 0)
    wv = w_in.rearrange("(kc p) n -> p kc n", p=128)
    for j in range(4):
        P.dma("pool", [(win[:, :, j * 512:(j + 1) * 512], wv[:, :, j * 512:(j + 1) * 512])], P.slot(),
              writes=[("win", j)])
    wout_slot = P.slot()
    P.dma("pool", [(wout[:], w_out.rearrange("(kc p) n -> p kc n", p=128))], wout_slot, writes=["wout"])
    P.op("dve", lambda h: h.tensor_copy(identb[:], cx.ident_f[:]), reads=["ident"], writes=["identb"])
    P.op("dve", lambda h: h.tensor_tensor(ws_b[:], ws_f[:], tril[:, None, :].to_broadcast([128, 16, 128]), ALU.mult),
         reads=["ws_f", "tril"], writes=["ws_b"])
    for g4 in range(4):
        b, ps = cx.psum.next()
        fns = [(lambda h, gi=gi, g4=g4, ps=ps: h.matmul(
            ps[:, gi * 128:(gi + 1) * 128], ws_b[:, g4 * 4 + gi, :], identb[:], start=True, stop=True))
            for gi in range(4)]
        P.mm_group(fns, reads=["ws_b", "identb"], writes=[("ps", b)])
        P.op("dve", lambda h, g4=g4, ps=ps: h.tensor_copy(
            wcT[:, g4 * 4:(g4 + 1) * 4, :], ps[:].rearrange("p (g t) -> p g t", g=4)),
            reads=[("ps", b)], writes=[("wcT", g4)])
    emit_load_xT(P, cx, x_in, xT)
    if stage == "load":
        for tt in range(NT):
            emit_store_x(P, cx, xT, y_out, tt)
        return
    ada_flush(cx)
    cx.ada_bg = AdaStepper(P, cx, ada_w1, cx.cols[:, 32:56], cx.mod[1], 1)
    f1 = fence_tokens(P, [("stg", 0), ("stg", 1), "tril", "identb", "ws_f", "ws_b"])
    for e_ in ("dve", "act", "pool"):
        P.wait_all(e_, f1)

    def mix_tile(tt):
        sl = slice(tt * TT, (tt + 1) * TT)
        pi = tt % 2
        hT = hT1[pi]
        def modulate(t2):
            p2 = t2 % 2
            sl2 = slice(t2 * TT, (t2 + 1) * TT)
            for c in range(8):
                P.op("dve", lambda h, c=c, p2=p2, sl2=sl2: h.tensor_scalar(
                    hT1[p2][:, c, :], xT[:, c, sl2], cx.mod[0][:, 8 + c:9 + c], cx.mod[0][:, c:c + 1], ALU.mult, ALU.add),
                    reads=[("x", c, t2), ("mod", 0)], writes=[("h1", p2, c)])
        if tt == 0:
            modulate(0)
        for tb in range(4):
            vi = (tt * 4 + tb) % 2
            vgb = vg[vi]
            for half in range(2):
                b, ps = cx.psum.next()
                fns = [(lambda h, kc=kc, half=half, ps=ps, hT=hT, tb=tb: h.matmul(
                    ps[:], hT[:, kc, tb * 128:(tb + 1) * 128],
                    win[:, kc, 1024 + half * 512:1024 + (half + 1) * 512], start=(kc == 0), stop=(kc == 7)))
                    for kc in range(8)]
                P.mm_group(fns, reads=[("win", 2 + half)] + [("h1", pi, kc) for kc in range(8)],
                           writes=[("ps", b)])
                if half == 0:
                    ada_tick(cx)
                P.op("dve", lambda h, ps=ps, half=half, vgb=vgb: h.tensor_tensor(
                    vgb[:, half * 512:(half + 1) * 512], ps[:], rows[:, 0, half * 512:(half + 1) * 512], ALU.add),
                    reads=[("ps", b), "rows"], writes=[("vg", vi, half)])
                P.op("act", lambda h, half=half, vgb=vgb: h.activation(
                    vgb[:, half * 512:(half + 1) * 512], vgb[:, half * 512:(half + 1) * 512], GELU),
                    reads=[("vg", vi, half)], writes=[("vg", vi, half)])
                P.op("dve", lambda h, half=half, vgb=vgb: h.bn_stats(
                    vst[:, half, :], vgb[:, half * 512:(half + 1) * 512]),
                    reads=[("vg", vi, half)], writes=[("vst", half)])
            P.op("dve", lambda h: h.bn_aggr(vmv[:], vst[:].rearrange("p a b -> p (a b)")),
                 reads=[("vst", 0), ("vst", 1)], writes=["vmv"])
            P.op("dve", lambda h: h.tensor_scalar(vrs[:], vmv[:, 1:2], LN_EPS, None, ALU.add),
                 reads=["vmv"], writes=["vrs"])
            P.op("dve", lambda h: h.reciprocal(vrs[:], vrs[:]), reads=["vrs"], writes=["vrs"])
            P.op("act", lambda h: h.activation(vrs[:], vrs[:], AF.Sqrt), reads=["vrs"], writes=["vrs"])
            P.op("dve", lambda h, vgb=vgb: h.tensor_scalar(
                vgb[:], vgb[:], vmv[:, 0:1], vrs[:, 0:1], ALU.subtract, ALU.mult),
                reads=[("vg", vi, 0), ("vg", vi, 1), "vmv", "vrs"], writes=[("vg", vi, 0), ("vg", vi, 1)])
            P.op("pool", lambda h, vgb=vgb: h.tensor_tensor(vgb[:], vgb[:], rows[:, 1, :], ALU.mult),
                 reads=[("vg", vi, 0), ("vg", vi, 1), "rows"], writes=[("vg", vi, 0), ("vg", vi, 1)])
            P.op("pool", lambda h, vgb=vgb, tb=tb: h.tensor_tensor(vtok[pi][:, tb, :], vgb[:], rows[:, 2, :], ALU.add),
                 reads=[("vg", vi, 0), ("vg", vi, 1), "rows"], writes=[("vtok", 0, tb)])
        if tt + 1 < NT:
            modulate(tt + 1)
        for fc in range(8):
            b, ps = cx.psum.next()
            fns = [(lambda h, kc=kc, fc=fc, ps=ps, hT=hT: h.matmul(
                ps[:], win[:, kc, fc * 128:(fc + 1) * 128], hT[:, kc, :], start=(kc == 0), stop=(kc == 7)))
                for kc in range(8)]
            P.mm_group(fns, reads=[("win", fc // 4)] + [("h1", pi, kc) for kc in range(8)], writes=[("ps", b)])
            if fc % 2 == 0:
                ada_tick(cx)
            P.op("act", lambda h, fc=fc, ps=ps: h.activation(
                uT[pi][:, fc, :], ps[:], GELU, bias=cx.cols[:, A_COLS["binu"] + fc:A_COLS["binu"] + fc + 1]),
                reads=[("ps", b), "cols"], writes=[("u", 0, fc)])
        for fc in range(8):
            b, ps = cx.psum.next()
            fns = []
            for tb in range(4):
                for gi in range(2):
                    g = 2 * fc + gi
                    fns.append(lambda h, tb=tb, gi=gi, g=g, ps=ps: h.matmul(
                        ps[gi * 64:(gi + 1) * 64, tb * 128:(tb + 1) * 128],
                        vtok[pi][:, tb, g * 64:(g + 1) * 64], wcT[:, g, :], start=True, stop=True))
            P.mm_group(fns, reads=[("vtok", 0, tb) for tb in range(4)] + [("wcT", (2 * fc) // 4)],
                       writes=[("ps", b)])
            k = fc % 2
            P.op("dve", lambda h, ps=ps, fc=fc, k=k: h.tensor_tensor(
                gt[k][:].rearrange("p (a t) -> p a t", a=4), ps[:].rearrange("p (a t) -> p a t", a=4),
                bbias[:, fc, None, :].to_broadcast([128, 4, 128]), ALU.add),
                reads=[("ps", b), "bbias"], writes=[("gt", k)])
            P.op("pool", lambda h, fc=fc, k=k: h.tensor_tensor(uT[pi][:, fc, :], gt[k][:], uT[pi][:, fc, :], ALU.mult),
                 reads=[("gt", k), ("u", 0, fc)], writes=[("u", 0, fc)])
        if tt >= 1:
            emit_ln_p2(P, cx, xT, tt - 1, cx.cols[:, A_COLS["lng0"]:A_COLS["lng0"] + 8],
                       cx.cols[:, A_COLS["lnb0"]:A_COLS["lnb0"] + 8])
        for oc in range(8):
            b, ps = cx.psum.next()
            fns = [(lambda h, fc=fc, oc=oc, ps=ps: h.matmul(
                ps[:], wout[:, fc, oc * 128:(oc + 1) * 128], uT[pi][:, fc, :], start=(fc == 0), stop=(fc == 7)))
                for fc in range(8)]
            P.mm_group(fns, reads=["wout"] + [("u", 0, fc) for fc in range(8)], writes=[("ps", b)])
            P.op("dve", lambda h, ps=ps, oc=oc: h.scalar_tensor_tensor(
                xT[:, oc, sl], ps[:], cx.mod[0][:, 16 + oc:17 + oc], xT[:, oc, sl], ALU.mult, ALU.add),
                reads=[("ps", b), ("mod", 0), ("x", oc, tt)], writes=[("x", oc, tt)])
        emit_ln_p1(P, cx, xT, tt)

    for tt in range(NT):
        mix_tile(tt)
    emit_ln_p2(P, cx, xT, NT - 1, cx.cols[:, A_COLS["lng0"]:A_COLS["lng0"] + 8],
               cx.cols[:, A_COLS["lnb0"]:A_COLS["lnb0"] + 8])

    if stage == "mixA":
        dbg = nc.dram_tensor("dbg", [128, 16384], F32, kind="ExternalOutput").ap()
        dsl = P.slot()
        allk = list(P.last_w.keys())
        cx.out_toks.append(P.dma("sp", [(dbg[:, 0:24], cx.mod[0][:])], dsl, reads=allk))
        cx.out_toks.append(P.dma("pool", [(dbg[:, 1024:3072], wcT[:].rearrange("p g t -> p (g t)"))], dsl, reads=allk))
        cx.out_toks.append(P.dma("pool", [(dbg[:, 4096:8192], hT1[1][:].rearrange("p g t -> p (g t)"))], dsl, reads=allk))
        cx.out_toks.append(P.dma("pool", [(dbg[:, 8192:12288], uT1[:].rearrange("p g t -> p (g t)"))], dsl, reads=allk))
        cx.out_toks.append(P.dma("pool", [(dbg[:, 12288:16384], vtok1[:].rearrange("p g t -> p (g t)"))], dsl, reads=allk))
        for e_ in ("pe", "act", "dve", "pool"):
            P.wait_all(e_, list(cx.out_toks))
        for tt in range(NT):
            emit_store_x(P, cx, xT, y_out, tt)
        return
    ada_flush(cx)
    if fused:
        cx.ada_bg = AdaStepper(P, cx, io["ada_w10"], cx.cols[:, A_NCOLS + 8:A_NCOLS + 32], cx.mod[0], 0)
        cx.pre_ada_b0 = True
    p1_keys = [("win", j) for j in range(4)] + ["wout", "rows", "bbias", "vst", "vmv", "vrs"] + \
              [("wcT", g4) for g4 in range(4)] + [("h1", p, c) for p in range(2) for c in range(8)] + \
              [("u", 0, c) for c in range(8)] + [("vtok", 0, tb) for tb in range(4)] + \
              [("vg", i, hf) for i in range(2) for hf in range(2)] + [("gt", 0), ("gt", 1)]
    f2 = fence_tokens(P, p1_keys)
    for e_ in ("dve", "act", "pool"):
        P.wait_all(e_, f2)
    for tt in range(NT):
        emit_modulate(P, cx, xT, h2T, cx.mod[1], 1, tt)

    def after_ln(tt):
        emit_store_x(P, cx, xT, y_out, tt)

    if fused:
        cx.cc_sem = P.new_sem()
        snd_slot = P.slot()
        cx.spill_slot = P.slot()
        cx.spill_toks = []

    def send_tile(tt):
        ada_flush(cx)
        modb = cx.mod[0]
        sl = slice(tt * TT, (tt + 1) * TT)
        for c in range(8):
            if c % 2:
                P.op("act", lambda h, c=c: h.activation(hst[:, c, :], xT[:, c, sl], AF.Identity,
                                                        bias=modb[:, c:c + 1], scale=modb[:, 8 + c:9 + c]),
                     reads=[("x", c, tt), ("mod", 0)], writes=[("hst", c)])
            else:
                P.op("dve", lambda h, c=c: h.tensor_scalar(hst[:, c, :], xT[:, c, sl], modb[:, 8 + c:9 + c],
                                                           modb[:, c:c + 1], ALU.mult, ALU.add),
                     reads=[("x", c, tt), ("mod", 0)], writes=[("hst", c)])
        snd, rcv = io["snd"], io["rcv"]
        t_snd = P.dma("sp", [(snd[tt * 2 + hf], hst[:, hf * 4:(hf + 1) * 4, :].rearrange("p c t -> p (c t)"))
                             for hf in range(2)], snd_slot, reads=[("hst", c) for c in range(8)])
        xsp_ = io["xsp"]
        cx.spill_toks.append(P.dma("sp", [(xsp_[:, c * T + tt * TT:c * T + (tt + 1) * TT], xT[:, c, sl]) for c in range(8)],
                                   cx.spill_slot, reads=[("x", c, tt) for c in range(8)]))
        ep = P.E["pool"]
        w_ = P._deps("pool", (), (), [t_snd])
        for hf in range(2):
            ep.ops.append((w_ if hf == 0 else [],
                           (lambda h, i=tt * 2 + hf: h.collective_compute(
                               "AllGather", ALU.bypass, replica_groups=[[0, 1], [2, 3], [4, 5], [6, 7]],
                               ins=[snd[i].opt()], outs=[rcv[i].opt()])),
                           (cx.cc_sem, 1)))

    emit_mlp(P, cx, xT, h2T, cx.mod[1], 1, w_up, w_down,
             cx.cols[:, A_COLS["lng1"]:A_COLS["lng1"] + 8], cx.cols[:, A_COLS["lnb1"]:A_COLS["lnb1"] + 8],
             after_ln=(send_tile if fused else after_ln))
    ada_flush(cx)


def build_A(stage=None):
    nc = bass.Bass("TRN2", target_bir_lowering=False)
    dt = lambda n, s: nc.dram_tensor(n, s, F32, kind="ExternalInput").ap()
    x_in = dt("x", [T, D])
    cols_in = dt("cols", [128, A_NCOLS])
    rows_in = dt("rows", [128, 3, 1024])
    bb_in = dt("bbias", [128, 8, 128])
    consts_in = dt("consts", [128, 3, 128])
    ada_w0 = dt("ada_w0", [D, 3 * D])
    ada_w1 = dt("ada_w1", [D, 3 * D])
    w_in = dt("a_w_in", [D, 2 * D])
    w_s = dt("a_w_s", [16, 128, 128])
    w_out = dt("a_w_out", [D, D])
    w_up = dt("mlp_w_up", [D, 4 * D])
    w_down = dt("mlp_w_down", [4 * D, D])
    y_out = nc.dram_tensor("y", [T, D], F32, kind="ExternalOutput").ap()

    with contextlib.ExitStack() as es:
        P = Prog(nc, es)
        A = Arena(nc)
        cx = Ctx()
        common_setup(nc, P, A, cx, A_NCOLS)
        xT, _, _ = A.alloc([128, 8, T], F32)
        io = dict(x=x_in, cols=cols_in, rows=rows_in, bbias=bb_in, constsA=consts_in, ada_w00=ada_w0, ada_w01=ada_w1,
                  a_w_in=w_in, a_w_s=w_s, a_w_out=w_out, mlp_w_up0=w_up, mlp_w_down0=w_down, y=y_out)
        body_A(nc, P, A, cx, io, xT, stage=stage, fused=False)
        P.wait_all("sp", cx.out_toks)
        P.finish()
    return nc


def host_consts():
    ident = np.eye(128, dtype=np.float32)
    ones = np.full((128, 128), 1.0 / 1024.0, dtype=np.float32)
    tril = np.tril(np.ones((128, 128), dtype=np.float32))
    return np.ascontiguousarray(np.stack([ident, ones, tril], axis=1))


def prep_A(inp):
    x = np.asarray(inp["x"], dtype=np.float32)
    c = np.asarray(inp["c"], dtype=np.float32)
    maps = []
    consts = host_consts()
    rows = np.stack([inp["a_b_in"][0][1024:], inp["a_vn_g"][0], inp["a_vn_b"][0]], axis=0).astype(np.float32)
    rows = np.ascontiguousarray(np.broadcast_to(rows[None], (128, 3, 1024)))
    bs = np.asarray(inp["a_b_s"][0], dtype=np.float32)
    bb = np.zeros((128, 8, 128), dtype=np.float32)
    for fc in range(8):
        bb[0:64, fc, :] = bs[2 * fc][None, :]
        bb[64:128, fc, :] = bs[2 * fc + 1][None, :]
    for core in range(NCORES):
        b, half = core // 2, core % 2
        cols = np.zeros((128, A_NCOLS), dtype=np.float32)
        cols[:, 0:8] = col(c[b])
        cols[:, 8:32] = col(inp["ada_b"][0, 0])
        cols[:, 32:56] = col(inp["ada_b"][0, 1])
        cols[:, 56:64] = col(inp["ln_g"][0, 0])
        cols[:, 64:72] = col(inp["ln_b"][0, 0])
        cols[:, 72:80] = col(inp["ln_g"][0, 1])
        cols[:, 80:88] = col(inp["ln_b"][0, 1])
        cols[:, 88:96] = col(inp["a_b_in"][0][:1024])
        maps.append({
            "x": np.ascontiguousarray(x[b, half * T:(half + 1) * T]),
            "cols": cols, "rows": rows, "bbias": bb, "consts": consts,
            "ada_w0": np.ascontiguousarray(inp["ada_w"][0, 0]), "ada_w1": np.ascontiguousarray(inp["ada_w"][0, 1]),
            "a_w_in": np.ascontiguousarray(inp["a_w_in"][0]), "a_w_s": np.ascontiguousarray(inp["a_w_s"][0]),
            "a_w_out": np.ascontiguousarray(inp["a_w_out"][0]),
            "mlp_w_up": np.ascontiguousarray(inp["mlp_w_up"][0]), "mlp_w_down": np.ascontiguousarray(inp["mlp_w_down"][0]),
        })
    return maps


def run_A(inp, trace=False, stage=None, ncores=NCORES):
    nc = build_A(stage)
    maps = prep_A(inp)[:ncores]
    res = run_bass_kernel_spmd(nc, maps, core_ids=list(range(ncores)), trace=trace)
    x1 = np.zeros((4, 4096, D), dtype=np.float32)
    for core in range(ncores):
        b, half = core // 2, core % 2
        x1[b, half * T:(half + 1) * T] = res.results[core]["y"]
    if stage == "mixA":
        np.save("dbg0.npy", res.results[0]["dbg"])
    return x1, res


B_NCOLS = 96
PATTERNS = ((128, 1), (512, 4), (2048, 16))


def host_consts_B():
    ident = np.eye(128, dtype=np.float32)
    ones = np.full((128, 128), 1.0 / 1024.0, dtype=np.float32)
    k = np.arange(128)[:, None]
    q = np.arange(128)[None, :]
    d_prev = np.where(k >= q, 128.0 + q - k, 0.0).astype(np.float32)
    d_cur = np.where(k <= q, (q - k) * 1.0, 0.0).astype(np.float32)
    m_prev = np.where(k >= q, 0.0, -240000.0).astype(np.float32)
    m_cur = np.where(k <= q, 0.0, -240000.0).astype(np.float32)
    m = np.arange(128)[None, :]
    permA = ((k == m + 64) & (m < 64)).astype(np.float32)
    permB = ((k == m - 64) & (m >= 64)).astype(np.float32)
    return np.ascontiguousarray(np.stack([ident, ones, d_prev, d_cur, permA, permB, m_prev, m_cur], axis=1))


def body_B(nc, P, A, cx, io, xT, xT_at, stage=None, fused=False, cb=0):
    x_in, xp_in, cols_in, consts_in = io.get("x"), io.get("xp"), io["cols"], io["constsB"]
    ada_w0, ada_w1, w_qkv, w_out = io["ada_w10"], io["ada_w11"], io["b_w_qkv"], io["b_w_out"]
    w_up, w_down, y_out, xsp = io["mlp_w_up1"], io["mlp_w_down1"], io["y"], io["xsp"]
    mark = xT_at
    r1_end = xT_at + 65536
    save_off = A.off
    A.off = mark
    kt0, _, _ = A.alloc([128, 32 * 128], BF16)
    KT = [kt0, kt0]
    VT, _, _ = A.alloc([128, 32 * 128], BF16)
    VA, _, _ = A.alloc([128, 32, 2, 128], BF16)
    ACC, _, _ = A.alloc([128, 2, T], F32)
    QT = [A.alloc([128, T], BF16)[0] for _ in range(2)]
    tmpB = [A.alloc([128, 256], F32)[0] for _ in range(2)]
    Pt = [A.alloc([128, 512], BF16)[0] for _ in range(2)]
    assert A.off <= r1_end, (A.off, r1_end)
    A.off = max(r1_end, save_off)
    mark2 = A.off
    hT, _, _ = A.alloc([128, 8, 2 * T], BF16)
    oT, oT_at, _ = A.alloc([128, 8, T], BF16)
    wqkv = [A.alloc([128, 3, 8, 128], BF16)[0] for _ in range(2)]
    Bhl = [A.alloc([128, 2, 2, 256], BF16)[0] for _ in range(2)]
    rden, _, _ = A.alloc([128, 512], F32)
    identb, _, _ = A.alloc([128, 128], BF16)
    diffm, _, _ = A.alloc([128, 2, 128], F32)
    perm, _, _ = A.alloc([128, 2, 128], F32)
    mask01, _, _ = A.alloc([128, 2, 128], F32)
    endB1 = A.off
    A.off = oT_at
    cx.stg = [A.alloc([128, 4, 1024], F32)[0] for _ in range(2)]
    assert A.off <= oT_at + 32768
    A.off = mark2
    wout, _, _ = A.alloc([128, 8, 1024], BF16)
    A.off = mark2
    h2T, _, _ = A.alloc([128, 8, T], BF16)
    cx.wu_buf = [A.alloc([128, 8, 1024], BF16)[0] for _ in range(2)]
    cx.wd_buf = [A.alloc([128, 8, 1024], BF16)[0] for _ in range(2)]
    cx.a_buf = [A.alloc([128, 8, TT], BF16)[0] for _ in range(2)]
    cx.ost = [A.alloc([128, 1024], F32)[0] for _ in range(2)]
    endB2 = A.off
    print("SBUF plan B: mark", mark, "r1_end", r1_end, "attn end", endB1, "mlp end", endB2)
    assert max(endB1, endB2) <= A.cap

    flag = cx.cols[:, cb + 88:cb + 89]
    if not fused:
        P.dma("sp", [(cx.cols[:], cols_in)], P.slot(), writes=["cols"])
        P.dma("sp", [(cx.ident_f[:], consts_in[:, 0, :]), (cx.onesm[:], consts_in[:, 1, :])], P.slot(),
              writes=["ident", "onesm"])
        P.op("act", lambda h: h.activation(cx.sc[:], cx.cols[:, 0:8], AF.Silu), reads=["cols"], writes=["sc"])
    P.dma("sp", [(diffm[:], consts_in[:, 2:4, :]), (perm[:], consts_in[:, 4:6, :]),
                 (mask01[:], consts_in[:, 6:8, :])], P.slot(), writes=["diffm", "perm"])
    P.op("dve", lambda h: h.tensor_copy(identb[:], cx.ident_f[:]), reads=["ident"], writes=["identb"])
    if not getattr(cx, "pre_ada_b0", False):
        cx.ada_bg = AdaStepper(P, cx, ada_w0, cx.cols[:, cb + 8:cb + 32], cx.mod[0], 0)
        ada_flush(cx)
    mod0 = cx.mod[0]
    cx.ada_bg = AdaStepper(P, cx, ada_w1, cx.cols[:, cb + 32:cb + 56], cx.mod[1], 1)

    xpv = xp_in.rearrange("(tb p) f -> p tb f", p=128) if xp_in is not None else None

    def load_prev_tile(tt):
        si = cx.stg_i % 2
        cx.stg_i += 1
        stg = cx.stg[si]
        P.dma("sp", [(stg[:, 0:2, :], xpv[:, tt * 4:tt * 4 + 2, :]),
                     (stg[:, 2:4, :], xpv[:, tt * 4 + 2:tt * 4 + 4, :])], cx.stg_slot[si],
              writes=[("stg", si)])
        for c in range(8):
            b, ps = cx.psum.next()
            fns = [(lambda h, tb=tb, c=c, ps=ps, stg=stg: h.matmul(
                ps[:, tb * 128:(tb + 1) * 128], stg[:, tb, c * 128:(c + 1) * 128], cx.ident[:],
                start=True, stop=True)) for tb in range(4)]
            P.mm_group(fns, reads=[("stg", si), "ident"], writes=[("ps", b)])
            P.op("dve", lambda h, ps=ps, c=c, tt=tt: h.tensor_scalar(
                hT[:, c, tt * TT:(tt + 1) * TT], ps[:], mod0[:, 8 + c:9 + c], mod0[:, c:c + 1], ALU.mult, ALU.add),
                reads=[("ps", b), ("mod", 0)], writes=[("hall", c, tt)])

    if not fused:
        for tt in range(NT):
            load_prev_tile(tt)
        emit_load_xT(P, cx, x_in, xT)

    def mod_own(tt):
        for c in range(8):
            P.op("act" if c % 2 else "dve",
                 (lambda h, c=c, tt=tt: h.activation(hT[:, c, T + tt * TT:T + (tt + 1) * TT], xT[:, c, tt * TT:(tt + 1) * TT],
                                                     AF.Identity, bias=mod0[:, c:c + 1], scale=mod0[:, 8 + c:9 + c]))
                 if c % 2 else
                 (lambda h, c=c, tt=tt: h.tensor_scalar(hT[:, c, T + tt * TT:T + (tt + 1) * TT], xT[:, c, tt * TT:(tt + 1) * TT],
                                                        mod0[:, 8 + c:9 + c], mod0[:, c:c + 1], ALU.mult, ALU.add)),
                 reads=[("x", c, tt), ("mod", 0)], writes=[("hall", c, NT + tt)])

    for tt in range(NT):
        mod_own(tt)
    if fused:
        rcv = io["rcv"]
        r_slot = P.slot()
        P.dma("sp", [(hT[:, hf * 4:(hf + 1) * 4, tt * TT:(tt + 1) * TT],
                      rcv[tt * 2 + hf][0:128, :].rearrange("p (c t) -> p c t", c=4))
                     for tt in range(NT) for hf in range(2)], r_slot,
              writes=[("hall", c, tt) for c in range(8) for tt in range(NT)], extra=[(cx.cc_sem, 8)])
    sp_slot = P.slot()
    xkeys = [("x", c, tt) for c in range(8) for tt in range(NT)]
    if fused:
        t_spills = list(cx.spill_toks)
    else:
        t_spills = [P.dma("sp", [(xsp[:, c * T:(c + 1) * T], xT[:, c, :]) for c in range(8)], sp_slot, reads=xkeys)]
    fsp = fence_tokens(P, xkeys) + t_spills
    for e_ in ("pe", "act", "dve", "pool"):
        P.wait_all(e_, fsp)
    P.op("pool", lambda h: h.memset(VA[:], 1.0), writes=["va_init"])
    P.op("act", lambda h: h.activation(VA[:, 0:16, 0, 64:128], VA[:, 0:16, 0, 64:128], AF.Identity, scale=flag),
         reads=["va_init", "cols"], writes=["va_init"])
    P.op("act", lambda h: h.activation(VA[:, 0:16, 1, 0:64], VA[:, 0:16, 1, 0:64], AF.Identity, scale=flag),
         reads=["va_init", "cols"], writes=["va_init"])
    va_tok = P.last_w["va_init"]

    wq_v = w_qkv.rearrange("(kc p) n -> p kc n", p=128)
    wq_slot = [P.slot() for _ in range(2)]
    st = {"i": 0, "e": 0, "p": 0}
    hall_keys_all = [("hall", c, t8) for c in range(8) for t8 in range(2 * NT)]

    def perm_view(buf2d, col0, d, tt, ps):
        if d == 1:
            return buf2d[:, col0 + tt * 512:col0 + (tt + 1) * 512], ps[:]
        if d == 4:
            dst = buf2d[:, col0 + tt * 512:col0 + (tt + 1) * 512].rearrange("p (r i) -> p i r", r=4)
            return dst, ps[:].rearrange("p (i r) -> p i r", r=4)
        dst = buf2d[:, col0:col0 + 2048].rearrange("p (r i) -> p i r", r=16)[:, tt * 32:(tt + 1) * 32, :]
        return dst, ps[:].rearrange("p (i r) -> p i r", r=16)

    import os as _os2
    _skip = set(_os2.environ.get("DBG_SKIP", "").split(","))

    def stage_prologue(hp, g, si):
        d = PATTERNS[g][1]
        wb = wqkv[si]
        bh = Bhl[si]
        pairs = []
        for t3 in range(3):
            c0 = ((g * 3 + t3) * 16 + 2 * hp) * 64
            pairs.append((wb[:, t3, :, :], wq_v[:, :, c0:c0 + 128]))
        P.dma("pool", pairs, wq_slot[si], writes=[("wqkv", si)])
        for hh in range(2):
            slope = 2.0 ** (-8.0 * (2 * hp + hh + 1) / 16.0)
            tb = tmpB[hh]
            dm = diffm[:].rearrange("p a q -> p (a q)")
            mk = mask01[:].rearrange("p a q -> p (a q)")
            P.op("pool", lambda h, tb=tb, slope=slope: h.tensor_scalar(tb[:], dm, -8.0 * slope * d, None, ALU.mult),
                 reads=["diffm"], writes=[("tmpB", hh)])
            P.op("pool", lambda h, tb=tb: h.tensor_tensor(tb[:], tb[:], mk, ALU.add),
                 reads=["diffm", ("tmpB", hh)], writes=[("tmpB", hh)])
            P.op("pool", lambda h, tb=tb, hh=hh: h.tensor_copy(bh[:, 0, hh, :], tb[:]),
                 reads=[("tmpB", hh)], writes=[("bhl", si, hh, 0)])
            P.op("pool", lambda h, tb=tb, hh=hh: h.tensor_tensor(bh[:, 1, hh, :], tb[:], bh[:, 0, hh, :], ALU.subtract),
                 reads=[("tmpB", hh), ("bhl", si, hh, 0)], writes=[("bhl", si, hh, 1)])

    def attn_stage(hp, g, si):
        d = PATTERNS[g][1]
        wb = wqkv[si]
        kt = KT[si]
        qt = QT[si]
        bh = Bhl[si]
        for tt in range(0 if "Q" in _skip else NT):
            b, ps = cx.psum.next()
            fns = [(lambda h, kc=kc, ps=ps, tt=tt: h.matmul(
                ps[:], wb[:, 0, kc, :], hT[:, kc, T + tt * TT:T + (tt + 1) * TT], start=(kc == 0), stop=(kc == 7)))
                for kc in range(8)]
            P.mm_group(fns, reads=[("wqkv", si)] + [("hall", kc, NT + tt) for kc in range(8)], writes=[("ps", b)])
            if tt % 2 == 0:
                ada_tick(cx)
            dst, src = perm_view(qt, 0, d, tt, ps)
            P.op("act", lambda h, dst=dst, src=src: h.activation(dst, src, AF.Identity),
                 reads=[("ps", b)], writes=[("qt", si)])
        for tt in range(0 if "K" in _skip else NT):
            b, ps = cx.psum.next()
            fns = [(lambda h, kc=kc, ps=ps, tt=tt: h.matmul(
                ps[:], wb[:, 1, kc, :], hT[:, kc, T + tt * TT:T + (tt + 1) * TT], start=(kc == 0), stop=(kc == 7)))
                for kc in range(8)]
            P.mm_group(fns, reads=[("wqkv", si)] + [("hall", kc, NT + tt) for kc in range(8)], writes=[("ps", b)])
            dst, src = perm_view(kt, 2048, d, tt, ps)
            P.op("dve", lambda h, dst=dst, src=src: h.tensor_copy(dst, src),
                 reads=[("ps", b)], writes=[("kt", 0)])
        if "K" in _skip:
            pass
        elif d == 1:
            b, ps = cx.psum.next()
            fns = [(lambda h, kc=kc, ps=ps: h.matmul(
                ps[:, 0:128], wb[:, 1, kc, :], hT[:, kc, T - 128:T], start=(kc == 0), stop=(kc == 7)))
                for kc in range(8)]
            P.mm_group(fns, reads=[("wqkv", si)] + [("hall", kc, NT - 1) for kc in range(8)], writes=[("ps", b)])
            P.op("dve", lambda h, ps=ps: h.tensor_copy(kt[:, 15 * 128:16 * 128], ps[:, 0:128]),
                 reads=[("ps", b)], writes=[("kt", 0)])
        elif d == 4:
            b, ps = cx.psum.next()
            fns = [(lambda h, kc=kc, ps=ps: h.matmul(
                ps[:], wb[:, 1, kc, :], hT[:, kc, T - 512:T], start=(kc == 0), stop=(kc == 7)))
                for kc in range(8)]
            P.mm_group(fns, reads=[("wqkv", si)] + [("hall", kc, NT - 1) for kc in range(8)], writes=[("ps", b)])
            dst = kt[:, 12 * 128:16 * 128].rearrange("p (r i) -> p i r", r=4)
            src = ps[:].rearrange("p (i r) -> p i r", r=4)
            P.op("dve", lambda h, dst=dst, src=src: h.tensor_copy(dst, src),
                 reads=[("ps", b)], writes=[("kt", 0)])
        else:
            for tt in range(NT):
                b, ps = cx.psum.next()
                fns = [(lambda h, kc=kc, ps=ps, tt=tt: h.matmul(
                    ps[:], wb[:, 1, kc, :], hT[:, kc, tt * TT:(tt + 1) * TT], start=(kc == 0), stop=(kc == 7)))
                    for kc in range(8)]
                P.mm_group(fns, reads=[("wqkv", si)] + [("hall", kc, tt) for kc in range(8)], writes=[("ps", b)])
                dst, src = perm_view(kt, 0, d, tt, ps)
                P.op("dve", lambda h, dst=dst, src=src: h.tensor_copy(dst, src),
                     reads=[("ps", b)], writes=[("kt", 0)])
        def vproj(tok0, ntok, col0, mode, tt, hkeys):
            b, ps = cx.psum.next()
            fns = [(lambda h, kc=kc, ps=ps: h.matmul(
                ps[:, 0:ntok], wb[:, 2, kc, :], hT[:, kc, tok0:tok0 + ntok], start=(kc == 0), stop=(kc == 7)))
                for kc in range(8)]
            P.mm_group(fns, reads=[("wqkv", si)] + hkeys, writes=[("ps", b)])
            if mode == "plain":
                dst, src = VT[:, col0:col0 + ntok], ps[:, 0:ntok]
            elif mode == "seg4":
                dst = VT[:, col0:col0 + 512].rearrange("p (r i) -> p i r", r=4)
                src = ps[:].rearrange("p (i r) -> p i r", r=4)
            else:
                dst, src = perm_view(VT, col0, d, tt, ps)
            P.op("act", lambda h, dst=dst, src=src: h.activation(dst, src, AF.Identity),
                 reads=[("ps", b)], writes=["vt"])

        for tt in range(0 if "V" in _skip else NT):
            hk = [("hall", kc, NT + tt) for kc in range(8)]
            if d == 1:
                vproj(T + tt * TT, TT, 2048 + tt * 512, "plain", tt, hk)
            elif d == 4:
                vproj(T + tt * TT, TT, 2048 + tt * 512, "seg4", tt, hk)
            else:
                vproj(T + tt * TT, TT, 2048, "perm", tt, hk)
        if "V" in _skip:
            pass
        elif d == 1:
            vproj(T - 128, 128, 15 * 128, "plain", 0, [("hall", kc, NT - 1) for kc in range(8)])
        elif d == 4:
            vproj(T - 512, 512, 12 * 128, "seg4", 0, [("hall", kc, NT - 1) for kc in range(8)])
        else:
            for tt in range(NT):
                vproj(tt * TT, TT, 0, "perm", tt, [("hall", kc, tt) for kc in range(8)])
        groups = []
        prev_slots = list(range(16 - d, 16))
        own_slots = list(range(16, 32))
        for i0 in range(0, len(prev_slots), 4):
            groups.append(prev_slots[i0:i0 + 4])
        for i0 in range(0, 16, 4):
            groups.append(own_slots[i0:i0 + 4])
        if "T" in _skip or "V" in _skip:
            groups = []
        for grp in groups:
            b, ps = cx.psum.next()
            fns = [(lambda h, gi=gi, s=s, ps=ps: h.matmul(
                ps[:, gi * 128:(gi + 1) * 128], VT[:, s * 128:(s + 1) * 128], identb[:], start=True, stop=True))
                for gi, s in enumerate(grp)]
            P.mm_group(fns, reads=["vt", "identb"], writes=[("ps", b)], extra=[va_tok])
            ng = len(grp)
            s0 = grp[0]
            srcv = ps[:, 0:ng * 128].rearrange("p (s c) -> p s c", c=128)
            if "X" in _skip:
                continue
            if "XP" in _skip and s0 < 16:
                continue
            if "XO" in _skip and s0 >= 16:
                continue
            if "ALTVA" in _skip:
                s0_ = min(grp[0], 30)
                P.op("dve", lambda h, ps=ps, s0_=s0_: h.tensor_copy(
                    VA[:, s0_:s0_ + 2, :, :].rearrange("p a b c -> p (a b c)"), ps[:]),
                    reads=[("ps", b)], writes=[("va", 0)])
                continue
            if "F2D" in _skip:
                for gi, sl_ in enumerate(grp):
                    if "NO20" in _skip and sl_ == 20:
                        continue
                    P.op("act", lambda h, gi=gi, sl_=sl_, ps=ps: h.activation(
                        VA[:, sl_, 0, :], ps[:, gi * 128:(gi + 1) * 128], AF.Identity),
                        reads=[("ps", b)], writes=[("va", 0)])
                    P.op("dve", lambda h, gi=gi, sl_=sl_, ps=ps: h.tensor_copy(
                        VA[:, sl_, 1, :], ps[:, gi * 128:(gi + 1) * 128]),
                        reads=[("ps", b)], writes=[("va", 1)])
                continue
            if "FULL" in _skip:
                P.op("act", lambda h, srcv=srcv, s0=s0, ng=ng: h.activation(
                    VA[:, s0:s0 + ng, 0, :], srcv, AF.Identity),
                    reads=[("ps", b)], writes=[("va", 0)])
                P.op("dve", lambda h, srcv=srcv, s0=s0, ng=ng: h.tensor_copy(
                    VA[:, s0:s0 + ng, 1, :], srcv),
                    reads=[("ps", b)], writes=[("va", 1)])
                continue
            if s0 < 16:
                P.op("act", lambda h, srcv=srcv, s0=s0, ng=ng: h.activation(
                    VA[:, s0:s0 + ng, 0, 0:64], srcv[:, :, 0:64], AF.Identity, scale=flag),
                    reads=[("ps", b), "cols"], writes=[("va", 0)])
                P.op("dve", lambda h, srcv=srcv, s0=s0, ng=ng: h.tensor_scalar(
                    VA[:, s0:s0 + ng, 1, 64:128], srcv[:, :, 64:128], flag, None, ALU.mult),
                    reads=[("ps", b), "cols"], writes=[("va", 1)])
            else:
                P.op("act", lambda h, srcv=srcv, s0=s0, ng=ng: h.activation(
                    VA[:, s0:s0 + ng, 0, 0:64], srcv[:, :, 0:64], AF.Identity),
                    reads=[("ps", b)], writes=[("va", 0)])
                P.op("dve", lambda h, srcv=srcv, s0=s0, ng=ng: h.tensor_copy(
                    VA[:, s0:s0 + ng, 1, 64:128], srcv[:, :, 64:128]),
                    reads=[("ps", b)], writes=[("va", 1)])
        nun = 0 if stage == 'proj' else 16
        ust = {}

        def unit_front(j):
            pi = st["p"] % 2
            st["p"] += 1
            pt = Pt[pi]
            banks = [cx.psum.next() for _ in range(2)]
            fns = []
            for hh in range(2):
                ps = banks[hh][1]
                fns.append(lambda h, ps=ps, hh=hh: h.matmul(ps[:, 0:256], identb[:], bh[:, 0, hh, :], start=True, stop=False))
                fns.append(lambda h, ps=ps, hh=hh: h.matmul(ps[:, 0:256], identb[:], bh[:, 1, hh, :], start=False, stop=False))
            for kb in range(2):
                slot = 16 + j - d if kb == 0 else 16 + j
                for hh in range(2):
                    ps = banks[hh][1]
                    fns.append(lambda h, hh=hh, kb=kb, slot=slot, ps=ps, j=j: h.matmul(
                        ps[:, kb * 128:(kb + 1) * 128],
                        kt[hh * 64:(hh + 1) * 64, slot * 128:(slot + 1) * 128],
                        qt[hh * 64:(hh + 1) * 64, j * 128:(j + 1) * 128], start=False, stop=(kb == 1)))
            P.mm_group(fns, reads=[("kt", 0), ("qt", si), "identb"] + [("bhl", si, hh, z) for hh in range(2) for z in range(2)],
                       writes=[("ps", banks[0][0]), ("ps", banks[1][0])])
            for hh in range(2):
                b, ps = banks[hh]
                P.op("act", lambda h, ps=ps, pt=pt, hh=hh: h.activation(
                    pt[:, hh * 256:(hh + 1) * 256], ps[:, 0:256], AF.Exp, scale=0.125),
                    reads=[("ps", b)], writes=[("pt", pi, hh)])
            ust[j] = (pi, pt)

        def unit_back(j):
            pi, pt = ust[j]
            n, r = j // d, j % d
            b2, ps2 = cx.psum.next()
            fns = []
            for hh in range(2):
                for kb in range(2):
                    slot = 16 + j - d if kb == 0 else 16 + j
                    fns.append(lambda h, hh=hh, kb=kb, slot=slot, ps2=ps2, pt=pt: h.matmul(
                        ps2[:, hh * 128:(hh + 1) * 128], VA[:, slot, hh, :],
                        pt[:, (hh * 2 + kb) * 128:(hh * 2 + kb + 1) * 128], start=(kb == 0), stop=(kb == 1)))
            P.mm_group(fns, reads=[("pt", pi, 0), ("pt", pi, 1), ("va", 0), ("va", 1)], writes=[("ps", b2)])
            start = n * 128 * d + r
            accv = ACC[:, :, start:start + 127 * d + 1:d]
            src2 = ps2[:, 0:256].rearrange("p (a q) -> p a q", a=2)
            wr = ["acc"] if j == nun - 1 else []
            if g == 0:
                P.op("dve", lambda h, accv=accv, src2=src2: h.tensor_copy(accv, src2),
                     reads=[("ps", b2)], writes=wr, extra=acc_dep)
            else:
                P.op("dve", lambda h, accv=accv, src2=src2: h.tensor_tensor(accv, src2, accv, ALU.add),
                     reads=[("ps", b2)], writes=wr, extra=acc_dep)

        acc_dep = fence_tokens(P, ["acc"])
        if nun:
            unit_front(0)
        for j in range(nun):
            if j + 1 < nun:
                unit_front(j + 1)
            unit_back(j)

    def finalize(hp):
        for tt in range(NT):
            sl = slice(tt * TT, (tt + 1) * TT)
            b, ps = cx.psum.next()
            fns = [lambda h, ps=ps, sl=sl: h.matmul(ps[:], perm[:, 0, :], ACC[:, 0, sl], start=True, stop=False),
                   lambda h, ps=ps, sl=sl: h.matmul(ps[:], perm[:, 1, :], ACC[:, 1, sl], start=False, stop=True)]
            P.mm_group(fns, reads=["acc", "perm"], writes=[("ps", b)])
            P.op("dve", lambda h, ps=ps: h.reciprocal(rden[:], ps[:]), reads=[("ps", b)], writes=["rden"])
            P.op("pool", lambda h, sl=sl: h.tensor_tensor(oT[0:64, hp, sl], ACC[0:64, 0, sl], rden[0:64, :], ALU.mult),
                 reads=["acc", "rden"], writes=[("o", hp, tt, 0)])
            P.op("pool", lambda h, sl=sl: h.tensor_tensor(oT[64:128, hp, sl], ACC[64:128, 1, sl], rden[64:128, :], ALU.mult),
                 reads=["acc", "rden"], writes=[("o", hp, tt, 1)])

    P.wait_all("pool", fence_tokens(P, [("stg", 0), ("stg", 1)]))
    nhp = {"pre": 0, "one": 1, "proj": 1}.get(stage, 8)
    import os as _os
    _gl = [int(x) for x in _os.environ.get("DBG_G", "0,1,2").split(",")]
    stages = [(hp, g) for hp in range(nhp) for g in _gl]
    if stages:
        stage_prologue(stages[0][0], stages[0][1], 0)
    for i_s, (hp, g) in enumerate(stages):
        if i_s + 1 < len(stages):
            stage_prologue(stages[i_s + 1][0], stages[i_s + 1][1], (i_s + 1) % 2)
        attn_stage(hp, g, i_s % 2)
        if stage != "proj" and g == _gl[-1]:
            finalize(hp)
    if stage in ("pre", "one", "proj"):
        fx = fence_tokens(P, ["vt", ("kt", 0), ("kt", 1), ("qt", 0), ("qt", 1), ("va", 0), ("va", 1), "acc", "va_init",
                              ("tmpB", 0), ("tmpB", 1), ("pt", 0, 0), ("pt", 0, 1), ("pt", 1, 0), ("pt", 1, 1)])
        P.wait_all("sp", fx)
        t_re = P.dma("sp", [(xT[:, c, :], xsp[:, c * T:(c + 1) * T]) for c in range(8)], P.slot(), writes=xkeys)
        for e_ in ("pe", "act", "dve", "pool"):
            P.wait_all(e_, fence_tokens(P, hall_keys_all + [("o", hp, tt, hf) for hp in range(nhp) for tt in range(NT) for hf in range(2)] + [("wqkv", 0), ("wqkv", 1)] + [("bhl", i, hh, z) for i in range(2) for hh in range(2) for z in range(2)]))
        for tt in range(NT):
            emit_store_x(P, cx, xT, y_out, tt)
        return

    att_keys = ["vt", ("kt", 0), ("kt", 1), ("qt", 0), ("qt", 1), ("va", 0), ("va", 1), "acc", "va_init",
                ("tmpB", 0), ("tmpB", 1), ("pt", 0, 0), ("pt", 0, 1), ("pt", 1, 0), ("pt", 1, 1)]
    fx = fence_tokens(P, att_keys)
    P.wait_all("sp", fx)
    x_slot = P.slot()
    P.dma("sp", [(xT[:, c, :], xsp[:, c * T:(c + 1) * T]) for c in range(8)], x_slot, writes=xkeys)
    fo = fence_tokens(P, hall_keys_all + [("wqkv", 0), ("wqkv", 1)])
    P.wait_all("pool", fo)
    P.dma("pool", [(wout[:], w_out.rearrange("(kc p) n -> p kc n", p=128))], P.slot(), writes=["wout"])

    def outproj_tile(tt):
        sl = slice(tt * TT, (tt + 1) * TT)
        for oc in range(8):
            b, ps = cx.psum.next()
            fns = [(lambda h, hp=hp, oc=oc, ps=ps: h.matmul(
                ps[:], wout[:, hp, oc * 128:(oc + 1) * 128], oT[:, hp, sl], start=(hp == 0), stop=(hp == 7)))
                for hp in range(8)]
            P.mm_group(fns, reads=["wout"] + [("o", hp, tt, hf) for hp in range(8) for hf in range(2)],
                       writes=[("ps", b)])
            P.op("dve", lambda h, ps=ps, oc=oc: h.scalar_tensor_tensor(
                xT[:, oc, sl], ps[:], mod0[:, 16 + oc:17 + oc], xT[:, oc, sl], ALU.mult, ALU.add),
                reads=[("ps", b), ("mod", 0), ("x", oc, tt)], writes=[("x", oc, tt)])
    for tt in range(NT):
        outproj_tile(tt)
        if tt >= 1:
            emit_ln_p2(P, cx, xT, tt - 1, cx.cols[:, cb + 56:cb + 64], cx.cols[:, cb + 64:cb + 72])
        emit_ln_p1(P, cx, xT, tt)
    emit_ln_p2(P, cx, xT, NT - 1, cx.cols[:, cb + 56:cb + 64], cx.cols[:, cb + 64:cb + 72])

    if stage == "attn":
        for e_ in ("pe", "act", "dve", "pool"):
            P.wait_all(e_, fence_tokens(P, ["wout"] + [("o", hp, tt, hf) for hp in range(8) for tt in range(NT) for hf in range(2)]))
        for tt in range(NT):
            emit_store_x(P, cx, xT, y_out, tt)
        return

    ada_flush(cx)
    f2 = fence_tokens(P, ["wout"] + [("o", hp, tt, hf) for hp in range(8) for tt in range(NT) for hf in range(2)]
                      + [("bhl", i, hh, z) for i in range(2) for hh in range(2) for z in range(2)] + ["rden", "diffm", "perm", "identb", "vt"])
    for e_ in ("dve", "act", "pool"):
        P.wait_all(e_, f2)
    for tt in range(NT):
        emit_modulate(P, cx, xT, h2T, cx.mod[1], 1, tt)

    def after_ln(tt):
        emit_store_x(P, cx, xT, y_out, tt)

    emit_mlp(P, cx, xT, h2T, cx.mod[1], 1, w_up, w_down, cx.cols[:, cb + 72:cb + 80], cx.cols[:, cb + 80:cb + 88], after_ln=after_ln)


def build_B(stage=None):
    nc = bass.Bass("TRN2", target_bir_lowering=False)
    dt = lambda n, s: nc.dram_tensor(n, s, F32, kind="ExternalInput").ap()
    x_in = dt("x", [T, D])
    xp_in = dt("xp", [T, D])
    cols_in = dt("cols", [128, B_NCOLS])
    consts_in = dt("consts", [128, 8, 128])
    ada_w0 = dt("ada_w0", [D, 3 * D])
    ada_w1 = dt("ada_w1", [D, 3 * D])
    w_qkv = dt("b_w_qkv", [D, 9 * D])
    w_out = dt("b_w_out", [D, D])
    w_up = dt("mlp_w_up", [D, 4 * D])
    w_down = dt("mlp_w_down", [4 * D, D])
    y_out = nc.dram_tensor("y", [T, D], F32, kind="ExternalOutput").ap()
    xsp = nc.dram_tensor("xsp", [128, 8 * T], F32).ap()

    with contextlib.ExitStack() as es:
        P = Prog(nc, es)
        A = Arena(nc)
        cx = Ctx()
        common_setup(nc, P, A, cx, B_NCOLS)
        xT, xT_at, _ = A.alloc([128, 8, T], F32)
        io = dict(x=x_in, xp=xp_in, cols=cols_in, constsB=consts_in, ada_w10=ada_w0, ada_w11=ada_w1, b_w_qkv=w_qkv,
                  b_w_out=w_out, mlp_w_up1=w_up, mlp_w_down1=w_down, y=y_out, xsp=xsp)
        body_B(nc, P, A, cx, io, xT, xT_at, stage=stage, fused=False, cb=0)
        P.wait_all("sp", cx.out_toks)
        P.finish()
    return nc


def prep_B(inp, x1):
    c = np.asarray(inp["c"], dtype=np.float32)
    consts = host_consts_B()
    maps = []
    zeros = np.zeros((T, D), dtype=np.float32)
    for core in range(NCORES):
        b, half = core // 2, core % 2
        cols = np.zeros((128, B_NCOLS), dtype=np.float32)
        cols[:, 0:8] = col(c[b])
        cols[:, 8:32] = col(inp["ada_b"][1, 0])
        cols[:, 32:56] = col(inp["ada_b"][1, 1])
        cols[:, 56:64] = col(inp["ln_g"][1, 0])
        cols[:, 64:72] = col(inp["ln_b"][1, 0])
        cols[:, 72:80] = col(inp["ln_g"][1, 1])
        cols[:, 80:88] = col(inp["ln_b"][1, 1])
        cols[:, 88] = float(half)
        maps.append({
            "x": np.ascontiguousarray(x1[b, half * T:(half + 1) * T]) if x1 is not None else None,
            "xp": (np.ascontiguousarray(x1[b, 0:T]) if half == 1 else zeros) if x1 is not None else None,
            "cols": cols, "consts": consts,
            "ada_w0": np.ascontiguousarray(inp["ada_w"][1, 0]), "ada_w1": np.ascontiguousarray(inp["ada_w"][1, 1]),
            "b_w_qkv": np.ascontiguousarray(inp["b_w_qkv"][0]), "b_w_out": np.ascontiguousarray(inp["b_w_out"][0]),
            "mlp_w_up": np.ascontiguousarray(inp["mlp_w_up"][1]), "mlp_w_down": np.ascontiguousarray(inp["mlp_w_down"][1]),
        })
    return maps


def run_B(inp, x1, trace=False, stage=None, ncores=NCORES):
    nc = build_B(stage)
    maps = prep_B(inp, x1)[:ncores]
    res = run_bass_kernel_spmd(nc, maps, core_ids=list(range(ncores)), trace=trace)
    out = np.zeros((4, 4096, D), dtype=np.float32)
    for core in range(ncores):
        b, half = core // 2, core % 2
        out[b, half * T:(half + 1) * T] = res.results[core]["y"]
    return out, res


F_NCOLS = A_NCOLS + B_NCOLS


def build_F():
    nc = bass.Bass("TRN2", target_bir_lowering=False)
    dt = lambda n, s: nc.dram_tensor(n, s, F32, kind="ExternalInput").ap()
    io = dict(
        x=dt("x", [T, D]), cols=dt("cols", [128, F_NCOLS]), rows=dt("rows", [128, 3, 1024]),
        bbias=dt("bbias", [128, 8, 128]), constsA=dt("constsA", [128, 3, 128]), constsB=dt("constsB", [128, 8, 128]),
        ada_w00=dt("ada_w00", [D, 3 * D]), ada_w01=dt("ada_w01", [D, 3 * D]),
        ada_w10=dt("ada_w10", [D, 3 * D]), ada_w11=dt("ada_w11", [D, 3 * D]),
        a_w_in=dt("a_w_in", [D, 2 * D]), a_w_s=dt("a_w_s", [16, 128, 128]), a_w_out=dt("a_w_out", [D, D]),
        mlp_w_up0=dt("mlp_w_up0", [D, 4 * D]), mlp_w_down0=dt("mlp_w_down0", [4 * D, D]),
        mlp_w_up1=dt("mlp_w_up1", [D, 4 * D]), mlp_w_down1=dt("mlp_w_down1", [4 * D, D]),
        b_w_qkv=dt("b_w_qkv", [D, 9 * D]), b_w_out=dt("b_w_out", [D, D]))
    io["y"] = nc.dram_tensor("y", [T, D], F32, kind="ExternalOutput").ap()
    io["xsp"] = nc.dram_tensor("xsp", [128, 8 * T], F32).ap()
    io["snd"] = [nc.dram_tensor(f"snd{c}", [128, T], BF16).ap() for c in range(8)]
    io["rcv"] = [nc.dram_tensor(f"rcv{c}", [256, T], BF16).ap() for c in range(8)]
    with contextlib.ExitStack() as es:
        P = Prog(nc, es)
        A = Arena(nc)
        cx = Ctx()
        common_setup(nc, P, A, cx, F_NCOLS)
        xT, xT_at, _ = A.alloc([128, 8, T], F32)
        mark = A.off
        body_A(nc, P, A, cx, io, xT, stage=None, fused=True)
        P.barrier()
        A.off = mark
        body_B(nc, P, A, cx, io, xT, xT_at, stage=None, fused=True, cb=A_NCOLS)
        P.wait_all("sp", cx.out_toks)
        P.finish()
    return nc


def prep_F(inp):
    mA = prep_A(inp)
    mB = prep_B(inp, None)
    maps = []
    for core in range(NCORES):
        a, b = mA[core], mB[core]
        maps.append({
            "x": a["x"], "cols": np.ascontiguousarray(np.concatenate([a["cols"], b["cols"]], axis=1)),
            "rows": a["rows"], "bbias": a["bbias"], "constsA": a["consts"], "constsB": b["consts"],
            "ada_w00": a["ada_w0"], "ada_w01": a["ada_w1"], "ada_w10": b["ada_w0"], "ada_w11": b["ada_w1"],
            "a_w_in": a["a_w_in"], "a_w_s": a["a_w_s"], "a_w_out": a["a_w_out"],
            "mlp_w_up0": a["mlp_w_up"], "mlp_w_down0": a["mlp_w_down"],
            "mlp_w_up1": b["mlp_w_up"], "mlp_w_down1": b["mlp_w_down"],
            "b_w_qkv": b["b_w_qkv"], "b_w_out": b["b_w_out"],
        })
    return maps


def run_F(inp, trace=False, ncores=NCORES):
    nc = build_F()
    maps = prep_F(inp)[:ncores]
    res = run_bass_kernel_spmd(nc, maps, core_ids=list(range(ncores)), trace=trace)
    out = np.zeros((4, 4096, D), dtype=np.float32)
    for core in range(ncores):
        b, half = core // 2, core % 2
        out[b, half * T:(half + 1) * T] = res.results[core]["y"]
    return out, res


def kernel(**inputs):
    inp = {k: np.asarray(v) for k, v in inputs.items()}
    out, _ = run_F(inp)
    return out
```

```python
import contextlib
import numpy as np
import concourse.bass as bass
import concourse.mybir as mybir
from concourse.bass_utils import run_bass_kernel_spmd

F32 = mybir.dt.float32
BF16 = mybir.dt.bfloat16
AF = mybir.ActivationFunctionType
ALU = mybir.AluOpType
AX = mybir.AxisListType

D = 1024
T = 2048
NT = 4
TT = 512
NCORES = 8
ALPHA = 4.0 ** 0.25
INV_ALPHA = 1.0 / ALPHA
LN_EPS = 1e-5
EPS_P = LN_EPS / (ALPHA * ALPHA)
GELU = AF.Gelu_apprx_tanh


class Eng:
    def __init__(self, name, sem):
        self.name = name
        self.sem = sem
        self.count = 0
        self.ops = []
        self.waited = {}


class Prog:
    def __init__(self, nc, es, nsem=100):
        self.nc = nc
        self.free_sems = [es.enter_context(nc.semaphore(f"s{i}")) for i in range(nsem)]
        self.E = {}
        for n in ["pe", "act", "dve", "pool", "sp"]:
            self.E[n] = Eng(n, self.free_sems.pop())
        self.last_w = {}
        self.readers = {}
        self.sem_id = {}

    def new_sem(self):
        return self.free_sems.pop()

    def slot(self):
        sl = {"sem": self.new_sem(), "count": 0}
        self.slots = getattr(self, "slots", [])
        self.slots.append(sl)
        return sl

    def barrier(self, extra=()):
        toks = [(e.sem, e.count) for e in self.E.values() if e.count > 0]
        toks += [(sl["sem"], sl["count"]) for sl in getattr(self, "slots", []) if sl["count"] > 0]
        toks += list(extra)
        for eng in self.E:
            self.wait_all(eng, toks)

    def _sid(self, s):
        return id(s)

    @staticmethod
    def _norm(reads, writes):
        r2, w2 = [], list(writes)
        for k in reads:
            if isinstance(k, tuple) and k and k[0] == "ps":
                if k not in w2:
                    w2.append(k)
            else:
                r2.append(k)
        return r2, w2

    def _deps(self, eng, reads, writes, extra):
        toks = []
        for r in reads:
            t = self.last_w.get(r)
            if t is not None:
                toks.append(t)
        for w in writes:
            t = self.last_w.get(w)
            if t is not None:
                toks.append(t)
            for t in self.readers.get(w, {}).values():
                toks.append(t)
        for t in extra:
            if t is not None:
                toks.append(t)
        e = self.E[eng]
        waits = []
        for (s, v) in toks:
            if eng == "pe" and s is e.sem:
                continue
            k = self._sid(s)
            if e.waited.get(k, 0) < v:
                e.waited[k] = v
                waits.append((s, v))
        return waits

    def _record(self, tok, reads, writes):
        for r in reads:
            d = self.readers.setdefault(r, {})
            k = self._sid(tok[0])
            if k not in d or d[k][1] < tok[1]:
                d[k] = tok
        for w in writes:
            self.last_w[w] = tok
            self.readers[w] = {}

    def op(self, eng, fn, reads=(), writes=(), extra=()):
        reads, writes = self._norm(reads, writes)
        e = self.E[eng]
        waits = self._deps(eng, reads, writes, extra)
        e.count += 1
        tok = (e.sem, e.count)
        e.ops.append((waits, fn, (e.sem, 1)))
        self._record(tok, reads, writes)
        return tok

    def mm_group(self, fns, reads=(), writes=(), extra=()):
        reads, writes = self._norm(reads, writes)
        e = self.E["pe"]
        waits = self._deps("pe", reads, writes, extra)
        for i, fn in enumerate(fns):
            last = i == len(fns) - 1
            e.ops.append((waits if i == 0 else [], fn, (e.sem, 1) if last else None))
        e.count += 1
        tok = (e.sem, e.count)
        self._record(tok, reads, writes)
        return tok

    def dma(self, eng, pairs, slot, reads=(), writes=(), extra=()):
        e = self.E[eng]
        waits = self._deps(eng, reads, writes, extra)
        for i, (o, i_) in enumerate(pairs):
            e.ops.append((waits if i == 0 else [],
                          (lambda h, o=o, i_=i_: h.dma_start(out=o, in_=i_)),
                          (slot["sem"], 16)))
            slot["count"] += 16
        tok = (slot["sem"], slot["count"])
        self._record(tok, reads, writes)
        return tok

    def wait_all(self, eng, toks):
        e = self.E[eng]
        waits = self._deps(eng, (), (), toks)
        e.ops.append((waits, None, None))

    def replay(self, eng, h):
        for waits, fn, inc in self.E[eng].ops:
            for s, v in waits:
                h.wait_ge(s, v)
            if fn is None:
                continue
            inst = fn(h)
            if inc is not None:
                inst.then_inc(inc[0], inc[1])

    def finish(self):
        nc = self.nc
        with nc.Block() as block:
            @block.tensor
            def _(h):
                self.replay("pe", h)

            @block.scalar
            def _(h):
                self.replay("act", h)

            @block.vector
            def _(h):
                self.replay("dve", h)

            @block.gpsimd
            def _(h):
                self.replay("pool", h)

            @block.sync
            def _(h):
                self.replay("sp", h)


class Arena:
    def __init__(self, nc, base=16512, cap=229376):
        self.nc = nc
        self.cap = cap
        self.off = base
        self.n = 0

    def alloc(self, shape, dtype, at=None, name=None):
        nbytes = int(np.prod(shape[1:])) * (4 if dtype == F32 else 2)
        nbytes = (nbytes + 63) // 64 * 64
        if at is None:
            at = self.off
            self.off += nbytes
            assert self.off <= self.cap, f"SBUF overflow {self.off}"
        else:
            assert at + nbytes <= self.cap, "SBUF overflow (at)"
        self.n += 1
        t = self.nc.alloc_sbuf_tensor_at(name or f"sb{self.n}", list(shape), dtype, offset=at, align_bytes=64)
        return t, at, nbytes


class Psum:
    def __init__(self, nc, P):
        self.banks = [nc.alloc_psum_tensor(f"psb{i}", [128, 512], F32) for i in range(8)]
        self.i = 0

    def next(self):
        b = self.i
        self.i = (self.i + 1) % 7
        return b, self.banks[b]


def col(v):
    v = np.asarray(v, dtype=np.float32)
    return np.ascontiguousarray(v.reshape(-1, 128).T)


class Ctx:
    pass


class AdaStepper:
    NP = 24

    def __init__(self, P, cx, wada_ap, bias_cols, out_mod, tag):
        self.P, self.cx = P, cx
        self.wv = wada_ap.rearrange("(kc p) n -> p kc n", p=128)
        self.bias_cols, self.out_mod, self.tag = bias_cols, out_mod, tag
        self.k = 0
        self.base = cx.ada_i
        cx.ada_i += self.NP
        self._dma(0)

    def _dma(self, pc):
        bi = (self.base + pc) % 2
        self.P.dma("pool", [(self.cx.ada_buf[bi][:], self.wv[:, :, pc * 128:(pc + 1) * 128])], self.cx.ada_slot[bi],
                   writes=[("adabuf", bi)])

    def tick(self):
        if self.k >= self.NP:
            return False
        P, cx, pc = self.P, self.cx, self.k
        if pc + 1 < self.NP:
            self._dma(pc + 1)
        bi = (self.base + pc) % 2
        buf = cx.ada_buf[bi]
        ps = cx.ps_ada
        fns = [(lambda h, kc=kc, pc=pc, buf=buf: h.matmul(
            ps[:, pc:pc + 1], buf[:, kc, :], cx.sc[:, kc:kc + 1], start=(kc == 0), stop=(kc == 7)))
            for kc in range(8)]
        P.mm_group(fns, reads=[("adabuf", bi), "sc"], writes=[("ps_ada", pc)])
        self.k += 1
        if self.k == self.NP:
            self._finish()
        return True

    def flush(self):
        while self.tick():
            pass

    def _finish(self):
        P, ps, out_mod, tag = self.P, self.cx.ps_ada, self.out_mod, self.tag
        rd = [("ps_ada", f) for f in range(24)]
        P.op("dve", lambda h: h.tensor_tensor(out_mod[:, :], ps[:, 0:24], self.bias_cols, ALU.add),
             reads=rd + ["cols"], writes=[("mod", tag)])
        P.op("dve", lambda h: h.tensor_scalar(out_mod[:, 8:16], out_mod[:, 8:16], 1.0, None, ALU.add),
             reads=[("mod", tag)], writes=[("mod", tag)])
        P.op("dve", lambda h: h.tensor_scalar(out_mod[:, 16:24], out_mod[:, 16:24], 1.0, INV_ALPHA, ALU.add, ALU.mult),
             reads=[("mod", tag)], writes=[("mod", tag)])


def ada_tick(cx):
    st_ = getattr(cx, "ada_bg", None)
    if st_ is not None:
        st_.tick()


def ada_flush(cx):
    st_ = getattr(cx, "ada_bg", None)
    if st_ is not None:
        st_.flush()
        cx.ada_bg = None


def emit_modulate(P, cx, xT, hT, mod, tag, tt, eng="dve"):
    sl = slice(tt * TT, (tt + 1) * TT)
    for c in range(8):
        if eng == "act":
            P.op("act", lambda h, c=c: h.activation(hT[:, c, sl], xT[:, c, sl], AF.Identity,
                                                    bias=mod[:, c:c + 1], scale=mod[:, 8 + c:9 + c]),
                 reads=[("x", c, tt), ("mod", tag)], writes=[("h", c, tt)])
        else:
            P.op(eng, lambda h, c=c: h.tensor_scalar(hT[:, c, sl], xT[:, c, sl], mod[:, 8 + c:9 + c],
                                                     mod[:, c:c + 1], ALU.mult, ALU.add),
                 reads=[("x", c, tt), ("mod", tag)], writes=[("h", c, tt)])


def emit_ln_p1(P, cx, xT, tt):
    sl = slice(tt * TT, (tt + 1) * TT)
    s1, s2, sq = cx.ln_s1, cx.ln_s2, cx.ln_sq
    P.op("pool", lambda h: h.tensor_tensor(s1[:], xT[:, 0, sl], xT[:, 1, sl], ALU.add),
         reads=[("x", 0, tt), ("x", 1, tt)], writes=["ln_s1"])
    for c in range(2, 8):
        P.op("pool", lambda h, c=c: h.tensor_tensor(s1[:], s1[:], xT[:, c, sl], ALU.add),
             reads=[("x", c, tt), "ln_s1"], writes=["ln_s1"])
    for c in range(8):
        if c == 0:
            P.op("act", lambda h: h.activation(s2[:], xT[:, 0, sl], AF.Square),
                 reads=[("x", 0, tt)], writes=["ln_s2"])
        else:
            k = c % 2
            P.op("act", lambda h, c=c, k=k: h.activation(sq[k][:], xT[:, c, sl], AF.Square),
                 reads=[("x", c, tt)], writes=[("ln_sq", k)])
            P.op("dve", lambda h, k=k: h.tensor_tensor(s2[:], s2[:], sq[k][:], ALU.add),
                 reads=[("ln_sq", k), "ln_s2"], writes=["ln_s2"])


def emit_ln_p2(P, cx, xT, tt, g_cols, b_cols):
    sl = slice(tt * TT, (tt + 1) * TT)
    s1, s2, sq = cx.ln_s1, cx.ln_s2, cx.ln_sq
    b1, ps1 = cx.psum.next()
    P.mm_group([lambda h: h.matmul(ps1[:], cx.onesm[:], s1[:], start=True, stop=True)],
               reads=["ln_s1", "onesm"], writes=[("ps", b1)])
    b2, ps2 = cx.psum.next()
    P.mm_group([lambda h: h.matmul(ps2[:], cx.onesm[:], s2[:], start=True, stop=True)],
               reads=["ln_s2", "onesm"], writes=[("ps", b2)])
    mean, msq, rstd = cx.ln_mean, cx.ln_tmp[1], cx.ln_rstd
    P.op("act", lambda h: h.activation(mean[:], ps1[:], AF.Identity), reads=[("ps", b1)], writes=["ln_mean"])
    P.op("act", lambda h: h.activation(msq[:], ps1[:], AF.Square), reads=[("ps", b1)], writes=[("ln_tmp", 1)])
    P.op("dve", lambda h: h.tensor_tensor(rstd[:], ps2[:], msq[:], ALU.subtract),
         reads=[("ps", b2), ("ln_tmp", 1)], writes=["ln_rstd"])
    P.op("dve", lambda h: h.tensor_scalar(rstd[:], rstd[:], EPS_P, None, ALU.add),
         reads=["ln_rstd"], writes=["ln_rstd"])
    P.op("dve", lambda h: h.reciprocal(rstd[:], rstd[:]), reads=["ln_rstd"], writes=["ln_rstd"])
    P.op("act", lambda h: h.activation(rstd[:], rstd[:], AF.Sqrt), reads=["ln_rstd"], writes=["ln_rstd"])
    for c in range(8):
        k = c % 2
        tmp = cx.ln_tmp[k]
        P.op("dve", lambda h, c=c, tmp=tmp: h.tensor_tensor(tmp[:], xT[:, c, sl], mean[:], ALU.subtract),
             reads=[("x", c, tt), "ln_mean"], writes=[("ln_tmp", k)])
        P.op("pool", lambda h, tmp=tmp: h.tensor_tensor(tmp[:], tmp[:], rstd[:], ALU.mult),
             reads=["ln_rstd", ("ln_tmp", k)], writes=[("ln_tmp", k)])
        P.op("act", lambda h, c=c, tmp=tmp: h.activation(xT[:, c, sl], tmp[:], AF.Identity,
                                                         bias=b_cols[:, c:c + 1], scale=g_cols[:, c:c + 1]),
             reads=[("ln_tmp", k), "cols"], writes=[("x", c, tt)])


def emit_ln(P, cx, xT, tt, g_cols, b_cols):
    emit_ln_p1(P, cx, xT, tt)
    emit_ln_p2(P, cx, xT, tt, g_cols, b_cols)


def emit_mlp(P, cx, xT, hT, mod, tag, w_up_ap, w_down_ap, g_cols, b_cols, after_ln=None):
    wu_v = w_up_ap.rearrange("(kc p) n -> p kc n", p=128)
    wd_v = w_down_ap.rearrange("(fc p) n -> p fc n", p=128)
    for q in range(4):
        bi = cx.mlp_i % 2
        cx.mlp_i += 1
        wu, wd = cx.wu_buf[bi], cx.wd_buf[bi]
        P.dma("pool", [(wu[:, 0:4, :], wu_v[:, 0:4, q * 1024:(q + 1) * 1024]),
                       (wu[:, 4:8, :], wu_v[:, 4:8, q * 1024:(q + 1) * 1024])],
              cx.wu_slot[bi], writes=[("wu", bi)])
        P.dma("pool", [(wd[:, 0:4, :], wd_v[:, q * 8:q * 8 + 4, :]),
                       (wd[:, 4:8, :], wd_v[:, q * 8 + 4:q * 8 + 8, :])],
              cx.wd_slot[bi], writes=[("wd", bi)])
        def mlp_tile(q, tt, bi, wu, wd):
            sl = slice(tt * TT, (tt + 1) * TT)
            ai = cx.a_i % 2
            cx.a_i += 1
            aT = cx.a_buf[ai]
            for f in range(8):
                b, ps = cx.psum.next()
                fns = [(lambda h, kc=kc, f=f, ps=ps, wu=wu: h.matmul(
                    ps[:], wu[:, kc, f * 128:(f + 1) * 128], hT[:, kc, sl], start=(kc == 0), stop=(kc == 7)))
                    for kc in range(8)]
                P.mm_group(fns, reads=[("wu", bi)] + [("h", kc, tt) for kc in range(8)], writes=[("ps", b)])
                if f % 4 == 0:
                    ada_tick(cx)
                k = cx.r_i % 2
                cx.r_i += 1
                rt = cx.relu_tmp[k]
                P.op("act", lambda h, ps=ps, rt=rt: h.activation(rt[:], ps[:], AF.Relu),
                     reads=[("ps", b)], writes=[("ln_sq", k)])
                P.op("pool", lambda h, rt=rt, aT=aT, f=f: h.tensor_tensor(aT[:, f, :], rt[:], rt[:], ALU.mult),
                     reads=[("ln_sq", k)], writes=[("a", ai, f)])
            for oc in range(8):
                b, ps = cx.psum.next()
                fns = [(lambda h, f=f, oc=oc, ps=ps, wd=wd, aT=aT: h.matmul(
                    ps[:], wd[:, f, oc * 128:(oc + 1) * 128], aT[:, f, :], start=(f == 0), stop=(f == 7)))
                    for f in range(8)]
                P.mm_group(fns, reads=[("wd", bi)] + [("a", ai, f) for f in range(8)], writes=[("ps", b)])
                P.op("dve", lambda h, ps=ps, oc=oc: h.scalar_tensor_tensor(
                    xT[:, oc, sl], ps[:], mod[:, 16 + oc:17 + oc], xT[:, oc, sl], ALU.mult, ALU.add),
                    reads=[("ps", b), ("mod", tag), ("x", oc, tt)], writes=[("x", oc, tt)])
        for tt in range(NT):
            mlp_tile(q, tt, bi, wu, wd)
            if q == 3 and tt >= 1:
                emit_ln_p2(P, cx, xT, tt - 1, g_cols, b_cols)
                if after_ln is not None:
                    after_ln(tt - 1)
            if q == 3:
                emit_ln_p1(P, cx, xT, tt)
        if q == 3:
            emit_ln_p2(P, cx, xT, NT - 1, g_cols, b_cols)
            if after_ln is not None:
                after_ln(NT - 1)


def emit_load_xT(P, cx, x_ap, xT, key="x"):
    xv = x_ap.rearrange("(tb p) f -> p tb f", p=128)
    for tt in range(NT):
        si = cx.stg_i % 2
        cx.stg_i += 1
        stg = cx.stg[si]
        P.dma("sp", [(stg[:, 0:2, :], xv[:, tt * 4:tt * 4 + 2, :]),
                     (stg[:, 2:4, :], xv[:, tt * 4 + 2:tt * 4 + 4, :])], cx.stg_slot[si],
              writes=[("stg", si)])
        for c in range(8):
            b, ps = cx.psum.next()
            fns = [(lambda h, tb=tb, c=c, ps=ps, stg=stg: h.matmul(
                ps[:, tb * 128:(tb + 1) * 128], stg[:, tb, c * 128:(c + 1) * 128], cx.ident[:],
                start=True, stop=True)) for tb in range(4)]
            P.mm_group(fns, reads=[("stg", si), "ident"], writes=[("ps", b)])
            ada_tick(cx)
            eng = "act" if c % 2 == 0 else "dve"
            sl = slice(tt * TT, (tt + 1) * TT)
            if eng == "act":
                P.op("act", lambda h, ps=ps, c=c, sl=sl: h.activation(xT[:, c, sl], ps[:], AF.Identity),
                     reads=[("ps", b)], writes=[(key, c, tt)])
            else:
                P.op("dve", lambda h, ps=ps, c=c, sl=sl: h.tensor_copy(xT[:, c, sl], ps[:]),
                     reads=[("ps", b)], writes=[(key, c, tt)])


def emit_store_x(P, cx, xT, out_ap, tt):
    ov = out_ap.rearrange("(tb p) f -> p tb f", p=128)
    for tb in range(4):
        si = cx.ost_i % 2
        cx.ost_i += 1
        stg = cx.ost[si]
        tsl = slice(tt * TT + tb * 128, tt * TT + (tb + 1) * 128)
        for half in range(2):
            b, ps = cx.psum.next()
            fns = [(lambda h, cc=cc, half=half, ps=ps, tsl=tsl: h.matmul(
                ps[:, cc * 128:(cc + 1) * 128], xT[:, half * 4 + cc, tsl], cx.ident[:],
                start=True, stop=True)) for cc in range(4)]
            P.mm_group(fns, reads=[("x", half * 4 + cc, tt) for cc in range(4)] + ["ident"],
                       writes=[("ps", b)])
            if half == 0:
                P.op("act", lambda h, ps=ps, stg=stg: h.activation(stg[:, 0:512], ps[:], AF.Identity),
                     reads=[("ps", b)], writes=[("ost", si, 0)])
            else:
                P.op("dve", lambda h, ps=ps, stg=stg: h.tensor_copy(stg[:, 512:1024], ps[:]),
                     reads=[("ps", b)], writes=[("ost", si, 1)])
        tok = P.dma("sp", [(ov[:, tt * 4 + tb, :], stg[:])], cx.out_slot[si],
                    reads=[("ost", si, 0), ("ost", si, 1)])
        cx.out_toks.append(tok)


def common_setup(nc, P, A, cx, ncols):
    cx.psum = Psum(nc, P)
    cx.cols, _, _ = A.alloc([128, ncols], F32)
    cx.ident_f, _, _ = A.alloc([128, 128], F32)
    cx.ident = cx.ident_f
    cx.onesm, _, _ = A.alloc([128, 128], F32)
    cx.sc, _, _ = A.alloc([128, 8], BF16)
    cx.mod = [A.alloc([128, 24], F32)[0] for _ in range(2)]
    cx.ln_s1, _, _ = A.alloc([128, TT], F32)
    cx.ln_s2, _, _ = A.alloc([128, TT], F32)
    cx.ln_sq = [A.alloc([128, TT], F32)[0] for _ in range(2)]
    cx.ln_mean, _, _ = A.alloc([128, TT], F32)
    cx.ln_rstd, _, _ = A.alloc([128, TT], F32)
    cx.ln_tmp = [A.alloc([128, TT], F32)[0] for _ in range(2)]
    cx.relu_tmp = cx.ln_sq
    cx.ada_buf = [A.alloc([128, 8, 128], BF16)[0] for _ in range(2)]
    cx.ada_slot = [P.slot() for _ in range(2)]
    cx.ada_i = 0
    cx.mlp_i = 0
    cx.a_i = 0
    cx.r_i = 0
    cx.stg_i = 0
    cx.ost_i = 0
    cx.out_toks = []
    cx.out_slot = [P.slot() for _ in range(2)]
    cx.const_slot = P.slot()
    cx.stg_slot = [P.slot() for _ in range(2)]
    cx.wu_slot = [P.slot() for _ in range(2)]
    cx.wd_slot = [P.slot() for _ in range(2)]
    cx.ps_ada = cx.psum.banks[7]


def fence_tokens(P, keys):
    toks = []
    for k in keys:
        t = P.last_w.get(k)
        if t is not None:
            toks.append(t)
        toks.extend(P.readers.get(k, {}).values())
    return toks


A_COLS = dict(c=0, adab0=8, adab1=32, lng0=56, lnb0=64, lng1=72, lnb1=80, binu=88)
A_NCOLS = 96


def body_A(nc, P, A, cx, io, xT, stage=None, fused=False):
    x_in, cols_in, rows_in, bb_in, consts_in = io["x"], io["cols"], io["rows"], io["bbias"], io["constsA"]
    ada_w0, ada_w1, w_in, w_s, w_out = io["ada_w00"], io["ada_w01"], io["a_w_in"], io["a_w_s"], io["a_w_out"]
    w_up, w_down, y_out = io["mlp_w_up0"], io["mlp_w_down0"], io["y"]
    mark = A.off
    win, _, _ = A.alloc([128, 8, 2048], BF16)
    wout, _, _ = A.alloc([128, 8, 1024], BF16)
    rows, _, _ = A.alloc([128, 3, 1024], F32)
    bbias, _, _ = A.alloc([128, 8, 128], F32)
    wcT, _, _ = A.alloc([128, 16, 128], BF16)
    vst, _, _ = A.alloc([128, 2, 6], F32)
    vmv, _, _ = A.alloc([128, 2], F32)
    vrs, _, _ = A.alloc([128, 1], F32)
    mark1b = A.off
    hT1 = [A.alloc([128, 8, TT], BF16)[0] for _ in range(2)]
    uT1, _, _ = A.alloc([128, 8, TT], BF16)
    uT = [uT1, uT1]
    vtok1, _, _ = A.alloc([128, 4, 1024], BF16)
    vtok = [vtok1, vtok1]
    vg = [A.alloc([128, 1024], F32)[0] for _ in range(2)]
    gt = [A.alloc([128, TT], F32)[0] for _ in range(2)]
    end1 = A.off
    A.off = mark1b
    cx.stg = [A.alloc([128, 4, 1024], F32)[0] for _ in range(2)]
    tril, _, _ = A.alloc([128, 128], F32)
    identb, _, _ = A.alloc([128, 128], BF16)
    ws_f, _, _ = A.alloc([128, 16, 128], F32)
    ws_b, _, _ = A.alloc([128, 16, 128], BF16)
    end1 = max(end1, A.off)
    A.off = mark
    h2T, _, _ = A.alloc([128, 8, T], BF16)
    cx.wu_buf = [A.alloc([128, 8, 1024], BF16)[0] for _ in range(2)]
    cx.wd_buf = [A.alloc([128, 8, 1024], BF16)[0] for _ in range(2)]
    cx.a_buf = [A.alloc([128, 8, TT], BF16)[0] for _ in range(2)]
    ost_at = A.off
    cx.ost = [A.alloc([128, 1024], F32)[0] for _ in range(2)]
    end2 = A.off
    hst = A.alloc([128, 8, TT], BF16, at=ost_at)[0] if fused else None
    A.off = max(end1, end2)
    print("SBUF plan A: persistent", mark, "phase1 end", end1, "phase2 end", end2)

    P.dma("sp", [(cx.cols[:], cols_in)], P.slot(), writes=["cols"])
    P.dma("sp", [(cx.ident_f[:], consts_in[:, 0, :]), (cx.onesm[:], consts_in[:, 1, :]),
                 (tril[:], consts_in[:, 2, :])], P.slot(), writes=["ident", "onesm", "tril"])
    P.dma("sp", [(rows[:], rows_in)], P.slot(), writes=["rows"])
    P.dma("sp", [(bbias[:], bb_in)], P.slot(), writes=["bbias"])
    P.dma("sp", [(ws_f[:], w_s.rearrange("g t s -> t g s"))], P.slot(), writes=["ws_f"])
    P.op("act", lambda h: h.activation(cx.sc[:], cx.cols[:, 0:8], AF.Silu), reads=["cols"], writes=["sc"])
    cx.ada_bg = AdaStepper(P, cx, ada_w0, cx.cols[:, 8:32], cx.mod[0], 0)
    wv = w_in.rearrange("(kc p) n -> p kc n", p=128)
    for j in range(4):
        P.dma("pool", [(win[:, :, j * 512:(j + 1) * 512], wv[:, :, j * 512:(j + 1) * 512])], P.slot(),
              writes=[("win", j)])
    wout_slot = P.slot()
    P.dma("pool", [(wout[:], w_out.rearrange("(kc p) n -> p kc n", p=128))], wout_slot, writes=["wout"])
    P.op("dve", lambda h: h.tensor_copy(identb[:], cx.ident_f[:]), reads=["ident"], writes=["identb"])
    P.op("dve", lambda h: h.tensor_tensor(ws_b[:], ws_f[:], tril[:, None, :].to_broadcast([128, 16, 128]), ALU.mult),
         reads=["ws_f", "tril"], writes=["ws_b"])
    for g4 in range(4):
        b, ps = cx.psum.next()
        fns = [(lambda h, gi=gi, g4=g4, ps=ps: h.matmul(
            ps[:, gi * 128:(gi + 1) * 128], ws_b[:, g4 * 4 + gi, :], identb[:], start=True, stop=True))
            for gi in range(4)]
        P.mm_group(fns, reads=["ws_b", "identb"], writes=[("ps", b)])
        P.op("dve", lambda h, g4=g4, ps=ps: h.tensor_copy(
            wcT[:, g4 * 4:(g4 + 1) * 4, :], ps[:].rearrange("p (g t) -> p g t", g=4)),
            reads=[("ps", b)], writes=[("wcT", g4)])
    emit_load_xT(P, cx, x_in, xT)
    if stage == "load":
        for tt in range(NT):
            emit_store_x(P, cx, xT, y_out, tt)
        return
    ada_flush(cx)
    cx.ada_bg = AdaStepper(P, cx, ada_w1, cx.cols[:, 32:56], cx.mod[1], 1)
    f1 = fence_tokens(P, [("stg", 0), ("stg", 1), "tril", "identb", "ws_f", "ws_b"])
    for e_ in ("dve", "act", "pool"):
        P.wait_all(e_, f1)

    def mix_tile(tt):
        sl = slice(tt * TT, (tt + 1) * TT)
        pi = tt % 2
        hT = hT1[pi]
        def modulate(t2):
            p2 = t2 % 2
            sl2 = slice(t2 * TT, (t2 + 1) * TT)
            for c in range(8):
                P.op("dve", lambda h, c=c, p2=p2, sl2=sl2: h.tensor_scalar(
                    hT1[p2][:, c, :], xT[:, c, sl2], cx.mod[0][:, 8 + c:9 + c], cx.mod[0][:, c:c + 1], ALU.mult, ALU.add),
                    reads=[("x", c, t2), ("mod", 0)], writes=[("h1", p2, c)])
        if tt == 0:
            modulate(0)
        for tb in range(4):
            vi = (tt * 4 + tb) % 2
            vgb = vg[vi]
            for half in range(2):
                b, ps = cx.psum.next()
                fns = [(lambda h, kc=kc, half=half, ps=ps, hT=hT, tb=tb: h.matmul(
                    ps[:], hT[:, kc, tb * 128:(tb + 1) * 128],
                    win[:, kc, 1024 + half * 512:1024 + (half + 1) * 512], start=(kc == 0), stop=(kc == 7)))
                    for kc in range(8)]
                P.mm_group(fns, reads=[("win", 2 + half)] + [("h1", pi, kc) for kc in range(8)],
                           writes=[("ps", b)])
                if half == 0:
                    ada_tick(cx)
                P.op("dve", lambda h, ps=ps, half=half, vgb=vgb: h.tensor_tensor(
                    vgb[:, half * 512:(half + 1) * 512], ps[:], rows[:, 0, half * 512:(half + 1) * 512], ALU.add),
                    reads=[("ps", b), "rows"], writes=[("vg", vi, half)])
                P.op("act", lambda h, half=half, vgb=vgb: h.activation(
                    vgb[:, half * 512:(half + 1) * 512], vgb[:, half * 512:(half + 1) * 512], GELU),
                    reads=[("vg", vi, half)], writes=[("vg", vi, half)])
                P.op("dve", lambda h, half=half, vgb=vgb: h.bn_stats(
                    vst[:, half, :], vgb[:, half * 512:(half + 1) * 512]),
                    reads=[("vg", vi, half)], writes=[("vst", half)])
            P.op("dve", lambda h: h.bn_aggr(vmv[:], vst[:].rearrange("p a b -> p (a b)")),
                 reads=[("vst", 0), ("vst", 1)], writes=["vmv"])
            P.op("dve", lambda h: h.tensor_scalar(vrs[:], vmv[:, 1:2], LN_EPS, None, ALU.add),
                 reads=["vmv"], writes=["vrs"])
            P.op("dve", lambda h: h.reciprocal(vrs[:], vrs[:]), reads=["vrs"], writes=["vrs"])
            P.op("act", lambda h: h.activation(vrs[:], vrs[:], AF.Sqrt), reads=["vrs"], writes=["vrs"])
            P.op("dve", lambda h, vgb=vgb: h.tensor_scalar(
                vgb[:], vgb[:], vmv[:, 0:1], vrs[:, 0:1], ALU.subtract, ALU.mult),
                reads=[("vg", vi, 0), ("vg", vi, 1), "vmv", "vrs"], writes=[("vg", vi, 0), ("vg", vi, 1)])
            P.op("pool", lambda h, vgb=vgb: h.tensor_tensor(vgb[:], vgb[:], rows[:, 1, :], ALU.mult),
                 reads=[("vg", vi, 0), ("vg", vi, 1), "rows"], writes=[("vg", vi, 0), ("vg", vi, 1)])
            P.op("pool", lambda h, vgb=vgb, tb=tb: h.tensor_tensor(vtok[pi][:, tb, :], vgb[:], rows[:, 2, :], ALU.add),
                 reads=[("vg", vi, 0), ("vg", vi, 1), "rows"], writes=[("vtok", 0, tb)])
        if tt + 1 < NT:
            modulate(tt + 1)
        for fc in range(8):
            b, ps = cx.psum.next()
            fns = [(lambda h, kc=kc, fc=fc, ps=ps, hT=hT: h.matmul(
                ps[:], win[:, kc, fc * 128:(fc + 1) * 128], hT[:, kc, :], start=(kc == 0), stop=(kc == 7)))
                for kc in range(8)]
            P.mm_group(fns, reads=[("win", fc // 4)] + [("h1", pi, kc) for kc in range(8)], writes=[("ps", b)])
            if fc % 2 == 0:
                ada_tick(cx)
            P.op("act", lambda h, fc=fc, ps=ps: h.activation(
                uT[pi][:, fc, :], ps[:], GELU, bias=cx.cols[:, A_COLS["binu"] + fc:A_COLS["binu"] + fc + 1]),
                reads=[("ps", b), "cols"], writes=[("u", 0, fc)])
        for fc in range(8):
            b, ps = cx.psum.next()
            fns = []
            for tb in range(4):
                for gi in range(2):
                    g = 2 * fc + gi
                    fns.append(lambda h, tb=tb, gi=gi, g=g, ps=ps: h.matmul(
                        ps[gi * 64:(gi + 1) * 64, tb * 128:(tb + 1) * 128],
                        vtok[pi][:, tb, g * 64:(g + 1) * 64], wcT[:, g, :], start=True, stop=True))
            P.mm_group(fns, reads=[("vtok", 0, tb) for tb in range(4)] + [("wcT", (2 * fc) // 4)],
                       writes=[("ps", b)])
            k = fc % 2
            P.op("dve", lambda h, ps=ps, fc=fc, k=k: h.tensor_tensor(
                gt[k][:].rearrange("p (a t) -> p a t", a=4), ps[:].rearrange("p (a t) -> p a t", a=4),
                bbias[:, fc, None, :].to_broadcast([128, 4, 128]), ALU.add),
                reads=[("ps", b), "bbias"], writes=[("gt", k)])
            P.op("pool", lambda h, fc=fc, k=k: h.tensor_tensor(uT[pi][:, fc, :], gt[k][:], uT[pi][:, fc, :], ALU.mult),
                 reads=[("gt", k), ("u", 0, fc)], writes=[("u", 0, fc)])
        if tt >= 1:
            emit_ln_p2(P, cx, xT, tt - 1, cx.cols[:, A_COLS["lng0"]:A_COLS["lng0"] + 8],
                       cx.cols[:, A_COLS["lnb0"]:A_COLS["lnb0"] + 8])
        for oc in range(8):
            b, ps = cx.psum.next()
            fns = [(lambda h, fc=fc, oc=oc, ps=ps: h.matmul(
                ps[:], wout[:, fc, oc * 128:(oc + 1) * 128], uT[pi][:, fc, :], start=(fc == 0), stop=(fc == 7)))
                for fc in range(8)]
            P.mm_group(fns, reads=["wout"] + [("u", 0, fc) for fc in range(8)], writes=[("ps", b)])
            P.op("dve", lambda h, ps=ps, oc=oc: h.scalar_tensor_tensor(
                xT[:, oc, sl], ps[:], cx.mod[0][:, 16 + oc:17 + oc], xT[:, oc, sl], ALU.mult, ALU.add),
                reads=[("ps", b), ("mod", 0), ("x", oc, tt)], writes=[("x", oc, tt)])
        emit_ln_p1(P, cx, xT, tt)

    for tt in range(NT):
        mix_tile(tt)
    emit_ln_p2(P, cx, xT, NT - 1, cx.cols[:, A_COLS["lng0"]:A_COLS["lng0"] + 8],
               cx.cols[:, A_COLS["lnb0"]:A_COLS["lnb0"] + 8])

    if stage == "mixA":
        dbg = nc.dram_tensor("dbg", [128, 16384], F32, kind="ExternalOutput").ap()
        dsl = P.slot()
        allk = list(P.last_w.keys())
        cx.out_toks.append(P.dma("sp", [(dbg[:, 0:24], cx.mod[0][:])], dsl, reads=allk))
        cx.out_toks.append(P.dma("pool", [(dbg[:, 1024:3072], wcT[:].rearrange("p g t -> p (g t)"))], dsl, reads=allk))
        cx.out_toks.append(P.dma("pool", [(dbg[:, 4096:8192], hT1[1][:].rearrange("p g t -> p (g t)"))], dsl, reads=allk))
        cx.out_toks.append(P.dma("pool", [(dbg[:, 8192:12288], uT1[:].rearrange("p g t -> p (g t)"))], dsl, reads=allk))
        cx.out_toks.append(P.dma("pool", [(dbg[:, 12288:16384], vtok1[:].rearrange("p g t -> p (g t)"))], dsl, reads=allk))
        for e_ in ("pe", "act", "dve", "pool"):
            P.wait_all(e_, list(cx.out_toks))
        for tt in range(NT):
            emit_store_x(P, cx, xT, y_out, tt)
        return
    ada_flush(cx)
    if fused:
        cx.ada_bg = AdaStepper(P, cx, io["ada_w10"], cx.cols[:, A_NCOLS + 8:A_NCOLS + 32], cx.mod[0], 0)
        cx.pre_ada_b0 = True
    p1_keys = [("win", j) for j in range(4)] + ["wout", "rows", "bbias", "vst", "vmv", "vrs"] + \
              [("wcT", g4) for g4 in range(4)] + [("h1", p, c) for p in range(2) for c in range(8)] + \
              [("u", 0, c) for c in range(8)] + [("vtok", 0, tb) for tb in range(4)] + \
              [("vg", i, hf) for i in range(2) for hf in range(2)] + [("gt", 0), ("gt", 1)]
    f2 = fence_tokens(P, p1_keys)
    for e_ in ("dve", "act", "pool"):
        P.wait_all(e_, f2)
    for tt in range(NT):
        emit_modulate(P, cx, xT, h2T, cx.mod[1], 1, tt)

    def after_ln(tt):
        emit_store_x(P, cx, xT, y_out, tt)

    if fused:
        cx.cc_sem = P.new_sem()
        snd_slot = P.slot()
        cx.spill_slot = P.slot()
        cx.spill_toks = []

    def send_tile(tt):
        ada_flush(cx)
        modb = cx.mod[0]
        sl = slice(tt * TT, (tt + 1) * TT)
        for c in range(8):
            if c % 2:
                P.op("act", lambda h, c=c: h.activation(hst[:, c, :], xT[:, c, sl], AF.Identity,
                                                        bias=modb[:, c:c + 1], scale=modb[:, 8 + c:9 + c]),
                     reads=[("x", c, tt), ("mod", 0)], writes=[("hst", c)])
            else:
                P.op("dve", lambda h, c=c: h.tensor_scalar(hst[:, c, :], xT[:, c, sl], modb[:, 8 + c:9 + c],
                                                           modb[:, c:c + 1], ALU.mult, ALU.add),
                     reads=[("x", c, tt), ("mod", 0)], writes=[("hst", c)])
        snd, rcv = io["snd"], io["rcv"]
        t_snd = P.dma("sp", [(snd[tt * 2 + hf], hst[:, hf * 4:(hf + 1) * 4, :].rearrange("p c t -> p (c t)"))
                             for hf in range(2)], snd_slot, reads=[("hst", c) for c in range(8)])
        xsp_ = io["xsp"]
        cx.spill_toks.append(P.dma("sp", [(xsp_[:, c * T + tt * TT:c * T + (tt + 1) * TT], xT[:, c, sl]) for c in range(8)],
                                   cx.spill_slot, reads=[("x", c, tt) for c in range(8)]))
        ep = P.E["pool"]
        w_ = P._deps("pool", (), (), [t_snd])
        for hf in range(2):
            ep.ops.append((w_ if hf == 0 else [],
                           (lambda h, i=tt * 2 + hf: h.collective_compute(
                               "AllGather", ALU.bypass, replica_groups=[[0, 1], [2, 3], [4, 5], [6, 7]],
                               ins=[snd[i].opt()], outs=[rcv[i].opt()])),
                           (cx.cc_sem, 1)))

    emit_mlp(P, cx, xT, h2T, cx.mod[1], 1, w_up, w_down,
             cx.cols[:, A_COLS["lng1"]:A_COLS["lng1"] + 8], cx.cols[:, A_COLS["lnb1"]:A_COLS["lnb1"] + 8],
             after_ln=(send_tile if fused else after_ln))
    ada_flush(cx)


def build_A(stage=None):
    nc = bass.Bass("TRN2", target_bir_lowering=False)
    dt = lambda n, s: nc.dram_tensor(n, s, F32, kind="ExternalInput").ap()
    x_in = dt("x", [T, D])
    cols_in = dt("cols", [128, A_NCOLS])
    rows_in = dt("rows", [128, 3, 1024])
    bb_in = dt("bbias", [128, 8, 128])
    consts_in = dt("consts", [128, 3, 128])
    ada_w0 = dt("ada_w0", [D, 3 * D])
    ada_w1 = dt("ada_w1", [D, 3 * D])
    w_in = dt("a_w_in", [D, 2 * D])
    w_s = dt("a_w_s", [16, 128, 128])
    w_out = dt("a_w_out", [D, D])
    w_up = dt("mlp_w_up", [D, 4 * D])
    w_down = dt("mlp_w_down", [4 * D, D])
    y_out = nc.dram_tensor("y", [T, D], F32, kind="ExternalOutput").ap()

    with contextlib.ExitStack() as es:
        P = Prog(nc, es)
        A = Arena(nc)
        cx = Ctx()
        common_setup(nc, P, A, cx, A_NCOLS)
        xT, _, _ = A.alloc([128, 8, T], F32)
        io = dict(x=x_in, cols=cols_in, rows=rows_in, bbias=bb_in, constsA=consts_in, ada_w00=ada_w0, ada_w01=ada_w1,
                  a_w_in=w_in, a_w_s=w_s, a_w_out=w_out, mlp_w_up0=w_up, mlp_w_down0=w_down, y=y_out)
        body_A(nc, P, A, cx, io, xT, stage=stage, fused=False)
        P.wait_all("sp", cx.out_toks)
        P.finish()
    return nc


def host_consts():
    ident = np.eye(128, dtype=np.float32)
    ones = np.full((128, 128), 1.0 / 1024.0, dtype=np.float32)
    tril = np.tril(np.ones((128, 128), dtype=np.float32))
    return np.ascontiguousarray(np.stack([ident, ones, tril], axis=1))


def prep_A(inp):
    x = np.asarray(inp["x"], dtype=np.float32)
    c = np.asarray(inp["c"], dtype=np.float32)
    maps = []
    consts = host_consts()
    rows = np.stack([inp["a_b_in"][0][1024:], inp["a_vn_g"][0], inp["a_vn_b"][0]], axis=0).astype(np.float32)
    rows = np.ascontiguousarray(np.broadcast_to(rows[None], (128, 3, 1024)))
    bs = np.asarray(inp["a_b_s"][0], dtype=np.float32)
    bb = np.zeros((128, 8, 128), dtype=np.float32)
    for fc in range(8):
        bb[0:64, fc, :] = bs[2 * fc][None, :]
        bb[64:128, fc, :] = bs[2 * fc + 1][None, :]
    for core in range(NCORES):
        b, half = core // 2, core % 2
        cols = np.zeros((128, A_NCOLS), dtype=np.float32)
        cols[:, 0:8] = col(c[b])
        cols[:, 8:32] = col(inp["ada_b"][0, 0])
        cols[:, 32:56] = col(inp["ada_b"][0, 1])
        cols[:, 56:64] = col(inp["ln_g"][0, 0])
        cols[:, 64:72] = col(inp["ln_b"][0, 0])
        cols[:, 72:80] = col(inp["ln_g"][0, 1])
        cols[:, 80:88] = col(inp["ln_b"][0, 1])
        cols[:, 88:96] = col(inp["a_b_in"][0][:1024])
        maps.append({
            "x": np.ascontiguousarray(x[b, half * T:(half + 1) * T]),
            "cols": cols, "rows": rows, "bbias": bb, "consts": consts,
            "ada_w0": np.ascontiguousarray(inp["ada_w"][0, 0]), "ada_w1": np.ascontiguousarray(inp["ada_w"][0, 1]),
            "a_w_in": np.ascontiguousarray(inp["a_w_in"][0]), "a_w_s": np.ascontiguousarray(inp["a_w_s"][0]),
            "a_w_out": np.ascontiguousarray(inp["a_w_out"][0]),
            "mlp_w_up": np.ascontiguousarray(inp["mlp_w_up"][0]), "mlp_w_down": np.ascontiguousarray(inp["mlp_w_down"][0]),
        })
    return maps


def run_A(inp, trace=False, stage=None, ncores=NCORES):
    nc = build_A(stage)
    maps = prep_A(inp)[:ncores]
    res = run_bass_kernel_spmd(nc, maps, core_ids=list(range(ncores)), trace=trace)
    x1 = np.zeros((4, 4096, D), dtype=np.float32)
    for core in range(ncores):
        b, half = core // 2, core % 2
        x1[b, half * T:(half + 1) * T] = res.results[core]["y"]
    if stage == "mixA":
        np.save("dbg0.npy", res.results[0]["dbg"])
    return x1, res


B_NCOLS = 96
PATTERNS = ((128, 1), (512, 4), (2048, 16))


def host_consts_B():
    ident = np.eye(128, dtype=np.float32)
    ones = np.full((128, 128), 1.0 / 1024.0, dtype=np.float32)
    k = np.arange(128)[:, None]
    q = np.arange(128)[None, :]
    d_prev = np.where(k >= q, 128.0 + q - k, 0.0).astype(np.float32)
    d_cur = np.where(k <= q, (q - k) * 1.0, 0.0).astype(np.float32)
    m_prev = np.where(k >= q, 0.0, -240000.0).astype(np.float32)
    m_cur = np.where(k <= q, 0.0, -240000.0).astype(np.float32)
    m = np.arange(128)[None, :]
    permA = ((k == m + 64) & (m < 64)).astype(np.float32)
    permB = ((k == m - 64) & (m >= 64)).astype(np.float32)
    return np.ascontiguousarray(np.stack([ident, ones, d_prev, d_cur, permA, permB, m_prev, m_cur], axis=1))


def body_B(nc, P, A, cx, io, xT, xT_at, stage=None, fused=False, cb=0):
    x_in, xp_in, cols_in, consts_in = io.get("x"), io.get("xp"), io["cols"], io["constsB"]
    ada_w0, ada_w1, w_qkv, w_out = io["ada_w10"], io["ada_w11"], io["b_w_qkv"], io["b_w_out"]
    w_up, w_down, y_out, xsp = io["mlp_w_up1"], io["mlp_w_down1"], io["y"], io["xsp"]
    mark = xT_at
    r1_end = xT_at + 65536
    save_off = A.off
    A.off = mark
    kt0, _, _ = A.alloc([128, 32 * 128], BF16)
    KT = [kt0, kt0]
    VT, _, _ = A.alloc([128, 32 * 128], BF16)
    VA, _, _ = A.alloc([128, 32, 2, 128], BF16)
    ACC, _, _ = A.alloc([128, 2, T], F32)
    QT = [A.alloc([128, T], BF16)[0] for _ in range(2)]
    tmpB = [A.alloc([128, 256], F32)[0] for _ in range(2)]
    Pt = [A.alloc([128, 512], BF16)[0] for _ in range(2)]
    assert A.off <= r1_end, (A.off, r1_end)
    A.off = max(r1_end, save_off)
    mark2 = A.off
    hT, _, _ = A.alloc([128, 8, 2 * T], BF16)
    oT, oT_at, _ = A.alloc([128, 8, T], BF16)
    wqkv = [A.alloc([128, 3, 8, 128], BF16)[0] for _ in range(2)]
    Bhl = [A.alloc([128, 2, 2, 256], BF16)[0] for _ in range(2)]
    rden, _, _ = A.alloc([128, 512], F32)
    identb, _, _ = A.alloc([128, 128], BF16)
    diffm, _, _ = A.alloc([128, 2, 128], F32)
    perm, _, _ = A.alloc([128, 2, 128], F32)
    mask01, _, _ = A.alloc([128, 2, 128], F32)
    endB1 = A.off
    A.off = oT_at
    cx.stg = [A.alloc([128, 4, 1024], F32)[0] for _ in range(2)]
    assert A.off <= oT_at + 32768
    A.off = mark2
    wout, _, _ = A.alloc([128, 8, 1024], BF16)
    A.off = mark2
    h2T, _, _ = A.alloc([128, 8, T], BF16)
    cx.wu_buf = [A.alloc([128, 8, 1024], BF16)[0] for _ in range(2)]
    cx.wd_buf = [A.alloc([128, 8, 1024], BF16)[0] for _ in range(2)]
    cx.a_buf = [A.alloc([128, 8, TT], BF16)[0] for _ in range(2)]
    cx.ost = [A.alloc([128, 1024], F32)[0] for _ in range(2)]
    endB2 = A.off
    print("SBUF plan B: mark", mark, "r1_end", r1_end, "attn end", endB1, "mlp end", endB2)
    assert max(endB1, endB2) <= A.cap

    flag = cx.cols[:, cb + 88:cb + 89]
    if not fused:
        P.dma("sp", [(cx.cols[:], cols_in)], P.slot(), writes=["cols"])
        P.dma("sp", [(cx.ident_f[:], consts_in[:, 0, :]), (cx.onesm[:], consts_in[:, 1, :])], P.slot(),
              writes=["ident", "onesm"])
        P.op("act", lambda h: h.activation(cx.sc[:], cx.cols[:, 0:8], AF.Silu), reads=["cols"], writes=["sc"])
    P.dma("sp", [(diffm[:], consts_in[:, 2:4, :]), (perm[:], consts_in[:, 4:6, :]),
                 (mask01[:], consts_in[:, 6:8, :])], P.slot(), writes=["diffm", "perm"])
    P.op("dve", lambda h: h.tensor_copy(identb[:], cx.ident_f[:]), reads=["ident"], writes=["identb"])
    if not getattr(cx, "pre_ada_b0", False):
        cx.ada_bg = AdaStepper(P, cx, ada_w0, cx.cols[:, cb + 8:cb + 32], cx.mod[0], 0)
        ada_flush(cx)
    mod0 = cx.mod[0]
    cx.ada_bg = AdaStepper(P, cx, ada_w1, cx.cols[:, cb + 32:cb + 56], cx.mod[1], 1)

    xpv = xp_in.rearrange("(tb p) f -> p tb f", p=128) if xp_in is not None else None

    def load_prev_tile(tt):
        si = cx.stg_i % 2
        cx.stg_i += 1
        stg = cx.stg[si]
        P.dma("sp", [(stg[:, 0:2, :], xpv[:, tt * 4:tt * 4 + 2, :]),
                     (stg[:, 2:4, :], xpv[:, tt * 4 + 2:tt * 4 + 4, :])], cx.stg_slot[si],
              writes=[("stg", si)])
        for c in range(8):
            b, ps = cx.psum.next()
            fns = [(lambda h, tb=tb, c=c, ps=ps, stg=stg: h.matmul(
                ps[:, tb * 128:(tb + 1) * 128], stg[:, tb, c * 128:(c + 1) * 128], cx.ident[:],
                start=True, stop=True)) for tb in range(4)]
            P.mm_group(fns, reads=[("stg", si), "ident"], writes=[("ps", b)])
            P.op("dve", lambda h, ps=ps, c=c, tt=tt: h.tensor_scalar(
                hT[:, c, tt * TT:(tt + 1) * TT], ps[:], mod0[:, 8 + c:9 + c], mod0[:, c:c + 1], ALU.mult, ALU.add),
                reads=[("ps", b), ("mod", 0)], writes=[("hall", c, tt)])

    if not fused:
        for tt in range(NT):
            load_prev_tile(tt)
        emit_load_xT(P, cx, x_in, xT)

    def mod_own(tt):
        for c in range(8):
            P.op("act" if c % 2 else "dve",
                 (lambda h, c=c, tt=tt: h.activation(hT[:, c, T + tt * TT:T + (tt + 1) * TT], xT[:, c, tt * TT:(tt + 1) * TT],
                                                     AF.Identity, bias=mod0[:, c:c + 1], scale=mod0[:, 8 + c:9 + c]))
                 if c % 2 else
                 (lambda h, c=c, tt=tt: h.tensor_scalar(hT[:, c, T + tt * TT:T + (tt + 1) * TT], xT[:, c, tt * TT:(tt + 1) * TT],
                                                        mod0[:, 8 + c:9 + c], mod0[:, c:c + 1], ALU.mult, ALU.add)),
                 reads=[("x", c, tt), ("mod", 0)], writes=[("hall", c, NT + tt)])

    for tt in range(NT):
        mod_own(tt)
    if fused:
        rcv = io["rcv"]
        r_slot = P.slot()
        P.dma("sp", [(hT[:, hf * 4:(hf + 1) * 4, tt * TT:(tt + 1) * TT],
                      rcv[tt * 2 + hf][0:128, :].rearrange("p (c t) -> p c t", c=4))
                     for tt in range(NT) for hf in range(2)], r_slot,
              writes=[("hall", c, tt) for c in range(8) for tt in range(NT)], extra=[(cx.cc_sem, 8)])
    sp_slot = P.slot()
    xkeys = [("x", c, tt) for c in range(8) for tt in range(NT)]
    if fused:
        t_spills = list(cx.spill_toks)
    else:
        t_spills = [P.dma("sp", [(xsp[:, c * T:(c + 1) * T], xT[:, c, :]) for c in range(8)], sp_slot, reads=xkeys)]
    fsp = fence_tokens(P, xkeys) + t_spills
    for e_ in ("pe", "act", "dve", "pool"):
        P.wait_all(e_, fsp)
    P.op("pool", lambda h: h.memset(VA[:], 1.0), writes=["va_init"])
    P.op("act", lambda h: h.activation(VA[:, 0:16, 0, 64:128], VA[:, 0:16, 0, 64:128], AF.Identity, scale=flag),
         reads=["va_init", "cols"], writes=["va_init"])
    P.op("act", lambda h: h.activation(VA[:, 0:16, 1, 0:64], VA[:, 0:16, 1, 0:64], AF.Identity, scale=flag),
         reads=["va_init", "cols"], writes=["va_init"])
    va_tok = P.last_w["va_init"]

    wq_v = w_qkv.rearrange("(kc p) n -> p kc n", p=128)
    wq_slot = [P.slot() for _ in range(2)]
    st = {"i": 0, "e": 0, "p": 0}
    hall_keys_all = [("hall", c, t8) for c in range(8) for t8 in range(2 * NT)]

    def perm_view(buf2d, col0, d, tt, ps):
        if d == 1:
            return buf2d[:, col0 + tt * 512:col0 + (tt + 1) * 512], ps[:]
        if d == 4:
            dst = buf2d[:, col0 + tt * 512:col0 + (tt + 1) * 512].rearrange("p (r i) -> p i r", r=4)
            return dst, ps[:].rearrange("p (i r) -> p i r", r=4)
        dst = buf2d[:, col0:col0 + 2048].rearrange("p (r i) -> p i r", r=16)[:, tt * 32:(tt + 1) * 32, :]
        return dst, ps[:].rearrange("p (i r) -> p i r", r=16)

    import os as _os2
    _skip = set(_os2.environ.get("DBG_SKIP", "").split(","))

    def stage_prologue(hp, g, si):
        d = PATTERNS[g][1]
        wb = wqkv[si]
        bh = Bhl[si]
        pairs = []
        for t3 in range(3):
            c0 = ((g * 3 + t3) * 16 + 2 * hp) * 64
            pairs.append((wb[:, t3, :, :], wq_v[:, :, c0:c0 + 128]))
        P.dma("pool", pairs, wq_slot[si], writes=[("wqkv", si)])
        for hh in range(2):
            slope = 2.0 ** (-8.0 * (2 * hp + hh + 1) / 16.0)
            tb = tmpB[hh]
            dm = diffm[:].rearrange("p a q -> p (a q)")
            mk = mask01[:].rearrange("p a q -> p (a q)")
            P.op("pool", lambda h, tb=tb, slope=slope: h.tensor_scalar(tb[:], dm, -8.0 * slope * d, None, ALU.mult),
                 reads=["diffm"], writes=[("tmpB", hh)])
            P.op("pool", lambda h, tb=tb: h.tensor_tensor(tb[:], tb[:], mk, ALU.add),
                 reads=["diffm", ("tmpB", hh)], writes=[("tmpB", hh)])
            P.op("pool", lambda h, tb=tb, hh=hh: h.tensor_copy(bh[:, 0, hh, :], tb[:]),
                 reads=[("tmpB", hh)], writes=[("bhl", si, hh, 0)])
            P.op("pool", lambda h, tb=tb, hh=hh: h.tensor_tensor(bh[:, 1, hh, :], tb[:], bh[:, 0, hh, :], ALU.subtract),
                 reads=[("tmpB", hh), ("bhl", si, hh, 0)], writes=[("bhl", si, hh, 1)])

    def attn_stage(hp, g, si):
        d = PATTERNS[g][1]
        wb = wqkv[si]
        kt = KT[si]
        qt = QT[si]
        bh = Bhl[si]
        for tt in range(0 if "Q" in _skip else NT):
            b, ps = cx.psum.next()
            fns = [(lambda h, kc=kc, ps=ps, tt=tt: h.matmul(
                ps[:], wb[:, 0, kc, :], hT[:, kc, T + tt * TT:T + (tt + 1) * TT], start=(kc == 0), stop=(kc == 7)))
                for kc in range(8)]
            P.mm_group(fns, reads=[("wqkv", si)] + [("hall", kc, NT + tt) for kc in range(8)], writes=[("ps", b)])
            if tt % 2 == 0:
                ada_tick(cx)
            dst, src = perm_view(qt, 0, d, tt, ps)
            P.op("act", lambda h, dst=dst, src=src: h.activation(dst, src, AF.Identity),
                 reads=[("ps", b)], writes=[("qt", si)])
        for tt in range(0 if "K" in _skip else NT):
            b, ps = cx.psum.next()
            fns = [(lambda h, kc=kc, ps=ps, tt=tt: h.matmul(
                ps[:], wb[:, 1, kc, :], hT[:, kc, T + tt * TT:T + (tt + 1) * TT], start=(kc == 0), stop=(kc == 7)))
                for kc in range(8)]
            P.mm_group(fns, reads=[("wqkv", si)] + [("hall", kc, NT + tt) for kc in range(8)], writes=[("ps", b)])
            dst, src = perm_view(kt, 2048, d, tt, ps)
            P.op("dve", lambda h, dst=dst, src=src: h.tensor_copy(dst, src),
                 reads=[("ps", b)], writes=[("kt", 0)])
        if "K" in _skip:
            pass
        elif d == 1:
            b, ps = cx.psum.next()
            fns = [(lambda h, kc=kc, ps=ps: h.matmul(
                ps[:, 0:128], wb[:, 1, kc, :], hT[:, kc, T - 128:T], start=(kc == 0), stop=(kc == 7)))
                for kc in range(8)]
            P.mm_group(fns, reads=[("wqkv", si)] + [("hall", kc, NT - 1) for kc in range(8)], writes=[("ps", b)])
            P.op("dve", lambda h, ps=ps: h.tensor_copy(kt[:, 15 * 128:16 * 128], ps[:, 0:128]),
                 reads=[("ps", b)], writes=[("kt", 0)])
        elif d == 4:
            b, ps = cx.psum.next()
            fns = [(lambda h, kc=kc, ps=ps: h.matmul(
                ps[:], wb[:, 1, kc, :], hT[:, kc, T - 512:T], start=(kc == 0), stop=(kc == 7)))
                for kc in range(8)]
            P.mm_group(fns, reads=[("wqkv", si)] + [("hall", kc, NT - 1) for kc in range(8)], writes=[("ps", b)])
            dst = kt[:, 12 * 128:16 * 128].rearrange("p (r i) -> p i r", r=4)
            src = ps[:].rearrange("p (i r) -> p i r", r=4)
            P.op("dve", lambda h, dst=dst, src=src: h.tensor_copy(dst, src),
                 reads=[("ps", b)], writes=[("kt", 0)])
        else:
            for tt in range(NT):
                b, ps = cx.psum.next()
                fns = [(lambda h, kc=kc, ps=ps, tt=tt: h.matmul(
                    ps[:], wb[:, 1, kc, :], hT[:, kc, tt * TT:(tt + 1) * TT], start=(kc == 0), stop=(kc == 7)))
                    for kc in range(8)]
                P.mm_group(fns, reads=[("wqkv", si)] + [("hall", kc, tt) for kc in range(8)], writes=[("ps", b)])
                dst, src = perm_view(kt, 0, d, tt, ps)
                P.op("dve", lambda h, dst=dst, src=src: h.tensor_copy(dst, src),
                     reads=[("ps", b)], writes=[("kt", 0)])
        def vproj(tok0, ntok, col0, mode, tt, hkeys):
            b, ps = cx.psum.next()
            fns = [(lambda h, kc=kc, ps=ps: h.matmul(
                ps[:, 0:ntok], wb[:, 2, kc, :], hT[:, kc, tok0:tok0 + ntok], start=(kc == 0), stop=(kc == 7)))
                for kc in range(8)]
            P.mm_group(fns, reads=[("wqkv", si)] + hkeys, writes=[("ps", b)])
            if mode == "plain":
                dst, src = VT[:, col0:col0 + ntok], ps[:, 0:ntok]
            elif mode == "seg4":
                dst = VT[:, col0:col0 + 512].rearrange("p (r i) -> p i r", r=4)
                src = ps[:].rearrange("p (i r) -> p i r", r=4)
            else:
                dst, src = perm_view(VT, col0, d, tt, ps)
            P.op("act", lambda h, dst=dst, src=src: h.activation(dst, src, AF.Identity),
                 reads=[("ps", b)], writes=["vt"])

        for tt in range(0 if "V" in _skip else NT):
            hk = [("hall", kc, NT + tt) for kc in range(8)]
            if d == 1:
                vproj(T + tt * TT, TT, 2048 + tt * 512, "plain", tt, hk)
            elif d == 4:
                vproj(T + tt * TT, TT, 2048 + tt * 512, "seg4", tt, hk)
            else:
                vproj(T + tt * TT, TT, 2048, "perm", tt, hk)
        if "V" in _skip:
            pass
        elif d == 1:
            vproj(T - 128, 128, 15 * 128, "plain", 0, [("hall", kc, NT - 1) for kc in range(8)])
        elif d == 4:
            vproj(T - 512, 512, 12 * 128, "seg4", 0, [("hall", kc, NT - 1) for kc in range(8)])
        else:
            for tt in range(NT):
                vproj(tt * TT, TT, 0, "perm", tt, [("hall", kc, tt) for kc in range(8)])
        groups = []
        prev_slots = list(range(16 - d, 16))
        own_slots = list(range(16, 32))
        for i0 in range(0, len(prev_slots), 4):
            groups.append(prev_slots[i0:i0 + 4])
        for i0 in range(0, 16, 4):
            groups.append(own_slots[i0:i0 + 4])
        if "T" in _skip or "V" in _skip:
            groups = []
        for grp in groups:
            b, ps = cx.psum.next()
            fns = [(lambda h, gi=gi, s=s, ps=ps: h.matmul(
                ps[:, gi * 128:(gi + 1) * 128], VT[:, s * 128:(s + 1) * 128], identb[:], start=True, stop=True))
                for gi, s in enumerate(grp)]
            P.mm_group(fns, reads=["vt", "identb"], writes=[("ps", b)], extra=[va_tok])
            ng = len(grp)
            s0 = grp[0]
            srcv = ps[:, 0:ng * 128].rearrange("p (s c) -> p s c", c=128)
            if "X" in _skip:
                continue
            if "XP" in _skip and s0 < 16:
                continue
            if "XO" in _skip and s0 >= 16:
                continue
            if "ALTVA" in _skip:
                s0_ = min(grp[0], 30)
                P.op("dve", lambda h, ps=ps, s0_=s0_: h.tensor_copy(
                    VA[:, s0_:s0_ + 2, :, :].rearrange("p a b c -> p (a b c)"), ps[:]),
                    reads=[("ps", b)], writes=[("va", 0)])
                continue
            if "F2D" in _skip:
                for gi, sl_ in enumerate(grp):
                    if "NO20" in _skip and sl_ == 20:
                        continue
                    P.op("act", lambda h, gi=gi, sl_=sl_, ps=ps: h.activation(
                        VA[:, sl_, 0, :], ps[:, gi * 128:(gi + 1) * 128], AF.Identity),
                        reads=[("ps", b)], writes=[("va", 0)])
                    P.op("dve", lambda h, gi=gi, sl_=sl_, ps=ps: h.tensor_copy(
                        VA[:, sl_, 1, :], ps[:, gi * 128:(gi + 1) * 128]),
                        reads=[("ps", b)], writes=[("va", 1)])
                continue
            if "FULL" in _skip:
                P.op("act", lambda h, srcv=srcv, s0=s0, ng=ng: h.activation(
                    VA[:, s0:s0 + ng, 0, :], srcv, AF.Identity),
                    reads=[("ps", b)], writes=[("va", 0)])
                P.op("dve", lambda h, srcv=srcv, s0=s0, ng=ng: h.tensor_copy(
                    VA[:, s0:s0 + ng, 1, :], srcv),
                    reads=[("ps", b)], writes=[("va", 1)])
                continue
            if s0 < 16:
                P.op("act", lambda h, srcv=srcv, s0=s0, ng=ng: h.activation(
                    VA[:, s0:s0 + ng, 0, 0:64], srcv[:, :, 0:64], AF.Identity, scale=flag),
                    reads=[("ps", b), "cols"], writes=[("va", 0)])
                P.op("dve", lambda h, srcv=srcv, s0=s0, ng=ng: h.tensor_scalar(
                    VA[:, s0:s0 + ng, 1, 64:128], srcv[:, :, 64:128], flag, None, ALU.mult),
                    reads=[("ps", b), "cols"], writes=[("va", 1)])
            else:
                P.op("act", lambda h, srcv=srcv, s0=s0, ng=ng: h.activation(
                    VA[:, s0:s0 + ng, 0, 0:64], srcv[:, :, 0:64], AF.Identity),
                    reads=[("ps", b)], writes=[("va", 0)])
                P.op("dve", lambda h, srcv=srcv, s0=s0, ng=ng: h.tensor_copy(
                    VA[:, s0:s0 + ng, 1, 64:128], srcv[:, :, 64:128]),
                    reads=[("ps", b)], writes=[("va", 1)])
        nun = 0 if stage == 'proj' else 16
        ust = {}

        def unit_front(j):
            pi = st["p"] % 2
            st["p"] += 1
            pt = Pt[pi]
            banks = [cx.psum.next() for _ in range(2)]
            fns = []
            for hh in range(2):
                ps = banks[hh][1]
                fns.append(lambda h, ps=ps, hh=hh: h.matmul(ps[:, 0:256], identb[:], bh[:, 0, hh, :], start=True, stop=False))
                fns.append(lambda h, ps=ps, hh=hh: h.matmul(ps[:, 0:256], identb[:], bh[:, 1, hh, :], start=False, stop=False))
            for kb in range(2):
                slot = 16 + j - d if kb == 0 else 16 + j
                for hh in range(2):
                    ps = banks[hh][1]
                    fns.append(lambda h, hh=hh, kb=kb, slot=slot, ps=ps, j=j: h.matmul(
                        ps[:, kb * 128:(kb + 1) * 128],
                        kt[hh * 64:(hh + 1) * 64, slot * 128:(slot + 1) * 128],
                        qt[hh * 64:(hh + 1) * 64, j * 128:(j + 1) * 128], start=False, stop=(kb == 1)))
            P.mm_group(fns, reads=[("kt", 0), ("qt", si), "identb"] + [("bhl", si, hh, z) for hh in range(2) for z in range(2)],
                       writes=[("ps", banks[0][0]), ("ps", banks[1][0])])
            for hh in range(2):
                b, ps = banks[hh]
                P.op("act", lambda h, ps=ps, pt=pt, hh=hh: h.activation(
                    pt[:, hh * 256:(hh + 1) * 256], ps[:, 0:256], AF.Exp, scale=0.125),
                    reads=[("ps", b)], writes=[("pt", pi, hh)])
            ust[j] = (pi, pt)

        def unit_back(j):
            pi, pt = ust[j]
            n, r = j // d, j % d
            b2, ps2 = cx.psum.next()
            fns = []
            for hh in range(2):
                for kb in range(2):
                    slot = 16 + j - d if kb == 0 else 16 + j
                    fns.append(lambda h, hh=hh, kb=kb, slot=slot, ps2=ps2, pt=pt: h.matmul(
                        ps2[:, hh * 128:(hh + 1) * 128], VA[:, slot, hh, :],
                        pt[:, (hh * 2 + kb) * 128:(hh * 2 + kb + 1) * 128], start=(kb == 0), stop=(kb == 1)))
            P.mm_group(fns, reads=[("pt", pi, 0), ("pt", pi, 1), ("va", 0), ("va", 1)], writes=[("ps", b2)])
            start = n * 128 * d + r
            accv = ACC[:, :, start:start + 127 * d + 1:d]
            src2 = ps2[:, 0:256].rearrange("p (a q) -> p a q", a=2)
            wr = ["acc"] if j == nun - 1 else []
            if g == 0:
                P.op("dve", lambda h, accv=accv, src2=src2: h.tensor_copy(accv, src2),
                     reads=[("ps", b2)], writes=wr, extra=acc_dep)
            else:
                P.op("dve", lambda h, accv=accv, src2=src2: h.tensor_tensor(accv, src2, accv, ALU.add),
                     reads=[("ps", b2)], writes=wr, extra=acc_dep)

        acc_dep = fence_tokens(P, ["acc"])
        if nun:
            unit_front(0)
        for j in range(nun):
            if j + 1 < nun:
                unit_front(j + 1)
            unit_back(j)

    def finalize(hp):
        for tt in range(NT):
            sl = slice(tt * TT, (tt + 1) * TT)
            b, ps = cx.psum.next()
            fns = [lambda h, ps=ps, sl=sl: h.matmul(ps[:], perm[:, 0, :], ACC[:, 0, sl], start=True, stop=False),
                   lambda h, ps=ps, sl=sl: h.matmul(ps[:], perm[:, 1, :], ACC[:, 1, sl], start=False, stop=True)]
            P.mm_group(fns, reads=["acc", "perm"], writes=[("ps", b)])
            P.op("dve", lambda h, ps=ps: h.reciprocal(rden[:], ps[:]), reads=[("ps", b)], writes=["rden"])
            P.op("pool", lambda h, sl=sl: h.tensor_tensor(oT[0:64, hp, sl], ACC[0:64, 0, sl], rden[0:64, :], ALU.mult),
                 reads=["acc", "rden"], writes=[("o", hp, tt, 0)])
            P.op("pool", lambda h, sl=sl: h.tensor_tensor(oT[64:128, hp, sl], ACC[64:128, 1, sl], rden[64:128, :], ALU.mult),
                 reads=["acc", "rden"], writes=[("o", hp, tt, 1)])

    P.wait_all("pool", fence_tokens(P, [("stg", 0), ("stg", 1)]))
    nhp = {"pre": 0, "one": 1, "proj": 1}.get(stage, 8)
    import os as _os
    _gl = [int(x) for x in _os.environ.get("DBG_G", "0,1,2").split(",")]
    stages = [(hp, g) for hp in range(nhp) for g in _gl]
    if stages:
        stage_prologue(stages[0][0], stages[0][1], 0)
    for i_s, (hp, g) in enumerate(stages):
        if i_s + 1 < len(stages):
            stage_prologue(stages[i_s + 1][0], stages[i_s + 1][1], (i_s + 1) % 2)
        attn_stage(hp, g, i_s % 2)
        if stage != "proj" and g == _gl[-1]:
            finalize(hp)
    if stage in ("pre", "one", "proj"):
        fx = fence_tokens(P, ["vt", ("kt", 0), ("kt", 1), ("qt", 0), ("qt", 1), ("va", 0), ("va", 1), "acc", "va_init",
                              ("tmpB", 0), ("tmpB", 1), ("pt", 0, 0), ("pt", 0, 1), ("pt", 1, 0), ("pt", 1, 1)])
        P.wait_all("sp", fx)
        t_re = P.dma("sp", [(xT[:, c, :], xsp[:, c * T:(c + 1) * T]) for c in range(8)], P.slot(), writes=xkeys)
        for e_ in ("pe", "act", "dve", "pool"):
            P.wait_all(e_, fence_tokens(P, hall_keys_all + [("o", hp, tt, hf) for hp in range(nhp) for tt in range(NT) for hf in range(2)] + [("wqkv", 0), ("wqkv", 1)] + [("bhl", i, hh, z) for i in range(2) for hh in range(2) for z in range(2)]))
        for tt in range(NT):
            emit_store_x(P, cx, xT, y_out, tt)
        return

    att_keys = ["vt", ("kt", 0), ("kt", 1), ("qt", 0), ("qt", 1), ("va", 0), ("va", 1), "acc", "va_init",
                ("tmpB", 0), ("tmpB", 1), ("pt", 0, 0), ("pt", 0, 1), ("pt", 1, 0), ("pt", 1, 1)]
    fx = fence_tokens(P, att_keys)
    P.wait_all("sp", fx)
    for tt in range(NT):
        P.dma("sp", [(xT[:, c, tt * TT:(tt + 1) * TT], xsp[:, c * T + tt * TT:c * T + (tt + 1) * TT]) for c in range(8)],
              P.slot(), writes=[("x", c, tt) for c in range(8)])
    fo = fence_tokens(P, hall_keys_all + [("wqkv", 0), ("wqkv", 1)])
    P.wait_all("pool", fo)
    P.dma("pool", [(wout[:], w_out.rearrange("(kc p) n -> p kc n", p=128))], P.slot(), writes=["wout"])

    def outproj_tile(tt):
        sl = slice(tt * TT, (tt + 1) * TT)
        for oc in range(8):
            b, ps = cx.psum.next()
            fns = [(lambda h, hp=hp, oc=oc, ps=ps: h.matmul(
                ps[:], wout[:, hp, oc * 128:(oc + 1) * 128], oT[:, hp, sl], start=(hp == 0), stop=(hp == 7)))
                for hp in range(8)]
            P.mm_group(fns, reads=["wout"] + [("o", hp, tt, hf) for hp in range(8) for hf in range(2)],
                       writes=[("ps", b)])
            P.op("dve", lambda h, ps=ps, oc=oc: h.scalar_tensor_tensor(
                xT[:, oc, sl], ps[:], mod0[:, 16 + oc:17 + oc], xT[:, oc, sl], ALU.mult, ALU.add),
                reads=[("ps", b), ("mod", 0), ("x", oc, tt)], writes=[("x", oc, tt)])
    for tt in range(NT):
        outproj_tile(tt)
        if tt >= 1:
            emit_ln_p2(P, cx, xT, tt - 1, cx.cols[:, cb + 56:cb + 64], cx.cols[:, cb + 64:cb + 72])
        emit_ln_p1(P, cx, xT, tt)
    emit_ln_p2(P, cx, xT, NT - 1, cx.cols[:, cb + 56:cb + 64], cx.cols[:, cb + 64:cb + 72])

    if stage == "attn":
        for e_ in ("pe", "act", "dve", "pool"):
            P.wait_all(e_, fence_tokens(P, ["wout"] + [("o", hp, tt, hf) for hp in range(8) for tt in range(NT) for hf in range(2)]))
        for tt in range(NT):
            emit_store_x(P, cx, xT, y_out, tt)
        return

    ada_flush(cx)
    f2 = fence_tokens(P, ["wout"] + [("o", hp, tt, hf) for hp in range(8) for tt in range(NT) for hf in range(2)]
                      + [("bhl", i, hh, z) for i in range(2) for hh in range(2) for z in range(2)] + ["rden", "diffm", "perm", "identb", "vt"])
    for e_ in ("dve", "act", "pool"):
        P.wait_all(e_, f2)
    for tt in range(NT):
        emit_modulate(P, cx, xT, h2T, cx.mod[1], 1, tt)

    def after_ln(tt):
        emit_store_x(P, cx, xT, y_out, tt)

    emit_mlp(P, cx, xT, h2T, cx.mod[1], 1, w_up, w_down, cx.cols[:, cb + 72:cb + 80], cx.cols[:, cb + 80:cb + 88], after_ln=after_ln)


def build_B(stage=None):
    nc = bass.Bass("TRN2", target_bir_lowering=False)
    dt = lambda n, s: nc.dram_tensor(n, s, F32, kind="ExternalInput").ap()
    x_in = dt("x", [T, D])
    xp_in = dt("xp", [T, D])
    cols_in = dt("cols", [128, B_NCOLS])
    consts_in = dt("consts", [128, 8, 128])
    ada_w0 = dt("ada_w0", [D, 3 * D])
    ada_w1 = dt("ada_w1", [D, 3 * D])
    w_qkv = dt("b_w_qkv", [D, 9 * D])
    w_out = dt("b_w_out", [D, D])
    w_up = dt("mlp_w_up", [D, 4 * D])
    w_down = dt("mlp_w_down", [4 * D, D])
    y_out = nc.dram_tensor("y", [T, D], F32, kind="ExternalOutput").ap()
    xsp = nc.dram_tensor("xsp", [128, 8 * T], F32).ap()

    with contextlib.ExitStack() as es:
        P = Prog(nc, es)
        A = Arena(nc)
        cx = Ctx()
        common_setup(nc, P, A, cx, B_NCOLS)
        xT, xT_at, _ = A.alloc([128, 8, T], F32)
        io = dict(x=x_in, xp=xp_in, cols=cols_in, constsB=consts_in, ada_w10=ada_w0, ada_w11=ada_w1, b_w_qkv=w_qkv,
                  b_w_out=w_out, mlp_w_up1=w_up, mlp_w_down1=w_down, y=y_out, xsp=xsp)
        body_B(nc, P, A, cx, io, xT, xT_at, stage=stage, fused=False, cb=0)
        P.wait_all("sp", cx.out_toks)
        P.finish()
    return nc


def prep_B(inp, x1):
    c = np.asarray(inp["c"], dtype=np.float32)
    consts = host_consts_B()
    maps = []
    zeros = np.zeros((T, D), dtype=np.float32)
    for core in range(NCORES):
        b, half = core // 2, core % 2
        cols = np.zeros((128, B_NCOLS), dtype=np.float32)
        cols[:, 0:8] = col(c[b])
        cols[:, 8:32] = col(inp["ada_b"][1, 0])
        cols[:, 32:56] = col(inp["ada_b"][1, 1])
        cols[:, 56:64] = col(inp["ln_g"][1, 0])
        cols[:, 64:72] = col(inp["ln_b"][1, 0])
        cols[:, 72:80] = col(inp["ln_g"][1, 1])
        cols[:, 80:88] = col(inp["ln_b"][1, 1])
        cols[:, 88] = float(half)
        maps.append({
            "x": np.ascontiguousarray(x1[b, half * T:(half + 1) * T]) if x1 is not None else None,
            "xp": (np.ascontiguousarray(x1[b, 0:T]) if half == 1 else zeros) if x1 is not None else None,
            "cols": cols, "consts": consts,
            "ada_w0": np.ascontiguousarray(inp["ada_w"][1, 0]), "ada_w1": np.ascontiguousarray(inp["ada_w"][1, 1]),
            "b_w_qkv": np.ascontiguousarray(inp["b_w_qkv"][0]), "b_w_out": np.ascontiguousarray(inp["b_w_out"][0]),
            "mlp_w_up": np.ascontiguousarray(inp["mlp_w_up"][1]), "mlp_w_down": np.ascontiguousarray(inp["mlp_w_down"][1]),
        })
    return maps


def run_B(inp, x1, trace=False, stage=None, ncores=NCORES):
    nc = build_B(stage)
    maps = prep_B(inp, x1)[:ncores]
    res = run_bass_kernel_spmd(nc, maps, core_ids=list(range(ncores)), trace=trace)
    out = np.zeros((4, 4096, D), dtype=np.float32)
    for core in range(ncores):
        b, half = core // 2, core % 2
        out[b, half * T:(half + 1) * T] = res.results[core]["y"]
    return out, res


F_NCOLS = A_NCOLS + B_NCOLS


def build_F():
    nc = bass.Bass("TRN2", target_bir_lowering=False)
    dt = lambda n, s: nc.dram_tensor(n, s, F32, kind="ExternalInput").ap()
    io = dict(
        x=dt("x", [T, D]), cols=dt("cols", [128, F_NCOLS]), rows=dt("rows", [128, 3, 1024]),
        bbias=dt("bbias", [128, 8, 128]), constsA=dt("constsA", [128, 3, 128]), constsB=dt("constsB", [128, 8, 128]),
        ada_w00=dt("ada_w00", [D, 3 * D]), ada_w01=dt("ada_w01", [D, 3 * D]),
        ada_w10=dt("ada_w10", [D, 3 * D]), ada_w11=dt("ada_w11", [D, 3 * D]),
        a_w_in=dt("a_w_in", [D, 2 * D]), a_w_s=dt("a_w_s", [16, 128, 128]), a_w_out=dt("a_w_out", [D, D]),
        mlp_w_up0=dt("mlp_w_up0", [D, 4 * D]), mlp_w_down0=dt("mlp_w_down0", [4 * D, D]),
        mlp_w_up1=dt("mlp_w_up1", [D, 4 * D]), mlp_w_down1=dt("mlp_w_down1", [4 * D, D]),
        b_w_qkv=dt("b_w_qkv", [D, 9 * D]), b_w_out=dt("b_w_out", [D, D]))
    io["y"] = nc.dram_tensor("y", [T, D], F32, kind="ExternalOutput").ap()
    io["xsp"] = nc.dram_tensor("xsp", [128, 8 * T], F32).ap()
    io["snd"] = [nc.dram_tensor(f"snd{c}", [128, T], BF16).ap() for c in range(8)]
    io["rcv"] = [nc.dram_tensor(f"rcv{c}", [256, T], BF16).ap() for c in range(8)]
    with contextlib.ExitStack() as es:
        P = Prog(nc, es)
        A = Arena(nc)
        cx = Ctx()
        common_setup(nc, P, A, cx, F_NCOLS)
        xT, xT_at, _ = A.alloc([128, 8, T], F32)
        mark = A.off
        body_A(nc, P, A, cx, io, xT, stage=None, fused=True)
        P.barrier()
        A.off = mark
        body_B(nc, P, A, cx, io, xT, xT_at, stage=None, fused=True, cb=A_NCOLS)
        P.wait_all("sp", cx.out_toks)
        P.finish()
    return nc


def prep_F(inp):
    mA = prep_A(inp)
    mB = prep_B(inp, None)
    maps = []
    for core in range(NCORES):
        a, b = mA[core], mB[core]
        maps.append({
            "x": a["x"], "cols": np.ascontiguousarray(np.concatenate([a["cols"], b["cols"]], axis=1)),
            "rows": a["rows"], "bbias": a["bbias"], "constsA": a["consts"], "constsB": b["consts"],
            "ada_w00": a["ada_w0"], "ada_w01": a["ada_w1"], "ada_w10": b["ada_w0"], "ada_w11": b["ada_w1"],
            "a_w_in": a["a_w_in"], "a_w_s": a["a_w_s"], "a_w_out": a["a_w_out"],
            "mlp_w_up0": a["mlp_w_up"], "mlp_w_down0": a["mlp_w_down"],
            "mlp_w_up1": b["mlp_w_up"], "mlp_w_down1": b["mlp_w_down"],
            "b_w_qkv": b["b_w_qkv"], "b_w_out": b["b_w_out"],
        })
    return maps


def run_F(inp, trace=False, ncores=NCORES):
    nc = build_F()
    maps = prep_F(inp)[:ncores]
    res = run_bass_kernel_spmd(nc, maps, core_ids=list(range(ncores)), trace=trace)
    out = np.zeros((4, 4096, D), dtype=np.float32)
    for core in range(ncores):
        b, half = core // 2, core % 2
        out[b, half * T:(half + 1) * T] = res.results[core]["y"]
    return out, res


def kernel(**inputs):
    inp = {k: np.asarray(v) for k, v in inputs.items()}
    out, _ = run_F(inp)
    return out
```
